# Optimizing a Trainium2 kernel written in Bass

```python
import math
import jax, jax.numpy as jnp
from jax import lax
import numpy as np


D_MODEL = 1024
BATCH = 8
SEQ = 4096
DEPTH = 2
DEC_BATCH = 16
DEC_SEQ = 4096
PAST_LEN = 128

GRID_W = 64
QBLK = 128
N_MEM = 256
EPS = 1e-6
ROPE_THETA = 10000.0
A_HEADS = 8
A_KV_HEADS = 2
A_GROUP = A_HEADS // A_KV_HEADS
A_DIM = 64
B_HEADS = 4
B_DIM = 64
B_VDIM = 2 * B_DIM
A_Q = A_HEADS * A_DIM
A_KV = A_KV_HEADS * A_DIM
B_QK = B_HEADS * 2 * B_DIM
B_V = B_HEADS * B_VDIM
EVEN_IN = A_Q + 2 * A_KV + 2 * B_QK + B_V
EVEN_MIX = A_HEADS * A_DIM + B_HEADS * B_VDIM
C_HEADS = 16
C_Q_RANK = 384
C_KV_RANK = 256
C_NOPE = 64
C_ROPE = 32
C_VDIM = 64
ODD_IN = C_Q_RANK + C_KV_RANK + C_ROPE
C_MIX = C_HEADS * C_VDIM
X_HEADS = 4
X_DIM = D_MODEL // X_HEADS
D_FF = ((-(-8 * D_MODEL // 3) + 255) // 256) * 256
N_EVEN = (DEPTH + 1) // 2
N_ODD = DEPTH // 2

kernel_name = 'hybrid_gqa_diff_mla_encoder'


def rms_norm(x, g):
    xf = x.astype(jnp.float32)
    y = xf * lax.rsqrt(jnp.mean(xf * xf, axis=-1, keepdims=True) + EPS)
    return (y * g.astype(jnp.float32)).astype(x.dtype)


def rope(x, ang):
    extra = x.ndim - 3
    c = jnp.cos(ang).reshape(ang.shape[0], *([1] * extra), -1).astype(x.dtype)
    s = jnp.sin(ang).reshape(ang.shape[0], *([1] * extra), -1).astype(x.dtype)
    x2 = x.reshape(*x.shape[:-1], -1, 2)
    x0, x1 = x2[..., 0], x2[..., 1]
    return jnp.stack([x0 * c - x1 * s, x0 * s + x1 * c], axis=-1).reshape(x.shape)


def rope_freqs(n_pairs):
    return ROPE_THETA ** (-jnp.arange(n_pairs, dtype=jnp.float32) / n_pairs)


def axial_angles(S):
    rows = S // GRID_W
    r = jnp.repeat(jnp.arange(rows, dtype=jnp.float32), GRID_W)
    c = jnp.tile(jnp.arange(GRID_W, dtype=jnp.float32), rows)
    f = rope_freqs(A_DIM // 4)
    return jnp.concatenate([r[:, None] * f, c[:, None] * f], axis=-1)


def linear_angles(S, dim):
    t = jnp.arange(S, dtype=jnp.float32)
    return t[:, None] * rope_freqs(dim // 2)


def alibi_slopes(n):
    return jnp.asarray(2.0 ** (-8.0 * np.arange(1, n + 1) / n), dtype=jnp.float32)


def query_block_sweep(fn, qs, S):
    nb = S // QBLK
    xs = tuple(jnp.swapaxes(q.reshape(q.shape[0], nb, QBLK, *q.shape[2:]), 0, 1) for q in qs)
    starts = jnp.arange(nb, dtype=jnp.int32) * QBLK
    out = lax.map(lambda a: fn(*a[0], a[1]), (xs, starts))
    return jnp.swapaxes(out, 0, 1).reshape(qs[0].shape[0], S, *out.shape[3:])


def gqa_axial(qa, ka, va, gq, gk, ang):
    B, S = qa.shape[:2]
    q = rope(rms_norm(qa.reshape(B, S, A_KV_HEADS, A_GROUP, A_DIM), gq), ang)
    k = rope(rms_norm(ka.reshape(B, S, A_KV_HEADS, A_DIM), gk), ang)
    v = va.reshape(B, S, A_KV_HEADS, A_DIM)
    scale = A_DIM ** -0.5

    def blk(qb, start):
        s = jnp.einsum('bqhgd,bkhd->bhgqk', qb, k).astype(jnp.float32) * scale
        p = jax.nn.softmax(s, axis=-1).astype(v.dtype)
        return jnp.einsum('bhgqk,bkhd->bqhgd', p, v)

    o = query_block_sweep(blk, (q,), S)
    return o.reshape(B, S, A_HEADS * A_DIM)


def diff_attn(qb_in, kb_in, vb_in, lq1, lk1, lq2, lk2, g_sub, lam_init):
    B, S = qb_in.shape[:2]
    q = qb_in.reshape(B, S, B_HEADS, 2, B_DIM)
    k = kb_in.reshape(B, S, B_HEADS, 2, B_DIM)
    v = vb_in.reshape(B, S, B_HEADS, B_VDIM)
    f32 = jnp.float32
    lam = (jnp.exp(jnp.sum(lq1.astype(f32) * lk1.astype(f32)))
           - jnp.exp(jnp.sum(lq2.astype(f32) * lk2.astype(f32))) + lam_init)
    slopes = alibi_slopes(B_HEADS)
    kpos = jnp.arange(S, dtype=f32)
    scale = B_DIM ** -0.5

    def blk(qblk, start):
        s = jnp.einsum('bqhcd,bkhcd->bhcqk', qblk, k).astype(f32) * scale
        qpos = start.astype(f32) + jnp.arange(QBLK, dtype=f32)
        bias = -slopes[:, None, None] * jnp.abs(qpos[:, None] - kpos[None, :])
        p = jax.nn.softmax(s + bias[None, :, None], axis=-1)
        a = (p[:, :, 0] - lam * p[:, :, 1]).astype(v.dtype)
        return jnp.einsum('bhqk,bkhe->bqhe', a, v)

    o = query_block_sweep(blk, (q,), S)
    o = rms_norm(o, g_sub) * (1.0 - lam_init)
    return o.reshape(B, S, B_V)


def mla(h, w_in, g_q, g_kv, w_uq, w_ukv, ang):
    B, S = h.shape[:2]
    a = h @ w_in
    cq, ckv, kr = jnp.split(a, [C_Q_RANK, C_Q_RANK + C_KV_RANK], axis=-1)
    q = (rms_norm(cq, g_q) @ w_uq).reshape(B, S, C_HEADS, C_NOPE + C_ROPE)
    qn, qr = q[..., :C_NOPE], rope(q[..., C_NOPE:], ang)
    kv = (rms_norm(ckv, g_kv) @ w_ukv).reshape(B, S, C_HEADS, C_NOPE + C_VDIM)
    kn, v = kv[..., :C_NOPE], kv[..., C_NOPE:]
    kr = rope(kr, ang)
    scale = (C_NOPE + C_ROPE) ** -0.5

    def blk(qnb, qrb, start):
        s = (jnp.einsum('bqhd,bkhd->bhqk', qnb, kn)
             + jnp.einsum('bqhd,bkd->bhqk', qrb, kr)).astype(jnp.float32) * scale
        p = jax.nn.softmax(s, axis=-1).astype(v.dtype)
        return jnp.einsum('bhqk,bkhd->bqhd', p, v)

    o = query_block_sweep(blk, (qn, qr), S)
    return o.reshape(B, S, C_MIX)


def memory_cross_attn(h, mem_n, w_q, w_kv, w_o):
    B, S = h.shape[:2]
    M = mem_n.shape[1]
    q = (h @ w_q).reshape(B, S, X_HEADS, X_DIM)
    kv = (mem_n @ w_kv).reshape(B, M, 2, X_HEADS, X_DIM)
    s = jnp.einsum('bqhd,bkhd->bhqk', q, kv[:, :, 0]).astype(jnp.float32) * (X_DIM ** -0.5)
    p = jax.nn.softmax(s, axis=-1).astype(h.dtype)
    o = jnp.einsum('bhqk,bkhd->bqhd', p, kv[:, :, 1]).reshape(B, S, X_HEADS * X_DIM)
    return o @ w_o


def swiglu(h, w_gu, w_down):
    g, u = jnp.split(h @ w_gu, 2, axis=-1)
    return (jax.nn.silu(g) * u) @ w_down


def trunk(x, mem, norm_mix, e_w_in, e_q_norm, e_k_norm, e_lam_q1, e_lam_k1, e_lam_q2,
          e_lam_k2, e_subln, e_w_out, o_w_in, o_q_norm, o_kv_norm, o_w_uq, o_w_ukv,
          o_w_out, norm_cross, norm_mem, w_cq, w_ckv, w_co, norm_ffn, w_gu, w_down,
          final_norm):
    S = x.shape[1]
    ang_axial = axial_angles(S)
    ang_lin = linear_angles(S, C_ROPE)
    splits = [A_Q, A_Q + A_KV, A_Q + 2 * A_KV, A_Q + 2 * A_KV + B_QK, A_Q + 2 * A_KV + 2 * B_QK]
    for layer in range(DEPTH):
        h = rms_norm(x, norm_mix[layer])
        if layer % 2 == 0:
            e = layer // 2
            qa, ka, va, qb, kb, vb = jnp.split(h @ e_w_in[e], splits, axis=-1)
            oa = gqa_axial(qa, ka, va, e_q_norm[e], e_k_norm[e], ang_axial)
            lam_init = 0.8 - 0.6 * math.exp(-0.3 * layer)
            ob = diff_attn(qb, kb, vb, e_lam_q1[e], e_lam_k1[e], e_lam_q2[e], e_lam_k2[e],
                           e_subln[e], lam_init)
            x = x + jnp.concatenate([oa, ob], axis=-1) @ e_w_out[e]
        else:
            o = layer // 2
            x = x + mla(h, o_w_in[o], o_q_norm[o], o_kv_norm[o], o_w_uq[o], o_w_ukv[o],
                        ang_lin) @ o_w_out[o]
        x = x + memory_cross_attn(rms_norm(x, norm_cross[layer]), rms_norm(mem, norm_mem[layer]),
                                  w_cq[layer], w_ckv[layer], w_co[layer])
        x = x + swiglu(rms_norm(x, norm_ffn[layer]), w_gu[layer], w_down[layer])
    return rms_norm(x, final_norm)


def setup_inputs(seed: int = 0) -> dict:
    key = jax.random.key(seed)
    ks = jax.random.split(key, 32)
    f32 = jnp.float32

    def w(k, shape, fan_in):
        return jax.random.normal(k, shape, f32) * (fan_in ** -0.5)

    def gain(k, shape):
        return 1.0 + 0.02 * jax.random.normal(k, shape, f32)

    def small(k, shape):
        return 0.1 * jax.random.normal(k, shape, f32)

    D = D_MODEL
    return {
        'x_prompt': jax.random.normal(ks[0], (BATCH, SEQ, D), f32),
        'x_sample': jax.random.normal(ks[1], (DEC_BATCH, DEC_SEQ, D), f32),
        'mem_prompt': jax.random.normal(ks[2], (BATCH, N_MEM, D), f32),
        'mem_sample': jax.random.normal(ks[3], (DEC_BATCH, N_MEM, D), f32),
        'norm_mix': gain(ks[4], (DEPTH, D)),
        'e_w_in': w(ks[5], (N_EVEN, D, EVEN_IN), D),
        'e_q_norm': gain(ks[6], (N_EVEN, A_DIM)),
        'e_k_norm': gain(ks[7], (N_EVEN, A_DIM)),
        'e_lam_q1': small(ks[8], (N_EVEN, B_DIM)),
        'e_lam_k1': small(ks[9], (N_EVEN, B_DIM)),
        'e_lam_q2': small(ks[10], (N_EVEN, B_DIM)),
        'e_lam_k2': small(ks[11], (N_EVEN, B_DIM)),
        'e_subln': gain(ks[12], (N_EVEN, B_VDIM)),
        'e_w_out': w(ks[13], (N_EVEN, EVEN_MIX, D), EVEN_MIX),
        'o_w_in': w(ks[14], (N_ODD, D, ODD_IN), D),
        'o_q_norm': gain(ks[15], (N_ODD, C_Q_RANK)),
        'o_kv_norm': gain(ks[16], (N_ODD, C_KV_RANK)),
        'o_w_uq': w(ks[17], (N_ODD, C_Q_RANK, C_HEADS * (C_NOPE + C_ROPE)), C_Q_RANK),
        'o_w_ukv': w(ks[18], (N_ODD, C_KV_RANK, C_HEADS * (C_NOPE + C_VDIM)), C_KV_RANK),
        'o_w_out': w(ks[19], (N_ODD, C_MIX, D), C_MIX),
        'norm_cross': gain(ks[20], (DEPTH, D)),
        'norm_mem': gain(ks[21], (DEPTH, D)),
        'w_cq': w(ks[22], (DEPTH, D, X_HEADS * X_DIM), D),
        'w_ckv': w(ks[23], (DEPTH, D, 2 * X_HEADS * X_DIM), D),
        'w_co': w(ks[24], (DEPTH, X_HEADS * X_DIM, D), X_HEADS * X_DIM),
        'norm_ffn': gain(ks[25], (DEPTH, D)),
        'w_gu': w(ks[26], (DEPTH, D, 2 * D_FF), D),
        'w_down': w(ks[27], (DEPTH, D_FF, D), D_FF),
        'final_norm': gain(ks[28], (D,)),
    }


def reference(x_prompt, x_sample, mem_prompt, mem_sample, norm_mix, e_w_in, e_q_norm,
              e_k_norm, e_lam_q1, e_lam_k1, e_lam_q2, e_lam_k2, e_subln, e_w_out, o_w_in,
              o_q_norm, o_kv_norm, o_w_uq, o_w_ukv, o_w_out, norm_cross, norm_mem, w_cq,
              w_ckv, w_co, norm_ffn, w_gu, w_down, final_norm):
    y_prompt = trunk(x_prompt, mem_prompt, norm_mix, e_w_in, e_q_norm, e_k_norm, e_lam_q1,
                     e_lam_k1, e_lam_q2, e_lam_k2, e_subln, e_w_out, o_w_in, o_q_norm,
                     o_kv_norm, o_w_uq, o_w_ukv, o_w_out, norm_cross, norm_mem, w_cq, w_ckv,
                     w_co, norm_ffn, w_gu, w_down, final_norm)
    y_sample = trunk(x_sample, mem_sample, norm_mix, e_w_in, e_q_norm, e_k_norm, e_lam_q1,
                     e_lam_k1, e_lam_q2, e_lam_k2, e_subln, e_w_out, o_w_in, o_q_norm,
                     o_kv_norm, o_w_uq, o_w_ukv, o_w_out, norm_cross, norm_mem, w_cq, w_ckv,
                     w_co, norm_ffn, w_gu, w_down, final_norm)
    return (y_prompt, y_sample)
```

```python
import math
from contextlib import ExitStack

import ml_dtypes
import numpy as np

import concourse.bass as bass
import concourse.mybir as mybir
from concourse.bass_utils import run_bass_kernel_spmd

F32 = mybir.dt.float32
BF16 = mybir.dt.bfloat16
ALU = mybir.AluOpType
AF = mybir.ActivationFunctionType
AX = mybir.AxisListType

D = 1024
DFF = 2816
NMEM = 256
EPS = 1e-6
GRID_W = 64
N_CORES = 8

ENGS = ["pe", "act", "dve", "pool", "sp"]
N_DMA_SEMS = 24


class Buf:
    __slots__ = ("name", "w", "r", "excl")

    def __init__(self, name, excl=False):
        self.name = name
        self.w = None
        self.r = []
        self.excl = excl


class Op:
    __slots__ = ("eng", "fn", "deps", "signal", "sem", "val", "dma", "waits")

    def __init__(self, eng, fn, dma):
        self.eng = eng
        self.fn = fn
        self.dma = dma
        self.deps = set()
        self.signal = False
        self.sem = None
        self.val = 0
        self.waits = []


class Plan:
    def __init__(self):
        self.ops = {e: [] for e in ENGS}
        self.all = []
        self.dma_rr = {"sp": 0, "pool": 0, "act": 0}
        self.dma_last = [None] * N_DMA_SEMS
        self.nbuf = 0

    def buf(self, name=None, excl=False):
        self.nbuf += 1
        return Buf(name or f"b{self.nbuf}", excl)

    def op(self, eng, fn, reads=(), writes=(), dma=False, after=()):
        o = Op(eng, fn, dma)
        deps = o.deps
        for b in reads:
            if b.w is not None:
                deps.add(b.w)
            if b.excl:
                for q in b.r:
                    if q.eng != eng:
                        deps.add(q)
        for b in writes:
            if b.w is not None:
                deps.add(b.w)
            deps.update(b.r)
        for a in after:
            if a is not None:
                deps.add(a)
        if dma:
            half = N_DMA_SEMS // 2
            k = (self.dma_rr[eng] % half) + (half if eng == "pool" else 0)
            self.dma_rr[eng] += 1
            prev = self.dma_last[k]
            if prev is not None:
                deps.add(prev)
            self.dma_last[k] = o
            o.sem = ("dma", k)
        else:
            o.sem = ("eng", eng)
        for b in reads:
            if not dma:
                b.r = [q for q in b.r if q.dma or q.eng != eng]
            b.r.append(o)
        for b in writes:
            b.w = o
            b.r = []
        self.ops[eng].append(o)
        self.all.append(o)
        return o

    def barrier(self):
        lasts = []
        for e in ENGS:
            if self.ops[e]:
                lasts.append(self.ops[e][-1])
        lasts += [d for d in self.dma_last if d is not None]
        for e in ENGS:
            self.op(e, None, after=lasts)

    @staticmethod
    def _skip(d, o):
        return (not d.dma) and (not o.dma) and d.eng == "pe" and o.eng == "pe"

    def finalize(self):
        for e in ENGS:
            for o in self.ops[e]:
                for d in o.deps:
                    if d.dma or self._skip(d, o):
                        continue
                    d.signal = True
        cnt = {}
        for o in self.all:
            if o.dma:
                cnt[o.sem] = cnt.get(o.sem, 0) + 16
                o.val = cnt[o.sem]
                o.signal = True
            elif o.signal:
                if o.fn is None:
                    o.signal = False
                    continue
                cnt[o.sem] = cnt.get(o.sem, 0) + 1
                o.val = cnt[o.sem]
        for e in ENGS:
            waited = {}
            for o in self.ops[e]:
                need = {}
                for d in o.deps:
                    if self._skip(d, o) or d is o:
                        continue
                    if d.fn is None:
                        continue
                    v = need.get(d.sem, 0)
                    if d.val > v:
                        need[d.sem] = d.val
                for s, v in need.items():
                    if waited.get(s, 0) < v:
                        waited[s] = v
                        o.waits.append((s, v))
        self.counts = cnt

    def emit(self, nc, stack):
        sems = {}
        for e in ENGS:
            sems[("eng", e)] = stack.enter_context(nc.semaphore(f"s_{e}"))
        for k in range(N_DMA_SEMS):
            sems[("dma", k)] = stack.enter_context(nc.semaphore(f"s_dma{k}"))
        block = stack.enter_context(nc.Block())
        plan = self

        def replay(ename):
            def run(h):
                for o in plan.ops[ename]:
                    for s, v in o.waits:
                        h.wait_ge(sems[s], v)
                    if o.fn is None:
                        continue
                    inst = o.fn(h)
                    if o.signal:
                        inst.then_inc(sems[o.sem], 16 if o.dma else 1)
            return run

        block.tensor(replay("pe"))
        block.scalar(replay("act"))
        block.vector(replay("dve"))
        block.gpsimd(replay("pool"))
        block.sync(replay("sp"))


def MM(out, lhsT, rhs, start=True, stop=True):
    return lambda e: e.matmul(out, lhsT=lhsT, rhs=rhs, start=start, stop=stop)


def TR(out, in_, ident):
    return lambda e: e.transpose(out=out, in_=in_, identity=ident)


def ACT(out, in_, func, bias=None, scale=None, accum=None):
    kw = {}
    if bias is not None:
        kw["bias"] = bias
    if scale is not None:
        kw["scale"] = scale
    if accum is not None:
        kw["accum_out"] = accum
    return lambda e: e.activation(out=out, in_=in_, func=func, **kw)


def DMA(out, in_, slow=False):
    if slow:
        return lambda e: e.dma_start(out=out, in_=in_, allow_slow_non_contiguous=True)
    return lambda e: e.dma_start(out=out, in_=in_)


def TS(out, in0, s1, s2=None, op0=ALU.mult, op1=None):
    if op1 is None:
        return lambda e: e.tensor_scalar(out=out, in0=in0, scalar1=s1, scalar2=None, op0=op0)
    return lambda e: e.tensor_scalar(out=out, in0=in0, scalar1=s1, scalar2=s2, op0=op0, op1=op1)


def TT(out, in0, in1, op):
    return lambda e: e.tensor_tensor(out=out, in0=in0, in1=in1, op=op)


def STT(out, in0, scalar, in1, op0, op1):
    return lambda e: e.scalar_tensor_tensor(out=out, in0=in0, scalar=scalar, in1=in1, op0=op0, op1=op1)


def CP(out, in_):
    return lambda e: e.tensor_copy(out=out, in_=in_)


def ACP(out, in_):
    return lambda e: e.copy(out=out, in_=in_)


def RECIP(out, in_):
    return lambda e: e.reciprocal(out=out, in_=in_)


def RSUM(out, in_):
    return lambda e: e.tensor_reduce(out=out, in_=in_, axis=AX.X, op=ALU.add)


def MSET(ap, v):
    return lambda e: e.memset(ap, v)


class Arena:
    def __init__(self, t, n):
        self.t = t
        self.n = n
        self.off = 0

    def reset(self):
        self.off = 0

    def alloc(self, shape):
        n = 1
        for s in shape[1:]:
            n *= s
        n = (n + 15) // 16 * 16
        o = self.off
        self.off += n
        assert self.off <= self.n, (self.off, self.n, shape)
        m = 1
        for s in shape[1:]:
            m *= s
        v = self.t[0:shape[0], o:o + m]
        if len(shape) == 3:
            v = v.rearrange("p (a b) -> p a b", b=shape[2])
        elif len(shape) == 4:
            v = v.rearrange("p (a b c) -> p a b c", b=shape[2], c=shape[3])
        return v


AB_N = 74752
AF_N = 6400

WNAMES = ["norm_mix", "e_w_in", "e_q_norm", "e_k_norm", "e_lam_q1", "e_lam_k1", "e_lam_q2", "e_lam_k2",
          "e_subln", "e_w_out", "o_w_in", "o_q_norm", "o_kv_norm", "o_w_uq", "o_w_ukv", "o_w_out",
          "norm_cross", "norm_mem", "w_cq", "w_ckv", "w_co", "norm_ffn", "w_gu", "w_down", "final_norm"]
WSHAPES = {
    "norm_mix": [2, D], "e_w_in": [1, D, 2304], "e_q_norm": [1, 64], "e_k_norm": [1, 64],
    "e_lam_q1": [1, 64], "e_lam_k1": [1, 64], "e_lam_q2": [1, 64], "e_lam_k2": [1, 64],
    "e_subln": [1, 128], "e_w_out": [1, D, D], "o_w_in": [1, D, 672], "o_q_norm": [1, 384],
    "o_kv_norm": [1, 256], "o_w_uq": [1, 384, 1536], "o_w_ukv": [1, 256, 2048], "o_w_out": [1, D, D],
    "norm_cross": [2, D], "norm_mem": [2, D], "w_cq": [2, D, D], "w_ckv": [2, D, 2 * D], "w_co": [2, D, D],
    "norm_ffn": [2, D], "w_gu": [2, D, 2 * DFF], "w_down": [2, DFF, D], "final_norm": [D],
}


def host_consts(S):
    c = {}
    c["ident"] = np.eye(128, dtype=np.float32).astype(ml_dtypes.bfloat16)
    c["ones_b"] = np.ones((128, 128), dtype=np.float32).astype(ml_dtypes.bfloat16)
    c["ones_f"] = np.ones((128, 64), dtype=np.float32)
    es = np.zeros((32, 96), dtype=np.float32)
    es[np.arange(32), 64 + np.arange(32)] = 1.0
    c["esel"] = es.astype(ml_dtypes.bfloat16)
    t = np.arange(S)
    fa = (10000.0 ** (-np.arange(16, dtype=np.float32) / 16)).astype(np.float32)
    r = (t // GRID_W).astype(np.float32)
    cc = (t % GRID_W).astype(np.float32)
    angA = np.concatenate([r[:, None] * fa, cc[:, None] * fa], axis=-1).astype(np.float32)
    fl = (10000.0 ** (-np.arange(16, dtype=np.float32) / 16)).astype(np.float32)
    angL = (t.astype(np.float32)[:, None] * fl).astype(np.float32)

    def exp_tabs(ang):
        co = np.repeat(np.cos(ang), 2, axis=-1).astype(np.float32)
        si = np.repeat(np.sin(ang), 2, axis=-1).astype(np.float32)
        si[:, 0::2] *= -1.0
        return co, si

    c["cexpA"], c["sexpA"] = exp_tabs(angA)
    c["cexpL"], c["sexpL"] = exp_tabs(angL)
    slopes = (2.0 ** (-8.0 * np.arange(1, 5) / 4)).astype(np.float64)
    p = np.arange(128)[:, None].astype(np.float64)
    j = np.arange(512)[None, :].astype(np.float64)
    m = np.arange(896)[None, :].astype(np.float64)
    ab = np.zeros((4, 128, 512)); bl = np.zeros((4, 128, 512)); st = np.zeros((4, 128, 896))
    for h in range(4):
        ab[h] = np.exp(-slopes[h] * (j - p + 127))
        bl[h] = np.exp(-slopes[h] * (p - j + 511))
        st[h] = np.exp(-slopes[h] * np.abs(m - 384 - p))
    c["al_above"] = ab.astype(np.float32).astype(ml_dtypes.bfloat16)
    c["al_below"] = bl.astype(np.float32).astype(ml_dtypes.bfloat16)
    c["al_strip"] = st.astype(np.float32).astype(ml_dtypes.bfloat16)
    bc = np.zeros((128, 4, 64), dtype=np.float32)
    for h in range(4):
        for mm_ in range(1, 32):
            bc[:, h, mm_] = -slopes[h] * (128 * mm_ - 127)
        for mm_ in range(4, 32):
            bc[:, h, 32 + mm_] = -slopes[h] * (128 * mm_ - 511)
    c["al_bias"] = bc.reshape(128, 256)
    return c


CONST_SPECS = lambda S: {
    "ident": ([128, 128], BF16), "ones_b": ([128, 128], BF16), "ones_f": ([128, 64], F32), "esel": ([32, 96], BF16),
    "cexpA": ([S, 64], F32), "sexpA": ([S, 64], F32), "cexpL": ([S, 32], F32), "sexpL": ([S, 32], F32),
    "al_above": ([4, 128, 512], BF16), "al_below": ([4, 128, 512], BF16), "al_strip": ([4, 128, 896], BF16),
    "al_bias": ([128, 256], F32),
}


def build(NSEQ, S, stop_after=None):
    NK = S // 128
    NT = S // 512
    nc = bass.Bass("TRN2", target_bir_lowering=False)

    def din(name, shape, dt=F32):
        return nc.dram_tensor(name, list(shape), dt, kind="ExternalInput").ap()

    def dscr(name, shape, dt=BF16):
        return nc.dram_tensor(name, list(shape), dt).ap()

    x_in = din("x", [NSEQ, S, D])
    mem_in = din("mem", [NSEQ, NMEM, D])
    W = {n: din(n, WSHAPES[n]) for n in WNAMES}
    C = {n: din(n, sh, dt) for n, (sh, dt) in CONST_SPECS(S).items()}
    y_out = nc.dram_tensor("y", [NSEQ, S, D], F32, kind="ExternalOutput").ap()

    Win0_s = dscr("Win0_s", [128, 8, 2304])
    Wout0a_s = dscr("Wout0a_s", [64, 8, 1024])
    Wout0b_s = dscr("Wout0b_s", [128, 4, 1024])
    Win1_s = dscr("Win1_s", [128, 8, 672])
    Wuq_s = dscr("Wuq_s", [128, 3, 1536])
    Wkp_s = dscr("Wkp_s", [128, 2, 16, 96])
    Wv_s = dscr("Wv_s", [128, 2, 1024])
    Wout1_s = dscr("Wout1_s", [64, 16, 1024])
    Wcq_s = [dscr(f"Wcq_s{l}", [128, 8, 1024]) for l in range(2)]
    Wckv_s = [dscr(f"Wckv_s{l}", [128, 8, 2048]) for l in range(2)]
    Wco_s = [dscr(f"Wco_s{l}", [128, 8, 1024]) for l in range(2)]
    Wgu_s = [dscr(f"Wgu_s{l}", [128, 8, 2 * DFF]) for l in range(2)]
    Wdn_s = [dscr(f"Wdn_s{l}", [128, 22, 1024]) for l in range(2)]
    QTa = dscr("QTa", [8, 64, S]); QTb = dscr("QTb", [4, 128, S]); QT1 = dscr("QT1", [16, 96, S])
    KTa = dscr("KTa", [128, S]); KTb = dscr("KTb", [4, 128, S]); KT1 = dscr("KT1", [16, 96, S])
    Va = dscr("Va", [S, 2, 65]); Vb = dscr("Vb", [S, 4, 128]); V1 = dscr("V1", [S, 16, 65])
    OTa = dscr("OTa", [8, 64, S]); OTb = dscr("OTb", [4, 128, S]); OT1 = dscr("OT1", [16, 64, S])
    X1 = dscr("X1", [S, D], F32)

    st = ExitStack()
    with st:
        abt = st.enter_context(nc.sbuf_tensor("arena_b", [128, AB_N], BF16))
        aft = st.enter_context(nc.sbuf_tensor("arena_f", [128, AF_N], F32))
        cft = st.enter_context(nc.sbuf_tensor("const_f", [128, 1792], F32))
        cbt = st.enter_context(nc.sbuf_tensor("const_b", [128, 128 + 128 + 96], BF16))
        psf = [st.enter_context(nc.psum_tensor(f"psf{i}", [128, 512], F32)) for i in range(8)]
        psT = [psf[6][:].bitcast(BF16), psf[7][:].bitcast(BF16)]
        AB = Arena(abt, AB_N)
        AFa = Arena(aft, AF_N)
        P = Plan()

        ident = cbt[:, 0:128]
        ones_b = cbt[:, 128:256]
        esel = cbt[0:32, 256:352]
        ones_f = cft[:, 0:64]
        gcols = cft[:, 64:64 + 80]
        GC = {"mix0": 0, "mix1": 8, "cross0": 16, "cross1": 24, "mem0": 32, "mem1": 40, "ffn0": 48, "ffn1": 56,
              "qn": 64, "kvn": 67, "one": 69}
        gq_b = cft[:, 160:224]; gk_b = cft[:, 224:288]; gqs_b = cft[:, 288:352]; gks_b = cft[:, 352:416]
        lam_t = cft[:, 416:420]
        gsub_c = cft[:, 420:421]
        lamw = cft[:, 424:424 + 4 * 64]
        fin_b = cft[:, 768:1792]
        b_const = P.buf("const")

        def ld(dst, src, slow=False):
            P.op("sp", DMA(dst, src, slow), writes=[b_const], dma=True)

        ld(ident, C["ident"][:, :]); ld(ones_b, C["ones_b"][:, :]); ld(esel, C["esel"][:, :]); ld(ones_f, C["ones_f"][:, :])
        for nm, src, kc in [("mix0", W["norm_mix"][0], 8), ("mix1", W["norm_mix"][1], 8),
                            ("cross0", W["norm_cross"][0], 8), ("cross1", W["norm_cross"][1], 8),
                            ("mem0", W["norm_mem"][0], 8), ("mem1", W["norm_mem"][1], 8),
                            ("ffn0", W["norm_ffn"][0], 8), ("ffn1", W["norm_ffn"][1], 8),
                            ("qn", W["o_q_norm"][0], 3), ("kvn", W["o_kv_norm"][0], 2)]:
            ld(gcols[:, GC[nm]:GC[nm] + kc], src.rearrange("(kc p) -> p kc", p=128), slow=True)
        P.op("dve", MSET(gcols[:, GC["one"]:GC["one"] + 1], 1.0), writes=[b_const])
        ld(gq_b, W["e_q_norm"][0].partition_broadcast(128)); ld(gk_b, W["e_k_norm"][0].partition_broadcast(128))
        ld(fin_b, W["final_norm"].partition_broadcast(128))
        for i, nm in enumerate(["e_lam_q1", "e_lam_k1", "e_lam_q2", "e_lam_k2"]):
            ld(lamw[:, i * 64:(i + 1) * 64], W[nm][0].partition_broadcast(128))
        ld(gsub_c, W["e_subln"][0].rearrange("(p o) -> p o", o=1), slow=True)
        for g, gs in [(gq_b, gqs_b), (gk_b, gks_b)]:
            gv = g.rearrange("p (i two) -> p i two", two=2); sv = gs.rearrange("p (i two) -> p i two", two=2)
            P.op("dve", CP(sv[:, :, 0:1], gv[:, :, 1:2]), reads=[b_const], writes=[b_const])
            P.op("dve", CP(sv[:, :, 1:2], gv[:, :, 0:1]), reads=[b_const], writes=[b_const])
        lam_init0 = 0.8 - 0.6 * math.exp(-0.3 * 0)
        tmpl = cft[:, 440 + 256:440 + 256 + 64]
        P.op("dve", TT(tmpl, lamw[:, 0:64], lamw[:, 64:128], ALU.mult), reads=[b_const], writes=[b_const])
        P.op("dve", RSUM(lam_t[:, 0:1], tmpl), reads=[b_const], writes=[b_const])
        P.op("dve", TT(tmpl, lamw[:, 128:192], lamw[:, 192:256], ALU.mult), reads=[b_const], writes=[b_const])
        P.op("dve", RSUM(lam_t[:, 1:2], tmpl), reads=[b_const], writes=[b_const])
        P.op("act", ACT(lam_t[:, 0:2], lam_t[:, 0:2], AF.Exp), reads=[b_const], writes=[b_const])
        P.op("dve", TT(lam_t[:, 2:3], lam_t[:, 0:1], lam_t[:, 1:2], ALU.subtract), reads=[b_const], writes=[b_const])
        P.op("dve", TS(lam_t[:, 3:4], lam_t[:, 2:3], lam_init0, -1.0, ALU.add, ALU.mult), reads=[b_const], writes=[b_const])
        nlam = lam_t[:, 3:4]
        P.op("dve", TS(gsub_c, gsub_c, 1.0 - lam_init0), reads=[b_const], writes=[b_const])

        AB.reset(); AFa.reset()
        NST = 3
        stf = [AFa.alloc([128, 2048]) for _ in range(NST)]
        stb = [AB.alloc([128, 2048]) for _ in range(NST)]
        zt = AB.alloc([128, 16, 32])
        b_stf = [P.buf() for _ in range(NST)]; b_stb = [P.buf() for _ in range(NST)]
        b_z = P.buf()
        P.op("dve", MSET(zt, 0.0), writes=[b_z])
        cstate = {"i": 0}

        def conv_piece(src2d, rows, cols, gain_col, outs):
            i = cstate["i"]; cstate["i"] += 1
            s = i % NST
            r0, r1 = rows; c0, c1 = cols
            np_, w = r1 - r0, c1 - c0
            P.op("sp", DMA(stf[s][0:np_, 0:w], src2d[r0:r1, c0:c1]), writes=[b_stf[s]], dma=True)
            if i % 2 == 0:
                P.op("dve", TS(stb[s][0:np_, 0:w], stf[s][0:np_, 0:w], gain_col[0:np_, :]), reads=[b_stf[s], b_const], writes=[b_stb[s]])
            else:
                P.op("act", ACT(stb[s][0:np_, 0:w], stf[s][0:np_, 0:w], AF.Copy, scale=gain_col[0:np_, :]), reads=[b_stf[s], b_const], writes=[b_stb[s]])
            for dst, vf in outs:
                P.op("pool", DMA(dst, vf(stb[s][0:np_, 0:w])), reads=[b_stb[s]], dma=True)

        def gc(nm, kc):
            return gcols[:, GC[nm] + kc:GC[nm] + kc + 1]

        one_c = gcols[:, GC["one"]:GC["one"] + 1]
        idv = lambda v: v

        def conv(src2d, KC, rows_p, col_ranges, dst, gname):
            for kc in range(KC):
                g = gc(gname, kc) if gname else one_c
                for (c0, c1, d0) in col_ranges:
                    for cc in range(c0, c1, 2048):
                        ce = min(cc + 2048, c1)
                        conv_piece(src2d, (kc * rows_p, (kc + 1) * rows_p), (cc, ce), g,
                                   [(dst[0:rows_p, kc, d0 + cc - c0:d0 + ce - c0], idv)])

        conv(W["e_w_in"][0], 8, 128, [(0, 512, 0), (768, 1280, 512), (512, 768, 1024), (1280, 2304, 1280)], Win0_s, "mix0")
        conv(W["e_w_out"][0][0:512, :], 8, 64, [(0, 1024, 0)], Wout0a_s, None)
        conv(W["e_w_out"][0][512:1024, :], 4, 128, [(0, 1024, 0)], Wout0b_s, None)
        conv(W["o_w_in"][0], 8, 128, [(0, 672, 0)], Win1_s, "mix1")
        conv(W["o_w_uq"][0], 3, 128, [(0, 1536, 0)], Wuq_s, "qn")
        for kc in range(2):
            conv_piece(W["o_w_ukv"][0], (kc * 128, (kc + 1) * 128), (0, 2048), gc("kvn", kc), [
                (Wkp_s[:, kc, :, 0:64], lambda v: v.rearrange("p (h e) -> p h e", e=128)[:, :, 0:64]),
                (Wv_s[:, kc, :].rearrange("p (h e) -> p h e", e=64), lambda v: v.rearrange("p (h e) -> p h e", e=128)[:, :, 64:128]),
            ])
            P.op("pool", DMA(Wkp_s[:, kc, :, 64:96], zt), reads=[b_z], dma=True)
        conv(W["o_w_out"][0], 16, 64, [(0, 1024, 0)], Wout1_s, None)
        for l in range(2):
            conv(W["w_cq"][l], 8, 128, [(0, 1024, 0)], Wcq_s[l], f"cross{l}")
            conv(W["w_ckv"][l], 8, 128, [(0, 2048, 0)], Wckv_s[l], f"mem{l}")
            conv(W["w_co"][l], 8, 128, [(0, 1024, 0)], Wco_s[l], None)
            conv(W["w_gu"][l], 8, 128, [(0, 2 * DFF, 0)], Wgu_s[l], f"ffn{l}")
            conv(W["w_down"][l], 22, 128, [(0, 1024, 0)], Wdn_s[l], None)
        P.barrier()

        bank_b = [P.buf(f"psf{i}", excl=True) for i in range(8)]
        ring = {"i": 0}

        def nextbank():
            i = ring["i"] % 6
            ring["i"] += 1
            return psf[i][:], bank_b[i]

        sring = {"i": 0}
        aring = {"i": 0}

        sbanks = {"l": [0, 1, 2]}

        def nextS():
            i = sbanks["l"][sring["i"] % len(sbanks["l"])]
            sring["i"] += 1
            return psf[i][:], bank_b[i]

        abanks = {"l": [3, 4, 5]}

        def nextA():
            i = abanks["l"][aring["i"] % len(abanks["l"])]
            aring["i"] += 1
            return psf[i][:], bank_b[i]

        tring = {"i": 0}

        def nextT():
            i = tring["i"] % 2
            tring["i"] += 1
            return psT[i][:, 0:512], bank_b[6 + i]

        def rms_rstd(ss, rs, n, b_ss, b_rs):
            P.op("act", ACT(rs, ss, AF.Ln, bias=EPS, scale=1.0 / n), reads=[b_ss], writes=[b_rs])
            P.op("act", ACT(rs, rs, AF.Exp, scale=-0.5), reads=[b_rs], writes=[b_rs])

        def rope(out, v, ta, tb, tmp, nh, nd, b_v, b_tab, b_tmp, b_out, eng="dve"):
            t1, t2 = tmp
            tab = ta.unsqueeze(1).to_broadcast([128, nh, nd])
            P.op(eng, TT(t1, v, tab, ALU.mult), reads=[b_v, b_tab], writes=[b_tmp])
            vv = v.rearrange("p h (i two) -> p h i two", two=2)
            t2v = t2.rearrange("p h (i two) -> p h i two", two=2)
            tbv = tb.rearrange("p (i two) -> p i two", two=2).unsqueeze(1).to_broadcast([128, nh, nd // 2, 2])
            P.op(eng, TT(t2v[:, :, :, 0:1], vv[:, :, :, 1:2], tbv[:, :, :, 0:1], ALU.mult), reads=[b_v, b_tab], writes=[b_tmp])
            P.op(eng, TT(t2v[:, :, :, 1:2], vv[:, :, :, 0:1], tbv[:, :, :, 1:2], ALU.mult), reads=[b_v, b_tab], writes=[b_tmp])
            P.op(eng, TT(out, t1, t2, ALU.add), reads=[b_tmp], writes=[b_out])

        def norm_to_hT(xt, b_xt, junk, b_junk, ss, rs, b_s, hb, b_hb, hT, b_hT, col0):
            P.op("act", ACT(junk, xt, AF.Square, accum=ss), reads=[b_xt], writes=[b_junk, b_s])
            rms_rstd(ss, rs, float(D), b_s, b_s)
            P.op("dve", TS(hb, xt, rs), reads=[b_xt, b_s], writes=[b_hb])
            for half in range(2):
                pt, b_pt = nextT()
                for k in range(4):
                    kc = half * 4 + k
                    P.op("pe", TR(pt[:, k * 128:(k + 1) * 128], hb[:, kc * 128:(kc + 1) * 128], ident), reads=[b_hb, b_const], writes=[b_pt])
                dst = hT[:, half * 4:half * 4 + 4, col0:col0 + 128]
                src = pt.rearrange("p (k t) -> p k t", t=128)
                if half == 0:
                    P.op("act", ACP(dst, src), reads=[b_pt], writes=[b_hT])
                else:
                    P.op("dve", CP(dst, src), reads=[b_pt], writes=[b_hT])

        def pass_A(layer, xsrc):
            AB.reset(); AFa.reset()
            ncols = 2304 if layer == 0 else 672
            Win = AB.alloc([128, 8, ncols]); b_Win = P.buf()
            Wsrc = Win0_s if layer == 0 else Win1_s
            for kc in range(8):
                P.op("sp", DMA(Win[:, kc, :], Wsrc[:, kc, :]), writes=[b_Win], dma=True)
            if layer == 1:
                Wuq = AB.alloc([128, 3, 1536]); Wkp = AB.alloc([128, 2, 16, 96]); Wv = AB.alloc([128, 2, 1024])
                for kc in range(3):
                    P.op("sp", DMA(Wuq[:, kc, :], Wuq_s[:, kc, :]), writes=[b_Win], dma=True)
                for kc in range(2):
                    P.op("sp", DMA(Wkp[:, kc, :, :], Wkp_s[:, kc, :, :]), writes=[b_Win], dma=True)
                    P.op("sp", DMA(Wv[:, kc, :], Wv_s[:, kc, :]), writes=[b_Win], dma=True)
            xt = [AFa.alloc([128, 1024]) for _ in range(2)]; b_xt = [P.buf() for _ in range(2)]
            junk = AFa.alloc([128, 1024]); b_junk = P.buf()
            ssr = [AFa.alloc([128, 4]) for _ in range(2)]; b_s = [P.buf() for _ in range(2)]
            hb = [AB.alloc([128, 1024]) for _ in range(2)]; b_hb = [P.buf() for _ in range(2)]
            hT = [AB.alloc([128, 8, 128]) for _ in range(2)]; b_hT = [P.buf() for _ in range(2)]
            tabs = [AFa.alloc([128, 128]) for _ in range(2)]; b_tabs = [P.buf() for _ in range(2)]
            tq = [AFa.alloc([128, 256]) for _ in range(2)]; b_tq = [P.buf() for _ in range(2)]
            t1 = AFa.alloc([128, 512]); t2 = AFa.alloc([128, 512]); b_t = P.buf()
            qn = AFa.alloc([128, 512]); b_qn = P.buf()
            sq8 = AFa.alloc([128, 16]); b_sq8 = P.buf()
            if layer == 0:
                nstT = 13
            else:
                nstT = 16
            if layer == 0:
                qf = AB.alloc([128, 1024]); b_qf = P.buf()
                kf = AB.alloc([128, 640]); b_kf = P.buf()
                stT = [AB.alloc([128, 13, 512]) for _ in range(2)]; b_stT = [P.buf() for _ in range(2)]
                vast = [AB.alloc([128, 2, 65]) for _ in range(2)]; b_va = [P.buf() for _ in range(2)]
                vbst = [AB.alloc([128, 512]) for _ in range(2)]; b_vb = [P.buf() for _ in range(2)]
                for v_ in vast:
                    P.op("dve", MSET(v_[:, :, 64:65], 1.0), writes=[b_va[0], b_va[1]])
            else:
                qf = AB.alloc([128, 16, 96]); b_qf = P.buf()
                cn = AB.alloc([128, 640]); b_cn = P.buf()
                krf = AB.alloc([128, 32]); b_krf = P.buf()
                cT = AB.alloc([128, 5, 128]); b_cT = P.buf()
                krT = AB.alloc([32, 128]); b_krT = P.buf()
                stQ = [AB.alloc([96, 16, 512]) for _ in range(2)]; b_stQ = [P.buf() for _ in range(2)]
                stK = [AB.alloc([96, 16, 512]) for _ in range(2)]; b_stK = [P.buf() for _ in range(2)]
                v1st = [AB.alloc([128, 16, 65]) for _ in range(2)]; b_v1 = [P.buf() for _ in range(2)]
                for v_ in v1st:
                    P.op("dve", MSET(v_[:, :, 64:65], 1.0), writes=[b_v1[0], b_v1[1]])
            sc0 = 0.125
            sc1 = 96.0 ** -0.5
            for i in range(NK):
                pa = i % 2
                t0 = i * 128
                blk = i // 4
                sb = blk % 2
                c4 = (i % 4) * 128
                P.op("sp", DMA(xt[pa], xsrc[t0:t0 + 128, :]), writes=[b_xt[pa]], dma=True)
                if layer == 0:
                    P.op("sp", DMA(tabs[pa][:, 0:64], C["cexpA"][t0:t0 + 128, :]), writes=[b_tabs[pa]], dma=True)
                    P.op("sp", DMA(tabs[pa][:, 64:128], C["sexpA"][t0:t0 + 128, :]), writes=[b_tabs[pa]], dma=True)
                else:
                    P.op("sp", DMA(tabs[pa][:, 0:32], C["cexpL"][t0:t0 + 128, :]), writes=[b_tabs[pa]], dma=True)
                    P.op("sp", DMA(tabs[pa][:, 64:96], C["sexpL"][t0:t0 + 128, :]), writes=[b_tabs[pa]], dma=True)
                norm_to_hT(xt[pa], b_xt[pa], junk, b_junk, ssr[pa][:, 0:1], ssr[pa][:, 1:2], b_s[pa], hb[pa], b_hb[pa], hT[pa], b_hT[pa], 0)
                if layer == 0:
                    P.op("dve", STT(tq[pa][:, 0:64], tabs[pa][:, 0:64], sc0, gq_b, ALU.mult, ALU.mult), reads=[b_tabs[pa], b_const], writes=[b_tq[pa]])
                    P.op("dve", STT(tq[pa][:, 64:128], tabs[pa][:, 64:128], sc0, gqs_b, ALU.mult, ALU.mult), reads=[b_tabs[pa], b_const], writes=[b_tq[pa]])
                    P.op("dve", TT(tq[pa][:, 128:192], tabs[pa][:, 0:64], gk_b, ALU.mult), reads=[b_tabs[pa], b_const], writes=[b_tq[pa]])
                    P.op("dve", TT(tq[pa][:, 192:256], tabs[pa][:, 64:128], gks_b, ALU.mult), reads=[b_tabs[pa], b_const], writes=[b_tq[pa]])
                    groups = [(0, 512), (512, 1024), (1024, 1280), (1280, 1792), (1792, 2304)]
                    pg = []
                    for (c0, c1) in groups:
                        bk, b_bk = nextbank()
                        for kc in range(8):
                            P.op("pe", MM(bk[:, 0:c1 - c0], hT[pa][:, kc, :], Win[:, kc, c0:c1], kc == 0, kc == 7), reads=[b_hT[pa], b_Win], writes=[b_bk])
                        pg.append((bk, b_bk))
                    (p0, b0), (p1, b1), (p2, b2), (p3, b3), (p4, b4) = pg
                    P.op("act", ACT(t1, p0, AF.Square), reads=[b0], writes=[b_t])
                    P.op("dve", RSUM(sq8[:, 0:8], t1.rearrange("p (h d) -> p h d", d=64)), reads=[b_t], writes=[b_sq8])
                    rms_rstd(sq8[:, 0:8], sq8[:, 0:8], 64.0, b_sq8, b_sq8)
                    qn3 = qn.rearrange("p (h d) -> p h d", d=64)
                    P.op("dve", TT(qn3, p0.rearrange("p (h d) -> p h d", d=64), sq8[:, 0:8].unsqueeze(2).to_broadcast([128, 8, 64]), ALU.mult), reads=[b0, b_sq8], writes=[b_qn])
                    rope(qf[:, 0:512].rearrange("p (h d) -> p h d", d=64), qn3, tq[pa][:, 0:64], tq[pa][:, 64:128],
                         (t1.rearrange("p (h d) -> p h d", d=64), t2.rearrange("p (h d) -> p h d", d=64)), 8, 64, b_qn, b_tq[pa], b_t, b_qf)
                    P.op("act", ACT(qf[:, 512:1024], p1, AF.Copy, scale=sc0), reads=[b1], writes=[b_qf])
                    P.op("act", ACT(t1[:, 0:128], p2[:, 0:128], AF.Square), reads=[b2], writes=[b_t])
                    P.op("dve", RSUM(sq8[:, 8:10], t1[:, 0:128].rearrange("p (h d) -> p h d", d=64)), reads=[b_t], writes=[b_sq8])
                    rms_rstd(sq8[:, 8:10], sq8[:, 8:10], 64.0, b_sq8, b_sq8)
                    kn3 = qn[:, 0:128].rearrange("p (h d) -> p h d", d=64)
                    P.op("dve", TT(kn3, p2[:, 0:128].rearrange("p (h d) -> p h d", d=64), sq8[:, 8:10].unsqueeze(2).to_broadcast([128, 2, 64]), ALU.mult), reads=[b2, b_sq8], writes=[b_qn])
                    rope(kf[:, 0:128].rearrange("p (h d) -> p h d", d=64), kn3, tq[pa][:, 128:192], tq[pa][:, 192:256],
                         (t1[:, 0:128].rearrange("p (h d) -> p h d", d=64), t2[:, 0:128].rearrange("p (h d) -> p h d", d=64)), 2, 64, b_qn, b_tq[pa], b_t, b_kf)
                    P.op("act", ACP(vast[pa][:, :, 0:64], p2[:, 128:256].rearrange("p (h d) -> p h d", d=64)), reads=[b2], writes=[b_va[pa]])
                    P.op("pool", DMA(Va[t0:t0 + 128, :, :], vast[pa]), reads=[b_va[pa]], dma=True)
                    P.op("act", ACP(kf[:, 128:640], p3), reads=[b3], writes=[b_kf])
                    P.op("dve", CP(vbst[pa], p4), reads=[b4], writes=[b_vb[pa]])
                    P.op("pool", DMA(Vb[t0:t0 + 128, :, :].rearrange("t h e -> t (h e)"), vbst[pa]), reads=[b_vb[pa]], dma=True)
                    srcs = [(qf, b_qf, c) for c in range(8)] + [(kf, b_kf, c) for c in range(5)]
                    for j0 in range(0, 13, 4):
                        pt, b_pt = nextT()
                        n = min(4, 13 - j0)
                        for k in range(n):
                            sa, sbuf_, c = srcs[j0 + k]
                            P.op("pe", TR(pt[:, k * 128:(k + 1) * 128], sa[:, c * 128:(c + 1) * 128], ident), reads=[sbuf_, b_const], writes=[b_pt])
                        dst = stT[sb][:, j0:j0 + n, c4:c4 + 128]
                        src = pt[:, 0:n * 128].rearrange("p (k t) -> p k t", t=128)
                        P.op("act" if (j0 // 4) % 2 == 0 else "dve", (ACP if (j0 // 4) % 2 == 0 else CP)(dst, src), reads=[b_pt], writes=[b_stT[sb]])
                    if i % 4 == 3:
                        tb0 = blk * 512
                        s_ = stT[sb]
                        for c in range(4):
                            P.op("pool", DMA(QTa[2 * c, :, tb0:tb0 + 512], s_[0:64, c, :]), reads=[b_stT[sb]], dma=True)
                            P.op("pool", DMA(QTa[2 * c + 1, :, tb0:tb0 + 512], s_[64:128, c, :]), reads=[b_stT[sb]], dma=True)
                            P.op("pool", DMA(QTb[c, :, tb0:tb0 + 512], s_[:, 4 + c, :]), reads=[b_stT[sb]], dma=True)
                            P.op("pool", DMA(KTb[c, :, tb0:tb0 + 512], s_[:, 9 + c, :]), reads=[b_stT[sb]], dma=True)
                        P.op("pool", DMA(KTa[:, tb0:tb0 + 512], s_[:, 8, :]), reads=[b_stT[sb]], dma=True)
                else:
                    p0, b0 = nextbank(); p1, b1 = nextbank()
                    for kc in range(8):
                        P.op("pe", MM(p0[:, 0:384], hT[pa][:, kc, :], Win[:, kc, 0:384], kc == 0, kc == 7), reads=[b_hT[pa], b_Win], writes=[b0])
                    for kc in range(8):
                        P.op("pe", MM(p1[:, 0:288], hT[pa][:, kc, :], Win[:, kc, 384:672], kc == 0, kc == 7), reads=[b_hT[pa], b_Win], writes=[b1])
                    P.op("act", ACT(t1[:, 0:384], p0[:, 0:384], AF.Square, accum=sq8[:, 0:1]), reads=[b0], writes=[b_t, b_sq8])
                    P.op("act", ACT(t1[:, 0:256], p1[:, 0:256], AF.Square, accum=sq8[:, 1:2]), reads=[b1], writes=[b_t, b_sq8])
                    rms_rstd(sq8[:, 0:1], sq8[:, 2:3], 384.0, b_sq8, b_sq8)
                    rms_rstd(sq8[:, 1:2], sq8[:, 3:4], 256.0, b_sq8, b_sq8)
                    P.op("dve", TS(cn[:, 0:384], p0[:, 0:384], sq8[:, 2:3]), reads=[b0, b_sq8], writes=[b_cn])
                    P.op("dve", TS(cn[:, 384:640], p1[:, 0:256], sq8[:, 3:4]), reads=[b1, b_sq8], writes=[b_cn])
                    P.op("act", ACP(qn[:, 0:32], p1[:, 256:288]), reads=[b1], writes=[b_qn])
                    rope(krf.rearrange("p (h d) -> p h d", h=1), qn[:, 0:32].rearrange("p (h d) -> p h d", h=1), tabs[pa][:, 0:32], tabs[pa][:, 64:96],
                         (t1[:, 0:32].rearrange("p (h d) -> p h d", h=1), t2[:, 0:32].rearrange("p (h d) -> p h d", h=1)), 1, 32, b_qn, b_tabs[pa], b_t, b_krf)
                    for (j0, n) in [(0, 4), (4, 1)]:
                        pt, b_pt = nextT()
                        for k in range(n):
                            c = j0 + k
                            P.op("pe", TR(pt[:, k * 128:(k + 1) * 128], cn[:, c * 128:(c + 1) * 128], ident), reads=[b_cn, b_const], writes=[b_pt])
                        if j0 == 4:
                            P.op("pe", TR(pt[0:32, 128:256], krf, ident), reads=[b_krf, b_const], writes=[b_pt])
                            P.op("dve", CP(krT, pt[0:32, 128:256]), reads=[b_pt], writes=[b_krT])
                        P.op("act", ACP(cT[:, j0:j0 + n, :], pt[:, 0:n * 128].rearrange("p (k t) -> p k t", t=128)), reads=[b_pt], writes=[b_cT])
                    for (h0, h1) in [(0, 5), (5, 10), (10, 15), (15, 16)]:
                        bk, b_bk = nextbank()
                        nh = h1 - h0
                        for kc in range(3):
                            P.op("pe", MM(bk[:, 0:nh * 96], cT[:, kc, :], Wuq[:, kc, h0 * 96:h1 * 96], kc == 0, kc == 2), reads=[b_cT, b_Win], writes=[b_bk])
                        bv = bk[:, 0:nh * 96].rearrange("p (h d) -> p h d", d=96)
                        P.op("act", ACT(qf[:, h0:h1, 0:64], bv[:, :, 0:64], AF.Copy, scale=sc1), reads=[b_bk], writes=[b_qf])
                        P.op("act", ACT(qn[:, 0:nh * 32].rearrange("p (h d) -> p h d", d=32), bv[:, :, 64:96], AF.Copy, scale=sc1), reads=[b_bk], writes=[b_qn])
                        rope(qf[:, h0:h1, 64:96], qn[:, 0:nh * 32].rearrange("p (h d) -> p h d", d=32), tabs[pa][:, 0:32], tabs[pa][:, 64:96],
                             (t1[:, 0:nh * 32].rearrange("p (h d) -> p h d", d=32), t2[:, 0:nh * 32].rearrange("p (h d) -> p h d", d=32)), nh, 32, b_qn, b_tabs[pa], b_t, b_qf)
                    for j0 in range(0, 16, 4):
                        pt, b_pt = nextT()
                        for k in range(4):
                            P.op("pe", TR(pt[0:96, k * 128:(k + 1) * 128], qf[:, j0 + k, :], ident), reads=[b_qf, b_const], writes=[b_pt])
                        dst = stQ[sb][:, j0:j0 + 4, c4:c4 + 128]
                        src = pt[0:96, :].rearrange("p (k t) -> p k t", t=128)
                        P.op("act" if (j0 // 4) % 2 == 0 else "dve", (ACP if (j0 // 4) % 2 == 0 else CP)(dst, src), reads=[b_pt], writes=[b_stQ[sb]])
                    for j0 in range(0, 16, 4):
                        bk, b_bk = nextbank()
                        for k in range(4):
                            h = j0 + k
                            o_ = bk[0:96, k * 128:(k + 1) * 128]
                            P.op("pe", MM(o_, esel, krT, True, False), reads=[b_krT, b_const], writes=[b_bk])
                            P.op("pe", MM(o_, Wkp[:, 0, h, :], cT[:, 3, :], False, False), reads=[b_cT, b_Win], writes=[b_bk])
                            P.op("pe", MM(o_, Wkp[:, 1, h, :], cT[:, 4, :], False, True), reads=[b_cT, b_Win], writes=[b_bk])
                        dst = stK[sb][:, j0:j0 + 4, c4:c4 + 128]
                        src = bk[0:96, :].rearrange("p (k t) -> p k t", t=128)
                        P.op("act" if (j0 // 4) % 2 == 1 else "dve", (ACP if (j0 // 4) % 2 == 1 else CP)(dst, src), reads=[b_bk], writes=[b_stK[sb]])
                    for hh in range(2):
                        bk, b_bk = nextbank()
                        for kc in range(2):
                            P.op("pe", MM(bk, cT[:, 3 + kc, :], Wv[:, kc, hh * 512:(hh + 1) * 512], kc == 0, kc == 1), reads=[b_cT, b_Win], writes=[b_bk])
                        P.op("dve" if hh == 0 else "act", (CP if hh == 0 else ACP)(v1st[pa][:, hh * 8:(hh + 1) * 8, 0:64], bk.rearrange("p (h d) -> p h d", d=64)), reads=[b_bk], writes=[b_v1[pa]])
                    P.op("pool", DMA(V1[t0:t0 + 128, :, :], v1st[pa]), reads=[b_v1[pa]], dma=True)
                    if i % 4 == 3:
                        tb0 = blk * 512
                        for h in range(16):
                            P.op("pool", DMA(QT1[h, :, tb0:tb0 + 512], stQ[sb][:, h, :]), reads=[b_stQ[sb]], dma=True)
                            P.op("pool", DMA(KT1[h, :, tb0:tb0 + 512], stK[sb][:, h, :]), reads=[b_stK[sb]], dma=True)
            P.barrier()

        LA = 3
        NP = 4

        def pass_B(layer):
            abanks["l"] = [4, 5, 6, 7]
            sbanks["l"] = [0, 1, 2, 3]
            ngroups = 1 if layer == 0 else 2
            for g in range(ngroups):
                AB.reset(); AFa.reset()
                b_kv = P.buf()
                if layer == 0:
                    KTa_sb = AB.alloc([128, S]); KTb_sb = AB.alloc([128, 4, S])
                    Va_sb = AB.alloc([128, NK, 2, 65]); Vb_sb = AB.alloc([128, NK, 4, 128])
                    for blk in range(NT):
                        sl = slice(blk * 512, (blk + 1) * 512)
                        P.op("sp", DMA(KTa_sb[:, sl], KTa[:, sl]), writes=[b_kv], dma=True)
                        for h in range(4):
                            P.op("sp", DMA(KTb_sb[:, h, sl], KTb[h, :, sl]), writes=[b_kv], dma=True)
                    for kt in range(NK):
                        P.op("sp", DMA(Va_sb[:, kt, :, :], Va[kt * 128:(kt + 1) * 128, :, :]), writes=[b_kv], dma=True)
                        P.op("sp", DMA(Vb_sb[:, kt, :, :], Vb[kt * 128:(kt + 1) * 128, :, :]), writes=[b_kv], dma=True)
                    al_ab = AB.alloc([128, 4, 512]); al_bl = AB.alloc([128, 4, 512]); al_st = AB.alloc([128, 4, 896])
                    al_bias = AFa.alloc([128, 256])
                    for h in range(4):
                        P.op("sp", DMA(al_ab[:, h, :], C["al_above"][h]), writes=[b_kv], dma=True)
                        P.op("sp", DMA(al_bl[:, h, :], C["al_below"][h]), writes=[b_kv], dma=True)
                        P.op("sp", DMA(al_st[:, h, :], C["al_strip"][h]), writes=[b_kv], dma=True)
                    P.op("sp", DMA(al_bias, C["al_bias"][:, :]), writes=[b_kv], dma=True)
                else:
                    KT_sb = AB.alloc([96, 8, S]); V_sb = AB.alloc([128, NK, 8, 65])
                    for blk in range(NT):
                        sl = slice(blk * 512, (blk + 1) * 512)
                        for h in range(8):
                            P.op("sp", DMA(KT_sb[:, h, sl], KT1[g * 8 + h, :, sl]), writes=[b_kv], dma=True)
                    for kt in range(NK):
                        P.op("sp", DMA(V_sb[:, kt, :, :], V1[kt * 128:(kt + 1) * 128, g * 8:(g + 1) * 8, :]), writes=[b_kv], dma=True)
                b_q = [P.buf() for _ in range(2)]; b_o = [P.buf() for _ in range(2)]
                b_qb1 = P.buf(); b_ob1 = P.buf()
                if layer == 0:
                    qa_t = [AB.alloc([128, 8, 512]) for _ in range(2)]
                    qb_t = AB.alloc([128, 4, 2, 512])
                    oa_st = AB.alloc([64, 8, 512]); ob_st = AB.alloc([128, 4, 512])
                    for pa_ in range(2):
                        P.op("dve", MSET(qa_t[pa_], 0.0), writes=[b_q[pa_]])
                    P.op("dve", MSET(qb_t, 0.0), writes=[b_qb1])
                else:
                    q1_t = [AB.alloc([96, 8, 512]) for _ in range(2)]
                    o1_st = [AB.alloc([64, 8, 512]) for _ in range(2)]
                Pt = [AB.alloc([128, 512]) for _ in range(NP)]; b_P = [P.buf() for _ in range(NP)]
                if layer == 0:
                    P2 = [AB.alloc([128, 512]) for _ in range(NP)]; b_P2 = [P.buf() for _ in range(NP)]
                    sqb = AB.alloc([128, 512]); b_sqb = P.buf()
                    tc_ = [AFa.alloc([128, 512]) for _ in range(2)]; b_tc = [P.buf() for _ in range(2)]
                    tt_ = AFa.alloc([128, 512]); b_tt = P.buf()
                rl = [AFa.alloc([128, 512]) for _ in range(2)]; b_rl = [P.buf() for _ in range(2)]
                bcs = [AFa.alloc([128, 512]) for _ in range(2)]; b_bcs = [P.buf() for _ in range(2)]
                pctr = {"i": 0, "u": 0}
                pend = []

                def defer(n, fn):
                    pend.append([n, fn])

                def tick():
                    for it in pend:
                        it[0] -= 1
                    while pend and pend[0][0] <= 0:
                        pend.pop(0)[1]()

                def load_qb(qt):
                    sl = slice(qt * 512, (qt + 1) * 512)
                    for h in range(4):
                        for c in range(2):
                            P.op("sp", DMA(qb_t[c * 64:c * 64 + 64, h, c, :], QTb[h, c * 64:c * 64 + 64, sl]), writes=[b_qb1], dma=True)

                def load_q(qt):
                    pa = qt % 2
                    sl = slice(qt * 512, (qt + 1) * 512)
                    if layer == 0:
                        for hq in range(8):
                            P.op("sp", DMA(qa_t[pa][(hq // 4) * 64:(hq // 4) * 64 + 64, hq, :], QTa[hq, :, sl]), writes=[b_q[pa]], dma=True)
                    else:
                        for h in range(8):
                            P.op("sp", DMA(q1_t[pa][:, h, :], QT1[g * 8 + h, :, sl]), writes=[b_q[pa]], dma=True)

                def store_oa(qt):
                    sl = slice(qt * 512, (qt + 1) * 512)
                    for hq in range(8):
                        P.op("pool", DMA(OTa[hq, :, sl], oa_st[:, hq, :]), reads=[b_o[0]], dma=True)

                def store_o(qt):
                    pa = qt % 2
                    sl = slice(qt * 512, (qt + 1) * 512)
                    if layer == 0:
                        for h in range(4):
                            P.op("pool", DMA(OTb[h, :, sl], ob_st[:, h, :]), reads=[b_ob1], dma=True)
                    else:
                        for h in range(8):
                            P.op("pool", DMA(OT1[g * 8 + h, :, sl], o1_st[pa][:, h, :]), reads=[b_o[pa]], dma=True)

                class AugUnit:
                    def __init__(self, Qap, Kfn, Vfn, o_dst, b_odst, b_qb):
                        self.Qap, self.Kfn, self.Vfn, self.o_dst, self.b_odst, self.b_qb = Qap, Kfn, Vfn, o_dst, b_odst, b_qb

                    def start(self):
                        self.O, self.b_O = nextA()
                        self.u = pctr["u"] % 2; pctr["u"] += 1

                    def front(self, kt):
                        sbk, b_sbk = nextS()
                        P.op("pe", MM(sbk, self.Kfn(kt), self.Qap, True, True), reads=[b_kv, self.b_qb], writes=[b_sbk])
                        pi = pctr["i"] % NP; pctr["i"] += 1
                        P.op("act", ACT(Pt[pi], sbk, AF.Exp), reads=[b_sbk], writes=[b_P[pi]])
                        return Pt[pi], b_P[pi]

                    def back(self, kt, pinfo):
                        pt, b_pt = pinfo
                        P.op("pe", MM(self.O[0:65, :], self.Vfn(kt), pt, kt == 0, kt == NK - 1), reads=[b_kv, b_pt], writes=[self.b_O])

                    def fin(self):
                        u = self.u; O = self.O; b_O = self.b_O
                        P.op("dve", RECIP(rl[u][64:65, :], O[64:65, :]), reads=[b_O], writes=[b_rl[u]])

                        def part2():
                            bcp, b_bcp = nextS()
                            P.op("pe", MM(bcp[0:64, :], ones_f[64:65, 0:64], rl[u][64:65, :], True, True), reads=[b_rl[u], b_const], writes=[b_bcp])
                            P.op("act", ACP(bcs[u][0:64, :], bcp[0:64, :]), reads=[b_bcp], writes=[b_bcs[u]])
                            P.op("dve", TT(self.o_dst, O[0:64, :], bcs[u][0:64, :], ALU.mult), reads=[b_O, b_bcs[u]], writes=[self.b_odst])
                        defer(5, part2)

                class DiffUnit:
                    def __init__(self, h, c, qt, Qap, Kfn, Vfn, o_dst, b_odst, b_qb):
                        self.h, self.c, self.qt = h, c, qt
                        self.Qap, self.Kfn, self.Vfn, self.o_dst, self.b_odst, self.b_qb = Qap, Kfn, Vfn, o_dst, b_odst, b_qb

                    def start(self):
                        self.O, self.b_O = nextA(); self.L, self.b_L = nextA()
                        self.u = pctr["u"] % 2; pctr["u"] += 1

                    def front(self, kt):
                        h, qt = self.h, self.qt
                        sbk, b_sbk = nextS()
                        P.op("pe", MM(sbk, self.Kfn(kt), self.Qap, True, True), reads=[b_kv, self.b_qb], writes=[b_sbk])
                        pi = pctr["i"] % NP; pctr["i"] += 1
                        if kt < 4 * qt:
                            m_ = (512 * qt - 128 * kt) // 128
                            bcol = al_bias[:, h * 64 + m_:h * 64 + m_ + 1]; tab = al_ab[:, h, :]
                        elif kt >= 4 * qt + 4:
                            m_ = (128 * kt - 512 * qt) // 128
                            bcol = al_bias[:, h * 64 + 32 + m_:h * 64 + 32 + m_ + 1]; tab = al_bl[:, h, :]
                        else:
                            dl = 512 * qt - 128 * kt
                            bcol = al_bias[:, h * 64:h * 64 + 1]; tab = al_st[:, h, 384 + dl:384 + dl + 512]
                        P.op("act", ACT(Pt[pi], sbk, AF.Exp, bias=bcol), reads=[b_sbk, b_kv], writes=[b_P[pi]])
                        P.op("dve", TT(P2[pi], Pt[pi], tab, ALU.mult), reads=[b_P[pi], b_kv], writes=[b_P2[pi]])
                        return P2[pi], b_P2[pi]

                    def back(self, kt, pinfo):
                        pt, b_pt = pinfo
                        P.op("pe", MM(self.O, self.Vfn(kt), pt, kt == 0, kt == NK - 1), reads=[b_kv, b_pt], writes=[self.b_O])
                        P.op("pe", MM(self.L, ones_b, pt, kt == 0, kt == NK - 1), reads=[b_const, b_pt], writes=[self.b_L])

                    def fin(self):
                        u = self.u; c = self.c
                        P.op("act", ACT(rl[u], self.L, AF.Ln), reads=[self.b_L], writes=[b_rl[u]])
                        P.op("act", ACT(rl[u], rl[u], AF.Exp, scale=-1.0), reads=[b_rl[u]], writes=[b_rl[u]])
                        P.op("dve", TT(tc_[c], self.O, rl[u], ALU.mult), reads=[self.b_O, b_rl[u]], writes=[b_tc[c]])
                        if c == 1:
                            P.op("dve", STT(tt_, tc_[1], nlam, tc_[0], ALU.mult, ALU.add), reads=[b_tc[0], b_tc[1], b_const], writes=[b_tt])
                            P.op("act", ACT(sqb, tt_, AF.Square), reads=[b_tt], writes=[b_sqb])

                            def part2():
                                ssb, b_ssb = nextS()
                                P.op("pe", MM(ssb, ones_b, sqb, True, True), reads=[b_const, b_sqb], writes=[b_ssb])
                                P.op("act", ACT(rl[u], ssb, AF.Ln, bias=EPS, scale=1.0 / 128), reads=[b_ssb], writes=[b_rl[u]])
                                P.op("act", ACT(rl[u], rl[u], AF.Exp, scale=-0.5), reads=[b_rl[u]], writes=[b_rl[u]])
                                P.op("dve", STT(self.o_dst, tt_, gsub_c, rl[u], ALU.mult, ALU.mult), reads=[b_tt, b_rl[u], b_const], writes=[self.b_odst])
                            defer(6, part2)

                stream = []
                for qt in range(NT):
                    pa = qt % 2
                    units = []
                    if layer == 0:
                        for hq in range(8):
                            kvh = hq // 4
                            units.append(AugUnit(qa_t[pa][:, hq, :],
                                                 lambda kt: KTa_sb[:, kt * 128:(kt + 1) * 128],
                                                 lambda kt, kvh=kvh: Va_sb[:, kt, kvh, :], oa_st[:, hq, :], b_o[0], b_q[pa]))
                        for h in range(4):
                            for c in range(2):
                                units.append(DiffUnit(h, c, qt, qb_t[:, h, c, :],
                                                      lambda kt, h=h: KTb_sb[:, h, kt * 128:(kt + 1) * 128],
                                                      lambda kt, h=h: Vb_sb[:, kt, h, :], ob_st[:, h, :], b_ob1, b_qb1))
                    else:
                        for h in range(8):
                            units.append(AugUnit(q1_t[pa][:, h, :], lambda kt, h=h: KT_sb[:, h, kt * 128:(kt + 1) * 128],
                                                 lambda kt, h=h: V_sb[:, kt, h, :], o1_st[pa][:, h, :], b_o[pa], b_q[pa]))
                    for ui, u_ in enumerate(units):
                        for kt in range(NK):
                            stream.append((u_, kt, qt, ui == 0 and kt == 0, ui == len(units) - 1 and kt == NK - 1,
                                           layer == 0 and ui == 7 and kt == NK - 1))
                load_q(0)
                if layer == 0:
                    load_qb(0)
                inflight = []
                for idx in range(len(stream) + LA):
                    if idx < len(stream):
                        u_, kt, qt, first, lastq, _ = stream[idx]
                        if first and qt + 1 < NT:
                            load_q(qt + 1)
                        if kt == 0:
                            u_.start()
                        inflight.append((u_, kt, qt, u_.front(kt)))
                        if lastq and layer == 0 and qt + 1 < NT:
                            load_qb(qt + 1)
                    if idx >= LA:
                        u_, kt, qt, pinfo = inflight.pop(0)
                        u_.back(kt, pinfo)
                        if kt == NK - 1:
                            u_.fin()
                            if stream[idx - LA][4]:
                                defer(9, lambda qt=qt: store_o(qt))
                            if stream[idx - LA][5]:
                                defer(9, lambda qt=qt: store_oa(qt))
                    tick()
                while pend:
                    pend.pop(0)[1]()
                P.barrier()

        def pass_C(layer, seq, xsrc, xdst, last):
            abanks["l"] = [3, 4, 5]
            sbanks["l"] = [0, 1, 2]
            AB.reset(); AFa.reset()
            b_w = [P.buf() for _ in range(2)]
            wst = [AB.alloc([128, 8, 1024]) for _ in range(2)]
            wctr = {"i": 0}
            Wdn = AB.alloc([128, 22, 1024]); b_Wdn = P.buf()
            Kmem = AB.alloc([128, 8, 256]); Vmem = AB.alloc([128, 2, 1024]); b_mem = P.buf()
            xt = AFa.alloc([128, 4, 1024]); b_xt = [P.buf() for _ in range(4)]
            junk = AFa.alloc([128, 1024]); b_junk = P.buf()
            ssr = AFa.alloc([128, 8]); b_s = [P.buf() for _ in range(4)]
            rlc = AFa.alloc([128, 512]); b_rlc = P.buf()
            sgl = AFa.alloc([128, 512]); b_sgl = P.buf()
            hb = [AB.alloc([128, 1024]) for _ in range(2)]; b_hb = [P.buf() for _ in range(2)]
            hT = AB.alloc([128, 8, 512]); b_hT = P.buf()
            qT = AB.alloc([128, 8, 512]); b_qT = P.buf()
            o2T = hT; b_o2T = b_hT
            if layer == 0:
                oa_t = AB.alloc([64, 8, 512]); ob_t = AB.alloc([128, 4, 512])
            else:
                o1_t = AB.alloc([64, 16, 512])
            b_ot = P.buf()
            actT = AB.alloc([128, 22, 512]); b_act = P.buf()
            Pc = [AB.alloc([128, 512]) for _ in range(2)]; b_Pc = [P.buf() for _ in range(2)]

            def wload(view_fn_list):
                s = wctr["i"] % 2; wctr["i"] += 1
                for dfn, src in view_fn_list:
                    P.op("sp", DMA(dfn(wst[s]), src), writes=[b_w[s]], dma=True)
                return wst[s], b_w[s]

            mt = junk
            for mi in range(2):
                P.op("sp", DMA(mt, mem_in[seq, mi * 128:(mi + 1) * 128, :]), writes=[b_junk], dma=True)
                P.op("act", ACT(xt[:, 0, :], mt, AF.Square, accum=ssr[:, 0:1]), reads=[b_junk], writes=[b_xt[0], b_s[0]])
                rms_rstd(ssr[:, 0:1], ssr[:, 1:2], float(D), b_s[0], b_s[0])
                P.op("dve", TS(hb[0], mt, ssr[:, 1:2]), reads=[b_junk, b_s[0]], writes=[b_hb[0]])
                for half in range(2):
                    pt, b_pt = nextT()
                    for k in range(4):
                        kc = half * 4 + k
                        P.op("pe", TR(pt[:, k * 128:(k + 1) * 128], hb[0][:, kc * 128:(kc + 1) * 128], ident), reads=[b_hb[0], b_const], writes=[b_pt])
                    P.op("act", ACP(hT[:, half * 4:half * 4 + 4, mi * 128:(mi + 1) * 128], pt.rearrange("p (k t) -> p k t", t=128)), reads=[b_pt], writes=[b_hT])
            for half in range(2):
                wv, b_wv = wload([(lambda a: a, Wckv_s[layer][:, :, half * 1024:(half + 1) * 1024])])
                if half == 0:
                    for m in range(8):
                        bk, b_bk = nextbank()
                        for kc in range(8):
                            P.op("pe", MM(bk[:, 0:256], wv[:, kc, m * 128:(m + 1) * 128], hT[:, kc, 0:256], kc == 0, kc == 7), reads=[b_wv, b_hT], writes=[b_bk])
                        P.op("act" if m % 2 else "dve", (ACP if m % 2 else CP)(Kmem[:, m, :], bk[:, 0:256]), reads=[b_bk], writes=[b_mem])
                else:
                    for mi in range(2):
                        for nh in range(2):
                            bk, b_bk = nextbank()
                            for kc in range(8):
                                P.op("pe", MM(bk, hT[:, kc, mi * 128:(mi + 1) * 128], wv[:, kc, nh * 512:(nh + 1) * 512], kc == 0, kc == 7), reads=[b_wv, b_hT], writes=[b_bk])
                            P.op("act" if nh else "dve", (ACP if nh else CP)(Vmem[:, mi, nh * 512:(nh + 1) * 512], bk), reads=[b_bk], writes=[b_mem])

            def add_proj(lhs_list, b_lhs, rhs_fn, b_rhs):
                for s in range(4):
                    for n in range(2):
                        bk, b_bk = nextbank()
                        nl = len(lhs_list)
                        for ci, lf in enumerate(lhs_list):
                            P.op("pe", MM(bk, lf(s), rhs_fn(ci, n), ci == 0, ci == nl - 1), reads=[b_lhs, b_rhs], writes=[b_bk])
                        P.op("dve", TT(xt[:, s, n * 512:(n + 1) * 512], xt[:, s, n * 512:(n + 1) * 512], bk, ALU.add), reads=[b_bk, b_xt[s]], writes=[b_xt[s]])

            def norm_tile():
                for s in range(4):
                    pa = s % 2
                    norm_to_hT(xt[:, s, :], b_xt[s], junk, b_junk, ssr[:, 0:1], ssr[:, 1:2], b_s[0], hb[pa], b_hb[pa], hT, b_hT, s * 128)

            for tt in range(NT):
                t0 = tt * 512
                sl = slice(t0, t0 + 512)
                for s in range(4):
                    P.op("sp", DMA(xt[:, s, :], xsrc[t0 + s * 128:t0 + (s + 1) * 128, :]), writes=[b_xt[s]], dma=True)
                if layer == 0:
                    for hq in range(8):
                        P.op("sp", DMA(oa_t[:, hq, :], OTa[hq, :, sl]), writes=[b_ot], dma=True)
                    for h in range(4):
                        P.op("sp", DMA(ob_t[:, h, :], OTb[h, :, sl]), writes=[b_ot], dma=True)
                else:
                    for h in range(16):
                        P.op("sp", DMA(o1_t[:, h, :], OT1[h, :, sl]), writes=[b_ot], dma=True)
                if layer == 0:
                    wv, b_wv = wload([(lambda a: a[0:64, :, :], Wout0a_s[:, :, :])])
                    wv2, b_wv2 = wload([(lambda a: a[:, 0:4, :], Wout0b_s[:, :, :])])
                    lhs = [(lambda s, hq=hq: oa_t[:, hq, s * 128:(s + 1) * 128]) for hq in range(8)]
                    add_proj(lhs, b_ot, lambda ci, n: wv[0:64, ci, n * 512:(n + 1) * 512], b_wv)
                    lhs = [(lambda s, h=h: ob_t[:, h, s * 128:(s + 1) * 128]) for h in range(4)]
                    add_proj(lhs, b_ot, lambda ci, n: wv2[:, ci, n * 512:(n + 1) * 512], b_wv2)
                else:
                    for hh in range(2):
                        wv, b_wv = wload([(lambda a: a[0:64, :, :], Wout1_s[:, hh * 8:(hh + 1) * 8, :])])
                        lhs = [(lambda s, h=h: o1_t[:, hh * 8 + h, s * 128:(s + 1) * 128]) for h in range(8)]
                        add_proj(lhs, b_ot, lambda ci, n, wv=wv: wv[0:64, ci, n * 512:(n + 1) * 512], b_wv)
                norm_tile()
                wv, b_wv = wload([(lambda a: a, Wcq_s[layer][:, :, :])])
                for m in range(8):
                    bk, b_bk = nextbank()
                    for kc in range(8):
                        P.op("pe", MM(bk, wv[:, kc, m * 128:(m + 1) * 128], hT[:, kc, :], kc == 0, kc == 7), reads=[b_wv, b_hT], writes=[b_bk])
                    P.op("act", ACT(qT[:, m, :], bk, AF.Copy, scale=1.0 / 16), reads=[b_bk], writes=[b_qT])
                for h in range(4):
                    L, b_L = nextA(); O0, b_O0 = nextA(); O1, b_O1 = nextA()
                    for mi in range(2):
                        sbk, b_sbk = nextS()
                        for dc in range(2):
                            P.op("pe", MM(sbk, Kmem[:, 2 * h + dc, mi * 128:(mi + 1) * 128], qT[:, 2 * h + dc, :], dc == 0, dc == 1), reads=[b_mem, b_qT], writes=[b_sbk])
                        P.op("act", ACT(Pc[mi], sbk, AF.Exp), reads=[b_sbk], writes=[b_Pc[mi]])
                        P.op("pe", MM(L, ones_b, Pc[mi], mi == 0, mi == 1), reads=[b_const, b_Pc[mi]], writes=[b_L])
                        P.op("pe", MM(O0, Vmem[:, mi, h * 256:h * 256 + 128], Pc[mi], mi == 0, mi == 1), reads=[b_mem, b_Pc[mi]], writes=[b_O0])
                        P.op("pe", MM(O1, Vmem[:, mi, h * 256 + 128:h * 256 + 256], Pc[mi], mi == 0, mi == 1), reads=[b_mem, b_Pc[mi]], writes=[b_O1])
                    P.op("act", ACT(rlc, L, AF.Ln), reads=[b_L], writes=[b_rlc])
                    P.op("act", ACT(rlc, rlc, AF.Exp, scale=-1.0), reads=[b_rlc], writes=[b_rlc])
                    P.op("dve", TT(o2T[:, 2 * h, :], O0, rlc, ALU.mult), reads=[b_O0, b_rlc], writes=[b_o2T])
                    P.op("dve", TT(o2T[:, 2 * h + 1, :], O1, rlc, ALU.mult), reads=[b_O1, b_rlc], writes=[b_o2T])
                wv, b_wv = wload([(lambda a: a, Wco_s[layer][:, :, :])])
                lhs = [(lambda s, kc=kc: o2T[:, kc, s * 128:(s + 1) * 128]) for kc in range(8)]
                add_proj(lhs, b_o2T, lambda ci, n, wv=wv: wv[:, ci, n * 512:(n + 1) * 512], b_wv)
                norm_tile()
                for kc in range(22):
                    P.op("sp", DMA(Wdn[:, kc, :], Wdn_s[layer][:, kc, :]), writes=[b_Wdn], dma=True)
                for j0 in range(0, 22, 4):
                    nj = min(4, 22 - j0)
                    wv, b_wv = wload([(lambda a, nj=nj: a[:, :, 0:nj * 128], Wgu_s[layer][:, :, j0 * 128:(j0 + nj) * 128]),
                                      (lambda a, nj=nj: a[:, :, 512:512 + nj * 128], Wgu_s[layer][:, :, DFF + j0 * 128:DFF + (j0 + nj) * 128])])
                    for jj in range(nj):
                        j = j0 + jj
                        gk, b_gk = nextbank(); uk, b_uk = nextbank()
                        for kc in range(8):
                            P.op("pe", MM(gk, wv[:, kc, jj * 128:(jj + 1) * 128], hT[:, kc, :], kc == 0, kc == 7), reads=[b_wv, b_hT], writes=[b_gk])
                        for kc in range(8):
                            P.op("pe", MM(uk, wv[:, kc, 512 + jj * 128:512 + (jj + 1) * 128], hT[:, kc, :], kc == 0, kc == 7), reads=[b_wv, b_hT], writes=[b_uk])
                        P.op("act", ACT(sgl, gk, AF.Silu), reads=[b_gk], writes=[b_sgl])
                        P.op("dve", TT(actT[:, j, :], sgl, uk, ALU.mult), reads=[b_sgl, b_uk], writes=[b_act])
                lhs = [(lambda s, j=j: actT[:, j, s * 128:(s + 1) * 128]) for j in range(22)]
                add_proj(lhs, b_act, lambda ci, n: Wdn[:, ci, n * 512:(n + 1) * 512], b_Wdn)
                for s in range(4):
                    rows = slice(t0 + s * 128, t0 + (s + 1) * 128)
                    if last:
                        P.op("act", ACT(junk, xt[:, s, :], AF.Square, accum=ssr[:, 2:3]), reads=[b_xt[s]], writes=[b_junk, b_s[1]])
                        rms_rstd(ssr[:, 2:3], ssr[:, 3:4], float(D), b_s[1], b_s[1])
                        P.op("dve", STT(xt[:, s, :], xt[:, s, :], ssr[:, 3:4], fin_b, ALU.mult, ALU.mult), reads=[b_xt[s], b_s[1], b_const], writes=[b_xt[s]])
                    P.op("pool", DMA(xdst[rows, :], xt[:, s, :]), reads=[b_xt[s]], dma=True)
            P.barrier()

        for seq in range(NSEQ if stop_after != "P" else 0):
            for layer in range(2):
                xsrc = x_in[seq] if layer == 0 else X1
                xdst = X1 if layer == 0 else y_out[seq]
                pass_A(layer, xsrc)
                if stop_after == "A":
                    break
                pass_B(layer)
                if stop_after == "B":
                    break
                pass_C(layer, seq, xsrc, xdst, layer == 1)
                if stop_after == "C0":
                    break
        P.barrier()
        P.finalize()
        build.stats = {e: len(P.ops[e]) for e in ENGS}
        P.emit(nc, st)
    return nc


_CACHE = {}


def kernel(x_prompt, x_sample, mem_prompt, mem_sample, **w):
    S = x_prompt.shape[1]
    xs = np.concatenate([np.asarray(x_prompt, np.float32), np.asarray(x_sample, np.float32)], axis=0)
    ms = np.concatenate([np.asarray(mem_prompt, np.float32), np.asarray(mem_sample, np.float32)], axis=0)
    ntot = xs.shape[0]
    nseq = ntot // N_CORES
    key = (nseq, S)
    if key not in _CACHE:
        _CACHE[key] = build(nseq, S)
    nc = _CACHE[key]
    consts = host_consts(S)
    wd = {n: np.ascontiguousarray(np.asarray(w[n], np.float32)) for n in WNAMES}
    in_maps = []
    for c in range(N_CORES):
        m = {"x": np.ascontiguousarray(xs[c * nseq:(c + 1) * nseq]), "mem": np.ascontiguousarray(ms[c * nseq:(c + 1) * nseq])}
        m.update(wd)
        m.update(consts)
        in_maps.append(m)
    res = run_bass_kernel_spmd(nc, in_maps, core_ids=list(range(N_CORES)))
    ys = np.concatenate([np.asarray(r["y"], np.float32) for r in res.results], axis=0)
    nb = x_prompt.shape[0]
    return (ys[:nb], ys[nb:])
```

```python
import math
from contextlib import ExitStack

import ml_dtypes
import numpy as np

import concourse.bass as bass
import concourse.mybir as mybir
from concourse.bass_utils import run_bass_kernel_spmd

F32 = mybir.dt.float32
BF16 = mybir.dt.bfloat16
ALU = mybir.AluOpType
AF = mybir.ActivationFunctionType
AX = mybir.AxisListType

D = 1024
DFF = 2816
NMEM = 256
EPS = 1e-6
GRID_W = 64
N_CORES = 8

ENGS = ["pe", "act", "dve", "pool", "sp"]
N_DMA_SEMS = 24


class Buf:
    __slots__ = ("name", "w", "r", "excl")

    def __init__(self, name, excl=False):
        self.name = name
        self.w = None
        self.r = []
        self.excl = excl


class Op:
    __slots__ = ("eng", "fn", "deps", "signal", "sem", "val", "dma", "waits")

    def __init__(self, eng, fn, dma):
        self.eng = eng
        self.fn = fn
        self.dma = dma
        self.deps = set()
        self.signal = False
        self.sem = None
        self.val = 0
        self.waits = []


class Plan:
    def __init__(self):
        self.ops = {e: [] for e in ENGS}
        self.all = []
        self.dma_rr = {"sp": 0, "pool": 0, "act": 0}
        self.dma_last = [None] * N_DMA_SEMS
        self.nbuf = 0

    def buf(self, name=None, excl=False):
        self.nbuf += 1
        return Buf(name or f"b{self.nbuf}", excl)

    def op(self, eng, fn, reads=(), writes=(), dma=False, after=()):
        o = Op(eng, fn, dma)
        deps = o.deps
        for b in reads:
            if b.w is not None:
                deps.add(b.w)
            if b.excl:
                for q in b.r:
                    if q.eng != eng:
                        deps.add(q)
        for b in writes:
            if b.w is not None:
                deps.add(b.w)
            deps.update(b.r)
        for a in after:
            if a is not None:
                deps.add(a)
        if dma:
            half = N_DMA_SEMS // 2
            k = (self.dma_rr[eng] % half) + (half if eng == "pool" else 0)
            self.dma_rr[eng] += 1
            prev = self.dma_last[k]
            if prev is not None:
                deps.add(prev)
            self.dma_last[k] = o
            o.sem = ("dma", k)
        else:
            o.sem = ("eng", eng)
        for b in reads:
            if not dma:
                b.r = [q for q in b.r if q.dma or q.eng != eng]
            b.r.append(o)
        for b in writes:
            b.w = o
            b.r = []
        self.ops[eng].append(o)
        self.all.append(o)
        return o

    def barrier(self):
        lasts = []
        for e in ENGS:
            if self.ops[e]:
                lasts.append(self.ops[e][-1])
        lasts += [d for d in self.dma_last if d is not None]
        for e in ENGS:
            self.op(e, None, after=lasts)

    @staticmethod
    def _skip(d, o):
        return (not d.dma) and (not o.dma) and d.eng == "pe" and o.eng == "pe"

    def finalize(self):
        for e in ENGS:
            for o in self.ops[e]:
                for d in o.deps:
                    if d.dma or self._skip(d, o):
                        continue
                    d.signal = True
        cnt = {}
        for o in self.all:
            if o.dma:
                cnt[o.sem] = cnt.get(o.sem, 0) + 16
                o.val = cnt[o.sem]
                o.signal = True
            elif o.signal:
                if o.fn is None:
                    o.signal = False
                    continue
                cnt[o.sem] = cnt.get(o.sem, 0) + 1
                o.val = cnt[o.sem]
        for e in ENGS:
            waited = {}
            for o in self.ops[e]:
                need = {}
                for d in o.deps:
                    if self._skip(d, o) or d is o:
                        continue
                    if d.fn is None:
                        continue
                    v = need.get(d.sem, 0)
                    if d.val > v:
                        need[d.sem] = d.val
                for s, v in need.items():
                    if waited.get(s, 0) < v:
                        waited[s] = v
                        o.waits.append((s, v))
        self.counts = cnt

    def emit(self, nc, stack):
        sems = {}
        for e in ENGS:
            sems[("eng", e)] = stack.enter_context(nc.semaphore(f"s_{e}"))
        for k in range(N_DMA_SEMS):
            sems[("dma", k)] = stack.enter_context(nc.semaphore(f"s_dma{k}"))
        block = stack.enter_context(nc.Block())
        plan = self

        def replay(ename):
            def run(h):
                for o in plan.ops[ename]:
                    for s, v in o.waits:
                        h.wait_ge(sems[s], v)
                    if o.fn is None:
                        continue
                    inst = o.fn(h)
                    if o.signal:
                        inst.then_inc(sems[o.sem], 16 if o.dma else 1)
            return run

        block.tensor(replay("pe"))
        block.scalar(replay("act"))
        block.vector(replay("dve"))
        block.gpsimd(replay("pool"))
        block.sync(replay("sp"))


def MM(out, lhsT, rhs, start=True, stop=True):
    return lambda e: e.matmul(out, lhsT=lhsT, rhs=rhs, start=start, stop=stop)


def TR(out, in_, ident):
    return lambda e: e.transpose(out=out, in_=in_, identity=ident)


def ACT(out, in_, func, bias=None, scale=None, accum=None):
    kw = {}
    if bias is not None:
        kw["bias"] = bias
    if scale is not None:
        kw["scale"] = scale
    if accum is not None:
        kw["accum_out"] = accum
    return lambda e: e.activation(out=out, in_=in_, func=func, **kw)


def DMA(out, in_, slow=False):
    if slow:
        return lambda e: e.dma_start(out=out, in_=in_, allow_slow_non_contiguous=True)
    return lambda e: e.dma_start(out=out, in_=in_)


def TS(out, in0, s1, s2=None, op0=ALU.mult, op1=None):
    if op1 is None:
        return lambda e: e.tensor_scalar(out=out, in0=in0, scalar1=s1, scalar2=None, op0=op0)
    return lambda e: e.tensor_scalar(out=out, in0=in0, scalar1=s1, scalar2=s2, op0=op0, op1=op1)


def TT(out, in0, in1, op):
    return lambda e: e.tensor_tensor(out=out, in0=in0, in1=in1, op=op)


def STT(out, in0, scalar, in1, op0, op1):
    return lambda e: e.scalar_tensor_tensor(out=out, in0=in0, scalar=scalar, in1=in1, op0=op0, op1=op1)


def CP(out, in_):
    return lambda e: e.tensor_copy(out=out, in_=in_)


def ACP(out, in_):
    return lambda e: e.copy(out=out, in_=in_)


def RECIP(out, in_):
    return lambda e: e.reciprocal(out=out, in_=in_)


def RSUM(out, in_):
    return lambda e: e.tensor_reduce(out=out, in_=in_, axis=AX.X, op=ALU.add)


def MSET(ap, v):
    return lambda e: e.memset(ap, v)


class Arena:
    def __init__(self, t, n):
        self.t = t
        self.n = n
        self.off = 0

    def reset(self):
        self.off = 0

    def alloc(self, shape):
        n = 1
        for s in shape[1:]:
            n *= s
        n = (n + 15) // 16 * 16
        o = self.off
        self.off += n
        assert self.off <= self.n, (self.off, self.n, shape)
        m = 1
        for s in shape[1:]:
            m *= s
        v = self.t[0:shape[0], o:o + m]
        if len(shape) == 3:
            v = v.rearrange("p (a b) -> p a b", b=shape[2])
        elif len(shape) == 4:
            v = v.rearrange("p (a b c) -> p a b c", b=shape[2], c=shape[3])
        return v


AB_N = 82944
AF_N = 8768

WNAMES = ["norm_mix", "e_w_in", "e_q_norm", "e_k_norm", "e_lam_q1", "e_lam_k1", "e_lam_q2", "e_lam_k2",
          "e_subln", "e_w_out", "o_w_in", "o_q_norm", "o_kv_norm", "o_w_uq", "o_w_ukv", "o_w_out",
          "norm_cross", "norm_mem", "w_cq", "w_ckv", "w_co", "norm_ffn", "w_gu", "w_down", "final_norm"]
WSHAPES = {
    "norm_mix": [2, D], "e_w_in": [1, D, 2304], "e_q_norm": [1, 64], "e_k_norm": [1, 64],
    "e_lam_q1": [1, 64], "e_lam_k1": [1, 64], "e_lam_q2": [1, 64], "e_lam_k2": [1, 64],
    "e_subln": [1, 128], "e_w_out": [1, D, D], "o_w_in": [1, D, 672], "o_q_norm": [1, 384],
    "o_kv_norm": [1, 256], "o_w_uq": [1, 384, 1536], "o_w_ukv": [1, 256, 2048], "o_w_out": [1, D, D],
    "norm_cross": [2, D], "norm_mem": [2, D], "w_cq": [2, D, D], "w_ckv": [2, D, 2 * D], "w_co": [2, D, D],
    "norm_ffn": [2, D], "w_gu": [2, D, 2 * DFF], "w_down": [2, DFF, D], "final_norm": [D],
}


def host_consts(S):
    c = {}
    c["ident"] = np.eye(128, dtype=np.float32).astype(ml_dtypes.bfloat16)
    c["ones_b"] = np.ones((128, 128), dtype=np.float32).astype(ml_dtypes.bfloat16)
    c["ones_f"] = np.ones((128, 64), dtype=np.float32)
    es = np.zeros((32, 96), dtype=np.float32)
    es[np.arange(32), 64 + np.arange(32)] = 1.0
    c["esel"] = es.astype(ml_dtypes.bfloat16)
    t = np.arange(S)
    fa = (10000.0 ** (-np.arange(16, dtype=np.float32) / 16)).astype(np.float32)
    r = (t // GRID_W).astype(np.float32)
    cc = (t % GRID_W).astype(np.float32)
    angA = np.concatenate([r[:, None] * fa, cc[:, None] * fa], axis=-1).astype(np.float32)
    fl = (10000.0 ** (-np.arange(16, dtype=np.float32) / 16)).astype(np.float32)
    angL = (t.astype(np.float32)[:, None] * fl).astype(np.float32)

    def exp_tabs(ang):
        co = np.repeat(np.cos(ang), 2, axis=-1).astype(np.float32)
        si = np.repeat(np.sin(ang), 2, axis=-1).astype(np.float32)
        si[:, 0::2] *= -1.0
        return co, si

    c["cexpA"], c["sexpA"] = exp_tabs(angA)
    c["cexpL"], c["sexpL"] = exp_tabs(angL)
    slopes = (2.0 ** (-8.0 * np.arange(1, 5) / 4)).astype(np.float64)
    p = np.arange(128)[:, None].astype(np.float64)
    j = np.arange(512)[None, :].astype(np.float64)
    m = np.arange(896)[None, :].astype(np.float64)
    ab = np.zeros((4, 128, 512)); bl = np.zeros((4, 128, 512)); st = np.zeros((4, 128, 896))
    for h in range(4):
        ab[h] = np.exp(-slopes[h] * (j - p + 127))
        bl[h] = np.exp(-slopes[h] * (p - j + 511))
        st[h] = np.exp(-slopes[h] * np.abs(m - 384 - p))
    c["al_above"] = ab.astype(np.float32).astype(ml_dtypes.bfloat16)
    c["al_below"] = bl.astype(np.float32).astype(ml_dtypes.bfloat16)
    c["al_strip"] = st.astype(np.float32).astype(ml_dtypes.bfloat16)
    bc = np.zeros((128, 4, 64), dtype=np.float32)
    for h in range(4):
        for mm_ in range(1, 32):
            bc[:, h, mm_] = -slopes[h] * (128 * mm_ - 127)
        for mm_ in range(4, 32):
            bc[:, h, 32 + mm_] = -slopes[h] * (128 * mm_ - 511)
    c["al_bias"] = bc.reshape(128, 256)
    return c


CONST_SPECS = lambda S: {
    "ident": ([128, 128], BF16), "ones_b": ([128, 128], BF16), "ones_f": ([128, 64], F32), "esel": ([32, 96], BF16),
    "cexpA": ([S, 64], F32), "sexpA": ([S, 64], F32), "cexpL": ([S, 32], F32), "sexpL": ([S, 32], F32),
    "al_above": ([4, 128, 512], BF16), "al_below": ([4, 128, 512], BF16), "al_strip": ([4, 128, 896], BF16),
    "al_bias": ([128, 256], F32),
}


def build(NSEQ, S, stop_after=None):
    NK = S // 128
    NT = S // 512
    nc = bass.Bass("TRN2", target_bir_lowering=False)

    def din(name, shape, dt=F32):
        return nc.dram_tensor(name, list(shape), dt, kind="ExternalInput").ap()

    def dscr(name, shape, dt=BF16):
        return nc.dram_tensor(name, list(shape), dt).ap()

    x_in = din("x", [NSEQ, S, D])
    mem_in = din("mem", [NSEQ, NMEM, D])
    W = {n: din(n, WSHAPES[n]) for n in WNAMES}
    C = {n: din(n, sh, dt) for n, (sh, dt) in CONST_SPECS(S).items()}
    y_out = nc.dram_tensor("y", [NSEQ, S, D], F32, kind="ExternalOutput").ap()

    Win0_s = dscr("Win0_s", [128, 8, 2304])
    Wout0_s = dscr("Wout0_s", [128, 8, 1024])
    Win1_s = dscr("Win1_s", [128, 8, 672])
    Wuq_s = dscr("Wuq_s", [128, 3, 1536])
    Wkp_s = dscr("Wkp_s", [128, 2, 16, 96])
    Wv_s = dscr("Wv_s", [128, 2, 1024])
    Wout1_s = dscr("Wout1_s", [128, 8, 1024])
    Wcq_s = [dscr(f"Wcq_s{l}", [128, 8, 1024]) for l in range(2)]
    Wckv_s = [dscr(f"Wckv_s{l}", [128, 8, 2048]) for l in range(2)]
    Wco_s = [dscr(f"Wco_s{l}", [128, 8, 1024]) for l in range(2)]
    Wgu_s = [dscr(f"Wgu_s{l}", [128, 8, 2 * DFF]) for l in range(2)]
    Wdn_s = [dscr(f"Wdn_s{l}", [128, 22, 1024]) for l in range(2)]
    QTa = dscr("QTa", [8, 64, S]); QTb = dscr("QTb", [4, 128, S]); QT1 = dscr("QT1", [16, 96, S])
    KTa = dscr("KTa", [128, S]); KTb = dscr("KTb", [4, 128, S]); KT1 = dscr("KT1", [16, 96, S])
    Va = dscr("Va", [S, 2, 65]); Vb = dscr("Vb", [S, 4, 128]); V1 = dscr("V1", [S, 16, 65])
    OTa = dscr("OTa", [8, 64, S]); OTb = dscr("OTb", [4, 128, S]); OT1 = dscr("OT1", [16, 64, S])
    X1 = dscr("X1", [S, D], F32)

    st = ExitStack()
    with st:
        abt = st.enter_context(nc.sbuf_tensor("arena_b", [128, AB_N], BF16))
        aft = st.enter_context(nc.sbuf_tensor("arena_f", [128, AF_N], F32))
        cft = st.enter_context(nc.sbuf_tensor("const_f", [128, 1792], F32))
        cbt = st.enter_context(nc.sbuf_tensor("const_b", [128, 128 + 128 + 96], BF16))
        psf = [st.enter_context(nc.psum_tensor(f"psf{i}", [128, 512], F32)) for i in range(8)]
        psT = [psf[6][:].bitcast(BF16), psf[7][:].bitcast(BF16)]
        AB = Arena(abt, AB_N)
        AFa = Arena(aft, AF_N)
        P = Plan()

        ident = cbt[:, 0:128]
        ones_b = cbt[:, 128:256]
        esel = cbt[0:32, 256:352]
        ones_f = cft[:, 0:64]
        gcols = cft[:, 64:64 + 80]
        GC = {"mix0": 0, "mix1": 8, "cross0": 16, "cross1": 24, "mem0": 32, "mem1": 40, "ffn0": 48, "ffn1": 56,
              "qn": 64, "kvn": 67, "one": 69}
        gq_b = cft[:, 160:224]; gk_b = cft[:, 224:288]; gqs_b = cft[:, 288:352]; gks_b = cft[:, 352:416]
        lam_t = cft[:, 416:420]
        gsub_c = cft[:, 420:421]
        lamw = cft[:, 424:424 + 4 * 64]
        fin_b = cft[:, 768:1792]
        b_const = P.buf("const")

        def ld(dst, src, slow=False):
            P.op("sp", DMA(dst, src, slow), writes=[b_const], dma=True)

        ld(ident, C["ident"][:, :]); ld(ones_b, C["ones_b"][:, :]); ld(esel, C["esel"][:, :]); ld(ones_f, C["ones_f"][:, :])
        for nm, src, kc in [("mix0", W["norm_mix"][0], 8), ("mix1", W["norm_mix"][1], 8),
                            ("cross0", W["norm_cross"][0], 8), ("cross1", W["norm_cross"][1], 8),
                            ("mem0", W["norm_mem"][0], 8), ("mem1", W["norm_mem"][1], 8),
                            ("ffn0", W["norm_ffn"][0], 8), ("ffn1", W["norm_ffn"][1], 8),
                            ("qn", W["o_q_norm"][0], 3), ("kvn", W["o_kv_norm"][0], 2)]:
            ld(gcols[:, GC[nm]:GC[nm] + kc], src.rearrange("(kc p) -> p kc", p=128), slow=True)
        P.op("dve", MSET(gcols[:, GC["one"]:GC["one"] + 1], 1.0), writes=[b_const])
        ld(gq_b, W["e_q_norm"][0].partition_broadcast(128)); ld(gk_b, W["e_k_norm"][0].partition_broadcast(128))
        ld(fin_b, W["final_norm"].partition_broadcast(128))
        for i, nm in enumerate(["e_lam_q1", "e_lam_k1", "e_lam_q2", "e_lam_k2"]):
            ld(lamw[:, i * 64:(i + 1) * 64], W[nm][0].partition_broadcast(128))
        ld(gsub_c, W["e_subln"][0].rearrange("(p o) -> p o", o=1), slow=True)
        for g, gs in [(gq_b, gqs_b), (gk_b, gks_b)]:
            gv = g.rearrange("p (i two) -> p i two", two=2); sv = gs.rearrange("p (i two) -> p i two", two=2)
            P.op("dve", CP(sv[:, :, 0:1], gv[:, :, 1:2]), reads=[b_const], writes=[b_const])
            P.op("dve", CP(sv[:, :, 1:2], gv[:, :, 0:1]), reads=[b_const], writes=[b_const])
        lam_init0 = 0.8 - 0.6 * math.exp(-0.3 * 0)
        tmpl = cft[:, 440 + 256:440 + 256 + 64]
        P.op("dve", TT(tmpl, lamw[:, 0:64], lamw[:, 64:128], ALU.mult), reads=[b_const], writes=[b_const])
        P.op("dve", RSUM(lam_t[:, 0:1], tmpl), reads=[b_const], writes=[b_const])
        P.op("dve", TT(tmpl, lamw[:, 128:192], lamw[:, 192:256], ALU.mult), reads=[b_const], writes=[b_const])
        P.op("dve", RSUM(lam_t[:, 1:2], tmpl), reads=[b_const], writes=[b_const])
        P.op("act", ACT(lam_t[:, 0:2], lam_t[:, 0:2], AF.Exp), reads=[b_const], writes=[b_const])
        P.op("dve", TT(lam_t[:, 2:3], lam_t[:, 0:1], lam_t[:, 1:2], ALU.subtract), reads=[b_const], writes=[b_const])
        P.op("dve", TS(lam_t[:, 3:4], lam_t[:, 2:3], lam_init0, -1.0, ALU.add, ALU.mult), reads=[b_const], writes=[b_const])
        nlam = lam_t[:, 3:4]
        P.op("dve", TS(gsub_c, gsub_c, 1.0 - lam_init0), reads=[b_const], writes=[b_const])

        AB.reset(); AFa.reset()
        NST = 3
        stf = [AFa.alloc([128, 2048]) for _ in range(NST)]
        stb = [AB.alloc([128, 2048]) for _ in range(NST)]
        zt = AB.alloc([128, 16, 32])
        b_stf = [P.buf() for _ in range(NST)]; b_stb = [P.buf() for _ in range(NST)]
        b_z = P.buf()
        P.op("dve", MSET(zt, 0.0), writes=[b_z])
        cstate = {"i": 0}

        def conv_piece(src2d, rows, cols, gain_col, outs):
            i = cstate["i"]; cstate["i"] += 1
            s = i % NST
            r0, r1 = rows; c0, c1 = cols
            np_, w = r1 - r0, c1 - c0
            P.op("sp", DMA(stf[s][0:np_, 0:w], src2d[r0:r1, c0:c1]), writes=[b_stf[s]], dma=True)
            if i % 2 == 0:
                P.op("dve", TS(stb[s][0:np_, 0:w], stf[s][0:np_, 0:w], gain_col[0:np_, :]), reads=[b_stf[s], b_const], writes=[b_stb[s]])
            else:
                P.op("act", ACT(stb[s][0:np_, 0:w], stf[s][0:np_, 0:w], AF.Copy, scale=gain_col[0:np_, :]), reads=[b_stf[s], b_const], writes=[b_stb[s]])
            for dst, vf in outs:
                P.op("pool", DMA(dst, vf(stb[s][0:np_, 0:w])), reads=[b_stb[s]], dma=True)

        def gc(nm, kc):
            return gcols[:, GC[nm] + kc:GC[nm] + kc + 1]

        one_c = gcols[:, GC["one"]:GC["one"] + 1]
        idv = lambda v: v

        def conv(src2d, KC, rows_p, col_ranges, dst, gname):
            for kc in range(KC):
                g = gc(gname, kc) if gname else one_c
                for (c0, c1, d0) in col_ranges:
                    for cc in range(c0, c1, 2048):
                        ce = min(cc + 2048, c1)
                        conv_piece(src2d, (kc * rows_p, (kc + 1) * rows_p), (cc, ce), g,
                                   [(dst[0:rows_p, kc, d0 + cc - c0:d0 + ce - c0], idv)])

        conv(W["e_w_in"][0], 8, 128, [(0, 512, 0), (768, 1280, 512), (512, 768, 1024), (1280, 2304, 1280)], Win0_s, "mix0")
        conv(W["e_w_out"][0], 8, 128, [(0, 1024, 0)], Wout0_s, None)
        conv(W["o_w_in"][0], 8, 128, [(0, 672, 0)], Win1_s, "mix1")
        conv(W["o_w_uq"][0], 3, 128, [(0, 1536, 0)], Wuq_s, "qn")
        for kc in range(2):
            conv_piece(W["o_w_ukv"][0], (kc * 128, (kc + 1) * 128), (0, 2048), gc("kvn", kc), [
                (Wkp_s[:, kc, :, 0:64], lambda v: v.rearrange("p (h e) -> p h e", e=128)[:, :, 0:64]),
                (Wv_s[:, kc, :].rearrange("p (h e) -> p h e", e=64), lambda v: v.rearrange("p (h e) -> p h e", e=128)[:, :, 64:128]),
            ])
            P.op("pool", DMA(Wkp_s[:, kc, :, 64:96], zt), reads=[b_z], dma=True)
        conv(W["o_w_out"][0], 8, 128, [(0, 1024, 0)], Wout1_s, None)
        for l in range(2):
            conv(W["w_cq"][l], 8, 128, [(0, 1024, 0)], Wcq_s[l], f"cross{l}")
            conv(W["w_ckv"][l], 8, 128, [(0, 2048, 0)], Wckv_s[l], f"mem{l}")
            conv(W["w_co"][l], 8, 128, [(0, 1024, 0)], Wco_s[l], None)
            conv(W["w_gu"][l], 8, 128, [(0, 2 * DFF, 0)], Wgu_s[l], f"ffn{l}")
            conv(W["w_down"][l], 22, 128, [(0, 1024, 0)], Wdn_s[l], None)
        P.barrier()

        bank_b = [P.buf(f"psf{i}", excl=True) for i in range(8)]
        ring = {"i": 0}

        def nextbank():
            i = ring["i"] % 6
            ring["i"] += 1
            return psf[i][:], bank_b[i]

        sring = {"i": 0}
        aring = {"i": 0}

        sbanks = {"l": [0, 1, 2]}

        def nextS():
            i = sbanks["l"][sring["i"] % len(sbanks["l"])]
            sring["i"] += 1
            return psf[i][:], bank_b[i]

        abanks = {"l": [3, 4, 5]}

        def nextA():
            i = abanks["l"][aring["i"] % len(abanks["l"])]
            aring["i"] += 1
            return psf[i][:], bank_b[i]

        tring = {"i": 0}

        def nextT():
            i = tring["i"] % 2
            tring["i"] += 1
            return psT[i][:, 0:512], bank_b[6 + i]

        def rms_rstd(ss, rs, n, b_ss, b_rs):
            P.op("act", ACT(rs, ss, AF.Ln, bias=EPS, scale=1.0 / n), reads=[b_ss], writes=[b_rs])
            P.op("act", ACT(rs, rs, AF.Exp, scale=-0.5), reads=[b_rs], writes=[b_rs])

        def rope(out, v, ta, tb, tmp, nh, nd, b_v, b_tab, b_tmp, b_out, eng="dve"):
            t1, t2 = tmp
            tab = ta.unsqueeze(1).to_broadcast([128, nh, nd])
            P.op(eng, TT(t1, v, tab, ALU.mult), reads=[b_v, b_tab], writes=[b_tmp])
            vv = v.rearrange("p h (i two) -> p h i two", two=2)
            t2v = t2.rearrange("p h (i two) -> p h i two", two=2)
            tbv = tb.rearrange("p (i two) -> p i two", two=2).unsqueeze(1).to_broadcast([128, nh, nd // 2, 2])
            P.op(eng, TT(t2v[:, :, :, 0:1], vv[:, :, :, 1:2], tbv[:, :, :, 0:1], ALU.mult), reads=[b_v, b_tab], writes=[b_tmp])
            P.op(eng, TT(t2v[:, :, :, 1:2], vv[:, :, :, 0:1], tbv[:, :, :, 1:2], ALU.mult), reads=[b_v, b_tab], writes=[b_tmp])
            P.op(eng, TT(out, t1, t2, ALU.add), reads=[b_tmp], writes=[b_out])

        def norm_to_hT(xt, b_xt, junk, b_junk, ss, rs, b_s, hb, b_hb, hT, b_hT, col0):
            P.op("act", ACT(hb, xt, AF.Square, accum=ss), reads=[b_xt], writes=[b_hb, b_s])
            rms_rstd(ss, rs, float(D), b_s, b_s)
            P.op("dve", TS(hb, xt, rs), reads=[b_xt, b_s], writes=[b_hb])
            for half in range(2):
                pt, b_pt = nextT()
                for k in range(4):
                    kc = half * 4 + k
                    P.op("pe", TR(pt[:, k * 128:(k + 1) * 128], hb[:, kc * 128:(kc + 1) * 128], ident), reads=[b_hb, b_const], writes=[b_pt])
                dst = hT[:, half * 4:half * 4 + 4, col0:col0 + 128]
                src = pt.rearrange("p (k t) -> p k t", t=128)
                if half == 0:
                    P.op("act", ACP(dst, src), reads=[b_pt], writes=[b_hT])
                else:
                    P.op("dve", CP(dst, src), reads=[b_pt], writes=[b_hT])

        def pass_A(layer, xsrc):
            AB.reset(); AFa.reset()
            ncols = 2304 if layer == 0 else 672
            Win = AB.alloc([128, 8, ncols]); b_Win = P.buf()
            Wsrc = Win0_s if layer == 0 else Win1_s
            for kc in range(8):
                P.op("sp", DMA(Win[:, kc, :], Wsrc[:, kc, :]), writes=[b_Win], dma=True)
            if layer == 1:
                Wuq = AB.alloc([128, 3, 1536]); Wkp = AB.alloc([128, 2, 16, 96]); Wv = AB.alloc([128, 2, 1024])
                for kc in range(3):
                    P.op("sp", DMA(Wuq[:, kc, :], Wuq_s[:, kc, :]), writes=[b_Win], dma=True)
                for kc in range(2):
                    P.op("sp", DMA(Wkp[:, kc, :, :], Wkp_s[:, kc, :, :]), writes=[b_Win], dma=True)
                    P.op("sp", DMA(Wv[:, kc, :], Wv_s[:, kc, :]), writes=[b_Win], dma=True)
            xt = [AFa.alloc([128, 1024]) for _ in range(2)]; b_xt = [P.buf() for _ in range(2)]
            junk = AFa.alloc([128, 1024]); b_junk = P.buf()
            ssr = [AFa.alloc([128, 4]) for _ in range(2)]; b_s = [P.buf() for _ in range(2)]
            hb = [AB.alloc([128, 1024]) for _ in range(2)]; b_hb = [P.buf() for _ in range(2)]
            hT = [AB.alloc([128, 8, 128]) for _ in range(2)]; b_hT = [P.buf() for _ in range(2)]
            tabs = [AFa.alloc([128, 128]) for _ in range(2)]; b_tabs = [P.buf() for _ in range(2)]
            tq = [AFa.alloc([128, 256]) for _ in range(2)]; b_tq = [P.buf() for _ in range(2)]
            t1 = AFa.alloc([128, 512]); t2 = AFa.alloc([128, 512]); b_t = P.buf()
            qn = AFa.alloc([128, 512]); b_qn = P.buf()
            sq8 = AFa.alloc([128, 16]); b_sq8 = P.buf()
            if layer == 0:
                nstT = 13
            else:
                nstT = 16
            if layer == 0:
                qf = AB.alloc([128, 1024]); b_qf = P.buf()
                kf = AB.alloc([128, 640]); b_kf = P.buf()
                stT = [AB.alloc([128, 13, 512]) for _ in range(2)]; b_stT = [P.buf() for _ in range(2)]
                vast = [AB.alloc([128, 2, 65]) for _ in range(2)]; b_va = [P.buf() for _ in range(2)]
                vbst = [AB.alloc([128, 512]) for _ in range(2)]; b_vb = [P.buf() for _ in range(2)]
                for v_ in vast:
                    P.op("dve", MSET(v_[:, :, 64:65], 1.0), writes=[b_va[0], b_va[1]])
            else:
                qf = AB.alloc([128, 16, 96]); b_qf = P.buf()
                cn = AB.alloc([128, 640]); b_cn = P.buf()
                krf = AB.alloc([128, 32]); b_krf = P.buf()
                cT = AB.alloc([128, 5, 128]); b_cT = P.buf()
                krT = AB.alloc([32, 128]); b_krT = P.buf()
                stQ = [AB.alloc([96, 16, 512]) for _ in range(2)]; b_stQ = [P.buf() for _ in range(2)]
                stK = [AB.alloc([96, 16, 512]) for _ in range(2)]; b_stK = [P.buf() for _ in range(2)]
                v1st = [AB.alloc([128, 16, 65]) for _ in range(2)]; b_v1 = [P.buf() for _ in range(2)]
                for v_ in v1st:
                    P.op("dve", MSET(v_[:, :, 64:65], 1.0), writes=[b_v1[0], b_v1[1]])
            sc0 = 0.125
            sc1 = 96.0 ** -0.5
            for i in range(NK):
                pa = i % 2
                t0 = i * 128
                blk = i // 4
                sb = blk % 2
                c4 = (i % 4) * 128
                P.op("sp", DMA(xt[pa], xsrc[t0:t0 + 128, :]), writes=[b_xt[pa]], dma=True)
                if layer == 0:
                    P.op("sp", DMA(tabs[pa][:, 0:64], C["cexpA"][t0:t0 + 128, :]), writes=[b_tabs[pa]], dma=True)
                    P.op("sp", DMA(tabs[pa][:, 64:128], C["sexpA"][t0:t0 + 128, :]), writes=[b_tabs[pa]], dma=True)
                else:
                    P.op("sp", DMA(tabs[pa][:, 0:32], C["cexpL"][t0:t0 + 128, :]), writes=[b_tabs[pa]], dma=True)
                    P.op("sp", DMA(tabs[pa][:, 64:96], C["sexpL"][t0:t0 + 128, :]), writes=[b_tabs[pa]], dma=True)
                norm_to_hT(xt[pa], b_xt[pa], junk, b_junk, ssr[pa][:, 0:1], ssr[pa][:, 1:2], b_s[pa], hb[pa], b_hb[pa], hT[pa], b_hT[pa], 0)
                if layer == 0:
                    P.op("dve", STT(tq[pa][:, 0:64], tabs[pa][:, 0:64], sc0, gq_b, ALU.mult, ALU.mult), reads=[b_tabs[pa], b_const], writes=[b_tq[pa]])
                    P.op("dve", STT(tq[pa][:, 64:128], tabs[pa][:, 64:128], sc0, gqs_b, ALU.mult, ALU.mult), reads=[b_tabs[pa], b_const], writes=[b_tq[pa]])
                    P.op("dve", TT(tq[pa][:, 128:192], tabs[pa][:, 0:64], gk_b, ALU.mult), reads=[b_tabs[pa], b_const], writes=[b_tq[pa]])
                    P.op("dve", TT(tq[pa][:, 192:256], tabs[pa][:, 64:128], gks_b, ALU.mult), reads=[b_tabs[pa], b_const], writes=[b_tq[pa]])
                    groups = [(0, 512), (512, 1024), (1024, 1280), (1280, 1792), (1792, 2304)]
                    pg = []
                    for (c0, c1) in groups:
                        bk, b_bk = nextbank()
                        for kc in range(8):
                            P.op("pe", MM(bk[:, 0:c1 - c0], hT[pa][:, kc, :], Win[:, kc, c0:c1], kc == 0, kc == 7), reads=[b_hT[pa], b_Win], writes=[b_bk])
                        pg.append((bk, b_bk))
                    (p0, b0), (p1, b1), (p2, b2), (p3, b3), (p4, b4) = pg
                    P.op("act", ACT(t1, p0, AF.Square), reads=[b0], writes=[b_t])
                    P.op("dve", RSUM(sq8[:, 0:8], t1.rearrange("p (h d) -> p h d", d=64)), reads=[b_t], writes=[b_sq8])
                    rms_rstd(sq8[:, 0:8], sq8[:, 0:8], 64.0, b_sq8, b_sq8)
                    qn3 = qn.rearrange("p (h d) -> p h d", d=64)
                    P.op("dve", TT(qn3, p0.rearrange("p (h d) -> p h d", d=64), sq8[:, 0:8].unsqueeze(2).to_broadcast([128, 8, 64]), ALU.mult), reads=[b0, b_sq8], writes=[b_qn])
                    rope(qf[:, 0:512].rearrange("p (h d) -> p h d", d=64), qn3, tq[pa][:, 0:64], tq[pa][:, 64:128],
                         (t1.rearrange("p (h d) -> p h d", d=64), t2.rearrange("p (h d) -> p h d", d=64)), 8, 64, b_qn, b_tq[pa], b_t, b_qf)
                    P.op("act", ACT(qf[:, 512:1024], p1, AF.Copy, scale=sc0), reads=[b1], writes=[b_qf])
                    P.op("act", ACT(t1[:, 0:128], p2[:, 0:128], AF.Square), reads=[b2], writes=[b_t])
                    P.op("dve", RSUM(sq8[:, 8:10], t1[:, 0:128].rearrange("p (h d) -> p h d", d=64)), reads=[b_t], writes=[b_sq8])
                    rms_rstd(sq8[:, 8:10], sq8[:, 8:10], 64.0, b_sq8, b_sq8)
                    kn3 = qn[:, 0:128].rearrange("p (h d) -> p h d", d=64)
                    P.op("dve", TT(kn3, p2[:, 0:128].rearrange("p (h d) -> p h d", d=64), sq8[:, 8:10].unsqueeze(2).to_broadcast([128, 2, 64]), ALU.mult), reads=[b2, b_sq8], writes=[b_qn])
                    rope(kf[:, 0:128].rearrange("p (h d) -> p h d", d=64), kn3, tq[pa][:, 128:192], tq[pa][:, 192:256],
                         (t1[:, 0:128].rearrange("p (h d) -> p h d", d=64), t2[:, 0:128].rearrange("p (h d) -> p h d", d=64)), 2, 64, b_qn, b_tq[pa], b_t, b_kf)
                    P.op("act", ACP(vast[pa][:, :, 0:64], p2[:, 128:256].rearrange("p (h d) -> p h d", d=64)), reads=[b2], writes=[b_va[pa]])
                    P.op("pool", DMA(Va[t0:t0 + 128, :, :], vast[pa]), reads=[b_va[pa]], dma=True)
                    P.op("act", ACP(kf[:, 128:640], p3), reads=[b3], writes=[b_kf])
                    P.op("dve", CP(vbst[pa], p4), reads=[b4], writes=[b_vb[pa]])
                    P.op("pool", DMA(Vb[t0:t0 + 128, :, :].rearrange("t h e -> t (h e)"), vbst[pa]), reads=[b_vb[pa]], dma=True)
                    srcs = [(qf, b_qf, c) for c in range(8)] + [(kf, b_kf, c) for c in range(5)]
                    for j0 in range(0, 13, 4):
                        pt, b_pt = nextT()
                        n = min(4, 13 - j0)
                        for k in range(n):
                            sa, sbuf_, c = srcs[j0 + k]
                            P.op("pe", TR(pt[:, k * 128:(k + 1) * 128], sa[:, c * 128:(c + 1) * 128], ident), reads=[sbuf_, b_const], writes=[b_pt])
                        dst = stT[sb][:, j0:j0 + n, c4:c4 + 128]
                        src = pt[:, 0:n * 128].rearrange("p (k t) -> p k t", t=128)
                        P.op("act" if (j0 // 4) % 2 == 0 else "dve", (ACP if (j0 // 4) % 2 == 0 else CP)(dst, src), reads=[b_pt], writes=[b_stT[sb]])
                    if i % 4 == 3:
                        tb0 = blk * 512
                        s_ = stT[sb]
                        for c in range(4):
                            P.op("pool", DMA(QTa[2 * c, :, tb0:tb0 + 512], s_[0:64, c, :]), reads=[b_stT[sb]], dma=True)
                            P.op("pool", DMA(QTa[2 * c + 1, :, tb0:tb0 + 512], s_[64:128, c, :]), reads=[b_stT[sb]], dma=True)
                            P.op("pool", DMA(QTb[c, :, tb0:tb0 + 512], s_[:, 4 + c, :]), reads=[b_stT[sb]], dma=True)
                            P.op("pool", DMA(KTb[c, :, tb0:tb0 + 512], s_[:, 9 + c, :]), reads=[b_stT[sb]], dma=True)
                        P.op("pool", DMA(KTa[:, tb0:tb0 + 512], s_[:, 8, :]), reads=[b_stT[sb]], dma=True)
                else:
                    p0, b0 = nextbank(); p1, b1 = nextbank()
                    for kc in range(8):
                        P.op("pe", MM(p0[:, 0:384], hT[pa][:, kc, :], Win[:, kc, 0:384], kc == 0, kc == 7), reads=[b_hT[pa], b_Win], writes=[b0])
                    for kc in range(8):
                        P.op("pe", MM(p1[:, 0:288], hT[pa][:, kc, :], Win[:, kc, 384:672], kc == 0, kc == 7), reads=[b_hT[pa], b_Win], writes=[b1])
                    P.op("act", ACT(t1[:, 0:384], p0[:, 0:384], AF.Square, accum=sq8[:, 0:1]), reads=[b0], writes=[b_t, b_sq8])
                    P.op("act", ACT(t1[:, 0:256], p1[:, 0:256], AF.Square, accum=sq8[:, 1:2]), reads=[b1], writes=[b_t, b_sq8])
                    rms_rstd(sq8[:, 0:1], sq8[:, 2:3], 384.0, b_sq8, b_sq8)
                    rms_rstd(sq8[:, 1:2], sq8[:, 3:4], 256.0, b_sq8, b_sq8)
                    P.op("dve", TS(cn[:, 0:384], p0[:, 0:384], sq8[:, 2:3]), reads=[b0, b_sq8], writes=[b_cn])
                    P.op("dve", TS(cn[:, 384:640], p1[:, 0:256], sq8[:, 3:4]), reads=[b1, b_sq8], writes=[b_cn])
                    P.op("act", ACP(qn[:, 0:32], p1[:, 256:288]), reads=[b1], writes=[b_qn])
                    rope(krf.rearrange("p (h d) -> p h d", h=1), qn[:, 0:32].rearrange("p (h d) -> p h d", h=1), tabs[pa][:, 0:32], tabs[pa][:, 64:96],
                         (t1[:, 0:32].rearrange("p (h d) -> p h d", h=1), t2[:, 0:32].rearrange("p (h d) -> p h d", h=1)), 1, 32, b_qn, b_tabs[pa], b_t, b_krf)
                    for (j0, n) in [(0, 4), (4, 1)]:
                        pt, b_pt = nextT()
                        for k in range(n):
                            c = j0 + k
                            P.op("pe", TR(pt[:, k * 128:(k + 1) * 128], cn[:, c * 128:(c + 1) * 128], ident), reads=[b_cn, b_const], writes=[b_pt])
                        if j0 == 4:
                            P.op("pe", TR(pt[0:32, 128:256], krf, ident), reads=[b_krf, b_const], writes=[b_pt])
                            P.op("dve", CP(krT, pt[0:32, 128:256]), reads=[b_pt], writes=[b_krT])
                        P.op("act", ACP(cT[:, j0:j0 + n, :], pt[:, 0:n * 128].rearrange("p (k t) -> p k t", t=128)), reads=[b_pt], writes=[b_cT])
                    for (h0, h1) in [(0, 5), (5, 10), (10, 15), (15, 16)]:
                        bk, b_bk = nextbank()
                        nh = h1 - h0
                        for kc in range(3):
                            P.op("pe", MM(bk[:, 0:nh * 96], cT[:, kc, :], Wuq[:, kc, h0 * 96:h1 * 96], kc == 0, kc == 2), reads=[b_cT, b_Win], writes=[b_bk])
                        bv = bk[:, 0:nh * 96].rearrange("p (h d) -> p h d", d=96)
                        P.op("act", ACT(qf[:, h0:h1, 0:64], bv[:, :, 0:64], AF.Copy, scale=sc1), reads=[b_bk], writes=[b_qf])
                        P.op("act", ACT(qn[:, 0:nh * 32].rearrange("p (h d) -> p h d", d=32), bv[:, :, 64:96], AF.Copy, scale=sc1), reads=[b_bk], writes=[b_qn])
                        rope(qf[:, h0:h1, 64:96], qn[:, 0:nh * 32].rearrange("p (h d) -> p h d", d=32), tabs[pa][:, 0:32], tabs[pa][:, 64:96],
                             (t1[:, 0:nh * 32].rearrange("p (h d) -> p h d", d=32), t2[:, 0:nh * 32].rearrange("p (h d) -> p h d", d=32)), nh, 32, b_qn, b_tabs[pa], b_t, b_qf)
                    for j0 in range(0, 16, 4):
                        pt, b_pt = nextT()
                        for k in range(4):
                            P.op("pe", TR(pt[0:96, k * 128:(k + 1) * 128], qf[:, j0 + k, :], ident), reads=[b_qf, b_const], writes=[b_pt])
                        dst = stQ[sb][:, j0:j0 + 4, c4:c4 + 128]
                        src = pt[0:96, :].rearrange("p (k t) -> p k t", t=128)
                        P.op("act" if (j0 // 4) % 2 == 0 else "dve", (ACP if (j0 // 4) % 2 == 0 else CP)(dst, src), reads=[b_pt], writes=[b_stQ[sb]])
                    for j0 in range(0, 16, 4):
                        bk, b_bk = nextbank()
                        for k in range(4):
                            h = j0 + k
                            o_ = bk[0:96, k * 128:(k + 1) * 128]
                            P.op("pe", MM(o_, esel, krT, True, False), reads=[b_krT, b_const], writes=[b_bk])
                            P.op("pe", MM(o_, Wkp[:, 0, h, :], cT[:, 3, :], False, False), reads=[b_cT, b_Win], writes=[b_bk])
                            P.op("pe", MM(o_, Wkp[:, 1, h, :], cT[:, 4, :], False, True), reads=[b_cT, b_Win], writes=[b_bk])
                        dst = stK[sb][:, j0:j0 + 4, c4:c4 + 128]
                        src = bk[0:96, :].rearrange("p (k t) -> p k t", t=128)
                        P.op("act" if (j0 // 4) % 2 == 1 else "dve", (ACP if (j0 // 4) % 2 == 1 else CP)(dst, src), reads=[b_bk], writes=[b_stK[sb]])
                    for hh in range(2):
                        bk, b_bk = nextbank()
                        for kc in range(2):
                            P.op("pe", MM(bk, cT[:, 3 + kc, :], Wv[:, kc, hh * 512:(hh + 1) * 512], kc == 0, kc == 1), reads=[b_cT, b_Win], writes=[b_bk])
                        P.op("dve" if hh == 0 else "act", (CP if hh == 0 else ACP)(v1st[pa][:, hh * 8:(hh + 1) * 8, 0:64], bk.rearrange("p (h d) -> p h d", d=64)), reads=[b_bk], writes=[b_v1[pa]])
                    P.op("pool", DMA(V1[t0:t0 + 128, :, :], v1st[pa]), reads=[b_v1[pa]], dma=True)
                    if i % 4 == 3:
                        tb0 = blk * 512
                        for h in range(16):
                            P.op("pool", DMA(QT1[h, :, tb0:tb0 + 512], stQ[sb][:, h, :]), reads=[b_stQ[sb]], dma=True)
                            P.op("pool", DMA(KT1[h, :, tb0:tb0 + 512], stK[sb][:, h, :]), reads=[b_stK[sb]], dma=True)
            P.barrier()

        LA = 3
        NP = 4

        def pass_B(layer):
            abanks["l"] = [4, 5, 6, 7]
            sbanks["l"] = [0, 1, 2, 3]
            ngroups = 1 if layer == 0 else 2
            for g in range(ngroups):
                AB.reset(); AFa.reset()
                b_kv = P.buf()
                if layer == 0:
                    KTa_sb = AB.alloc([128, S]); KTb_sb = AB.alloc([128, 4, S])
                    Va_sb = AB.alloc([128, NK, 2, 65]); Vb_sb = AB.alloc([128, NK, 4, 128])
                    for blk in range(NT):
                        sl = slice(blk * 512, (blk + 1) * 512)
                        P.op("sp", DMA(KTa_sb[:, sl], KTa[:, sl]), writes=[b_kv], dma=True)
                        for h in range(4):
                            P.op("sp", DMA(KTb_sb[:, h, sl], KTb[h, :, sl]), writes=[b_kv], dma=True)
                    for kt in range(NK):
                        P.op("sp", DMA(Va_sb[:, kt, :, :], Va[kt * 128:(kt + 1) * 128, :, :]), writes=[b_kv], dma=True)
                        P.op("sp", DMA(Vb_sb[:, kt, :, :], Vb[kt * 128:(kt + 1) * 128, :, :]), writes=[b_kv], dma=True)
                    al_ab = AB.alloc([128, 4, 512]); al_bl = AB.alloc([128, 4, 512]); al_st = AB.alloc([128, 4, 896])
                    al_bias = AFa.alloc([128, 256])
                    for h in range(4):
                        P.op("sp", DMA(al_ab[:, h, :], C["al_above"][h]), writes=[b_kv], dma=True)
                        P.op("sp", DMA(al_bl[:, h, :], C["al_below"][h]), writes=[b_kv], dma=True)
                        P.op("sp", DMA(al_st[:, h, :], C["al_strip"][h]), writes=[b_kv], dma=True)
                    P.op("sp", DMA(al_bias, C["al_bias"][:, :]), writes=[b_kv], dma=True)
                else:
                    KT_sb = AB.alloc([96, 8, S]); V_sb = AB.alloc([128, NK, 8, 65])
                    for blk in range(NT):
                        sl = slice(blk * 512, (blk + 1) * 512)
                        for h in range(8):
                            P.op("sp", DMA(KT_sb[:, h, sl], KT1[g * 8 + h, :, sl]), writes=[b_kv], dma=True)
                    for kt in range(NK):
                        P.op("sp", DMA(V_sb[:, kt, :, :], V1[kt * 128:(kt + 1) * 128, g * 8:(g + 1) * 8, :]), writes=[b_kv], dma=True)
                b_q = [P.buf() for _ in range(2)]; b_o = [P.buf() for _ in range(2)]
                b_qb1 = P.buf(); b_ob1 = P.buf()
                if layer == 0:
                    qa_t = [AB.alloc([128, 8, 512]) for _ in range(2)]
                    qb_t = AB.alloc([128, 4, 2, 512])
                    oa_st = AB.alloc([64, 8, 512]); ob_st = AB.alloc([128, 4, 512])
                    for pa_ in range(2):
                        P.op("dve", MSET(qa_t[pa_], 0.0), writes=[b_q[pa_]])
                    P.op("dve", MSET(qb_t, 0.0), writes=[b_qb1])
                else:
                    q1_t = [AB.alloc([96, 8, 512]) for _ in range(2)]
                    o1_st = [AB.alloc([64, 8, 512]) for _ in range(2)]
                Pt = [AB.alloc([128, 512]) for _ in range(NP)]; b_P = [P.buf() for _ in range(NP)]
                if layer == 0:
                    P2 = [AB.alloc([128, 512]) for _ in range(NP)]; b_P2 = [P.buf() for _ in range(NP)]
                    sqb = AB.alloc([128, 512]); b_sqb = P.buf()
                    tc_ = [AFa.alloc([128, 512]) for _ in range(2)]; b_tc = [P.buf() for _ in range(2)]
                    tt_ = AFa.alloc([128, 512]); b_tt = P.buf()
                rl = [AFa.alloc([128, 512]) for _ in range(2)]; b_rl = [P.buf() for _ in range(2)]
                bcs = [AFa.alloc([128, 512]) for _ in range(2)]; b_bcs = [P.buf() for _ in range(2)]
                pctr = {"i": 0, "u": 0}
                pend = []

                def defer(n, fn):
                    pend.append([n, fn])

                def tick():
                    for it in pend:
                        it[0] -= 1
                    while pend and pend[0][0] <= 0:
                        pend.pop(0)[1]()

                def load_qb(qt):
                    sl = slice(qt * 512, (qt + 1) * 512)
                    for h in range(4):
                        for c in range(2):
                            P.op("sp", DMA(qb_t[c * 64:c * 64 + 64, h, c, :], QTb[h, c * 64:c * 64 + 64, sl]), writes=[b_qb1], dma=True)

                def load_q(qt):
                    pa = qt % 2
                    sl = slice(qt * 512, (qt + 1) * 512)
                    if layer == 0:
                        for hq in range(8):
                            P.op("sp", DMA(qa_t[pa][(hq // 4) * 64:(hq // 4) * 64 + 64, hq, :], QTa[hq, :, sl]), writes=[b_q[pa]], dma=True)
                    else:
                        for h in range(8):
                            P.op("sp", DMA(q1_t[pa][:, h, :], QT1[g * 8 + h, :, sl]), writes=[b_q[pa]], dma=True)

                def store_oa(qt):
                    sl = slice(qt * 512, (qt + 1) * 512)
                    for hq in range(8):
                        P.op("pool", DMA(OTa[hq, :, sl], oa_st[:, hq, :]), reads=[b_o[0]], dma=True)

                def store_o(qt):
                    pa = qt % 2
                    sl = slice(qt * 512, (qt + 1) * 512)
                    if layer == 0:
                        for h in range(4):
                            P.op("pool", DMA(OTb[h, :, sl], ob_st[:, h, :]), reads=[b_ob1], dma=True)
                    else:
                        for h in range(8):
                            P.op("pool", DMA(OT1[g * 8 + h, :, sl], o1_st[pa][:, h, :]), reads=[b_o[pa]], dma=True)

                class AugUnit:
                    def __init__(self, Qap, Kfn, Vfn, o_dst, b_odst, b_qb):
                        self.Qap, self.Kfn, self.Vfn, self.o_dst, self.b_odst, self.b_qb = Qap, Kfn, Vfn, o_dst, b_odst, b_qb
                        self.kts = list(range(NK))

                    def start(self):
                        self.O, self.b_O = nextA()
                        self.u = pctr["u"] % 2; pctr["u"] += 1

                    def front(self, kt):
                        sbk, b_sbk = nextS()
                        P.op("pe", MM(sbk, self.Kfn(kt), self.Qap, True, True), reads=[b_kv, self.b_qb], writes=[b_sbk])
                        pi = pctr["i"] % NP; pctr["i"] += 1
                        P.op("act", ACT(Pt[pi], sbk, AF.Exp), reads=[b_sbk], writes=[b_P[pi]])
                        return Pt[pi], b_P[pi]

                    def back(self, kt, pinfo):
                        pt, b_pt = pinfo
                        P.op("pe", MM(self.O[0:65, :], self.Vfn(kt), pt, kt == self.kts[0], kt == self.kts[-1]), reads=[b_kv, b_pt], writes=[self.b_O])

                    def fin(self):
                        u = self.u; O = self.O; b_O = self.b_O
                        P.op("dve", RECIP(rl[u][64:65, :], O[64:65, :]), reads=[b_O], writes=[b_rl[u]])

                        def part2():
                            bcp, b_bcp = nextS()
                            P.op("pe", MM(bcp[0:64, :], ones_f[64:65, 0:64], rl[u][64:65, :], True, True), reads=[b_rl[u], b_const], writes=[b_bcp])
                            P.op("act", ACP(bcs[u][0:64, :], bcp[0:64, :]), reads=[b_bcp], writes=[b_bcs[u]])
                            P.op("dve", TT(self.o_dst, O[0:64, :], bcs[u][0:64, :], ALU.mult), reads=[b_O, b_bcs[u]], writes=[self.b_odst])
                        defer(5, part2)

                def diff_kts(h, qt):
                    sl_ = 2.0 ** (-8.0 * (h + 1) / 4)
                    out = []
                    for kt in range(NK):
                        if kt < 4 * qt:
                            dmin = 512 * qt - 128 * kt - 127
                        elif kt >= 4 * qt + 4:
                            dmin = 128 * kt - 512 * qt - 511
                        else:
                            dmin = 0
                        if sl_ * dmin < 100.0:
                            out.append(kt)
                    return out

                class DiffUnit:
                    def __init__(self, h, c, qt, Qap, Kfn, Vfn, o_dst, b_odst, b_qb):
                        self.h, self.c, self.qt = h, c, qt
                        self.kts = diff_kts(h, qt)
                        self.Qap, self.Kfn, self.Vfn, self.o_dst, self.b_odst, self.b_qb = Qap, Kfn, Vfn, o_dst, b_odst, b_qb

                    def start(self):
                        self.O, self.b_O = nextA(); self.L, self.b_L = nextA()
                        self.u = pctr["u"] % 2; pctr["u"] += 1

                    def front(self, kt):
                        h, qt = self.h, self.qt
                        sbk, b_sbk = nextS()
                        P.op("pe", MM(sbk, self.Kfn(kt), self.Qap, True, True), reads=[b_kv, self.b_qb], writes=[b_sbk])
                        pi = pctr["i"] % NP; pctr["i"] += 1
                        if kt < 4 * qt:
                            m_ = (512 * qt - 128 * kt) // 128
                            bcol = al_bias[:, h * 64 + m_:h * 64 + m_ + 1]; tab = al_ab[:, h, :]
                        elif kt >= 4 * qt + 4:
                            m_ = (128 * kt - 512 * qt) // 128
                            bcol = al_bias[:, h * 64 + 32 + m_:h * 64 + 32 + m_ + 1]; tab = al_bl[:, h, :]
                        else:
                            dl = 512 * qt - 128 * kt
                            bcol = al_bias[:, h * 64:h * 64 + 1]; tab = al_st[:, h, 384 + dl:384 + dl + 512]
                        P.op("act", ACT(Pt[pi], sbk, AF.Exp, bias=bcol), reads=[b_sbk, b_kv], writes=[b_P[pi]])
                        P.op("dve", TT(P2[pi], Pt[pi], tab, ALU.mult), reads=[b_P[pi], b_kv], writes=[b_P2[pi]])
                        return P2[pi], b_P2[pi]

                    def back(self, kt, pinfo):
                        pt, b_pt = pinfo
                        P.op("pe", MM(self.O, self.Vfn(kt), pt, kt == self.kts[0], kt == self.kts[-1]), reads=[b_kv, b_pt], writes=[self.b_O])
                        P.op("pe", MM(self.L, ones_b, pt, kt == self.kts[0], kt == self.kts[-1]), reads=[b_const, b_pt], writes=[self.b_L])

                    def fin(self):
                        u = self.u; c = self.c
                        P.op("act", ACT(rl[u], self.L, AF.Ln), reads=[self.b_L], writes=[b_rl[u]])
                        P.op("act", ACT(rl[u], rl[u], AF.Exp, scale=-1.0), reads=[b_rl[u]], writes=[b_rl[u]])
                        P.op("dve", TT(tc_[c], self.O, rl[u], ALU.mult), reads=[self.b_O, b_rl[u]], writes=[b_tc[c]])
                        if c == 1:
                            P.op("dve", STT(tt_, tc_[1], nlam, tc_[0], ALU.mult, ALU.add), reads=[b_tc[0], b_tc[1], b_const], writes=[b_tt])
                            P.op("act", ACT(sqb, tt_, AF.Square), reads=[b_tt], writes=[b_sqb])

                            def part2():
                                ssb, b_ssb = nextS()
                                P.op("pe", MM(ssb, ones_b, sqb, True, True), reads=[b_const, b_sqb], writes=[b_ssb])
                                P.op("act", ACT(rl[u], ssb, AF.Ln, bias=EPS, scale=1.0 / 128), reads=[b_ssb], writes=[b_rl[u]])
                                P.op("act", ACT(rl[u], rl[u], AF.Exp, scale=-0.5), reads=[b_rl[u]], writes=[b_rl[u]])
                                P.op("dve", STT(self.o_dst, tt_, gsub_c, rl[u], ALU.mult, ALU.mult), reads=[b_tt, b_rl[u], b_const], writes=[self.b_odst])
                            defer(6, part2)

                stream = []
                for qt in range(NT):
                    pa = qt % 2
                    units = []
                    if layer == 0:
                        for hq in range(8):
                            kvh = hq // 4
                            units.append(AugUnit(qa_t[pa][:, hq, :],
                                                 lambda kt: KTa_sb[:, kt * 128:(kt + 1) * 128],
                                                 lambda kt, kvh=kvh: Va_sb[:, kt, kvh, :], oa_st[:, hq, :], b_o[0], b_q[pa]))
                        for h in range(4):
                            for c in range(2):
                                units.append(DiffUnit(h, c, qt, qb_t[:, h, c, :],
                                                      lambda kt, h=h: KTb_sb[:, h, kt * 128:(kt + 1) * 128],
                                                      lambda kt, h=h: Vb_sb[:, kt, h, :], ob_st[:, h, :], b_ob1, b_qb1))
                    else:
                        for h in range(8):
                            units.append(AugUnit(q1_t[pa][:, h, :], lambda kt, h=h: KT_sb[:, h, kt * 128:(kt + 1) * 128],
                                                 lambda kt, h=h: V_sb[:, kt, h, :], o1_st[pa][:, h, :], b_o[pa], b_q[pa]))
                    for ui, u_ in enumerate(units):
                        for kt in u_.kts:
                            stream.append((u_, kt, qt, ui == 0 and kt == u_.kts[0], ui == len(units) - 1 and kt == u_.kts[-1],
                                           layer == 0 and ui == 7 and kt == u_.kts[-1]))
                load_q(0)
                if layer == 0:
                    load_qb(0)
                inflight = []
                for idx in range(len(stream) + LA):
                    if idx < len(stream):
                        u_, kt, qt, first, lastq, _ = stream[idx]
                        if first and qt + 1 < NT:
                            load_q(qt + 1)
                        if kt == u_.kts[0]:
                            u_.start()
                        inflight.append((u_, kt, qt, u_.front(kt)))
                        if lastq and layer == 0 and qt + 1 < NT:
                            load_qb(qt + 1)
                    if idx >= LA:
                        u_, kt, qt, pinfo = inflight.pop(0)
                        u_.back(kt, pinfo)
                        if kt == u_.kts[-1]:
                            u_.fin()
                            if stream[idx - LA][4]:
                                defer(9, lambda qt=qt: store_o(qt))
                            if stream[idx - LA][5]:
                                defer(9, lambda qt=qt: store_oa(qt))
                    tick()
                while pend:
                    pend.pop(0)[1]()
                P.barrier()

        def pass_C(layer, seq, xsrc, xdst, last):
            abanks["l"] = [3, 4, 5]
            sbanks["l"] = [0, 1, 2]
            AB.reset(); AFa.reset()
            NW = 3
            b_w = [P.buf() for _ in range(NW)]
            wst = [AB.alloc([128, 8, 1024]) for _ in range(NW)]
            wctr = {"i": 0}
            Wdn = AB.alloc([128, 22, 1024]); b_Wdn = P.buf()
            Kmem = AB.alloc([128, 8, 256]); Vmem = AB.alloc([128, 2, 1024]); b_mem = P.buf()
            xts = [AFa.alloc([128, 4, 1024]) for _ in range(2)]; b_xts = [[P.buf() for _ in range(4)] for _ in range(2)]
            ssr = AFa.alloc([128, 8]); b_s = [P.buf() for _ in range(4)]
            rlc = AFa.alloc([128, 512]); b_rlc = P.buf()
            hb = [AB.alloc([128, 1024]) for _ in range(2)]; b_hb = [P.buf() for _ in range(2)]
            hT = AB.alloc([128, 8, 512]); b_hT = P.buf()
            qT = AB.alloc([128, 8, 512]); b_qT = P.buf()
            o2T = hT; b_o2T = b_hT
            ots = [AB.alloc([128, 8, 512]) for _ in range(2)]; b_ots = [P.buf() for _ in range(2)]
            actT = AB.alloc([128, 22, 512]); b_act = P.buf()
            Pc = [AB.alloc([128, 512]) for _ in range(2)]; b_Pc = [P.buf() for _ in range(2)]
            sgl = Pc[0]; b_sgl = b_Pc[0]

            def wload(view_fn_list):
                s_ = wctr["i"] % NW; wctr["i"] += 1
                for dfn, src in view_fn_list:
                    P.op("sp", DMA(dfn(wst[s_]), src), writes=[b_w[s_]], dma=True)
                return wst[s_], b_w[s_]

            def load_tile(tt):
                pa = tt % 2
                t0 = tt * 512
                sl = slice(t0, t0 + 512)
                for s_ in range(4):
                    P.op("sp", DMA(xts[pa][:, s_, :], xsrc[t0 + s_ * 128:t0 + (s_ + 1) * 128, :]), writes=[b_xts[pa][s_]], dma=True)
                if layer == 0:
                    for hq in range(8):
                        P.op("sp", DMA(ots[pa][(hq % 2) * 64:(hq % 2) * 64 + 64, hq // 2, :], OTa[hq, :, sl]), writes=[b_ots[pa]], dma=True)
                    for h in range(4):
                        P.op("sp", DMA(ots[pa][:, 4 + h, :], OTb[h, :, sl]), writes=[b_ots[pa]], dma=True)
                else:
                    for h in range(16):
                        P.op("sp", DMA(ots[pa][(h % 2) * 64:(h % 2) * 64 + 64, h // 2, :], OT1[h, :, sl]), writes=[b_ots[pa]], dma=True)

            load_tile(0)
            mt = xts[1][:, 0, :]; b_mt = b_xts[1][0]
            for mi in range(2):
                P.op("sp", DMA(mt, mem_in[seq, mi * 128:(mi + 1) * 128, :]), writes=[b_mt], dma=True)
                P.op("act", ACT(hb[0], mt, AF.Square, accum=ssr[:, 0:1]), reads=[b_mt], writes=[b_hb[0], b_s[0]])
                rms_rstd(ssr[:, 0:1], ssr[:, 1:2], float(D), b_s[0], b_s[0])
                P.op("dve", TS(hb[0], mt, ssr[:, 1:2]), reads=[b_mt, b_s[0]], writes=[b_hb[0]])
                for half in range(2):
                    pt, b_pt = nextT()
                    for k in range(4):
                        kc = half * 4 + k
                        P.op("pe", TR(pt[:, k * 128:(k + 1) * 128], hb[0][:, kc * 128:(kc + 1) * 128], ident), reads=[b_hb[0], b_const], writes=[b_pt])
                    P.op("act", ACP(hT[:, half * 4:half * 4 + 4, mi * 128:(mi + 1) * 128], pt.rearrange("p (k t) -> p k t", t=128)), reads=[b_pt], writes=[b_hT])
            for half in range(2):
                wv, b_wv = wload([(lambda a: a, Wckv_s[layer][:, :, half * 1024:(half + 1) * 1024])])
                if half == 0:
                    for m in range(8):
                        bk, b_bk = nextbank()
                        for kc in range(8):
                            P.op("pe", MM(bk[:, 0:256], wv[:, kc, m * 128:(m + 1) * 128], hT[:, kc, 0:256], kc == 0, kc == 7), reads=[b_wv, b_hT], writes=[b_bk])
                        P.op("act" if m % 2 else "dve", (ACP if m % 2 else CP)(Kmem[:, m, :], bk[:, 0:256]), reads=[b_bk], writes=[b_mem])
                else:
                    for mi in range(2):
                        for nh in range(2):
                            bk, b_bk = nextbank()
                            for kc in range(8):
                                P.op("pe", MM(bk, hT[:, kc, mi * 128:(mi + 1) * 128], wv[:, kc, nh * 512:(nh + 1) * 512], kc == 0, kc == 7), reads=[b_wv, b_hT], writes=[b_bk])
                            P.op("act" if nh else "dve", (ACP if nh else CP)(Vmem[:, mi, nh * 512:(nh + 1) * 512], bk), reads=[b_bk], writes=[b_mem])

            for tt in range(NT):
                pa = tt % 2
                xt = xts[pa]; b_xt = b_xts[pa]
                ot = ots[pa]; b_ot = b_ots[pa]
                t0 = tt * 512
                if tt + 1 < NT:
                    load_tile(tt + 1)

                def add_proj(lhs_list, b_lhs, rhs_fn, b_rhs):
                    for s_ in range(4):
                        for n in range(2):
                            bk, b_bk = nextbank()
                            nl = len(lhs_list)
                            for ci, lf in enumerate(lhs_list):
                                P.op("pe", MM(bk, lf(s_), rhs_fn(ci, n), ci == 0, ci == nl - 1), reads=[b_lhs, b_rhs], writes=[b_bk])
                            P.op("dve", TT(xt[:, s_, n * 512:(n + 1) * 512], xt[:, s_, n * 512:(n + 1) * 512], bk, ALU.add), reads=[b_bk, b_xt[s_]], writes=[b_xt[s_]])

                def norm_tile():
                    for s_ in range(4):
                        p2 = s_ % 2
                        norm_to_hT(xt[:, s_, :], b_xt[s_], None, None, ssr[:, 0:1], ssr[:, 1:2], b_s[0], hb[p2], b_hb[p2], hT, b_hT, s_ * 128)

                wv, b_wv = wload([(lambda a: a, (Wout0_s if layer == 0 else Wout1_s)[:, :, :])])
                lhs = [(lambda s_, c=c: ot[:, c, s_ * 128:(s_ + 1) * 128]) for c in range(8)]
                add_proj(lhs, b_ot, lambda ci, n, wv=wv: wv[:, ci, n * 512:(n + 1) * 512], b_wv)
                norm_tile()
                wv, b_wv = wload([(lambda a: a, Wcq_s[layer][:, :, :])])
                for m in range(8):
                    bk, b_bk = nextbank()
                    for kc in range(8):
                        P.op("pe", MM(bk, wv[:, kc, m * 128:(m + 1) * 128], hT[:, kc, :], kc == 0, kc == 7), reads=[b_wv, b_hT], writes=[b_bk])
                    P.op("act", ACT(qT[:, m, :], bk, AF.Copy, scale=1.0 / 16), reads=[b_bk], writes=[b_qT])
                for h in range(4):
                    L, b_L = nextA(); O0, b_O0 = nextA(); O1, b_O1 = nextA()
                    for mi in range(2):
                        sbk, b_sbk = nextS()
                        for dc in range(2):
                            P.op("pe", MM(sbk, Kmem[:, 2 * h + dc, mi * 128:(mi + 1) * 128], qT[:, 2 * h + dc, :], dc == 0, dc == 1), reads=[b_mem, b_qT], writes=[b_sbk])
                        P.op("act", ACT(Pc[mi], sbk, AF.Exp), reads=[b_sbk], writes=[b_Pc[mi]])
                        P.op("pe", MM(L, ones_b, Pc[mi], mi == 0, mi == 1), reads=[b_const, b_Pc[mi]], writes=[b_L])
                        P.op("pe", MM(O0, Vmem[:, mi, h * 256:h * 256 + 128], Pc[mi], mi == 0, mi == 1), reads=[b_mem, b_Pc[mi]], writes=[b_O0])
                        P.op("pe", MM(O1, Vmem[:, mi, h * 256 + 128:h * 256 + 256], Pc[mi], mi == 0, mi == 1), reads=[b_mem, b_Pc[mi]], writes=[b_O1])
                    P.op("act", ACT(rlc, L, AF.Ln), reads=[b_L], writes=[b_rlc])
                    P.op("act", ACT(rlc, rlc, AF.Exp, scale=-1.0), reads=[b_rlc], writes=[b_rlc])
                    P.op("dve", TT(o2T[:, 2 * h, :], O0, rlc, ALU.mult), reads=[b_O0, b_rlc], writes=[b_o2T])
                    P.op("dve", TT(o2T[:, 2 * h + 1, :], O1, rlc, ALU.mult), reads=[b_O1, b_rlc], writes=[b_o2T])
                wv, b_wv = wload([(lambda a: a, Wco_s[layer][:, :, :])])
                lhs = [(lambda s_, kc=kc: o2T[:, kc, s_ * 128:(s_ + 1) * 128]) for kc in range(8)]
                add_proj(lhs, b_o2T, lambda ci, n, wv=wv: wv[:, ci, n * 512:(n + 1) * 512], b_wv)
                norm_tile()
                for kc in range(22):
                    P.op("sp", DMA(Wdn[:, kc, :], Wdn_s[layer][:, kc, :]), writes=[b_Wdn], dma=True)
                for j0 in range(0, 22, 4):
                    nj = min(4, 22 - j0)
                    wv, b_wv = wload([(lambda a, nj=nj: a[:, :, 0:nj * 128], Wgu_s[layer][:, :, j0 * 128:(j0 + nj) * 128]),
                                      (lambda a, nj=nj: a[:, :, 512:512 + nj * 128], Wgu_s[layer][:, :, DFF + j0 * 128:DFF + (j0 + nj) * 128])])
                    for jj in range(nj):
                        j = j0 + jj
                        gk, b_gk = nextbank(); uk, b_uk = nextbank()
                        for kc in range(8):
                            P.op("pe", MM(gk, wv[:, kc, jj * 128:(jj + 1) * 128], hT[:, kc, :], kc == 0, kc == 7), reads=[b_wv, b_hT], writes=[b_gk])
                        for kc in range(8):
                            P.op("pe", MM(uk, wv[:, kc, 512 + jj * 128:512 + (jj + 1) * 128], hT[:, kc, :], kc == 0, kc == 7), reads=[b_wv, b_hT], writes=[b_uk])
                        P.op("act", ACT(sgl, gk, AF.Silu), reads=[b_gk], writes=[b_sgl])
                        P.op("dve", TT(actT[:, j, :], sgl, uk, ALU.mult), reads=[b_sgl, b_uk], writes=[b_act])
                lhs = [(lambda s_, j=j: actT[:, j, s_ * 128:(s_ + 1) * 128]) for j in range(22)]
                add_proj(lhs, b_act, lambda ci, n: Wdn[:, ci, n * 512:(n + 1) * 512], b_Wdn)
                for s_ in range(4):
                    rows = slice(t0 + s_ * 128, t0 + (s_ + 1) * 128)
                    if last:
                        P.op("act", ACT(hb[s_ % 2], xt[:, s_, :], AF.Square, accum=ssr[:, 2:3]), reads=[b_xt[s_]], writes=[b_hb[s_ % 2], b_s[1]])
                        rms_rstd(ssr[:, 2:3], ssr[:, 3:4], float(D), b_s[1], b_s[1])
                        P.op("dve", STT(xt[:, s_, :], xt[:, s_, :], ssr[:, 3:4], fin_b, ALU.mult, ALU.mult), reads=[b_xt[s_], b_s[1], b_const], writes=[b_xt[s_]])
                    P.op("pool", DMA(xdst[rows, :], xt[:, s_, :]), reads=[b_xt[s_]], dma=True)
            P.barrier()

        for seq in range(NSEQ if stop_after != "P" else 0):
            for layer in range(2):
                xsrc = x_in[seq] if layer == 0 else X1
                xdst = X1 if layer == 0 else y_out[seq]
                pass_A(layer, xsrc)
                if stop_after == "A":
                    break
                pass_B(layer)
                if stop_after == "B":
                    break
                pass_C(layer, seq, xsrc, xdst, layer == 1)
                if stop_after == "C0":
                    break
        P.barrier()
        P.finalize()
        build.stats = {e: len(P.ops[e]) for e in ENGS}
        P.emit(nc, st)
    return nc


_CACHE = {}


def kernel(x_prompt, x_sample, mem_prompt, mem_sample, **w):
    S = x_prompt.shape[1]
    xs = np.concatenate([np.asarray(x_prompt, np.float32), np.asarray(x_sample, np.float32)], axis=0)
    ms = np.concatenate([np.asarray(mem_prompt, np.float32), np.asarray(mem_sample, np.float32)], axis=0)
    ntot = xs.shape[0]
    nseq = ntot // N_CORES
    key = (nseq, S)
    if key not in _CACHE:
        _CACHE[key] = build(nseq, S)
    nc = _CACHE[key]
    consts = host_consts(S)
    wd = {n: np.ascontiguousarray(np.asarray(w[n], np.float32)) for n in WNAMES}
    in_maps = []
    for c in range(N_CORES):
        m = {"x": np.ascontiguousarray(xs[c * nseq:(c + 1) * nseq]), "mem": np.ascontiguousarray(ms[c * nseq:(c + 1) * nseq])}
        m.update(wd)
        m.update(consts)
        in_maps.append(m)
    res = run_bass_kernel_spmd(nc, in_maps, core_ids=list(range(N_CORES)))
    ys = np.concatenate([np.asarray(r["y"], np.float32) for r in res.results], axis=0)
    nb = x_prompt.shape[0]
    return (ys[:nb], ys[nb:])
```

```python
import math
from contextlib import ExitStack

import ml_dtypes
import numpy as np

import concourse.bass as bass
import concourse.mybir as mybir
from concourse.bass_utils import run_bass_kernel_spmd

F32 = mybir.dt.float32
BF16 = mybir.dt.bfloat16
ALU = mybir.AluOpType
AF = mybir.ActivationFunctionType
AX = mybir.AxisListType

D = 1024
DFF = 2816
NMEM = 256
EPS = 1e-6
GRID_W = 64
N_CORES = 8

ENGS = ["pe", "act", "dve", "pool", "sp"]
N_DMA_SEMS = 24


class Buf:
    __slots__ = ("name", "w", "r", "excl")

    def __init__(self, name, excl=False):
        self.name = name
        self.w = None
        self.r = []
        self.excl = excl


class Op:
    __slots__ = ("eng", "fn", "deps", "signal", "sem", "val", "dma", "waits")

    def __init__(self, eng, fn, dma):
        self.eng = eng
        self.fn = fn
        self.dma = dma
        self.deps = set()
        self.signal = False
        self.sem = None
        self.val = 0
        self.waits = []


class Plan:
    def __init__(self):
        self.ops = {e: [] for e in ENGS}
        self.all = []
        self.dma_rr = {"sp": 0, "pool": 0, "act": 0}
        self.dma_last = [None] * N_DMA_SEMS
        self.nbuf = 0

    def buf(self, name=None, excl=False):
        self.nbuf += 1
        return Buf(name or f"b{self.nbuf}", excl)

    def op(self, eng, fn, reads=(), writes=(), dma=False, after=()):
        o = Op(eng, fn, dma)
        deps = o.deps
        for b in reads:
            if b.w is not None:
                deps.add(b.w)
            if b.excl:
                for q in b.r:
                    if q.eng != eng:
                        deps.add(q)
        for b in writes:
            if b.w is not None:
                deps.add(b.w)
            deps.update(b.r)
        for a in after:
            if a is not None:
                deps.add(a)
        if dma:
            half = N_DMA_SEMS // 2
            k = (self.dma_rr[eng] % half) + (half if eng == "pool" else 0)
            self.dma_rr[eng] += 1
            prev = self.dma_last[k]
            if prev is not None:
                deps.add(prev)
            self.dma_last[k] = o
            o.sem = ("dma", k)
        else:
            o.sem = ("eng", eng)
        for b in reads:
            if not dma:
                b.r = [q for q in b.r if q.dma or q.eng != eng]
            b.r.append(o)
        for b in writes:
            b.w = o
            b.r = []
        self.ops[eng].append(o)
        self.all.append(o)
        return o

    def barrier(self):
        lasts = []
        for e in ENGS:
            if self.ops[e]:
                lasts.append(self.ops[e][-1])
        lasts += [d for d in self.dma_last if d is not None]
        for e in ENGS:
            self.op(e, None, after=lasts)

    @staticmethod
    def _skip(d, o):
        return (not d.dma) and (not o.dma) and d.eng == "pe" and o.eng == "pe"

    def finalize(self):
        for e in ENGS:
            for o in self.ops[e]:
                for d in o.deps:
                    if d.dma or self._skip(d, o):
                        continue
                    d.signal = True
        cnt = {}
        for o in self.all:
            if o.dma:
                cnt[o.sem] = cnt.get(o.sem, 0) + 16
                o.val = cnt[o.sem]
                o.signal = True
            elif o.signal:
                if o.fn is None:
                    o.signal = False
                    continue
                cnt[o.sem] = cnt.get(o.sem, 0) + 1
                o.val = cnt[o.sem]
        for e in ENGS:
            waited = {}
            for o in self.ops[e]:
                need = {}
                for d in o.deps:
                    if self._skip(d, o) or d is o:
                        continue
                    if d.fn is None:
                        continue
                    v = need.get(d.sem, 0)
                    if d.val > v:
                        need[d.sem] = d.val
                for s, v in need.items():
                    if waited.get(s, 0) < v:
                        waited[s] = v
                        o.waits.append((s, v))
        self.counts = cnt

    def emit(self, nc, stack):
        sems = {}
        for e in ENGS:
            sems[("eng", e)] = stack.enter_context(nc.semaphore(f"s_{e}"))
        for k in range(N_DMA_SEMS):
            sems[("dma", k)] = stack.enter_context(nc.semaphore(f"s_dma{k}"))
        block = stack.enter_context(nc.Block())
        plan = self

        def replay(ename):
            def run(h):
                for o in plan.ops[ename]:
                    for s, v in o.waits:
                        h.wait_ge(sems[s], v)
                    if o.fn is None:
                        continue
                    inst = o.fn(h)
                    if o.signal:
                        inst.then_inc(sems[o.sem], 16 if o.dma else 1)
            return run

        block.tensor(replay("pe"))
        block.scalar(replay("act"))
        block.vector(replay("dve"))
        block.gpsimd(replay("pool"))
        block.sync(replay("sp"))


def MM(out, lhsT, rhs, start=True, stop=True):
    return lambda e: e.matmul(out, lhsT=lhsT, rhs=rhs, start=start, stop=stop)


def TR(out, in_, ident):
    return lambda e: e.transpose(out=out, in_=in_, identity=ident)


def ACT(out, in_, func, bias=None, scale=None, accum=None):
    kw = {}
    if bias is not None:
        kw["bias"] = bias
    if scale is not None:
        kw["scale"] = scale
    if accum is not None:
        kw["accum_out"] = accum
    return lambda e: e.activation(out=out, in_=in_, func=func, **kw)


def DMA(out, in_, slow=False):
    if slow:
        return lambda e: e.dma_start(out=out, in_=in_, allow_slow_non_contiguous=True)
    return lambda e: e.dma_start(out=out, in_=in_)


def TS(out, in0, s1, s2=None, op0=ALU.mult, op1=None):
    if op1 is None:
        return lambda e: e.tensor_scalar(out=out, in0=in0, scalar1=s1, scalar2=None, op0=op0)
    return lambda e: e.tensor_scalar(out=out, in0=in0, scalar1=s1, scalar2=s2, op0=op0, op1=op1)


def TT(out, in0, in1, op):
    return lambda e: e.tensor_tensor(out=out, in0=in0, in1=in1, op=op)


def STT(out, in0, scalar, in1, op0, op1):
    return lambda e: e.scalar_tensor_tensor(out=out, in0=in0, scalar=scalar, in1=in1, op0=op0, op1=op1)


def CP(out, in_):
    return lambda e: e.tensor_copy(out=out, in_=in_)


def ACP(out, in_):
    return lambda e: e.copy(out=out, in_=in_)


def RECIP(out, in_):
    return lambda e: e.reciprocal(out=out, in_=in_)


def RSUM(out, in_):
    return lambda e: e.tensor_reduce(out=out, in_=in_, axis=AX.X, op=ALU.add)


def MSET(ap, v):
    return lambda e: e.memset(ap, v)


class Arena:
    def __init__(self, t, n):
        self.t = t
        self.n = n
        self.off = 0

    def reset(self):
        self.off = 0

    def alloc(self, shape):
        n = 1
        for s in shape[1:]:
            n *= s
        n = (n + 15) // 16 * 16
        o = self.off
        self.off += n
        assert self.off <= self.n, (self.off, self.n, shape)
        m = 1
        for s in shape[1:]:
            m *= s
        v = self.t[0:shape[0], o:o + m]
        if len(shape) == 3:
            v = v.rearrange("p (a b) -> p a b", b=shape[2])
        elif len(shape) == 4:
            v = v.rearrange("p (a b c) -> p a b c", b=shape[2], c=shape[3])
        return v


AB_N = 82944
AF_N = 8768

WNAMES = ["norm_mix", "e_w_in", "e_q_norm", "e_k_norm", "e_lam_q1", "e_lam_k1", "e_lam_q2", "e_lam_k2",
          "e_subln", "e_w_out", "o_w_in", "o_q_norm", "o_kv_norm", "o_w_uq", "o_w_ukv", "o_w_out",
          "norm_cross", "norm_mem", "w_cq", "w_ckv", "w_co", "norm_ffn", "w_gu", "w_down", "final_norm"]
WSHAPES = {
    "norm_mix": [2, D], "e_w_in": [1, D, 2304], "e_q_norm": [1, 64], "e_k_norm": [1, 64],
    "e_lam_q1": [1, 64], "e_lam_k1": [1, 64], "e_lam_q2": [1, 64], "e_lam_k2": [1, 64],
    "e_subln": [1, 128], "e_w_out": [1, D, D], "o_w_in": [1, D, 672], "o_q_norm": [1, 384],
    "o_kv_norm": [1, 256], "o_w_uq": [1, 384, 1536], "o_w_ukv": [1, 256, 2048], "o_w_out": [1, D, D],
    "norm_cross": [2, D], "norm_mem": [2, D], "w_cq": [2, D, D], "w_ckv": [2, D, 2 * D], "w_co": [2, D, D],
    "norm_ffn": [2, D], "w_gu": [2, D, 2 * DFF], "w_down": [2, DFF, D], "final_norm": [D],
}


def host_consts(S):
    c = {}
    c["ident"] = np.eye(128, dtype=np.float32).astype(ml_dtypes.bfloat16)
    c["ones_b"] = np.ones((128, 128), dtype=np.float32).astype(ml_dtypes.bfloat16)
    c["ones_f"] = np.ones((128, 64), dtype=np.float32)
    es = np.zeros((32, 96), dtype=np.float32)
    es[np.arange(32), 64 + np.arange(32)] = 1.0
    c["esel"] = es.astype(ml_dtypes.bfloat16)
    s64 = np.zeros((128, 64), dtype=np.float32)
    s64[64, :] = 1.0
    c["sel64"] = s64.astype(ml_dtypes.bfloat16)
    t = np.arange(S)
    fa = (10000.0 ** (-np.arange(16, dtype=np.float32) / 16)).astype(np.float32)
    r = (t // GRID_W).astype(np.float32)
    cc = (t % GRID_W).astype(np.float32)
    angA = np.concatenate([r[:, None] * fa, cc[:, None] * fa], axis=-1).astype(np.float32)
    fl = (10000.0 ** (-np.arange(16, dtype=np.float32) / 16)).astype(np.float32)
    angL = (t.astype(np.float32)[:, None] * fl).astype(np.float32)

    def exp_tabs(ang):
        co = np.repeat(np.cos(ang), 2, axis=-1).astype(np.float32)
        si = np.repeat(np.sin(ang), 2, axis=-1).astype(np.float32)
        si[:, 0::2] *= -1.0
        return co, si

    c["cexpA"], c["sexpA"] = exp_tabs(angA)
    c["cexpL"], c["sexpL"] = exp_tabs(angL)
    slopes = (2.0 ** (-8.0 * np.arange(1, 5) / 4)).astype(np.float64)
    p = np.arange(128)[:, None].astype(np.float64)
    j = np.arange(512)[None, :].astype(np.float64)
    m = np.arange(896)[None, :].astype(np.float64)
    ab = np.zeros((4, 128, 512)); bl = np.zeros((4, 128, 512)); st = np.zeros((4, 128, 896))
    for h in range(4):
        ab[h] = np.exp(-slopes[h] * (j - p + 127))
        bl[h] = np.exp(-slopes[h] * (p - j + 511))
        st[h] = np.exp(-slopes[h] * np.abs(m - 384 - p))
    c["al_above"] = ab.astype(np.float32).astype(ml_dtypes.bfloat16)
    c["al_below"] = bl.astype(np.float32).astype(ml_dtypes.bfloat16)
    c["al_strip"] = st.astype(np.float32).astype(ml_dtypes.bfloat16)
    bc = np.zeros((128, 4, 64), dtype=np.float32)
    for h in range(4):
        for mm_ in range(1, 32):
            bc[:, h, mm_] = -slopes[h] * (128 * mm_ - 127)
        for mm_ in range(4, 32):
            bc[:, h, 32 + mm_] = -slopes[h] * (128 * mm_ - 511)
    c["al_bias"] = bc.reshape(128, 256)
    return c


CONST_SPECS = lambda S: {
    "ident": ([128, 128], BF16), "ones_b": ([128, 128], BF16), "ones_f": ([128, 64], F32), "esel": ([32, 96], BF16), "sel64": ([128, 64], BF16),
    "cexpA": ([S, 64], F32), "sexpA": ([S, 64], F32), "cexpL": ([S, 32], F32), "sexpL": ([S, 32], F32),
    "al_above": ([4, 128, 512], BF16), "al_below": ([4, 128, 512], BF16), "al_strip": ([4, 128, 896], BF16),
    "al_bias": ([128, 256], F32),
}


def build(NSEQ, S, stop_after=None):
    NK = S // 128
    NT = S // 512
    nc = bass.Bass("TRN2", target_bir_lowering=False)

    def din(name, shape, dt=F32):
        return nc.dram_tensor(name, list(shape), dt, kind="ExternalInput").ap()

    def dscr(name, shape, dt=BF16):
        return nc.dram_tensor(name, list(shape), dt).ap()

    x_in = din("x", [NSEQ, S, D])
    mem_in = din("mem", [NSEQ, NMEM, D])
    W = {n: din(n, WSHAPES[n]) for n in WNAMES}
    C = {n: din(n, sh, dt) for n, (sh, dt) in CONST_SPECS(S).items()}
    y_out = nc.dram_tensor("y", [NSEQ, S, D], F32, kind="ExternalOutput").ap()

    Win0_s = dscr("Win0_s", [128, 8, 2304])
    Wout0_s = dscr("Wout0_s", [128, 8, 1024])
    Win1_s = dscr("Win1_s", [128, 8, 672])
    Wuq_s = dscr("Wuq_s", [128, 3, 1536])
    Wkp_s = dscr("Wkp_s", [128, 2, 16, 96])
    Wv_s = dscr("Wv_s", [128, 2, 1024])
    Wout1_s = dscr("Wout1_s", [128, 8, 1024])
    Wcq_s = [dscr(f"Wcq_s{l}", [128, 8, 1024]) for l in range(2)]
    Wckv_s = [dscr(f"Wckv_s{l}", [128, 8, 2048]) for l in range(2)]
    Wco_s = [dscr(f"Wco_s{l}", [128, 8, 1024]) for l in range(2)]
    Wgu_s = [dscr(f"Wgu_s{l}", [128, 8, 2 * DFF]) for l in range(2)]
    Wdn_s = [dscr(f"Wdn_s{l}", [128, 22, 1024]) for l in range(2)]
    QTa = dscr("QTa", [8, 64, S]); QTb = dscr("QTb", [4, 128, S]); QT1 = dscr("QT1", [16, 96, S])
    KTa = dscr("KTa", [128, S]); KTb = dscr("KTb", [4, 128, S]); KT1 = dscr("KT1", [16, 96, S])
    Va = dscr("Va", [S, 2, 65]); Vb = dscr("Vb", [S, 4, 128]); V1 = dscr("V1", [S, 16, 65])
    OTa = dscr("OTa", [8, 64, S]); OTb = dscr("OTb", [4, 128, S]); OT1 = dscr("OT1", [16, 64, S])
    X1 = dscr("X1", [S, D], F32)

    st = ExitStack()
    with st:
        abt = st.enter_context(nc.sbuf_tensor("arena_b", [128, AB_N], BF16))
        aft = st.enter_context(nc.sbuf_tensor("arena_f", [128, AF_N], F32))
        cft = st.enter_context(nc.sbuf_tensor("const_f", [128, 1792], F32))
        cbt = st.enter_context(nc.sbuf_tensor("const_b", [128, 128 + 128 + 96 + 64], BF16))
        psf = [st.enter_context(nc.psum_tensor(f"psf{i}", [128, 512], F32)) for i in range(8)]
        psT = [psf[6][:].bitcast(BF16), psf[7][:].bitcast(BF16)]
        AB = Arena(abt, AB_N)
        AFa = Arena(aft, AF_N)
        P = Plan()

        ident = cbt[:, 0:128]
        ones_b = cbt[:, 128:256]
        esel = cbt[0:32, 256:352]
        sel64 = cbt[:, 352:416]
        ones_f = cft[:, 0:64]
        gcols = cft[:, 64:64 + 80]
        GC = {"mix0": 0, "mix1": 8, "cross0": 16, "cross1": 24, "mem0": 32, "mem1": 40, "ffn0": 48, "ffn1": 56,
              "qn": 64, "kvn": 67, "one": 69}
        gq_b = cft[:, 160:224]; gk_b = cft[:, 224:288]; gqs_b = cft[:, 288:352]; gks_b = cft[:, 352:416]
        lam_t = cft[:, 416:420]
        gsub_c = cft[:, 420:421]
        lamw = cft[:, 424:424 + 4 * 64]
        fin_b = cft[:, 768:1792]
        b_const = P.buf("const")

        def ld(dst, src, slow=False):
            P.op("sp", DMA(dst, src, slow), writes=[b_const], dma=True)

        ld(ident, C["ident"][:, :]); ld(ones_b, C["ones_b"][:, :]); ld(esel, C["esel"][:, :]); ld(ones_f, C["ones_f"][:, :]); ld(sel64, C["sel64"][:, :])
        for nm, src, kc in [("mix0", W["norm_mix"][0], 8), ("mix1", W["norm_mix"][1], 8),
                            ("cross0", W["norm_cross"][0], 8), ("cross1", W["norm_cross"][1], 8),
                            ("mem0", W["norm_mem"][0], 8), ("mem1", W["norm_mem"][1], 8),
                            ("ffn0", W["norm_ffn"][0], 8), ("ffn1", W["norm_ffn"][1], 8),
                            ("qn", W["o_q_norm"][0], 3), ("kvn", W["o_kv_norm"][0], 2)]:
            ld(gcols[:, GC[nm]:GC[nm] + kc], src.rearrange("(kc p) -> p kc", p=128), slow=True)
        P.op("dve", MSET(gcols[:, GC["one"]:GC["one"] + 1], 1.0), writes=[b_const])
        ld(gq_b, W["e_q_norm"][0].partition_broadcast(128)); ld(gk_b, W["e_k_norm"][0].partition_broadcast(128))
        ld(fin_b, W["final_norm"].partition_broadcast(128))
        for i, nm in enumerate(["e_lam_q1", "e_lam_k1", "e_lam_q2", "e_lam_k2"]):
            ld(lamw[:, i * 64:(i + 1) * 64], W[nm][0].partition_broadcast(128))
        ld(gsub_c, W["e_subln"][0].rearrange("(p o) -> p o", o=1), slow=True)
        for g, gs in [(gq_b, gqs_b), (gk_b, gks_b)]:
            gv = g.rearrange("p (i two) -> p i two", two=2); sv = gs.rearrange("p (i two) -> p i two", two=2)
            P.op("dve", CP(sv[:, :, 0:1], gv[:, :, 1:2]), reads=[b_const], writes=[b_const])
            P.op("dve", CP(sv[:, :, 1:2], gv[:, :, 0:1]), reads=[b_const], writes=[b_const])
        lam_init0 = 0.8 - 0.6 * math.exp(-0.3 * 0)
        tmpl = cft[:, 440 + 256:440 + 256 + 64]
        P.op("dve", TT(tmpl, lamw[:, 0:64], lamw[:, 64:128], ALU.mult), reads=[b_const], writes=[b_const])
        P.op("dve", RSUM(lam_t[:, 0:1], tmpl), reads=[b_const], writes=[b_const])
        P.op("dve", TT(tmpl, lamw[:, 128:192], lamw[:, 192:256], ALU.mult), reads=[b_const], writes=[b_const])
        P.op("dve", RSUM(lam_t[:, 1:2], tmpl), reads=[b_const], writes=[b_const])
        P.op("act", ACT(lam_t[:, 0:2], lam_t[:, 0:2], AF.Exp), reads=[b_const], writes=[b_const])
        P.op("dve", TT(lam_t[:, 2:3], lam_t[:, 0:1], lam_t[:, 1:2], ALU.subtract), reads=[b_const], writes=[b_const])
        P.op("dve", TS(lam_t[:, 3:4], lam_t[:, 2:3], lam_init0, -1.0, ALU.add, ALU.mult), reads=[b_const], writes=[b_const])
        nlam = lam_t[:, 3:4]
        P.op("dve", TS(gsub_c, gsub_c, 1.0 - lam_init0), reads=[b_const], writes=[b_const])

        AB.reset(); AFa.reset()
        NST = 3
        stf = [AFa.alloc([128, 2048]) for _ in range(NST)]
        stb = [AB.alloc([128, 2048]) for _ in range(NST)]
        zt = AB.alloc([128, 16, 32])
        b_stf = [P.buf() for _ in range(NST)]; b_stb = [P.buf() for _ in range(NST)]
        b_z = P.buf()
        P.op("dve", MSET(zt, 0.0), writes=[b_z])
        cstate = {"i": 0}

        def conv_piece(src2d, rows, cols, gain_col, outs):
            i = cstate["i"]; cstate["i"] += 1
            s = i % NST
            r0, r1 = rows; c0, c1 = cols
            np_, w = r1 - r0, c1 - c0
            P.op("sp", DMA(stf[s][0:np_, 0:w], src2d[r0:r1, c0:c1]), writes=[b_stf[s]], dma=True)
            if i % 2 == 0:
                P.op("dve", TS(stb[s][0:np_, 0:w], stf[s][0:np_, 0:w], gain_col[0:np_, :]), reads=[b_stf[s], b_const], writes=[b_stb[s]])
            else:
                P.op("act", ACT(stb[s][0:np_, 0:w], stf[s][0:np_, 0:w], AF.Copy, scale=gain_col[0:np_, :]), reads=[b_stf[s], b_const], writes=[b_stb[s]])
            for dst, vf in outs:
                P.op("pool", DMA(dst, vf(stb[s][0:np_, 0:w])), reads=[b_stb[s]], dma=True)

        def gc(nm, kc):
            return gcols[:, GC[nm] + kc:GC[nm] + kc + 1]

        one_c = gcols[:, GC["one"]:GC["one"] + 1]
        idv = lambda v: v

        def conv(src2d, KC, rows_p, col_ranges, dst, gname):
            for kc in range(KC):
                g = gc(gname, kc) if gname else one_c
                for (c0, c1, d0) in col_ranges:
                    for cc in range(c0, c1, 2048):
                        ce = min(cc + 2048, c1)
                        conv_piece(src2d, (kc * rows_p, (kc + 1) * rows_p), (cc, ce), g,
                                   [(dst[0:rows_p, kc, d0 + cc - c0:d0 + ce - c0], idv)])

        conv(W["e_w_in"][0], 8, 128, [(0, 512, 0), (768, 1280, 512), (512, 768, 1024), (1280, 2304, 1280)], Win0_s, "mix0")
        conv(W["e_w_out"][0], 8, 128, [(0, 1024, 0)], Wout0_s, None)
        conv(W["o_w_in"][0], 8, 128, [(0, 672, 0)], Win1_s, "mix1")
        conv(W["o_w_uq"][0], 3, 128, [(0, 1536, 0)], Wuq_s, "qn")
        for kc in range(2):
            conv_piece(W["o_w_ukv"][0], (kc * 128, (kc + 1) * 128), (0, 2048), gc("kvn", kc), [
                (Wkp_s[:, kc, :, 0:64], lambda v: v.rearrange("p (h e) -> p h e", e=128)[:, :, 0:64]),
                (Wv_s[:, kc, :].rearrange("p (h e) -> p h e", e=64), lambda v: v.rearrange("p (h e) -> p h e", e=128)[:, :, 64:128]),
            ])
            P.op("pool", DMA(Wkp_s[:, kc, :, 64:96], zt), reads=[b_z], dma=True)
        conv(W["o_w_out"][0], 8, 128, [(0, 1024, 0)], Wout1_s, None)
        for l in range(2):
            conv(W["w_cq"][l], 8, 128, [(0, 1024, 0)], Wcq_s[l], f"cross{l}")
            conv(W["w_ckv"][l], 8, 128, [(0, 2048, 0)], Wckv_s[l], f"mem{l}")
            conv(W["w_co"][l], 8, 128, [(0, 1024, 0)], Wco_s[l], None)
            conv(W["w_gu"][l], 8, 128, [(0, 2 * DFF, 0)], Wgu_s[l], f"ffn{l}")
            conv(W["w_down"][l], 22, 128, [(0, 1024, 0)], Wdn_s[l], None)
        P.barrier()

        bank_b = [P.buf(f"psf{i}", excl=True) for i in range(8)]
        ring = {"i": 0}

        def nextbank():
            i = ring["i"] % 6
            ring["i"] += 1
            return psf[i][:], bank_b[i]

        sring = {"i": 0}
        aring = {"i": 0}

        sbanks = {"l": [0, 1, 2]}

        def nextS():
            i = sbanks["l"][sring["i"] % len(sbanks["l"])]
            sring["i"] += 1
            return psf[i][:], bank_b[i]

        abanks = {"l": [3, 4, 5]}

        def nextA():
            i = abanks["l"][aring["i"] % len(abanks["l"])]
            aring["i"] += 1
            return psf[i][:], bank_b[i]

        tring = {"i": 0}

        def nextT():
            i = tring["i"] % 2
            tring["i"] += 1
            return psT[i][:, 0:512], bank_b[6 + i]

        def rms_rstd(ss, rs, n, b_ss, b_rs):
            P.op("act", ACT(rs, ss, AF.Ln, bias=EPS, scale=1.0 / n), reads=[b_ss], writes=[b_rs])
            P.op("act", ACT(rs, rs, AF.Exp, scale=-0.5), reads=[b_rs], writes=[b_rs])

        def rope(out, v, ta, tb, tmp, nh, nd, b_v, b_tab, b_tmp, b_out, eng="dve"):
            t1, t2 = tmp
            tab = ta.unsqueeze(1).to_broadcast([128, nh, nd])
            P.op(eng, TT(t1, v, tab, ALU.mult), reads=[b_v, b_tab], writes=[b_tmp])
            vv = v.rearrange("p h (i two) -> p h i two", two=2)
            t2v = t2.rearrange("p h (i two) -> p h i two", two=2)
            tbv = tb.rearrange("p (i two) -> p i two", two=2).unsqueeze(1).to_broadcast([128, nh, nd // 2, 2])
            P.op(eng, TT(t2v[:, :, :, 0:1], vv[:, :, :, 1:2], tbv[:, :, :, 0:1], ALU.mult), reads=[b_v, b_tab], writes=[b_tmp])
            P.op(eng, TT(t2v[:, :, :, 1:2], vv[:, :, :, 0:1], tbv[:, :, :, 1:2], ALU.mult), reads=[b_v, b_tab], writes=[b_tmp])
            P.op(eng, TT(out, t1, t2, ALU.add), reads=[b_tmp], writes=[b_out])

        def norm_to_hT(xt, b_xt, junk, b_junk, ss, rs, b_s, hb, b_hb, hT, b_hT, col0):
            P.op("act", ACT(hb, xt, AF.Square, accum=ss), reads=[b_xt], writes=[b_hb, b_s])
            rms_rstd(ss, rs, float(D), b_s, b_s)
            P.op("dve", TS(hb, xt, rs), reads=[b_xt, b_s], writes=[b_hb])
            for half in range(2):
                pt, b_pt = nextT()
                for k in range(4):
                    kc = half * 4 + k
                    P.op("pe", TR(pt[:, k * 128:(k + 1) * 128], hb[:, kc * 128:(kc + 1) * 128], ident), reads=[b_hb, b_const], writes=[b_pt])
                dst = hT[:, half * 4:half * 4 + 4, col0:col0 + 128]
                src = pt.rearrange("p (k t) -> p k t", t=128)
                if half == 0:
                    P.op("act", ACP(dst, src), reads=[b_pt], writes=[b_hT])
                else:
                    P.op("dve", CP(dst, src), reads=[b_pt], writes=[b_hT])

        def pass_A(layer, xsrc):
            AB.reset(); AFa.reset()
            ncols = 2304 if layer == 0 else 672
            Win = AB.alloc([128, 8, ncols]); b_Win = P.buf()
            Wsrc = Win0_s if layer == 0 else Win1_s
            for kc in range(8):
                P.op("sp", DMA(Win[:, kc, :], Wsrc[:, kc, :]), writes=[b_Win], dma=True)
            if layer == 1:
                Wuq = AB.alloc([128, 3, 1536]); Wkp = AB.alloc([128, 2, 16, 96]); Wv = AB.alloc([128, 2, 1024])
                for kc in range(3):
                    P.op("sp", DMA(Wuq[:, kc, :], Wuq_s[:, kc, :]), writes=[b_Win], dma=True)
                for kc in range(2):
                    P.op("sp", DMA(Wkp[:, kc, :, :], Wkp_s[:, kc, :, :]), writes=[b_Win], dma=True)
                    P.op("sp", DMA(Wv[:, kc, :], Wv_s[:, kc, :]), writes=[b_Win], dma=True)
            xt = [AFa.alloc([128, 1024]) for _ in range(2)]; b_xt = [P.buf() for _ in range(2)]
            junk = AFa.alloc([128, 1024]); b_junk = P.buf()
            ssr = [AFa.alloc([128, 4]) for _ in range(2)]; b_s = [P.buf() for _ in range(2)]
            hb = [AB.alloc([128, 1024]) for _ in range(2)]; b_hb = [P.buf() for _ in range(2)]
            hT = [AB.alloc([128, 8, 128]) for _ in range(2)]; b_hT = [P.buf() for _ in range(2)]
            tabs = [AFa.alloc([128, 128]) for _ in range(2)]; b_tabs = [P.buf() for _ in range(2)]
            tq = [AFa.alloc([128, 256]) for _ in range(2)]; b_tq = [P.buf() for _ in range(2)]
            t1 = AFa.alloc([128, 512]); t2 = AFa.alloc([128, 512]); b_t = P.buf()
            qn = AFa.alloc([128, 512]); b_qn = P.buf()
            sq8 = AFa.alloc([128, 16]); b_sq8 = P.buf()
            if layer == 0:
                nstT = 13
            else:
                nstT = 16
            if layer == 0:
                qf = AB.alloc([128, 1024]); b_qf = P.buf()
                kf = AB.alloc([128, 640]); b_kf = P.buf()
                stT = [AB.alloc([128, 13, 512]) for _ in range(2)]; b_stT = [P.buf() for _ in range(2)]
                vast = [AB.alloc([128, 2, 65]) for _ in range(2)]; b_va = [P.buf() for _ in range(2)]
                vbst = [AB.alloc([128, 512]) for _ in range(2)]; b_vb = [P.buf() for _ in range(2)]
                for v_ in vast:
                    P.op("dve", MSET(v_[:, :, 64:65], 1.0), writes=[b_va[0], b_va[1]])
            else:
                qf = AB.alloc([128, 16, 96]); b_qf = P.buf()
                cn = AB.alloc([128, 640]); b_cn = P.buf()
                krf = AB.alloc([128, 32]); b_krf = P.buf()
                cT = AB.alloc([128, 5, 128]); b_cT = P.buf()
                krT = AB.alloc([32, 128]); b_krT = P.buf()
                stQ = [AB.alloc([96, 16, 512]) for _ in range(2)]; b_stQ = [P.buf() for _ in range(2)]
                stK = [AB.alloc([96, 16, 512]) for _ in range(2)]; b_stK = [P.buf() for _ in range(2)]
                v1st = [AB.alloc([128, 16, 65]) for _ in range(2)]; b_v1 = [P.buf() for _ in range(2)]
                for v_ in v1st:
                    P.op("dve", MSET(v_[:, :, 64:65], 1.0), writes=[b_v1[0], b_v1[1]])
            sc0 = 0.125
            sc1 = 96.0 ** -0.5
            for i in range(NK):
                pa = i % 2
                t0 = i * 128
                blk = i // 4
                sb = blk % 2
                c4 = (i % 4) * 128
                P.op("sp", DMA(xt[pa], xsrc[t0:t0 + 128, :]), writes=[b_xt[pa]], dma=True)
                if layer == 0:
                    P.op("sp", DMA(tabs[pa][:, 0:64], C["cexpA"][t0:t0 + 128, :]), writes=[b_tabs[pa]], dma=True)
                    P.op("sp", DMA(tabs[pa][:, 64:128], C["sexpA"][t0:t0 + 128, :]), writes=[b_tabs[pa]], dma=True)
                else:
                    P.op("sp", DMA(tabs[pa][:, 0:32], C["cexpL"][t0:t0 + 128, :]), writes=[b_tabs[pa]], dma=True)
                    P.op("sp", DMA(tabs[pa][:, 64:96], C["sexpL"][t0:t0 + 128, :]), writes=[b_tabs[pa]], dma=True)
                norm_to_hT(xt[pa], b_xt[pa], junk, b_junk, ssr[pa][:, 0:1], ssr[pa][:, 1:2], b_s[pa], hb[pa], b_hb[pa], hT[pa], b_hT[pa], 0)
                if layer == 0:
                    P.op("dve", STT(tq[pa][:, 0:64], tabs[pa][:, 0:64], sc0, gq_b, ALU.mult, ALU.mult), reads=[b_tabs[pa], b_const], writes=[b_tq[pa]])
                    P.op("dve", STT(tq[pa][:, 64:128], tabs[pa][:, 64:128], sc0, gqs_b, ALU.mult, ALU.mult), reads=[b_tabs[pa], b_const], writes=[b_tq[pa]])
                    P.op("dve", TT(tq[pa][:, 128:192], tabs[pa][:, 0:64], gk_b, ALU.mult), reads=[b_tabs[pa], b_const], writes=[b_tq[pa]])
                    P.op("dve", TT(tq[pa][:, 192:256], tabs[pa][:, 64:128], gks_b, ALU.mult), reads=[b_tabs[pa], b_const], writes=[b_tq[pa]])
                    groups = [(0, 512), (512, 1024), (1024, 1280), (1280, 1792), (1792, 2304)]
                    pg = []
                    for (c0, c1) in groups:
                        bk, b_bk = nextbank()
                        for kc in range(8):
                            P.op("pe", MM(bk[:, 0:c1 - c0], hT[pa][:, kc, :], Win[:, kc, c0:c1], kc == 0, kc == 7), reads=[b_hT[pa], b_Win], writes=[b_bk])
                        pg.append((bk, b_bk))
                    (p0, b0), (p1, b1), (p2, b2), (p3, b3), (p4, b4) = pg
                    P.op("act", ACT(t1, p0, AF.Square), reads=[b0], writes=[b_t])
                    P.op("dve", RSUM(sq8[:, 0:8], t1.rearrange("p (h d) -> p h d", d=64)), reads=[b_t], writes=[b_sq8])
                    rms_rstd(sq8[:, 0:8], sq8[:, 0:8], 64.0, b_sq8, b_sq8)
                    qn3 = qn.rearrange("p (h d) -> p h d", d=64)
                    P.op("dve", TT(qn3, p0.rearrange("p (h d) -> p h d", d=64), sq8[:, 0:8].unsqueeze(2).to_broadcast([128, 8, 64]), ALU.mult), reads=[b0, b_sq8], writes=[b_qn])
                    rope(qf[:, 0:512].rearrange("p (h d) -> p h d", d=64), qn3, tq[pa][:, 0:64], tq[pa][:, 64:128],
                         (t1.rearrange("p (h d) -> p h d", d=64), t2.rearrange("p (h d) -> p h d", d=64)), 8, 64, b_qn, b_tq[pa], b_t, b_qf)
                    P.op("act", ACT(qf[:, 512:1024], p1, AF.Copy, scale=sc0), reads=[b1], writes=[b_qf])
                    P.op("act", ACT(t1[:, 0:128], p2[:, 0:128], AF.Square), reads=[b2], writes=[b_t])
                    P.op("dve", RSUM(sq8[:, 8:10], t1[:, 0:128].rearrange("p (h d) -> p h d", d=64)), reads=[b_t], writes=[b_sq8])
                    rms_rstd(sq8[:, 8:10], sq8[:, 8:10], 64.0, b_sq8, b_sq8)
                    kn3 = qn[:, 0:128].rearrange("p (h d) -> p h d", d=64)
                    P.op("dve", TT(kn3, p2[:, 0:128].rearrange("p (h d) -> p h d", d=64), sq8[:, 8:10].unsqueeze(2).to_broadcast([128, 2, 64]), ALU.mult), reads=[b2, b_sq8], writes=[b_qn])
                    rope(kf[:, 0:128].rearrange("p (h d) -> p h d", d=64), kn3, tq[pa][:, 128:192], tq[pa][:, 192:256],
                         (t1[:, 0:128].rearrange("p (h d) -> p h d", d=64), t2[:, 0:128].rearrange("p (h d) -> p h d", d=64)), 2, 64, b_qn, b_tq[pa], b_t, b_kf)
                    P.op("act", ACP(vast[pa][:, :, 0:64], p2[:, 128:256].rearrange("p (h d) -> p h d", d=64)), reads=[b2], writes=[b_va[pa]])
                    P.op("pool", DMA(Va[t0:t0 + 128, :, :], vast[pa]), reads=[b_va[pa]], dma=True)
                    P.op("act", ACP(kf[:, 128:640], p3), reads=[b3], writes=[b_kf])
                    P.op("dve", CP(vbst[pa], p4), reads=[b4], writes=[b_vb[pa]])
                    P.op("pool", DMA(Vb[t0:t0 + 128, :, :].rearrange("t h e -> t (h e)"), vbst[pa]), reads=[b_vb[pa]], dma=True)
                    srcs = [(qf, b_qf, c) for c in range(8)] + [(kf, b_kf, c) for c in range(5)]
                    for j0 in range(0, 13, 4):
                        pt, b_pt = nextT()
                        n = min(4, 13 - j0)
                        for k in range(n):
                            sa, sbuf_, c = srcs[j0 + k]
                            P.op("pe", TR(pt[:, k * 128:(k + 1) * 128], sa[:, c * 128:(c + 1) * 128], ident), reads=[sbuf_, b_const], writes=[b_pt])
                        dst = stT[sb][:, j0:j0 + n, c4:c4 + 128]
                        src = pt[:, 0:n * 128].rearrange("p (k t) -> p k t", t=128)
                        P.op("act" if (j0 // 4) % 2 == 0 else "dve", (ACP if (j0 // 4) % 2 == 0 else CP)(dst, src), reads=[b_pt], writes=[b_stT[sb]])
                    if i % 4 == 3:
                        tb0 = blk * 512
                        s_ = stT[sb]
                        for c in range(4):
                            P.op("pool", DMA(QTa[2 * c, :, tb0:tb0 + 512], s_[0:64, c, :]), reads=[b_stT[sb]], dma=True)
                            P.op("pool", DMA(QTa[2 * c + 1, :, tb0:tb0 + 512], s_[64:128, c, :]), reads=[b_stT[sb]], dma=True)
                            P.op("pool", DMA(QTb[c, :, tb0:tb0 + 512], s_[:, 4 + c, :]), reads=[b_stT[sb]], dma=True)
                            P.op("pool", DMA(KTb[c, :, tb0:tb0 + 512], s_[:, 9 + c, :]), reads=[b_stT[sb]], dma=True)
                        P.op("pool", DMA(KTa[:, tb0:tb0 + 512], s_[:, 8, :]), reads=[b_stT[sb]], dma=True)
                else:
                    p0, b0 = nextbank(); p1, b1 = nextbank()
                    for kc in range(8):
                        P.op("pe", MM(p0[:, 0:384], hT[pa][:, kc, :], Win[:, kc, 0:384], kc == 0, kc == 7), reads=[b_hT[pa], b_Win], writes=[b0])
                    for kc in range(8):
                        P.op("pe", MM(p1[:, 0:288], hT[pa][:, kc, :], Win[:, kc, 384:672], kc == 0, kc == 7), reads=[b_hT[pa], b_Win], writes=[b1])
                    P.op("act", ACT(t1[:, 0:384], p0[:, 0:384], AF.Square, accum=sq8[:, 0:1]), reads=[b0], writes=[b_t, b_sq8])
                    P.op("act", ACT(t1[:, 0:256], p1[:, 0:256], AF.Square, accum=sq8[:, 1:2]), reads=[b1], writes=[b_t, b_sq8])
                    rms_rstd(sq8[:, 0:1], sq8[:, 2:3], 384.0, b_sq8, b_sq8)
                    rms_rstd(sq8[:, 1:2], sq8[:, 3:4], 256.0, b_sq8, b_sq8)
                    P.op("dve", TS(cn[:, 0:384], p0[:, 0:384], sq8[:, 2:3]), reads=[b0, b_sq8], writes=[b_cn])
                    P.op("dve", TS(cn[:, 384:640], p1[:, 0:256], sq8[:, 3:4]), reads=[b1, b_sq8], writes=[b_cn])
                    P.op("act", ACP(qn[:, 0:32], p1[:, 256:288]), reads=[b1], writes=[b_qn])
                    rope(krf.rearrange("p (h d) -> p h d", h=1), qn[:, 0:32].rearrange("p (h d) -> p h d", h=1), tabs[pa][:, 0:32], tabs[pa][:, 64:96],
                         (t1[:, 0:32].rearrange("p (h d) -> p h d", h=1), t2[:, 0:32].rearrange("p (h d) -> p h d", h=1)), 1, 32, b_qn, b_tabs[pa], b_t, b_krf)
                    for (j0, n) in [(0, 4), (4, 1)]:
                        pt, b_pt = nextT()
                        for k in range(n):
                            c = j0 + k
                            P.op("pe", TR(pt[:, k * 128:(k + 1) * 128], cn[:, c * 128:(c + 1) * 128], ident), reads=[b_cn, b_const], writes=[b_pt])
                        if j0 == 4:
                            P.op("pe", TR(pt[0:32, 128:256], krf, ident), reads=[b_krf, b_const], writes=[b_pt])
                            P.op("dve", CP(krT, pt[0:32, 128:256]), reads=[b_pt], writes=[b_krT])
                        P.op("act", ACP(cT[:, j0:j0 + n, :], pt[:, 0:n * 128].rearrange("p (k t) -> p k t", t=128)), reads=[b_pt], writes=[b_cT])
                    for (h0, h1) in [(0, 5), (5, 10), (10, 15), (15, 16)]:
                        bk, b_bk = nextbank()
                        nh = h1 - h0
                        for kc in range(3):
                            P.op("pe", MM(bk[:, 0:nh * 96], cT[:, kc, :], Wuq[:, kc, h0 * 96:h1 * 96], kc == 0, kc == 2), reads=[b_cT, b_Win], writes=[b_bk])
                        bv = bk[:, 0:nh * 96].rearrange("p (h d) -> p h d", d=96)
                        P.op("act", ACT(qf[:, h0:h1, 0:64], bv[:, :, 0:64], AF.Copy, scale=sc1), reads=[b_bk], writes=[b_qf])
                        P.op("act", ACT(qn[:, 0:nh * 32].rearrange("p (h d) -> p h d", d=32), bv[:, :, 64:96], AF.Copy, scale=sc1), reads=[b_bk], writes=[b_qn])
                        rope(qf[:, h0:h1, 64:96], qn[:, 0:nh * 32].rearrange("p (h d) -> p h d", d=32), tabs[pa][:, 0:32], tabs[pa][:, 64:96],
                             (t1[:, 0:nh * 32].rearrange("p (h d) -> p h d", d=32), t2[:, 0:nh * 32].rearrange("p (h d) -> p h d", d=32)), nh, 32, b_qn, b_tabs[pa], b_t, b_qf)
                    for j0 in range(0, 16, 4):
                        pt, b_pt = nextT()
                        for k in range(4):
                            P.op("pe", TR(pt[0:96, k * 128:(k + 1) * 128], qf[:, j0 + k, :], ident), reads=[b_qf, b_const], writes=[b_pt])
                        dst = stQ[sb][:, j0:j0 + 4, c4:c4 + 128]
                        src = pt[0:96, :].rearrange("p (k t) -> p k t", t=128)
                        P.op("act" if (j0 // 4) % 2 == 0 else "dve", (ACP if (j0 // 4) % 2 == 0 else CP)(dst, src), reads=[b_pt], writes=[b_stQ[sb]])
                    for j0 in range(0, 16, 4):
                        bk, b_bk = nextbank()
                        for k in range(4):
                            h = j0 + k
                            o_ = bk[0:96, k * 128:(k + 1) * 128]
                            P.op("pe", MM(o_, esel, krT, True, False), reads=[b_krT, b_const], writes=[b_bk])
                            P.op("pe", MM(o_, Wkp[:, 0, h, :], cT[:, 3, :], False, False), reads=[b_cT, b_Win], writes=[b_bk])
                            P.op("pe", MM(o_, Wkp[:, 1, h, :], cT[:, 4, :], False, True), reads=[b_cT, b_Win], writes=[b_bk])
                        dst = stK[sb][:, j0:j0 + 4, c4:c4 + 128]
                        src = bk[0:96, :].rearrange("p (k t) -> p k t", t=128)
                        P.op("act" if (j0 // 4) % 2 == 1 else "dve", (ACP if (j0 // 4) % 2 == 1 else CP)(dst, src), reads=[b_bk], writes=[b_stK[sb]])
                    for hh in range(2):
                        bk, b_bk = nextbank()
                        for kc in range(2):
                            P.op("pe", MM(bk, cT[:, 3 + kc, :], Wv[:, kc, hh * 512:(hh + 1) * 512], kc == 0, kc == 1), reads=[b_cT, b_Win], writes=[b_bk])
                        P.op("dve" if hh == 0 else "act", (CP if hh == 0 else ACP)(v1st[pa][:, hh * 8:(hh + 1) * 8, 0:64], bk.rearrange("p (h d) -> p h d", d=64)), reads=[b_bk], writes=[b_v1[pa]])
                    P.op("pool", DMA(V1[t0:t0 + 128, :, :], v1st[pa]), reads=[b_v1[pa]], dma=True)
                    if i % 4 == 3:
                        tb0 = blk * 512
                        for h in range(16):
                            P.op("pool", DMA(QT1[h, :, tb0:tb0 + 512], stQ[sb][:, h, :]), reads=[b_stQ[sb]], dma=True)
                            P.op("pool", DMA(KT1[h, :, tb0:tb0 + 512], stK[sb][:, h, :]), reads=[b_stK[sb]], dma=True)
            P.barrier()

        LA = 3
        NP = 4

        def pass_B(layer):
            abanks["l"] = [4, 5, 6, 7]
            sbanks["l"] = [0, 1, 2, 3]
            ngroups = 1 if layer == 0 else 2
            for g in range(ngroups):
                AB.reset(); AFa.reset()
                b_kv = P.buf()
                if layer == 0:
                    KTa_sb = AB.alloc([128, S]); KTb_sb = AB.alloc([128, 4, S])
                    Va_sb = AB.alloc([128, NK, 2, 65]); Vb_sb = AB.alloc([128, NK, 4, 128])
                    for blk in range(NT):
                        sl = slice(blk * 512, (blk + 1) * 512)
                        P.op("sp", DMA(KTa_sb[:, sl], KTa[:, sl]), writes=[b_kv], dma=True)
                        for h in range(4):
                            P.op("sp", DMA(KTb_sb[:, h, sl], KTb[h, :, sl]), writes=[b_kv], dma=True)
                    for kt in range(NK):
                        P.op("sp", DMA(Va_sb[:, kt, :, :], Va[kt * 128:(kt + 1) * 128, :, :]), writes=[b_kv], dma=True)
                        P.op("sp", DMA(Vb_sb[:, kt, :, :], Vb[kt * 128:(kt + 1) * 128, :, :]), writes=[b_kv], dma=True)
                    al_ab = AB.alloc([128, 4, 512]); al_bl = AB.alloc([128, 4, 512]); al_st = AB.alloc([128, 4, 896])
                    al_bias = AFa.alloc([128, 256])
                    for h in range(4):
                        P.op("sp", DMA(al_ab[:, h, :], C["al_above"][h]), writes=[b_kv], dma=True)
                        P.op("sp", DMA(al_bl[:, h, :], C["al_below"][h]), writes=[b_kv], dma=True)
                        P.op("sp", DMA(al_st[:, h, :], C["al_strip"][h]), writes=[b_kv], dma=True)
                    P.op("sp", DMA(al_bias, C["al_bias"][:, :]), writes=[b_kv], dma=True)
                else:
                    KT_sb = AB.alloc([96, 8, S]); V_sb = AB.alloc([128, NK, 8, 65])
                    for blk in range(NT):
                        sl = slice(blk * 512, (blk + 1) * 512)
                        for h in range(8):
                            P.op("sp", DMA(KT_sb[:, h, sl], KT1[g * 8 + h, :, sl]), writes=[b_kv], dma=True)
                    for kt in range(NK):
                        P.op("sp", DMA(V_sb[:, kt, :, :], V1[kt * 128:(kt + 1) * 128, g * 8:(g + 1) * 8, :]), writes=[b_kv], dma=True)
                b_q = [P.buf() for _ in range(2)]; b_o = [P.buf() for _ in range(2)]
                b_qb1 = P.buf(); b_ob1 = P.buf()
                if layer == 0:
                    qa_t = [AB.alloc([128, 8, 512]) for _ in range(2)]
                    qb_t = AB.alloc([128, 4, 2, 512])
                    oa_st = AB.alloc([64, 8, 512]); ob_st = AB.alloc([128, 4, 512])
                    for pa_ in range(2):
                        P.op("dve", MSET(qa_t[pa_], 0.0), writes=[b_q[pa_]])
                    P.op("dve", MSET(qb_t, 0.0), writes=[b_qb1])
                else:
                    q1_t = [AB.alloc([96, 8, 512]) for _ in range(2)]
                    o1_st = [AB.alloc([64, 8, 512]) for _ in range(2)]
                Pt = [AB.alloc([128, 512]) for _ in range(NP)]; b_P = [P.buf() for _ in range(NP)]
                if layer == 0:
                    P2 = [AB.alloc([128, 512]) for _ in range(NP)]; b_P2 = [P.buf() for _ in range(NP)]
                    sqb = AB.alloc([128, 512]); b_sqb = P.buf()
                    tc_ = [AFa.alloc([128, 512]) for _ in range(2)]; b_tc = [P.buf() for _ in range(2)]
                    tt_ = AFa.alloc([128, 512]); b_tt = P.buf()
                rl = [AFa.alloc([128, 512]) for _ in range(2)]; b_rl = [P.buf() for _ in range(2)]
                rlb = [AB.alloc([128, 2, 512]) for _ in range(2)]; b_rlb = [P.buf() for _ in range(2)]
                for u_i in range(2):
                    P.op("dve", MSET(rlb[u_i], 0.0), writes=[b_rlb[u_i]])
                bcs = [AFa.alloc([128, 512]) for _ in range(2)]; b_bcs = [P.buf() for _ in range(2)]
                pctr = {"i": 0, "u": 0}
                pend = []

                def defer(n, fn, tag=None):
                    pend.append([n, fn, tag])

                def flush_for(bufs):
                    last = -1
                    for i_, it in enumerate(pend):
                        if it[2] is not None and any(it[2] is b_ for b_ in bufs):
                            last = i_
                    for _ in range(last + 1):
                        pend.pop(0)[1]()

                def tick():
                    for it in pend:
                        it[0] -= 1
                    while pend and pend[0][0] <= 0:
                        pend.pop(0)[1]()

                def load_qb(qt):
                    sl = slice(qt * 512, (qt + 1) * 512)
                    for h in range(4):
                        for c in range(2):
                            P.op("sp", DMA(qb_t[c * 64:c * 64 + 64, h, c, :], QTb[h, c * 64:c * 64 + 64, sl]), writes=[b_qb1], dma=True)

                def load_q(qt):
                    pa = qt % 2
                    sl = slice(qt * 512, (qt + 1) * 512)
                    if layer == 0:
                        for hq in range(8):
                            P.op("sp", DMA(qa_t[pa][(hq // 4) * 64:(hq // 4) * 64 + 64, hq, :], QTa[hq, :, sl]), writes=[b_q[pa]], dma=True)
                    else:
                        for h in range(8):
                            P.op("sp", DMA(q1_t[pa][:, h, :], QT1[g * 8 + h, :, sl]), writes=[b_q[pa]], dma=True)

                def store_oa(qt):
                    sl = slice(qt * 512, (qt + 1) * 512)
                    for hq in range(8):
                        P.op("pool", DMA(OTa[hq, :, sl], oa_st[:, hq, :]), reads=[b_o[0]], dma=True)

                def store_o(qt):
                    pa = qt % 2
                    sl = slice(qt * 512, (qt + 1) * 512)
                    if layer == 0:
                        for h in range(4):
                            P.op("pool", DMA(OTb[h, :, sl], ob_st[:, h, :]), reads=[b_ob1], dma=True)
                    else:
                        for h in range(8):
                            P.op("pool", DMA(OT1[g * 8 + h, :, sl], o1_st[pa][:, h, :]), reads=[b_o[pa]], dma=True)

                class AugUnit:
                    def __init__(self, Qap, Kfn, Vfn, o_dst, b_odst, b_qb):
                        self.Qap, self.Kfn, self.Vfn, self.o_dst, self.b_odst, self.b_qb = Qap, Kfn, Vfn, o_dst, b_odst, b_qb
                        self.kts = list(range(NK))

                    def start(self):
                        self.O, self.b_O = nextA()
                        flush_for([self.b_O])
                        self.u = pctr["u"] % 2; pctr["u"] += 1

                    def front(self, kt):
                        sbk, b_sbk = nextS()
                        P.op("pe", MM(sbk, self.Kfn(kt), self.Qap, True, True), reads=[b_kv, self.b_qb], writes=[b_sbk])
                        pi = pctr["i"] % NP; pctr["i"] += 1
                        P.op("act", ACT(Pt[pi], sbk, AF.Exp), reads=[b_sbk], writes=[b_P[pi]])
                        return Pt[pi], b_P[pi]

                    def back(self, kt, pinfo):
                        pt, b_pt = pinfo
                        P.op("pe", MM(self.O[0:65, :], self.Vfn(kt), pt, kt == self.kts[0], kt == self.kts[-1]), reads=[b_kv, b_pt], writes=[self.b_O])

                    def fin(self):
                        u = self.u; O = self.O; b_O = self.b_O
                        P.op("dve", RECIP(rl[u][64:65, :], O[64:65, :]), reads=[b_O], writes=[b_rl[u]])
                        P.op("dve", CP(rlb[u][64:65, 0, :], rl[u][64:65, :]), reads=[b_rl[u]], writes=[b_rlb[u]])
                        P.op("dve", TT(rlb[u][64:65, 1, :], rl[u][64:65, :], rlb[u][64:65, 0, :], ALU.subtract), reads=[b_rl[u], b_rlb[u]], writes=[b_rlb[u]])

                        def part2():
                            bcp, b_bcp = nextS()
                            P.op("pe", MM(bcp[0:64, :], sel64, rlb[u][:, 0, :], True, False), reads=[b_rlb[u], b_const], writes=[b_bcp])
                            P.op("pe", MM(bcp[0:64, :], sel64, rlb[u][:, 1, :], False, True), reads=[b_rlb[u], b_const], writes=[b_bcp])
                            P.op("act", ACP(bcs[u][0:64, :], bcp[0:64, :]), reads=[b_bcp], writes=[b_bcs[u]])
                            P.op("dve", TT(self.o_dst, O[0:64, :], bcs[u][0:64, :], ALU.mult), reads=[b_O, b_bcs[u]], writes=[self.b_odst])
                        defer(10, part2, b_O)

                def diff_kts(h, qt):
                    sl_ = 2.0 ** (-8.0 * (h + 1) / 4)
                    out = []
                    for kt in range(NK):
                        if kt < 4 * qt:
                            dmin = 512 * qt - 128 * kt - 127
                        elif kt >= 4 * qt + 4:
                            dmin = 128 * kt - 512 * qt - 511
                        else:
                            dmin = 0
                        if sl_ * dmin < 100.0:
                            out.append(kt)
                    return out

                class DiffUnit:
                    def __init__(self, h, c, qt, Qap, Kfn, Vfn, o_dst, b_odst, b_qb):
                        self.h, self.c, self.qt = h, c, qt
                        self.kts = diff_kts(h, qt)
                        self.Qap, self.Kfn, self.Vfn, self.o_dst, self.b_odst, self.b_qb = Qap, Kfn, Vfn, o_dst, b_odst, b_qb

                    def start(self):
                        self.O, self.b_O = nextA(); self.L, self.b_L = nextA()
                        flush_for([self.b_O, self.b_L])
                        self.u = pctr["u"] % 2; pctr["u"] += 1

                    def front(self, kt):
                        h, qt = self.h, self.qt
                        sbk, b_sbk = nextS()
                        P.op("pe", MM(sbk, self.Kfn(kt), self.Qap, True, True), reads=[b_kv, self.b_qb], writes=[b_sbk])
                        pi = pctr["i"] % NP; pctr["i"] += 1
                        if kt < 4 * qt:
                            m_ = (512 * qt - 128 * kt) // 128
                            bcol = al_bias[:, h * 64 + m_:h * 64 + m_ + 1]; tab = al_ab[:, h, :]
                        elif kt >= 4 * qt + 4:
                            m_ = (128 * kt - 512 * qt) // 128
                            bcol = al_bias[:, h * 64 + 32 + m_:h * 64 + 32 + m_ + 1]; tab = al_bl[:, h, :]
                        else:
                            dl = 512 * qt - 128 * kt
                            bcol = al_bias[:, h * 64:h * 64 + 1]; tab = al_st[:, h, 384 + dl:384 + dl + 512]
                        P.op("act", ACT(Pt[pi], sbk, AF.Exp, bias=bcol), reads=[b_sbk, b_kv], writes=[b_P[pi]])
                        P.op("dve", TT(P2[pi], Pt[pi], tab, ALU.mult), reads=[b_P[pi], b_kv], writes=[b_P2[pi]])
                        return P2[pi], b_P2[pi]

                    def back(self, kt, pinfo):
                        pt, b_pt = pinfo
                        P.op("pe", MM(self.O, self.Vfn(kt), pt, kt == self.kts[0], kt == self.kts[-1]), reads=[b_kv, b_pt], writes=[self.b_O])
                        P.op("pe", MM(self.L, ones_b, pt, kt == self.kts[0], kt == self.kts[-1]), reads=[b_const, b_pt], writes=[self.b_L])

                    def fin(self):
                        u = self.u; c = self.c
                        P.op("act", ACT(rl[u], self.L, AF.Ln), reads=[self.b_L], writes=[b_rl[u]])
                        P.op("act", ACT(rl[u], rl[u], AF.Exp, scale=-1.0), reads=[b_rl[u]], writes=[b_rl[u]])
                        P.op("dve", TT(tc_[c], self.O, rl[u], ALU.mult), reads=[self.b_O, b_rl[u]], writes=[b_tc[c]])
                        if c == 1:
                            P.op("dve", STT(tt_, tc_[1], nlam, tc_[0], ALU.mult, ALU.add), reads=[b_tc[0], b_tc[1], b_const], writes=[b_tt])
                            P.op("act", ACT(sqb, tt_, AF.Square), reads=[b_tt], writes=[b_sqb])

                            def part2():
                                ssb, b_ssb = nextS()
                                P.op("pe", MM(ssb, ones_b, sqb, True, True), reads=[b_const, b_sqb], writes=[b_ssb])
                                P.op("act", ACT(rl[u], ssb, AF.Ln, bias=EPS, scale=1.0 / 128), reads=[b_ssb], writes=[b_rl[u]])
                                P.op("act", ACT(rl[u], rl[u], AF.Exp, scale=-0.5), reads=[b_rl[u]], writes=[b_rl[u]])
                                P.op("dve", STT(self.o_dst, tt_, gsub_c, rl[u], ALU.mult, ALU.mult), reads=[b_tt, b_rl[u], b_const], writes=[self.b_odst])
                            defer(6, part2)

                stream = []
                for qt in range(NT):
                    pa = qt % 2
                    units = []
                    if layer == 0:
                        for hq in range(8):
                            kvh = hq // 4
                            units.append(AugUnit(qa_t[pa][:, hq, :],
                                                 lambda kt: KTa_sb[:, kt * 128:(kt + 1) * 128],
                                                 lambda kt, kvh=kvh: Va_sb[:, kt, kvh, :], oa_st[:, hq, :], b_o[0], b_q[pa]))
                        for h in range(4):
                            for c in range(2):
                                units.append(DiffUnit(h, c, qt, qb_t[:, h, c, :],
                                                      lambda kt, h=h: KTb_sb[:, h, kt * 128:(kt + 1) * 128],
                                                      lambda kt, h=h: Vb_sb[:, kt, h, :], ob_st[:, h, :], b_ob1, b_qb1))
                    else:
                        for h in range(8):
                            units.append(AugUnit(q1_t[pa][:, h, :], lambda kt, h=h: KT_sb[:, h, kt * 128:(kt + 1) * 128],
                                                 lambda kt, h=h: V_sb[:, kt, h, :], o1_st[pa][:, h, :], b_o[pa], b_q[pa]))
                    for ui, u_ in enumerate(units):
                        for kt in u_.kts:
                            stream.append((u_, kt, qt, ui == 0 and kt == u_.kts[0], ui == len(units) - 1 and kt == u_.kts[-1],
                                           layer == 0 and ui == 7 and kt == u_.kts[-1]))
                load_q(0)
                if layer == 0:
                    load_qb(0)
                inflight = []
                for idx in range(len(stream) + LA):
                    if idx < len(stream):
                        u_, kt, qt, first, lastq, _ = stream[idx]
                        if first and qt + 1 < NT:
                            load_q(qt + 1)
                        if kt == u_.kts[0]:
                            u_.start()
                        inflight.append((u_, kt, qt, u_.front(kt)))
                        if lastq and layer == 0 and qt + 1 < NT:
                            load_qb(qt + 1)
                    if idx >= LA:
                        u_, kt, qt, pinfo = inflight.pop(0)
                        u_.back(kt, pinfo)
                        if kt == u_.kts[-1]:
                            u_.fin()
                            if stream[idx - LA][4]:
                                defer(14, lambda qt=qt: store_o(qt))
                            if stream[idx - LA][5]:
                                defer(14, lambda qt=qt: store_oa(qt))
                    tick()
                while pend:
                    pend.pop(0)[1]()
                P.barrier()

        def pass_C(layer, seq, xsrc, xdst, last):
            abanks["l"] = [3, 4, 5]
            sbanks["l"] = [0, 1, 2]
            AB.reset(); AFa.reset()
            NW = 3
            b_w = [P.buf() for _ in range(NW)]
            wst = [AB.alloc([128, 8, 1024]) for _ in range(NW)]
            wctr = {"i": 0}
            Wdn = AB.alloc([128, 22, 1024]); b_Wdn = P.buf()
            Kmem = AB.alloc([128, 8, 256]); Vmem = AB.alloc([128, 2, 1024]); b_mem = P.buf()
            xts = [AFa.alloc([128, 4, 1024]) for _ in range(2)]; b_xts = [[P.buf() for _ in range(4)] for _ in range(2)]
            ssr = AFa.alloc([128, 8]); b_s = [P.buf() for _ in range(4)]
            rlc = AFa.alloc([128, 512]); b_rlc = P.buf()
            hb = [AB.alloc([128, 1024]) for _ in range(2)]; b_hb = [P.buf() for _ in range(2)]
            hT = AB.alloc([128, 8, 512]); b_hT = P.buf()
            qT = AB.alloc([128, 8, 512]); b_qT = P.buf()
            o2T = hT; b_o2T = b_hT
            ots = [AB.alloc([128, 8, 512]) for _ in range(2)]; b_ots = [P.buf() for _ in range(2)]
            actT = AB.alloc([128, 22, 512]); b_act = P.buf()
            Pc = [AB.alloc([128, 512]) for _ in range(2)]; b_Pc = [P.buf() for _ in range(2)]
            sgl = Pc[0]; b_sgl = b_Pc[0]

            def wload(view_fn_list):
                s_ = wctr["i"] % NW; wctr["i"] += 1
                for dfn, src in view_fn_list:
                    P.op("sp", DMA(dfn(wst[s_]), src), writes=[b_w[s_]], dma=True)
                return wst[s_], b_w[s_]

            def load_tile(tt):
                pa = tt % 2
                t0 = tt * 512
                sl = slice(t0, t0 + 512)
                for s_ in range(4):
                    P.op("sp", DMA(xts[pa][:, s_, :], xsrc[t0 + s_ * 128:t0 + (s_ + 1) * 128, :]), writes=[b_xts[pa][s_]], dma=True)
                if layer == 0:
                    for hq in range(8):
                        P.op("sp", DMA(ots[pa][(hq % 2) * 64:(hq % 2) * 64 + 64, hq // 2, :], OTa[hq, :, sl]), writes=[b_ots[pa]], dma=True)
                    for h in range(4):
                        P.op("sp", DMA(ots[pa][:, 4 + h, :], OTb[h, :, sl]), writes=[b_ots[pa]], dma=True)
                else:
                    for h in range(16):
                        P.op("sp", DMA(ots[pa][(h % 2) * 64:(h % 2) * 64 + 64, h // 2, :], OT1[h, :, sl]), writes=[b_ots[pa]], dma=True)

            load_tile(0)
            mt = xts[1][:, 0, :]; b_mt = b_xts[1][0]
            for mi in range(2):
                P.op("sp", DMA(mt, mem_in[seq, mi * 128:(mi + 1) * 128, :]), writes=[b_mt], dma=True)
                P.op("act", ACT(hb[0], mt, AF.Square, accum=ssr[:, 0:1]), reads=[b_mt], writes=[b_hb[0], b_s[0]])
                rms_rstd(ssr[:, 0:1], ssr[:, 1:2], float(D), b_s[0], b_s[0])
                P.op("dve", TS(hb[0], mt, ssr[:, 1:2]), reads=[b_mt, b_s[0]], writes=[b_hb[0]])
                for half in range(2):
                    pt, b_pt = nextT()
                    for k in range(4):
                        kc = half * 4 + k
                        P.op("pe", TR(pt[:, k * 128:(k + 1) * 128], hb[0][:, kc * 128:(kc + 1) * 128], ident), reads=[b_hb[0], b_const], writes=[b_pt])
                    P.op("act", ACP(hT[:, half * 4:half * 4 + 4, mi * 128:(mi + 1) * 128], pt.rearrange("p (k t) -> p k t", t=128)), reads=[b_pt], writes=[b_hT])
            for half in range(2):
                wv, b_wv = wload([(lambda a: a, Wckv_s[layer][:, :, half * 1024:(half + 1) * 1024])])
                if half == 0:
                    for m in range(8):
                        bk, b_bk = nextbank()
                        for kc in range(8):
                            P.op("pe", MM(bk[:, 0:256], wv[:, kc, m * 128:(m + 1) * 128], hT[:, kc, 0:256], kc == 0, kc == 7), reads=[b_wv, b_hT], writes=[b_bk])
                        P.op("act" if m % 2 else "dve", (ACP if m % 2 else CP)(Kmem[:, m, :], bk[:, 0:256]), reads=[b_bk], writes=[b_mem])
                else:
                    for mi in range(2):
                        for nh in range(2):
                            bk, b_bk = nextbank()
                            for kc in range(8):
                                P.op("pe", MM(bk, hT[:, kc, mi * 128:(mi + 1) * 128], wv[:, kc, nh * 512:(nh + 1) * 512], kc == 0, kc == 7), reads=[b_wv, b_hT], writes=[b_bk])
                            P.op("act" if nh else "dve", (ACP if nh else CP)(Vmem[:, mi, nh * 512:(nh + 1) * 512], bk), reads=[b_bk], writes=[b_mem])

            def preload_first():
                a_ = wload([(lambda a: a, (Wout0_s if layer == 0 else Wout1_s)[:, :, :])])
                b__ = wload([(lambda a: a, Wcq_s[layer][:, :, :])])
                return a_, b__

            pre = preload_first()
            for tt in range(NT):
                pa = tt % 2
                xt = xts[pa]; b_xt = b_xts[pa]
                ot = ots[pa]; b_ot = b_ots[pa]
                t0 = tt * 512
                (w_out, b_w_out), (w_cq, b_w_cq) = pre
                if tt + 1 < NT:
                    load_tile(tt + 1)

                def add_proj(lhs_list, b_lhs, rhs_fn, b_rhs):
                    for s_ in range(4):
                        for n in range(2):
                            bk, b_bk = nextbank()
                            nl = len(lhs_list)
                            for ci, lf in enumerate(lhs_list):
                                P.op("pe", MM(bk, lf(s_), rhs_fn(ci, n), ci == 0, ci == nl - 1), reads=[b_lhs, b_rhs], writes=[b_bk])
                            P.op("dve", TT(xt[:, s_, n * 512:(n + 1) * 512], xt[:, s_, n * 512:(n + 1) * 512], bk, ALU.add), reads=[b_bk, b_xt[s_]], writes=[b_xt[s_]])

                def norm_tile():
                    for s_ in range(4):
                        p2 = s_ % 2
                        norm_to_hT(xt[:, s_, :], b_xt[s_], None, None, ssr[:, 0:1], ssr[:, 1:2], b_s[0], hb[p2], b_hb[p2], hT, b_hT, s_ * 128)

                wv, b_wv = w_out, b_w_out
                lhs = [(lambda s_, c=c: ot[:, c, s_ * 128:(s_ + 1) * 128]) for c in range(8)]
                add_proj(lhs, b_ot, lambda ci, n, wv=wv: wv[:, ci, n * 512:(n + 1) * 512], b_wv)
                norm_tile()
                wv, b_wv = w_cq, b_w_cq
                for m in range(8):
                    bk, b_bk = nextbank()
                    for kc in range(8):
                        P.op("pe", MM(bk, wv[:, kc, m * 128:(m + 1) * 128], hT[:, kc, :], kc == 0, kc == 7), reads=[b_wv, b_hT], writes=[b_bk])
                    P.op("act", ACT(qT[:, m, :], bk, AF.Copy, scale=1.0 / 16), reads=[b_bk], writes=[b_qT])
                for h in range(4):
                    L, b_L = nextA(); O0, b_O0 = nextA(); O1, b_O1 = nextA()
                    for mi in range(2):
                        sbk, b_sbk = nextS()
                        for dc in range(2):
                            P.op("pe", MM(sbk, Kmem[:, 2 * h + dc, mi * 128:(mi + 1) * 128], qT[:, 2 * h + dc, :], dc == 0, dc == 1), reads=[b_mem, b_qT], writes=[b_sbk])
                        P.op("act", ACT(Pc[mi], sbk, AF.Exp), reads=[b_sbk], writes=[b_Pc[mi]])
                        P.op("pe", MM(L, ones_b, Pc[mi], mi == 0, mi == 1), reads=[b_const, b_Pc[mi]], writes=[b_L])
                        P.op("pe", MM(O0, Vmem[:, mi, h * 256:h * 256 + 128], Pc[mi], mi == 0, mi == 1), reads=[b_mem, b_Pc[mi]], writes=[b_O0])
                        P.op("pe", MM(O1, Vmem[:, mi, h * 256 + 128:h * 256 + 256], Pc[mi], mi == 0, mi == 1), reads=[b_mem, b_Pc[mi]], writes=[b_O1])
                    P.op("act", ACT(rlc, L, AF.Ln), reads=[b_L], writes=[b_rlc])
                    P.op("act", ACT(rlc, rlc, AF.Exp, scale=-1.0), reads=[b_rlc], writes=[b_rlc])
                    P.op("dve", TT(o2T[:, 2 * h, :], O0, rlc, ALU.mult), reads=[b_O0, b_rlc], writes=[b_o2T])
                    P.op("dve", TT(o2T[:, 2 * h + 1, :], O1, rlc, ALU.mult), reads=[b_O1, b_rlc], writes=[b_o2T])
                wv, b_wv = wload([(lambda a: a, Wco_s[layer][:, :, :])])
                lhs = [(lambda s_, kc=kc: o2T[:, kc, s_ * 128:(s_ + 1) * 128]) for kc in range(8)]
                add_proj(lhs, b_o2T, lambda ci, n, wv=wv: wv[:, ci, n * 512:(n + 1) * 512], b_wv)
                norm_tile()
                for kc in range(22):
                    P.op("sp", DMA(Wdn[:, kc, :], Wdn_s[layer][:, kc, :]), writes=[b_Wdn], dma=True)
                for j0 in range(0, 22, 4):
                    nj = min(4, 22 - j0)
                    wv, b_wv = wload([(lambda a, nj=nj: a[:, :, 0:nj * 128], Wgu_s[layer][:, :, j0 * 128:(j0 + nj) * 128]),
                                      (lambda a, nj=nj: a[:, :, 512:512 + nj * 128], Wgu_s[layer][:, :, DFF + j0 * 128:DFF + (j0 + nj) * 128])])
                    for jj in range(nj):
                        j = j0 + jj
                        gk, b_gk = nextbank(); uk, b_uk = nextbank()
                        for kc in range(8):
                            P.op("pe", MM(gk, wv[:, kc, jj * 128:(jj + 1) * 128], hT[:, kc, :], kc == 0, kc == 7), reads=[b_wv, b_hT], writes=[b_gk])
                        for kc in range(8):
                            P.op("pe", MM(uk, wv[:, kc, 512 + jj * 128:512 + (jj + 1) * 128], hT[:, kc, :], kc == 0, kc == 7), reads=[b_wv, b_hT], writes=[b_uk])
                        P.op("act", ACT(sgl, gk, AF.Silu), reads=[b_gk], writes=[b_sgl])
                        P.op("dve", TT(actT[:, j, :], sgl, uk, ALU.mult), reads=[b_sgl, b_uk], writes=[b_act])
                if tt + 1 < NT:
                    pre = preload_first()
                lhs = [(lambda s_, j=j: actT[:, j, s_ * 128:(s_ + 1) * 128]) for j in range(22)]
                add_proj(lhs, b_act, lambda ci, n: Wdn[:, ci, n * 512:(n + 1) * 512], b_Wdn)
                for s_ in range(4):
                    rows = slice(t0 + s_ * 128, t0 + (s_ + 1) * 128)
                    if last:
                        P.op("act", ACT(hb[s_ % 2], xt[:, s_, :], AF.Square, accum=ssr[:, 2:3]), reads=[b_xt[s_]], writes=[b_hb[s_ % 2], b_s[1]])
                        rms_rstd(ssr[:, 2:3], ssr[:, 3:4], float(D), b_s[1], b_s[1])
                        P.op("dve", STT(xt[:, s_, :], xt[:, s_, :], ssr[:, 3:4], fin_b, ALU.mult, ALU.mult), reads=[b_xt[s_], b_s[1], b_const], writes=[b_xt[s_]])
                    P.op("pool", DMA(xdst[rows, :], xt[:, s_, :]), reads=[b_xt[s_]], dma=True)
            P.barrier()

        for seq in range(NSEQ if stop_after != "P" else 0):
            for layer in range(2):
                xsrc = x_in[seq] if layer == 0 else X1
                xdst = X1 if layer == 0 else y_out[seq]
                pass_A(layer, xsrc)
                if stop_after == "A":
                    break
                pass_B(layer)
                if stop_after == "B":
                    break
                pass_C(layer, seq, xsrc, xdst, layer == 1)
                if stop_after == "C0":
                    break
        P.barrier()
        P.finalize()
        build.stats = {e: len(P.ops[e]) for e in ENGS}
        P.emit(nc, st)
    return nc


_CACHE = {}


def kernel(x_prompt, x_sample, mem_prompt, mem_sample, **w):
    S = x_prompt.shape[1]
    xs = np.concatenate([np.asarray(x_prompt, np.float32), np.asarray(x_sample, np.float32)], axis=0)
    ms = np.concatenate([np.asarray(mem_prompt, np.float32), np.asarray(mem_sample, np.float32)], axis=0)
    ntot = xs.shape[0]
    nseq = ntot // N_CORES
    key = (nseq, S)
    if key not in _CACHE:
        _CACHE[key] = build(nseq, S)
    nc = _CACHE[key]
    consts = host_consts(S)
    wd = {n: np.ascontiguousarray(np.asarray(w[n], np.float32)) for n in WNAMES}
    in_maps = []
    for c in range(N_CORES):
        m = {"x": np.ascontiguousarray(xs[c * nseq:(c + 1) * nseq]), "mem": np.ascontiguousarray(ms[c * nseq:(c + 1) * nseq])}
        m.update(wd)
        m.update(consts)
        in_maps.append(m)
    res = run_bass_kernel_spmd(nc, in_maps, core_ids=list(range(N_CORES)))
    ys = np.concatenate([np.asarray(r["y"], np.float32) for r in res.results], axis=0)
    nb = x_prompt.shape[0]
    return (ys[:nb], ys[nb:])
```

```python
import math
from contextlib import ExitStack

import ml_dtypes
import numpy as np

import concourse.bass as bass
import concourse.mybir as mybir
from concourse.bass_utils import run_bass_kernel_spmd

F32 = mybir.dt.float32
BF16 = mybir.dt.bfloat16
ALU = mybir.AluOpType
AF = mybir.ActivationFunctionType
AX = mybir.AxisListType

D = 1024
DFF = 2816
NMEM = 256
EPS = 1e-6
GRID_W = 64
N_CORES = 8

ENGS = ["pe", "act", "dve", "pool", "sp"]
N_DMA_SEMS = 24


class Buf:
    __slots__ = ("name", "w", "r", "excl")

    def __init__(self, name, excl=False):
        self.name = name
        self.w = None
        self.r = []
        self.excl = excl


class Op:
    __slots__ = ("eng", "fn", "deps", "signal", "sem", "val", "dma", "waits")

    def __init__(self, eng, fn, dma):
        self.eng = eng
        self.fn = fn
        self.dma = dma
        self.deps = set()
        self.signal = False
        self.sem = None
        self.val = 0
        self.waits = []


class Plan:
    def __init__(self):
        self.ops = {e: [] for e in ENGS}
        self.all = []
        self.dma_rr = {"sp": 0, "pool": 0, "act": 0}
        self.dma_last = [None] * N_DMA_SEMS
        self.nbuf = 0

    def buf(self, name=None, excl=False):
        self.nbuf += 1
        return Buf(name or f"b{self.nbuf}", excl)

    def op(self, eng, fn, reads=(), writes=(), dma=False, after=()):
        o = Op(eng, fn, dma)
        deps = o.deps
        for b in reads:
            if b.w is not None:
                deps.add(b.w)
            if b.excl:
                for q in b.r:
                    if q.eng != eng:
                        deps.add(q)
        for b in writes:
            if b.w is not None:
                deps.add(b.w)
            deps.update(b.r)
        for a in after:
            if a is not None:
                deps.add(a)
        if dma:
            half = N_DMA_SEMS // 2
            k = (self.dma_rr[eng] % half) + (half if eng == "pool" else 0)
            self.dma_rr[eng] += 1
            prev = self.dma_last[k]
            if prev is not None:
                deps.add(prev)
            self.dma_last[k] = o
            o.sem = ("dma", k)
        else:
            o.sem = ("eng", eng)
        for b in reads:
            if not dma:
                b.r = [q for q in b.r if q.dma or q.eng != eng]
            b.r.append(o)
        for b in writes:
            b.w = o
            b.r = []
        self.ops[eng].append(o)
        self.all.append(o)
        return o

    def barrier(self):
        lasts = []
        for e in ENGS:
            if self.ops[e]:
                lasts.append(self.ops[e][-1])
        lasts += [d for d in self.dma_last if d is not None]
        for e in ENGS:
            self.op(e, None, after=lasts)

    @staticmethod
    def _skip(d, o):
        return (not d.dma) and (not o.dma) and d.eng == "pe" and o.eng == "pe"

    def finalize(self):
        for e in ENGS:
            for o in self.ops[e]:
                for d in o.deps:
                    if d.dma or self._skip(d, o):
                        continue
                    d.signal = True
        cnt = {}
        for o in self.all:
            if o.dma:
                cnt[o.sem] = cnt.get(o.sem, 0) + 16
                o.val = cnt[o.sem]
                o.signal = True
            elif o.signal:
                if o.fn is None:
                    o.signal = False
                    continue
                cnt[o.sem] = cnt.get(o.sem, 0) + 1
                o.val = cnt[o.sem]
        for e in ENGS:
            waited = {}
            for o in self.ops[e]:
                need = {}
                for d in o.deps:
                    if self._skip(d, o) or d is o:
                        continue
                    if d.fn is None:
                        continue
                    v = need.get(d.sem, 0)
                    if d.val > v:
                        need[d.sem] = d.val
                for s, v in need.items():
                    if waited.get(s, 0) < v:
                        waited[s] = v
                        o.waits.append((s, v))
        self.counts = cnt

    def emit(self, nc, stack):
        sems = {}
        for e in ENGS:
            sems[("eng", e)] = stack.enter_context(nc.semaphore(f"s_{e}"))
        for k in range(N_DMA_SEMS):
            sems[("dma", k)] = stack.enter_context(nc.semaphore(f"s_dma{k}"))
        block = stack.enter_context(nc.Block())
        plan = self

        def replay(ename):
            def run(h):
                for o in plan.ops[ename]:
                    for s, v in o.waits:
                        h.wait_ge(sems[s], v)
                    if o.fn is None:
                        continue
                    inst = o.fn(h)
                    if o.signal:
                        inst.then_inc(sems[o.sem], 16 if o.dma else 1)
            return run

        block.tensor(replay("pe"))
        block.scalar(replay("act"))
        block.vector(replay("dve"))
        block.gpsimd(replay("pool"))
        block.sync(replay("sp"))


def MM(out, lhsT, rhs, start=True, stop=True):
    return lambda e: e.matmul(out, lhsT=lhsT, rhs=rhs, start=start, stop=stop)


def TR(out, in_, ident):
    return lambda e: e.transpose(out=out, in_=in_, identity=ident)


def ACT(out, in_, func, bias=None, scale=None, accum=None):
    kw = {}
    if bias is not None:
        kw["bias"] = bias
    if scale is not None:
        kw["scale"] = scale
    if accum is not None:
        kw["accum_out"] = accum
    return lambda e: e.activation(out=out, in_=in_, func=func, **kw)


def DMA(out, in_, slow=False):
    if slow:
        return lambda e: e.dma_start(out=out, in_=in_, allow_slow_non_contiguous=True)
    return lambda e: e.dma_start(out=out, in_=in_)


def TS(out, in0, s1, s2=None, op0=ALU.mult, op1=None):
    if op1 is None:
        return lambda e: e.tensor_scalar(out=out, in0=in0, scalar1=s1, scalar2=None, op0=op0)
    return lambda e: e.tensor_scalar(out=out, in0=in0, scalar1=s1, scalar2=s2, op0=op0, op1=op1)


def TT(out, in0, in1, op):
    return lambda e: e.tensor_tensor(out=out, in0=in0, in1=in1, op=op)


def STT(out, in0, scalar, in1, op0, op1):
    return lambda e: e.scalar_tensor_tensor(out=out, in0=in0, scalar=scalar, in1=in1, op0=op0, op1=op1)


def CP(out, in_):
    return lambda e: e.tensor_copy(out=out, in_=in_)


def ACP(out, in_):
    return lambda e: e.copy(out=out, in_=in_)


def RECIP(out, in_):
    return lambda e: e.reciprocal(out=out, in_=in_)


def RSUM(out, in_):
    return lambda e: e.tensor_reduce(out=out, in_=in_, axis=AX.X, op=ALU.add)


def MSET(ap, v):
    return lambda e: e.memset(ap, v)


class Arena:
    def __init__(self, t, n):
        self.t = t
        self.n = n
        self.off = 0

    def reset(self):
        self.off = 0

    def alloc(self, shape):
        n = 1
        for s in shape[1:]:
            n *= s
        n = (n + 15) // 16 * 16
        o = self.off
        self.off += n
        assert self.off <= self.n, (self.off, self.n, shape)
        m = 1
        for s in shape[1:]:
            m *= s
        v = self.t[0:shape[0], o:o + m]
        if len(shape) == 3:
            v = v.rearrange("p (a b) -> p a b", b=shape[2])
        elif len(shape) == 4:
            v = v.rearrange("p (a b c) -> p a b c", b=shape[2], c=shape[3])
        return v


AB_N = 82944
AF_N = 8768

WNAMES = ["norm_mix", "e_w_in", "e_q_norm", "e_k_norm", "e_lam_q1", "e_lam_k1", "e_lam_q2", "e_lam_k2",
          "e_subln", "e_w_out", "o_w_in", "o_q_norm", "o_kv_norm", "o_w_uq", "o_w_ukv", "o_w_out",
          "norm_cross", "norm_mem", "w_cq", "w_ckv", "w_co", "norm_ffn", "w_gu", "w_down", "final_norm"]
WSHAPES = {
    "norm_mix": [2, D], "e_w_in": [1, D, 2304], "e_q_norm": [1, 64], "e_k_norm": [1, 64],
    "e_lam_q1": [1, 64], "e_lam_k1": [1, 64], "e_lam_q2": [1, 64], "e_lam_k2": [1, 64],
    "e_subln": [1, 128], "e_w_out": [1, D, D], "o_w_in": [1, D, 672], "o_q_norm": [1, 384],
    "o_kv_norm": [1, 256], "o_w_uq": [1, 384, 1536], "o_w_ukv": [1, 256, 2048], "o_w_out": [1, D, D],
    "norm_cross": [2, D], "norm_mem": [2, D], "w_cq": [2, D, D], "w_ckv": [2, D, 2 * D], "w_co": [2, D, D],
    "norm_ffn": [2, D], "w_gu": [2, D, 2 * DFF], "w_down": [2, DFF, D], "final_norm": [D],
}


def host_consts(S):
    c = {}
    c["ident"] = np.eye(128, dtype=np.float32).astype(ml_dtypes.bfloat16)
    c["ones_b"] = np.ones((128, 128), dtype=np.float32).astype(ml_dtypes.bfloat16)
    c["ones_f"] = np.ones((128, 64), dtype=np.float32)
    es = np.zeros((32, 96), dtype=np.float32)
    es[np.arange(32), 64 + np.arange(32)] = 1.0
    c["esel"] = es.astype(ml_dtypes.bfloat16)
    s64 = np.zeros((128, 64), dtype=np.float32)
    s64[64, :] = 1.0
    c["sel64"] = s64.astype(ml_dtypes.bfloat16)
    t = np.arange(S)
    fa = (10000.0 ** (-np.arange(16, dtype=np.float32) / 16)).astype(np.float32)
    r = (t // GRID_W).astype(np.float32)
    cc = (t % GRID_W).astype(np.float32)
    angA = np.concatenate([r[:, None] * fa, cc[:, None] * fa], axis=-1).astype(np.float32)
    fl = (10000.0 ** (-np.arange(16, dtype=np.float32) / 16)).astype(np.float32)
    angL = (t.astype(np.float32)[:, None] * fl).astype(np.float32)

    def exp_tabs(ang):
        co = np.repeat(np.cos(ang), 2, axis=-1).astype(np.float32)
        si = np.repeat(np.sin(ang), 2, axis=-1).astype(np.float32)
        si[:, 0::2] *= -1.0
        return co, si

    c["cexpA"], c["sexpA"] = exp_tabs(angA)
    c["cexpL"], c["sexpL"] = exp_tabs(angL)
    slopes = (2.0 ** (-8.0 * np.arange(1, 5) / 4)).astype(np.float64)
    p = np.arange(128)[:, None].astype(np.float64)
    j = np.arange(512)[None, :].astype(np.float64)
    m = np.arange(896)[None, :].astype(np.float64)
    ab = np.zeros((4, 128, 512)); bl = np.zeros((4, 128, 512)); st = np.zeros((4, 128, 896))
    for h in range(4):
        ab[h] = np.exp(-slopes[h] * (j - p + 127))
        bl[h] = np.exp(-slopes[h] * (p - j + 511))
        st[h] = np.exp(-slopes[h] * np.abs(m - 384 - p))
    c["al_above"] = ab.astype(np.float32).astype(ml_dtypes.bfloat16)
    c["al_below"] = bl.astype(np.float32).astype(ml_dtypes.bfloat16)
    c["al_strip"] = st.astype(np.float32).astype(ml_dtypes.bfloat16)
    bc = np.zeros((128, 4, 64), dtype=np.float32)
    for h in range(4):
        for mm_ in range(1, 32):
            bc[:, h, mm_] = -slopes[h] * (128 * mm_ - 127)
        for mm_ in range(4, 32):
            bc[:, h, 32 + mm_] = -slopes[h] * (128 * mm_ - 511)
    c["al_bias"] = bc.reshape(128, 256)
    return c


CONST_SPECS = lambda S: {
    "ident": ([128, 128], BF16), "ones_b": ([128, 128], BF16), "ones_f": ([128, 64], F32), "esel": ([32, 96], BF16), "sel64": ([128, 64], BF16),
    "cexpA": ([S, 64], F32), "sexpA": ([S, 64], F32), "cexpL": ([S, 32], F32), "sexpL": ([S, 32], F32),
    "al_above": ([4, 128, 512], BF16), "al_below": ([4, 128, 512], BF16), "al_strip": ([4, 128, 896], BF16),
    "al_bias": ([128, 256], F32),
}


def build(NSEQ, S, stop_after=None):
    NK = S // 128
    NT = S // 512
    nc = bass.Bass("TRN2", target_bir_lowering=False)

    def din(name, shape, dt=F32):
        return nc.dram_tensor(name, list(shape), dt, kind="ExternalInput").ap()

    def dscr(name, shape, dt=BF16):
        return nc.dram_tensor(name, list(shape), dt).ap()

    x_in = din("x", [NSEQ, S, D])
    mem_in = din("mem", [NSEQ, NMEM, D])
    W = {n: din(n, WSHAPES[n]) for n in WNAMES}
    C = {n: din(n, sh, dt) for n, (sh, dt) in CONST_SPECS(S).items()}
    y_out = nc.dram_tensor("y", [NSEQ, S, D], F32, kind="ExternalOutput").ap()

    Win0_s = dscr("Win0_s", [128, 8, 2304])
    Wout0_s = dscr("Wout0_s", [128, 8, 1024])
    Win1_s = dscr("Win1_s", [128, 8, 672])
    Wuq_s = dscr("Wuq_s", [128, 3, 1536])
    Wkp_s = dscr("Wkp_s", [128, 2, 16, 96])
    Wv_s = dscr("Wv_s", [128, 2, 1024])
    Wout1_s = dscr("Wout1_s", [128, 8, 1024])
    Wcq_s = [dscr(f"Wcq_s{l}", [128, 8, 1024]) for l in range(2)]
    Wckv_s = [dscr(f"Wckv_s{l}", [128, 8, 2048]) for l in range(2)]
    Wco_s = [dscr(f"Wco_s{l}", [128, 8, 1024]) for l in range(2)]
    Wgu_s = [dscr(f"Wgu_s{l}", [128, 8, 2 * DFF]) for l in range(2)]
    Wdn_s = [dscr(f"Wdn_s{l}", [128, 22, 1024]) for l in range(2)]
    QTa = dscr("QTa", [8, 64, S]); QTb = dscr("QTb", [4, 128, S]); QT1 = dscr("QT1", [16, 96, S])
    KTa = dscr("KTa", [128, S]); KTb = dscr("KTb", [4, 128, S]); KT1 = dscr("KT1", [16, 96, S])
    Va = dscr("Va", [S, 2, 65]); Vb = dscr("Vb", [S, 4, 128]); V1 = dscr("V1", [S, 16, 65])
    OTa = dscr("OTa", [8, 64, S]); OTb = dscr("OTb", [4, 128, S]); OT1 = dscr("OT1", [16, 64, S])
    X1 = dscr("X1", [S, D], F32)

    st = ExitStack()
    with st:
        abt = st.enter_context(nc.sbuf_tensor("arena_b", [128, AB_N], BF16))
        aft = st.enter_context(nc.sbuf_tensor("arena_f", [128, AF_N], F32))
        cft = st.enter_context(nc.sbuf_tensor("const_f", [128, 1792], F32))
        cbt = st.enter_context(nc.sbuf_tensor("const_b", [128, 128 + 128 + 96 + 64], BF16))
        psf = [st.enter_context(nc.psum_tensor(f"psf{i}", [128, 512], F32)) for i in range(8)]
        psT = [psf[6][:].bitcast(BF16), psf[7][:].bitcast(BF16)]
        AB = Arena(abt, AB_N)
        AFa = Arena(aft, AF_N)
        P = Plan()

        ident = cbt[:, 0:128]
        ones_b = cbt[:, 128:256]
        esel = cbt[0:32, 256:352]
        sel64 = cbt[:, 352:416]
        ones_f = cft[:, 0:64]
        gcols = cft[:, 64:64 + 80]
        GC = {"mix0": 0, "mix1": 8, "cross0": 16, "cross1": 24, "mem0": 32, "mem1": 40, "ffn0": 48, "ffn1": 56,
              "qn": 64, "kvn": 67, "one": 69}
        gq_b = cft[:, 160:224]; gk_b = cft[:, 224:288]; gqs_b = cft[:, 288:352]; gks_b = cft[:, 352:416]
        lam_t = cft[:, 416:420]
        gsub_c = cft[:, 420:421]
        lamw = cft[:, 424:424 + 4 * 64]
        fin_b = cft[:, 768:1792]
        b_const = P.buf("const")

        def ld(dst, src, slow=False):
            P.op("sp", DMA(dst, src, slow), writes=[b_const], dma=True)

        ld(ident, C["ident"][:, :]); ld(ones_b, C["ones_b"][:, :]); ld(esel, C["esel"][:, :]); ld(ones_f, C["ones_f"][:, :]); ld(sel64, C["sel64"][:, :])
        for nm, src, kc in [("mix0", W["norm_mix"][0], 8), ("mix1", W["norm_mix"][1], 8),
                            ("cross0", W["norm_cross"][0], 8), ("cross1", W["norm_cross"][1], 8),
                            ("mem0", W["norm_mem"][0], 8), ("mem1", W["norm_mem"][1], 8),
                            ("ffn0", W["norm_ffn"][0], 8), ("ffn1", W["norm_ffn"][1], 8),
                            ("qn", W["o_q_norm"][0], 3), ("kvn", W["o_kv_norm"][0], 2)]:
            ld(gcols[:, GC[nm]:GC[nm] + kc], src.rearrange("(kc p) -> p kc", p=128), slow=True)
        P.op("dve", MSET(gcols[:, GC["one"]:GC["one"] + 1], 1.0), writes=[b_const])
        ld(gq_b, W["e_q_norm"][0].partition_broadcast(128)); ld(gk_b, W["e_k_norm"][0].partition_broadcast(128))
        ld(fin_b, W["final_norm"].partition_broadcast(128))
        for i, nm in enumerate(["e_lam_q1", "e_lam_k1", "e_lam_q2", "e_lam_k2"]):
            ld(lamw[:, i * 64:(i + 1) * 64], W[nm][0].partition_broadcast(128))
        ld(gsub_c, W["e_subln"][0].rearrange("(p o) -> p o", o=1), slow=True)
        for g, gs in [(gq_b, gqs_b), (gk_b, gks_b)]:
            gv = g.rearrange("p (i two) -> p i two", two=2); sv = gs.rearrange("p (i two) -> p i two", two=2)
            P.op("dve", CP(sv[:, :, 0:1], gv[:, :, 1:2]), reads=[b_const], writes=[b_const])
            P.op("dve", CP(sv[:, :, 1:2], gv[:, :, 0:1]), reads=[b_const], writes=[b_const])
        lam_init0 = 0.8 - 0.6 * math.exp(-0.3 * 0)
        tmpl = cft[:, 440 + 256:440 + 256 + 64]
        P.op("dve", TT(tmpl, lamw[:, 0:64], lamw[:, 64:128], ALU.mult), reads=[b_const], writes=[b_const])
        P.op("dve", RSUM(lam_t[:, 0:1], tmpl), reads=[b_const], writes=[b_const])
        P.op("dve", TT(tmpl, lamw[:, 128:192], lamw[:, 192:256], ALU.mult), reads=[b_const], writes=[b_const])
        P.op("dve", RSUM(lam_t[:, 1:2], tmpl), reads=[b_const], writes=[b_const])
        P.op("act", ACT(lam_t[:, 0:2], lam_t[:, 0:2], AF.Exp), reads=[b_const], writes=[b_const])
        P.op("dve", TT(lam_t[:, 2:3], lam_t[:, 0:1], lam_t[:, 1:2], ALU.subtract), reads=[b_const], writes=[b_const])
        P.op("dve", TS(lam_t[:, 3:4], lam_t[:, 2:3], lam_init0, -1.0, ALU.add, ALU.mult), reads=[b_const], writes=[b_const])
        nlam = lam_t[:, 3:4]
        P.op("dve", TS(gsub_c, gsub_c, 1.0 - lam_init0), reads=[b_const], writes=[b_const])

        AB.reset(); AFa.reset()
        NST = 3
        stf = [AFa.alloc([128, 2048]) for _ in range(NST)]
        stb = [AB.alloc([128, 2048]) for _ in range(NST)]
        zt = AB.alloc([128, 16, 32])
        b_stf = [P.buf() for _ in range(NST)]; b_stb = [P.buf() for _ in range(NST)]
        b_z = P.buf()
        P.op("dve", MSET(zt, 0.0), writes=[b_z])
        cstate = {"i": 0}

        def conv_piece(src2d, rows, cols, gain_col, outs):
            i = cstate["i"]; cstate["i"] += 1
            s = i % NST
            r0, r1 = rows; c0, c1 = cols
            np_, w = r1 - r0, c1 - c0
            P.op("sp", DMA(stf[s][0:np_, 0:w], src2d[r0:r1, c0:c1]), writes=[b_stf[s]], dma=True)
            if i % 2 == 0:
                P.op("dve", TS(stb[s][0:np_, 0:w], stf[s][0:np_, 0:w], gain_col[0:np_, :]), reads=[b_stf[s], b_const], writes=[b_stb[s]])
            else:
                P.op("act", ACT(stb[s][0:np_, 0:w], stf[s][0:np_, 0:w], AF.Copy, scale=gain_col[0:np_, :]), reads=[b_stf[s], b_const], writes=[b_stb[s]])
            for dst, vf in outs:
                P.op("pool", DMA(dst, vf(stb[s][0:np_, 0:w])), reads=[b_stb[s]], dma=True)

        def gc(nm, kc):
            return gcols[:, GC[nm] + kc:GC[nm] + kc + 1]

        one_c = gcols[:, GC["one"]:GC["one"] + 1]
        idv = lambda v: v

        def conv(src2d, KC, rows_p, col_ranges, dst, gname):
            for kc in range(KC):
                g = gc(gname, kc) if gname else one_c
                for (c0, c1, d0) in col_ranges:
                    for cc in range(c0, c1, 2048):
                        ce = min(cc + 2048, c1)
                        conv_piece(src2d, (kc * rows_p, (kc + 1) * rows_p), (cc, ce), g,
                                   [(dst[0:rows_p, kc, d0 + cc - c0:d0 + ce - c0], idv)])

        conv(W["e_w_in"][0], 8, 128, [(0, 512, 0), (768, 1280, 512), (512, 768, 1024), (1280, 2304, 1280)], Win0_s, "mix0")
        conv(W["e_w_out"][0], 8, 128, [(0, 1024, 0)], Wout0_s, None)
        conv(W["o_w_in"][0], 8, 128, [(0, 672, 0)], Win1_s, "mix1")
        conv(W["o_w_uq"][0], 3, 128, [(0, 1536, 0)], Wuq_s, "qn")
        for kc in range(2):
            conv_piece(W["o_w_ukv"][0], (kc * 128, (kc + 1) * 128), (0, 2048), gc("kvn", kc), [
                (Wkp_s[:, kc, :, 0:64], lambda v: v.rearrange("p (h e) -> p h e", e=128)[:, :, 0:64]),
                (Wv_s[:, kc, :].rearrange("p (h e) -> p h e", e=64), lambda v: v.rearrange("p (h e) -> p h e", e=128)[:, :, 64:128]),
            ])
            P.op("pool", DMA(Wkp_s[:, kc, :, 64:96], zt), reads=[b_z], dma=True)
        conv(W["o_w_out"][0], 8, 128, [(0, 1024, 0)], Wout1_s, None)
        for l in range(2):
            conv(W["w_cq"][l], 8, 128, [(0, 1024, 0)], Wcq_s[l], f"cross{l}")
            conv(W["w_ckv"][l], 8, 128, [(0, 2048, 0)], Wckv_s[l], f"mem{l}")
            conv(W["w_co"][l], 8, 128, [(0, 1024, 0)], Wco_s[l], None)
            conv(W["w_gu"][l], 8, 128, [(0, 2 * DFF, 0)], Wgu_s[l], f"ffn{l}")
            conv(W["w_down"][l], 22, 128, [(0, 1024, 0)], Wdn_s[l], None)
        P.barrier()

        bank_b = [P.buf(f"psf{i}", excl=True) for i in range(8)]
        ring = {"i": 0}

        def nextbank():
            i = ring["i"] % 6
            ring["i"] += 1
            return psf[i][:], bank_b[i]

        sring = {"i": 0}
        aring = {"i": 0}

        sbanks = {"l": [0, 1, 2]}

        def nextS():
            i = sbanks["l"][sring["i"] % len(sbanks["l"])]
            sring["i"] += 1
            return psf[i][:], bank_b[i]

        abanks = {"l": [3, 4, 5]}

        def nextA():
            i = abanks["l"][aring["i"] % len(abanks["l"])]
            aring["i"] += 1
            return psf[i][:], bank_b[i]

        tring = {"i": 0}

        def nextT():
            i = tring["i"] % 2
            tring["i"] += 1
            return psT[i][:, 0:512], bank_b[6 + i]

        def rms_rstd(ss, rs, n, b_ss, b_rs):
            P.op("act", ACT(rs, ss, AF.Ln, bias=EPS, scale=1.0 / n), reads=[b_ss], writes=[b_rs])
            P.op("act", ACT(rs, rs, AF.Exp, scale=-0.5), reads=[b_rs], writes=[b_rs])

        def rope(out, v, ta, tb, tmp, nh, nd, b_v, b_tab, b_tmp, b_out, eng="dve"):
            t1, t2 = tmp
            tab = ta.unsqueeze(1).to_broadcast([128, nh, nd])
            P.op(eng, TT(t1, v, tab, ALU.mult), reads=[b_v, b_tab], writes=[b_tmp])
            vv = v.rearrange("p h (i two) -> p h i two", two=2)
            t2v = t2.rearrange("p h (i two) -> p h i two", two=2)
            tbv = tb.rearrange("p (i two) -> p i two", two=2).unsqueeze(1).to_broadcast([128, nh, nd // 2, 2])
            P.op(eng, TT(t2v[:, :, :, 0:1], vv[:, :, :, 1:2], tbv[:, :, :, 0:1], ALU.mult), reads=[b_v, b_tab], writes=[b_tmp])
            P.op(eng, TT(t2v[:, :, :, 1:2], vv[:, :, :, 0:1], tbv[:, :, :, 1:2], ALU.mult), reads=[b_v, b_tab], writes=[b_tmp])
            P.op(eng, TT(out, t1, t2, ALU.add), reads=[b_tmp], writes=[b_out])

        def norm_to_hT(xt, b_xt, junk, b_junk, ss, rs, b_s, hb, b_hb, hT, b_hT, col0):
            P.op("act", ACT(hb, xt, AF.Square, accum=ss), reads=[b_xt], writes=[b_hb, b_s])
            rms_rstd(ss, rs, float(D), b_s, b_s)
            P.op("dve", TS(hb, xt, rs), reads=[b_xt, b_s], writes=[b_hb])
            for half in range(2):
                pt, b_pt = nextT()
                for k in range(4):
                    kc = half * 4 + k
                    P.op("pe", TR(pt[:, k * 128:(k + 1) * 128], hb[:, kc * 128:(kc + 1) * 128], ident), reads=[b_hb, b_const], writes=[b_pt])
                dst = hT[:, half * 4:half * 4 + 4, col0:col0 + 128]
                src = pt.rearrange("p (k t) -> p k t", t=128)
                if half == 0:
                    P.op("act", ACP(dst, src), reads=[b_pt], writes=[b_hT])
                else:
                    P.op("dve", CP(dst, src), reads=[b_pt], writes=[b_hT])

        def pass_A(layer, xsrc):
            AB.reset(); AFa.reset()
            ncols = 2304 if layer == 0 else 672
            Win = AB.alloc([128, 8, ncols]); b_Win = P.buf()
            Wsrc = Win0_s if layer == 0 else Win1_s
            for kc in range(8):
                P.op("sp", DMA(Win[:, kc, :], Wsrc[:, kc, :]), writes=[b_Win], dma=True)
            if layer == 1:
                Wuq = AB.alloc([128, 3, 1536]); Wkp = AB.alloc([128, 2, 16, 96]); Wv = AB.alloc([128, 2, 1024])
                for kc in range(3):
                    P.op("sp", DMA(Wuq[:, kc, :], Wuq_s[:, kc, :]), writes=[b_Win], dma=True)
                for kc in range(2):
                    P.op("sp", DMA(Wkp[:, kc, :, :], Wkp_s[:, kc, :, :]), writes=[b_Win], dma=True)
                    P.op("sp", DMA(Wv[:, kc, :], Wv_s[:, kc, :]), writes=[b_Win], dma=True)
            xt = [AFa.alloc([128, 1024]) for _ in range(2)]; b_xt = [P.buf() for _ in range(2)]
            junk = AFa.alloc([128, 1024]); b_junk = P.buf()
            ssr = [AFa.alloc([128, 4]) for _ in range(2)]; b_s = [P.buf() for _ in range(2)]
            hb = [AB.alloc([128, 1024]) for _ in range(2)]; b_hb = [P.buf() for _ in range(2)]
            hT = [AB.alloc([128, 8, 128]) for _ in range(2)]; b_hT = [P.buf() for _ in range(2)]
            tabs = [AFa.alloc([128, 128]) for _ in range(2)]; b_tabs = [P.buf() for _ in range(2)]
            tq = [AFa.alloc([128, 256]) for _ in range(2)]; b_tq = [P.buf() for _ in range(2)]
            t1 = AFa.alloc([128, 512]); t2 = AFa.alloc([128, 512]); b_t = P.buf()
            qn = AFa.alloc([128, 512]); b_qn = P.buf()
            sq8 = AFa.alloc([128, 16]); b_sq8 = P.buf()
            if layer == 0:
                nstT = 13
            else:
                nstT = 16
            if layer == 0:
                qf2 = [AB.alloc([128, 1024]) for _ in range(2)]; b_qf2 = [P.buf() for _ in range(2)]
                kf2 = [AB.alloc([128, 640]) for _ in range(2)]; b_kf2 = [P.buf() for _ in range(2)]
                stT = [AB.alloc([128, 13, 512]) for _ in range(2)]; b_stT = [P.buf() for _ in range(2)]
                vast = [AB.alloc([128, 2, 65]) for _ in range(2)]; b_va = [P.buf() for _ in range(2)]
                vbst = [AB.alloc([128, 512]) for _ in range(2)]; b_vb = [P.buf() for _ in range(2)]
                for v_ in vast:
                    P.op("dve", MSET(v_[:, :, 64:65], 1.0), writes=[b_va[0], b_va[1]])
            else:
                qf2 = [AB.alloc([128, 16, 96]) for _ in range(2)]; b_qf2 = [P.buf() for _ in range(2)]
                cn = AB.alloc([128, 640]); b_cn = P.buf()
                krf = AB.alloc([128, 32]); b_krf = P.buf()
                cT = AB.alloc([128, 5, 128]); b_cT = P.buf()
                krT = AB.alloc([32, 128]); b_krT = P.buf()
                stQ = [AB.alloc([96, 16, 512]) for _ in range(2)]; b_stQ = [P.buf() for _ in range(2)]
                stK = [AB.alloc([96, 16, 512]) for _ in range(2)]; b_stK = [P.buf() for _ in range(2)]
                v1st = [AB.alloc([128, 16, 65]) for _ in range(2)]; b_v1 = [P.buf() for _ in range(2)]
                for v_ in v1st:
                    P.op("dve", MSET(v_[:, :, 64:65], 1.0), writes=[b_v1[0], b_v1[1]])
            sc0 = 0.125
            sc1 = 96.0 ** -0.5
            tail = {"f": None}
            for i in range(NK):
                pa = i % 2
                qf = qf2[pa]; b_qf = b_qf2[pa]
                if layer == 0:
                    kf = kf2[pa]; b_kf = b_kf2[pa]
                t0 = i * 128
                blk = i // 4
                sb = blk % 2
                c4 = (i % 4) * 128
                P.op("sp", DMA(xt[pa], xsrc[t0:t0 + 128, :]), writes=[b_xt[pa]], dma=True)
                if layer == 0:
                    P.op("sp", DMA(tabs[pa][:, 0:64], C["cexpA"][t0:t0 + 128, :]), writes=[b_tabs[pa]], dma=True)
                    P.op("sp", DMA(tabs[pa][:, 64:128], C["sexpA"][t0:t0 + 128, :]), writes=[b_tabs[pa]], dma=True)
                else:
                    P.op("sp", DMA(tabs[pa][:, 0:32], C["cexpL"][t0:t0 + 128, :]), writes=[b_tabs[pa]], dma=True)
                    P.op("sp", DMA(tabs[pa][:, 64:96], C["sexpL"][t0:t0 + 128, :]), writes=[b_tabs[pa]], dma=True)
                norm_to_hT(xt[pa], b_xt[pa], junk, b_junk, ssr[pa][:, 0:1], ssr[pa][:, 1:2], b_s[pa], hb[pa], b_hb[pa], hT[pa], b_hT[pa], 0)
                if layer == 0:
                    P.op("dve", STT(tq[pa][:, 0:64], tabs[pa][:, 0:64], sc0, gq_b, ALU.mult, ALU.mult), reads=[b_tabs[pa], b_const], writes=[b_tq[pa]])
                    P.op("dve", STT(tq[pa][:, 64:128], tabs[pa][:, 64:128], sc0, gqs_b, ALU.mult, ALU.mult), reads=[b_tabs[pa], b_const], writes=[b_tq[pa]])
                    P.op("dve", TT(tq[pa][:, 128:192], tabs[pa][:, 0:64], gk_b, ALU.mult), reads=[b_tabs[pa], b_const], writes=[b_tq[pa]])
                    P.op("dve", TT(tq[pa][:, 192:256], tabs[pa][:, 64:128], gks_b, ALU.mult), reads=[b_tabs[pa], b_const], writes=[b_tq[pa]])
                    groups = [(0, 512), (512, 1024), (1024, 1280), (1280, 1792), (1792, 2304)]
                    pg = []
                    for (c0, c1) in groups:
                        bk, b_bk = nextbank()
                        for kc in range(8):
                            P.op("pe", MM(bk[:, 0:c1 - c0], hT[pa][:, kc, :], Win[:, kc, c0:c1], kc == 0, kc == 7), reads=[b_hT[pa], b_Win], writes=[b_bk])
                        pg.append((bk, b_bk))
                    (p0, b0), (p1, b1), (p2, b2), (p3, b3), (p4, b4) = pg
                    P.op("act", ACT(t1, p0, AF.Square), reads=[b0], writes=[b_t])
                    P.op("dve", RSUM(sq8[:, 0:8], t1.rearrange("p (h d) -> p h d", d=64)), reads=[b_t], writes=[b_sq8])
                    rms_rstd(sq8[:, 0:8], sq8[:, 0:8], 64.0, b_sq8, b_sq8)
                    qn3 = qn.rearrange("p (h d) -> p h d", d=64)
                    P.op("dve", TT(qn3, p0.rearrange("p (h d) -> p h d", d=64), sq8[:, 0:8].unsqueeze(2).to_broadcast([128, 8, 64]), ALU.mult), reads=[b0, b_sq8], writes=[b_qn])
                    rope(qf[:, 0:512].rearrange("p (h d) -> p h d", d=64), qn3, tq[pa][:, 0:64], tq[pa][:, 64:128],
                         (t1.rearrange("p (h d) -> p h d", d=64), t2.rearrange("p (h d) -> p h d", d=64)), 8, 64, b_qn, b_tq[pa], b_t, b_qf)
                    P.op("act", ACT(qf[:, 512:1024], p1, AF.Copy, scale=sc0), reads=[b1], writes=[b_qf])
                    P.op("act", ACT(t1[:, 0:128], p2[:, 0:128], AF.Square), reads=[b2], writes=[b_t])
                    P.op("dve", RSUM(sq8[:, 8:10], t1[:, 0:128].rearrange("p (h d) -> p h d", d=64)), reads=[b_t], writes=[b_sq8])
                    rms_rstd(sq8[:, 8:10], sq8[:, 8:10], 64.0, b_sq8, b_sq8)
                    kn3 = qn[:, 0:128].rearrange("p (h d) -> p h d", d=64)
                    P.op("dve", TT(kn3, p2[:, 0:128].rearrange("p (h d) -> p h d", d=64), sq8[:, 8:10].unsqueeze(2).to_broadcast([128, 2, 64]), ALU.mult), reads=[b2, b_sq8], writes=[b_qn])
                    rope(kf[:, 0:128].rearrange("p (h d) -> p h d", d=64), kn3, tq[pa][:, 128:192], tq[pa][:, 192:256],
                         (t1[:, 0:128].rearrange("p (h d) -> p h d", d=64), t2[:, 0:128].rearrange("p (h d) -> p h d", d=64)), 2, 64, b_qn, b_tq[pa], b_t, b_kf)
                    P.op("act", ACP(vast[pa][:, :, 0:64], p2[:, 128:256].rearrange("p (h d) -> p h d", d=64)), reads=[b2], writes=[b_va[pa]])
                    P.op("pool", DMA(Va[t0:t0 + 128, :, :], vast[pa]), reads=[b_va[pa]], dma=True)
                    P.op("act", ACP(kf[:, 128:640], p3), reads=[b3], writes=[b_kf])
                    P.op("dve", CP(vbst[pa], p4), reads=[b4], writes=[b_vb[pa]])
                    P.op("pool", DMA(Vb[t0:t0 + 128, :, :].rearrange("t h e -> t (h e)"), vbst[pa]), reads=[b_vb[pa]], dma=True)
                    def tail0(i=i, sb=sb, c4=c4, blk=blk, qf=qf, b_qf=b_qf, kf=kf, b_kf=b_kf):
                        srcs = [(qf, b_qf, c) for c in range(8)] + [(kf, b_kf, c) for c in range(5)]
                        for j0 in range(0, 13, 4):
                            pt, b_pt = nextT()
                            n = min(4, 13 - j0)
                            for k in range(n):
                                sa, sbuf_, c = srcs[j0 + k]
                                P.op("pe", TR(pt[:, k * 128:(k + 1) * 128], sa[:, c * 128:(c + 1) * 128], ident), reads=[sbuf_, b_const], writes=[b_pt])
                            dst = stT[sb][:, j0:j0 + n, c4:c4 + 128]
                            src = pt[:, 0:n * 128].rearrange("p (k t) -> p k t", t=128)
                            P.op("act" if (j0 // 4) % 2 == 0 else "dve", (ACP if (j0 // 4) % 2 == 0 else CP)(dst, src), reads=[b_pt], writes=[b_stT[sb]])
                        if i % 4 == 3:
                            tb0 = blk * 512
                            s_ = stT[sb]
                            for c in range(4):
                                P.op("pool", DMA(QTa[2 * c, :, tb0:tb0 + 512], s_[0:64, c, :]), reads=[b_stT[sb]], dma=True)
                                P.op("pool", DMA(QTa[2 * c + 1, :, tb0:tb0 + 512], s_[64:128, c, :]), reads=[b_stT[sb]], dma=True)
                                P.op("pool", DMA(QTb[c, :, tb0:tb0 + 512], s_[:, 4 + c, :]), reads=[b_stT[sb]], dma=True)
                                P.op("pool", DMA(KTb[c, :, tb0:tb0 + 512], s_[:, 9 + c, :]), reads=[b_stT[sb]], dma=True)
                            P.op("pool", DMA(KTa[:, tb0:tb0 + 512], s_[:, 8, :]), reads=[b_stT[sb]], dma=True)
                    if tail["f"] is not None:
                        tail["f"]()
                    tail["f"] = tail0
                else:
                    p0, b0 = nextbank(); p1, b1 = nextbank()
                    for kc in range(8):
                        P.op("pe", MM(p0[:, 0:384], hT[pa][:, kc, :], Win[:, kc, 0:384], kc == 0, kc == 7), reads=[b_hT[pa], b_Win], writes=[b0])
                    for kc in range(8):
                        P.op("pe", MM(p1[:, 0:288], hT[pa][:, kc, :], Win[:, kc, 384:672], kc == 0, kc == 7), reads=[b_hT[pa], b_Win], writes=[b1])
                    P.op("act", ACT(t1[:, 0:384], p0[:, 0:384], AF.Square, accum=sq8[:, 0:1]), reads=[b0], writes=[b_t, b_sq8])
                    P.op("act", ACT(t1[:, 0:256], p1[:, 0:256], AF.Square, accum=sq8[:, 1:2]), reads=[b1], writes=[b_t, b_sq8])
                    rms_rstd(sq8[:, 0:1], sq8[:, 2:3], 384.0, b_sq8, b_sq8)
                    rms_rstd(sq8[:, 1:2], sq8[:, 3:4], 256.0, b_sq8, b_sq8)
                    P.op("dve", TS(cn[:, 0:384], p0[:, 0:384], sq8[:, 2:3]), reads=[b0, b_sq8], writes=[b_cn])
                    P.op("dve", TS(cn[:, 384:640], p1[:, 0:256], sq8[:, 3:4]), reads=[b1, b_sq8], writes=[b_cn])
                    P.op("act", ACP(qn[:, 0:32], p1[:, 256:288]), reads=[b1], writes=[b_qn])
                    rope(krf.rearrange("p (h d) -> p h d", h=1), qn[:, 0:32].rearrange("p (h d) -> p h d", h=1), tabs[pa][:, 0:32], tabs[pa][:, 64:96],
                         (t1[:, 0:32].rearrange("p (h d) -> p h d", h=1), t2[:, 0:32].rearrange("p (h d) -> p h d", h=1)), 1, 32, b_qn, b_tabs[pa], b_t, b_krf)
                    for (j0, n) in [(0, 4), (4, 1)]:
                        pt, b_pt = nextT()
                        for k in range(n):
                            c = j0 + k
                            P.op("pe", TR(pt[:, k * 128:(k + 1) * 128], cn[:, c * 128:(c + 1) * 128], ident), reads=[b_cn, b_const], writes=[b_pt])
                        if j0 == 4:
                            P.op("pe", TR(pt[0:32, 128:256], krf, ident), reads=[b_krf, b_const], writes=[b_pt])
                            P.op("dve", CP(krT, pt[0:32, 128:256]), reads=[b_pt], writes=[b_krT])
                        P.op("act", ACP(cT[:, j0:j0 + n, :], pt[:, 0:n * 128].rearrange("p (k t) -> p k t", t=128)), reads=[b_pt], writes=[b_cT])
                    for (h0, h1) in [(0, 5), (5, 10), (10, 15), (15, 16)]:
                        bk, b_bk = nextbank()
                        nh = h1 - h0
                        for kc in range(3):
                            P.op("pe", MM(bk[:, 0:nh * 96], cT[:, kc, :], Wuq[:, kc, h0 * 96:h1 * 96], kc == 0, kc == 2), reads=[b_cT, b_Win], writes=[b_bk])
                        bv = bk[:, 0:nh * 96].rearrange("p (h d) -> p h d", d=96)
                        P.op("act", ACT(qf[:, h0:h1, 0:64], bv[:, :, 0:64], AF.Copy, scale=sc1), reads=[b_bk], writes=[b_qf])
                        P.op("act", ACT(qn[:, 0:nh * 32].rearrange("p (h d) -> p h d", d=32), bv[:, :, 64:96], AF.Copy, scale=sc1), reads=[b_bk], writes=[b_qn])
                        rope(qf[:, h0:h1, 64:96], qn[:, 0:nh * 32].rearrange("p (h d) -> p h d", d=32), tabs[pa][:, 0:32], tabs[pa][:, 64:96],
                             (t1[:, 0:nh * 32].rearrange("p (h d) -> p h d", d=32), t2[:, 0:nh * 32].rearrange("p (h d) -> p h d", d=32)), nh, 32, b_qn, b_tabs[pa], b_t, b_qf)
                    def tail1(i=i, sb=sb, c4=c4, blk=blk, qf=qf, b_qf=b_qf):
                        for j0 in range(0, 16, 4):
                            pt, b_pt = nextT()
                            for k in range(4):
                                P.op("pe", TR(pt[0:96, k * 128:(k + 1) * 128], qf[:, j0 + k, :], ident), reads=[b_qf, b_const], writes=[b_pt])
                            dst = stQ[sb][:, j0:j0 + 4, c4:c4 + 128]
                            src = pt[0:96, :].rearrange("p (k t) -> p k t", t=128)
                            P.op("act" if (j0 // 4) % 2 == 0 else "dve", (ACP if (j0 // 4) % 2 == 0 else CP)(dst, src), reads=[b_pt], writes=[b_stQ[sb]])
                        if i % 4 == 3:
                            tb0 = blk * 512
                            for h in range(16):
                                P.op("pool", DMA(QT1[h, :, tb0:tb0 + 512], stQ[sb][:, h, :]), reads=[b_stQ[sb]], dma=True)
                    for j0 in range(0, 16, 4):
                        bk, b_bk = nextbank()
                        for k in range(4):
                            h = j0 + k
                            o_ = bk[0:96, k * 128:(k + 1) * 128]
                            P.op("pe", MM(o_, esel, krT, True, False), reads=[b_krT, b_const], writes=[b_bk])
                            P.op("pe", MM(o_, Wkp[:, 0, h, :], cT[:, 3, :], False, False), reads=[b_cT, b_Win], writes=[b_bk])
                            P.op("pe", MM(o_, Wkp[:, 1, h, :], cT[:, 4, :], False, True), reads=[b_cT, b_Win], writes=[b_bk])
                        dst = stK[sb][:, j0:j0 + 4, c4:c4 + 128]
                        src = bk[0:96, :].rearrange("p (k t) -> p k t", t=128)
                        P.op("act" if (j0 // 4) % 2 == 1 else "dve", (ACP if (j0 // 4) % 2 == 1 else CP)(dst, src), reads=[b_bk], writes=[b_stK[sb]])
                    for hh in range(2):
                        bk, b_bk = nextbank()
                        for kc in range(2):
                            P.op("pe", MM(bk, cT[:, 3 + kc, :], Wv[:, kc, hh * 512:(hh + 1) * 512], kc == 0, kc == 1), reads=[b_cT, b_Win], writes=[b_bk])
                        P.op("dve" if hh == 0 else "act", (CP if hh == 0 else ACP)(v1st[pa][:, hh * 8:(hh + 1) * 8, 0:64], bk.rearrange("p (h d) -> p h d", d=64)), reads=[b_bk], writes=[b_v1[pa]])
                    P.op("pool", DMA(V1[t0:t0 + 128, :, :], v1st[pa]), reads=[b_v1[pa]], dma=True)
                    if i % 4 == 3:
                        tb0 = blk * 512
                        for h in range(16):
                            P.op("pool", DMA(KT1[h, :, tb0:tb0 + 512], stK[sb][:, h, :]), reads=[b_stK[sb]], dma=True)
                    if tail["f"] is not None:
                        tail["f"]()
                    tail["f"] = tail1
            if tail["f"] is not None:
                tail["f"]()
            P.barrier()

        LA = 3
        NP = 4

        def pass_B(layer):
            abanks["l"] = [4, 5, 6, 7]
            sbanks["l"] = [0, 1, 2, 3]
            ngroups = 1 if layer == 0 else 2
            for g in range(ngroups):
                AB.reset(); AFa.reset()
                b_kv = P.buf()
                if layer == 0:
                    KTa_sb = AB.alloc([128, S]); KTb_sb = AB.alloc([128, 4, S])
                    Va_sb = AB.alloc([128, NK, 2, 65]); Vb_sb = AB.alloc([128, NK, 4, 128])
                    for blk in range(NT):
                        sl = slice(blk * 512, (blk + 1) * 512)
                        P.op("sp", DMA(KTa_sb[:, sl], KTa[:, sl]), writes=[b_kv], dma=True)
                        for h in range(4):
                            P.op("sp", DMA(KTb_sb[:, h, sl], KTb[h, :, sl]), writes=[b_kv], dma=True)
                    for kt in range(NK):
                        P.op("sp", DMA(Va_sb[:, kt, :, :], Va[kt * 128:(kt + 1) * 128, :, :]), writes=[b_kv], dma=True)
                        P.op("sp", DMA(Vb_sb[:, kt, :, :], Vb[kt * 128:(kt + 1) * 128, :, :]), writes=[b_kv], dma=True)
                    al_ab = AB.alloc([128, 4, 512]); al_bl = AB.alloc([128, 4, 512]); al_st = AB.alloc([128, 4, 896])
                    al_bias = AFa.alloc([128, 256])
                    for h in range(4):
                        P.op("sp", DMA(al_ab[:, h, :], C["al_above"][h]), writes=[b_kv], dma=True)
                        P.op("sp", DMA(al_bl[:, h, :], C["al_below"][h]), writes=[b_kv], dma=True)
                        P.op("sp", DMA(al_st[:, h, :], C["al_strip"][h]), writes=[b_kv], dma=True)
                    P.op("sp", DMA(al_bias, C["al_bias"][:, :]), writes=[b_kv], dma=True)
                else:
                    KT_sb = AB.alloc([96, 8, S]); V_sb = AB.alloc([128, NK, 8, 65])
                    for blk in range(NT):
                        sl = slice(blk * 512, (blk + 1) * 512)
                        for h in range(8):
                            P.op("sp", DMA(KT_sb[:, h, sl], KT1[g * 8 + h, :, sl]), writes=[b_kv], dma=True)
                    for kt in range(NK):
                        P.op("sp", DMA(V_sb[:, kt, :, :], V1[kt * 128:(kt + 1) * 128, g * 8:(g + 1) * 8, :]), writes=[b_kv], dma=True)
                b_q = [P.buf() for _ in range(2)]; b_o = [P.buf() for _ in range(2)]
                b_qb1 = P.buf(); b_ob1 = P.buf()
                if layer == 0:
                    qa_t = [AB.alloc([128, 8, 512]) for _ in range(2)]
                    qb_t = AB.alloc([128, 4, 2, 512])
                    oa_st = AB.alloc([64, 8, 512]); ob_st = AB.alloc([128, 4, 512])
                    for pa_ in range(2):
                        P.op("dve", MSET(qa_t[pa_], 0.0), writes=[b_q[pa_]])
                    P.op("dve", MSET(qb_t, 0.0), writes=[b_qb1])
                else:
                    q1_t = [AB.alloc([96, 8, 512]) for _ in range(2)]
                    o1_st = [AB.alloc([64, 8, 512]) for _ in range(2)]
                Pt = [AB.alloc([128, 512]) for _ in range(NP)]; b_P = [P.buf() for _ in range(NP)]
                if layer == 0:
                    P2 = [AB.alloc([128, 512]) for _ in range(NP)]; b_P2 = [P.buf() for _ in range(NP)]
                    sqb = AB.alloc([128, 512]); b_sqb = P.buf()
                    tc_ = [AFa.alloc([128, 512]) for _ in range(2)]; b_tc = [P.buf() for _ in range(2)]
                    tt_ = AFa.alloc([128, 512]); b_tt = P.buf()
                rl = [AFa.alloc([128, 512]) for _ in range(2)]; b_rl = [P.buf() for _ in range(2)]
                rlb = [AB.alloc([128, 2, 512]) for _ in range(2)]; b_rlb = [P.buf() for _ in range(2)]
                for u_i in range(2):
                    P.op("dve", MSET(rlb[u_i], 0.0), writes=[b_rlb[u_i]])
                bcs = [AFa.alloc([128, 512]) for _ in range(2)]; b_bcs = [P.buf() for _ in range(2)]
                pctr = {"i": 0, "u": 0}
                pend = []

                def defer(n, fn, tag=None):
                    pend.append([n, fn, tag])

                def flush_for(bufs):
                    last = -1
                    for i_, it in enumerate(pend):
                        if it[2] is not None and any(it[2] is b_ for b_ in bufs):
                            last = i_
                    for _ in range(last + 1):
                        pend.pop(0)[1]()

                def tick():
                    for it in pend:
                        it[0] -= 1
                    while pend and pend[0][0] <= 0:
                        pend.pop(0)[1]()

                def load_qb(qt):
                    sl = slice(qt * 512, (qt + 1) * 512)
                    for h in range(4):
                        for c in range(2):
                            P.op("sp", DMA(qb_t[c * 64:c * 64 + 64, h, c, :], QTb[h, c * 64:c * 64 + 64, sl]), writes=[b_qb1], dma=True)

                def load_q(qt):
                    pa = qt % 2
                    sl = slice(qt * 512, (qt + 1) * 512)
                    if layer == 0:
                        for hq in range(8):
                            P.op("sp", DMA(qa_t[pa][(hq // 4) * 64:(hq // 4) * 64 + 64, hq, :], QTa[hq, :, sl]), writes=[b_q[pa]], dma=True)
                    else:
                        for h in range(8):
                            P.op("sp", DMA(q1_t[pa][:, h, :], QT1[g * 8 + h, :, sl]), writes=[b_q[pa]], dma=True)

                def store_oa(qt):
                    sl = slice(qt * 512, (qt + 1) * 512)
                    for hq in range(8):
                        P.op("pool", DMA(OTa[hq, :, sl], oa_st[:, hq, :]), reads=[b_o[0]], dma=True)

                def store_o(qt):
                    pa = qt % 2
                    sl = slice(qt * 512, (qt + 1) * 512)
                    if layer == 0:
                        for h in range(4):
                            P.op("pool", DMA(OTb[h, :, sl], ob_st[:, h, :]), reads=[b_ob1], dma=True)
                    else:
                        for h in range(8):
                            P.op("pool", DMA(OT1[g * 8 + h, :, sl], o1_st[pa][:, h, :]), reads=[b_o[pa]], dma=True)

                class AugUnit:
                    def __init__(self, Qap, Kfn, Vfn, o_dst, b_odst, b_qb):
                        self.Qap, self.Kfn, self.Vfn, self.o_dst, self.b_odst, self.b_qb = Qap, Kfn, Vfn, o_dst, b_odst, b_qb
                        self.kts = list(range(NK))

                    def start(self):
                        self.O, self.b_O = nextA()
                        flush_for([self.b_O])
                        self.u = pctr["u"] % 2; pctr["u"] += 1

                    def front(self, kt):
                        sbk, b_sbk = nextS()
                        P.op("pe", MM(sbk, self.Kfn(kt), self.Qap, True, True), reads=[b_kv, self.b_qb], writes=[b_sbk])
                        pi = pctr["i"] % NP; pctr["i"] += 1
                        P.op("act", ACT(Pt[pi], sbk, AF.Exp), reads=[b_sbk], writes=[b_P[pi]])
                        return Pt[pi], b_P[pi]

                    def back(self, kt, pinfo):
                        pt, b_pt = pinfo
                        P.op("pe", MM(self.O[0:65, :], self.Vfn(kt), pt, kt == self.kts[0], kt == self.kts[-1]), reads=[b_kv, b_pt], writes=[self.b_O])

                    def fin(self):
                        u = self.u; O = self.O; b_O = self.b_O
                        P.op("dve", RECIP(rl[u][64:65, :], O[64:65, :]), reads=[b_O], writes=[b_rl[u]])
                        P.op("dve", CP(rlb[u][64:65, 0, :], rl[u][64:65, :]), reads=[b_rl[u]], writes=[b_rlb[u]])
                        P.op("dve", TT(rlb[u][64:65, 1, :], rl[u][64:65, :], rlb[u][64:65, 0, :], ALU.subtract), reads=[b_rl[u], b_rlb[u]], writes=[b_rlb[u]])

                        def part2():
                            bcp, b_bcp = nextS()
                            P.op("pe", MM(bcp[0:64, :], sel64, rlb[u][:, 0, :], True, False), reads=[b_rlb[u], b_const], writes=[b_bcp])
                            P.op("pe", MM(bcp[0:64, :], sel64, rlb[u][:, 1, :], False, True), reads=[b_rlb[u], b_const], writes=[b_bcp])
                            P.op("act", ACP(bcs[u][0:64, :], bcp[0:64, :]), reads=[b_bcp], writes=[b_bcs[u]])
                            P.op("dve", TT(self.o_dst, O[0:64, :], bcs[u][0:64, :], ALU.mult), reads=[b_O, b_bcs[u]], writes=[self.b_odst])
                        defer(10, part2, b_O)

                def diff_kts(h, qt):
                    sl_ = 2.0 ** (-8.0 * (h + 1) / 4)
                    out = []
                    for kt in range(NK):
                        if kt < 4 * qt:
                            dmin = 512 * qt - 128 * kt - 127
                        elif kt >= 4 * qt + 4:
                            dmin = 128 * kt - 512 * qt - 511
                        else:
                            dmin = 0
                        if sl_ * dmin < 100.0:
                            out.append(kt)
                    return out

                class DiffUnit:
                    def __init__(self, h, c, qt, Qap, Kfn, Vfn, o_dst, b_odst, b_qb):
                        self.h, self.c, self.qt = h, c, qt
                        self.kts = diff_kts(h, qt)
                        self.Qap, self.Kfn, self.Vfn, self.o_dst, self.b_odst, self.b_qb = Qap, Kfn, Vfn, o_dst, b_odst, b_qb

                    def start(self):
                        self.O, self.b_O = nextA(); self.L, self.b_L = nextA()
                        flush_for([self.b_O, self.b_L])
                        self.u = pctr["u"] % 2; pctr["u"] += 1

                    def front(self, kt):
                        h, qt = self.h, self.qt
                        sbk, b_sbk = nextS()
                        P.op("pe", MM(sbk, self.Kfn(kt), self.Qap, True, True), reads=[b_kv, self.b_qb], writes=[b_sbk])
                        pi = pctr["i"] % NP; pctr["i"] += 1
                        if kt < 4 * qt:
                            m_ = (512 * qt - 128 * kt) // 128
                            bcol = al_bias[:, h * 64 + m_:h * 64 + m_ + 1]; tab = al_ab[:, h, :]
                        elif kt >= 4 * qt + 4:
                            m_ = (128 * kt - 512 * qt) // 128
                            bcol = al_bias[:, h * 64 + 32 + m_:h * 64 + 32 + m_ + 1]; tab = al_bl[:, h, :]
                        else:
                            dl = 512 * qt - 128 * kt
                            bcol = al_bias[:, h * 64:h * 64 + 1]; tab = al_st[:, h, 384 + dl:384 + dl + 512]
                        P.op("act", ACT(Pt[pi], sbk, AF.Exp, bias=bcol), reads=[b_sbk, b_kv], writes=[b_P[pi]])
                        P.op("dve", TT(P2[pi], Pt[pi], tab, ALU.mult), reads=[b_P[pi], b_kv], writes=[b_P2[pi]])
                        return P2[pi], b_P2[pi]

                    def back(self, kt, pinfo):
                        pt, b_pt = pinfo
                        P.op("pe", MM(self.O, self.Vfn(kt), pt, kt == self.kts[0], kt == self.kts[-1]), reads=[b_kv, b_pt], writes=[self.b_O])
                        P.op("pe", MM(self.L, ones_b, pt, kt == self.kts[0], kt == self.kts[-1]), reads=[b_const, b_pt], writes=[self.b_L])

                    def fin(self):
                        u = self.u; c = self.c
                        P.op("act", ACT(rl[u], self.L, AF.Ln), reads=[self.b_L], writes=[b_rl[u]])
                        P.op("act", ACT(rl[u], rl[u], AF.Exp, scale=-1.0), reads=[b_rl[u]], writes=[b_rl[u]])
                        P.op("dve", TT(tc_[c], self.O, rl[u], ALU.mult), reads=[self.b_O, b_rl[u]], writes=[b_tc[c]])
                        if c == 1:
                            P.op("dve", STT(tt_, tc_[1], nlam, tc_[0], ALU.mult, ALU.add), reads=[b_tc[0], b_tc[1], b_const], writes=[b_tt])
                            P.op("act", ACT(sqb, tt_, AF.Square), reads=[b_tt], writes=[b_sqb])

                            def part2():
                                ssb, b_ssb = nextS()
                                P.op("pe", MM(ssb, ones_b, sqb, True, True), reads=[b_const, b_sqb], writes=[b_ssb])
                                P.op("act", ACT(rl[u], ssb, AF.Ln, bias=EPS, scale=1.0 / 128), reads=[b_ssb], writes=[b_rl[u]])
                                P.op("act", ACT(rl[u], rl[u], AF.Exp, scale=-0.5), reads=[b_rl[u]], writes=[b_rl[u]])
                                P.op("dve", STT(self.o_dst, tt_, gsub_c, rl[u], ALU.mult, ALU.mult), reads=[b_tt, b_rl[u], b_const], writes=[self.b_odst])
                            defer(6, part2)

                stream = []
                for qt in range(NT):
                    pa = qt % 2
                    units = []
                    if layer == 0:
                        for hq in range(8):
                            kvh = hq // 4
                            units.append(AugUnit(qa_t[pa][:, hq, :],
                                                 lambda kt: KTa_sb[:, kt * 128:(kt + 1) * 128],
                                                 lambda kt, kvh=kvh: Va_sb[:, kt, kvh, :], oa_st[:, hq, :], b_o[0], b_q[pa]))
                        for h in range(4):
                            for c in range(2):
                                units.append(DiffUnit(h, c, qt, qb_t[:, h, c, :],
                                                      lambda kt, h=h: KTb_sb[:, h, kt * 128:(kt + 1) * 128],
                                                      lambda kt, h=h: Vb_sb[:, kt, h, :], ob_st[:, h, :], b_ob1, b_qb1))
                    else:
                        for h in range(8):
                            units.append(AugUnit(q1_t[pa][:, h, :], lambda kt, h=h: KT_sb[:, h, kt * 128:(kt + 1) * 128],
                                                 lambda kt, h=h: V_sb[:, kt, h, :], o1_st[pa][:, h, :], b_o[pa], b_q[pa]))
                    for ui, u_ in enumerate(units):
                        for kt in u_.kts:
                            stream.append((u_, kt, qt, ui == 0 and kt == u_.kts[0], ui == len(units) - 1 and kt == u_.kts[-1],
                                           layer == 0 and ui == 7 and kt == u_.kts[-1]))
                load_q(0)
                if layer == 0:
                    load_qb(0)
                inflight = []
                for idx in range(len(stream) + LA):
                    if idx < len(stream):
                        u_, kt, qt, first, lastq, _ = stream[idx]
                        if first and qt + 1 < NT:
                            load_q(qt + 1)
                        if kt == u_.kts[0]:
                            u_.start()
                        inflight.append((u_, kt, qt, u_.front(kt)))
                        if lastq and layer == 0 and qt + 1 < NT:
                            load_qb(qt + 1)
                    if idx >= LA:
                        u_, kt, qt, pinfo = inflight.pop(0)
                        u_.back(kt, pinfo)
                        if kt == u_.kts[-1]:
                            u_.fin()
                            if stream[idx - LA][4]:
                                defer(14, lambda qt=qt: store_o(qt))
                            if stream[idx - LA][5]:
                                defer(14, lambda qt=qt: store_oa(qt))
                    tick()
                while pend:
                    pend.pop(0)[1]()
                P.barrier()

        def pass_C(layer, seq, xsrc, xdst, last):
            abanks["l"] = [3, 4, 5]
            sbanks["l"] = [0, 1, 2]
            AB.reset(); AFa.reset()
            NW = 2
            b_w = [P.buf() for _ in range(NW)]
            wst = [AB.alloc([128, 8, 1024]) for _ in range(NW)]
            wctr = {"i": 0}
            Wdn = AB.alloc([128, 22, 1024]); b_Wdn = P.buf()
            Kmem = AB.alloc([128, 8, 256]); Vmem = AB.alloc([128, 2, 1024]); b_mem = P.buf()
            xts = [AFa.alloc([128, 4, 1024]) for _ in range(2)]; b_xts = [[P.buf() for _ in range(4)] for _ in range(2)]
            ssr = AFa.alloc([128, 8]); b_s = [P.buf() for _ in range(4)]
            ssn = AFa.alloc([128, 8]); b_sn = [P.buf() for _ in range(4)]
            rlc = AFa.alloc([128, 512]); b_rlc = P.buf()
            hb = [AB.alloc([128, 1024]) for _ in range(4)]; b_hb = [P.buf() for _ in range(4)]
            hT = AB.alloc([128, 8, 512]); b_hT = P.buf()
            qT = AB.alloc([128, 8, 512]); b_qT = P.buf()
            o2T = hT; b_o2T = b_hT
            ots = [AB.alloc([128, 8, 512]) for _ in range(2)]; b_ots = [P.buf() for _ in range(2)]
            actT = AB.alloc([128, 22, 512]); b_act = P.buf()
            Pc = [AB.alloc([128, 512]) for _ in range(2)]; b_Pc = [P.buf() for _ in range(2)]
            sgl = Pc[0]; b_sgl = b_Pc[0]

            def wload(view_fn_list):
                s_ = wctr["i"] % NW; wctr["i"] += 1
                for dfn, src in view_fn_list:
                    P.op("sp", DMA(dfn(wst[s_]), src), writes=[b_w[s_]], dma=True)
                return wst[s_], b_w[s_]

            def load_tile(tt):
                pa = tt % 2
                t0 = tt * 512
                sl = slice(t0, t0 + 512)
                for s_ in range(4):
                    P.op("sp", DMA(xts[pa][:, s_, :], xsrc[t0 + s_ * 128:t0 + (s_ + 1) * 128, :]), writes=[b_xts[pa][s_]], dma=True)
                if layer == 0:
                    for hq in range(8):
                        P.op("sp", DMA(ots[pa][(hq % 2) * 64:(hq % 2) * 64 + 64, hq // 2, :], OTa[hq, :, sl]), writes=[b_ots[pa]], dma=True)
                    for h in range(4):
                        P.op("sp", DMA(ots[pa][:, 4 + h, :], OTb[h, :, sl]), writes=[b_ots[pa]], dma=True)
                else:
                    for h in range(16):
                        P.op("sp", DMA(ots[pa][(h % 2) * 64:(h % 2) * 64 + 64, h // 2, :], OT1[h, :, sl]), writes=[b_ots[pa]], dma=True)

            load_tile(0)
            mt = xts[1][:, 0, :]; b_mt = b_xts[1][0]
            for mi in range(2):
                P.op("sp", DMA(mt, mem_in[seq, mi * 128:(mi + 1) * 128, :]), writes=[b_mt], dma=True)
                P.op("act", ACT(hb[0], mt, AF.Square, accum=ssr[:, 0:1]), reads=[b_mt], writes=[b_hb[0], b_s[0]])
                rms_rstd(ssr[:, 0:1], ssr[:, 1:2], float(D), b_s[0], b_s[0])
                P.op("dve", TS(hb[0], mt, ssr[:, 1:2]), reads=[b_mt, b_s[0]], writes=[b_hb[0]])
                for half in range(2):
                    pt, b_pt = nextT()
                    for k in range(4):
                        kc = half * 4 + k
                        P.op("pe", TR(pt[:, k * 128:(k + 1) * 128], hb[0][:, kc * 128:(kc + 1) * 128], ident), reads=[b_hb[0], b_const], writes=[b_pt])
                    P.op("act", ACP(hT[:, half * 4:half * 4 + 4, mi * 128:(mi + 1) * 128], pt.rearrange("p (k t) -> p k t", t=128)), reads=[b_pt], writes=[b_hT])
            for half in range(2):
                wv, b_wv = wload([(lambda a: a, Wckv_s[layer][:, :, half * 1024:(half + 1) * 1024])])
                if half == 0:
                    for m in range(8):
                        bk, b_bk = nextbank()
                        for kc in range(8):
                            P.op("pe", MM(bk[:, 0:256], wv[:, kc, m * 128:(m + 1) * 128], hT[:, kc, 0:256], kc == 0, kc == 7), reads=[b_wv, b_hT], writes=[b_bk])
                        P.op("act" if m % 2 else "dve", (ACP if m % 2 else CP)(Kmem[:, m, :], bk[:, 0:256]), reads=[b_bk], writes=[b_mem])
                else:
                    for mi in range(2):
                        for nh in range(2):
                            bk, b_bk = nextbank()
                            for kc in range(8):
                                P.op("pe", MM(bk, hT[:, kc, mi * 128:(mi + 1) * 128], wv[:, kc, nh * 512:(nh + 1) * 512], kc == 0, kc == 7), reads=[b_wv, b_hT], writes=[b_bk])
                            P.op("act" if nh else "dve", (ACP if nh else CP)(Vmem[:, mi, nh * 512:(nh + 1) * 512], bk), reads=[b_bk], writes=[b_mem])

            def preload_first():
                a_ = wload([(lambda a: a, (Wout0_s if layer == 0 else Wout1_s)[:, :, :])])
                b__ = wload([(lambda a: a, Wcq_s[layer][:, :, :])])
                return a_, b__

            pre = preload_first()
            for tt in range(NT):
                pa = tt % 2
                xt = xts[pa]; b_xt = b_xts[pa]
                ot = ots[pa]; b_ot = b_ots[pa]
                t0 = tt * 512
                (w_out, b_w_out), (w_cq, b_w_cq) = pre
                if tt + 1 < NT:
                    load_tile(tt + 1)

                def norm_stage1(s_):
                    ss_ = ssn[:, 2 * s_:2 * s_ + 1]; rs_ = ssn[:, 2 * s_ + 1:2 * s_ + 2]
                    P.op("act", ACT(hb[s_], xt[:, s_, :], AF.Square, accum=ss_), reads=[b_xt[s_]], writes=[b_hb[s_], b_sn[s_]])
                    rms_rstd(ss_, rs_, float(D), b_sn[s_], b_sn[s_])

                def norm_stage1b(s_):
                    rs_ = ssn[:, 2 * s_ + 1:2 * s_ + 2]
                    P.op("dve", TS(hb[s_], xt[:, s_, :], rs_), reads=[b_xt[s_], b_sn[s_]], writes=[b_hb[s_]])

                def norm_stage2():
                    for s_ in range(4):
                        for half in range(2):
                            pt, b_pt = nextT()
                            for k in range(4):
                                kc = half * 4 + k
                                P.op("pe", TR(pt[:, k * 128:(k + 1) * 128], hb[s_][:, kc * 128:(kc + 1) * 128], ident), reads=[b_hb[s_], b_const], writes=[b_pt])
                            dst = hT[:, half * 4:half * 4 + 4, s_ * 128:(s_ + 1) * 128]
                            src = pt.rearrange("p (k t) -> p k t", t=128)
                            if half == 0:
                                P.op("act", ACP(dst, src), reads=[b_pt], writes=[b_hT])
                            else:
                                P.op("dve", CP(dst, src), reads=[b_pt], writes=[b_hT])

                def add_proj(lhs_list, b_lhs, rhs_fn, b_rhs, then_norm=False):
                    for s_ in range(4):
                        for n in range(2):
                            bk, b_bk = nextbank()
                            nl = len(lhs_list)
                            for ci, lf in enumerate(lhs_list):
                                P.op("pe", MM(bk, lf(s_), rhs_fn(ci, n), ci == 0, ci == nl - 1), reads=[b_lhs, b_rhs], writes=[b_bk])
                            P.op("dve", TT(xt[:, s_, n * 512:(n + 1) * 512], xt[:, s_, n * 512:(n + 1) * 512], bk, ALU.add), reads=[b_bk, b_xt[s_]], writes=[b_xt[s_]])
                        if then_norm:
                            norm_stage1(s_)
                            if s_ > 0:
                                norm_stage1b(s_ - 1)
                    if then_norm:
                        norm_stage1b(3)
                        norm_stage2()

                wv, b_wv = w_out, b_w_out
                lhs = [(lambda s_, c=c: ot[:, c, s_ * 128:(s_ + 1) * 128]) for c in range(8)]
                add_proj(lhs, b_ot, lambda ci, n, wv=wv: wv[:, ci, n * 512:(n + 1) * 512], b_wv, then_norm=True)
                wv, b_wv = w_cq, b_w_cq
                for m in range(8):
                    bk, b_bk = nextbank()
                    for kc in range(8):
                        P.op("pe", MM(bk, wv[:, kc, m * 128:(m + 1) * 128], hT[:, kc, :], kc == 0, kc == 7), reads=[b_wv, b_hT], writes=[b_bk])
                    P.op("act", ACT(qT[:, m, :], bk, AF.Copy, scale=1.0 / 16), reads=[b_bk], writes=[b_qT])
                for h in range(4):
                    L, b_L = nextA(); O0, b_O0 = nextA(); O1, b_O1 = nextA()
                    for mi in range(2):
                        sbk, b_sbk = nextS()
                        for dc in range(2):
                            P.op("pe", MM(sbk, Kmem[:, 2 * h + dc, mi * 128:(mi + 1) * 128], qT[:, 2 * h + dc, :], dc == 0, dc == 1), reads=[b_mem, b_qT], writes=[b_sbk])
                        P.op("act", ACT(Pc[mi], sbk, AF.Exp), reads=[b_sbk], writes=[b_Pc[mi]])
                        P.op("pe", MM(L, ones_b, Pc[mi], mi == 0, mi == 1), reads=[b_const, b_Pc[mi]], writes=[b_L])
                        P.op("pe", MM(O0, Vmem[:, mi, h * 256:h * 256 + 128], Pc[mi], mi == 0, mi == 1), reads=[b_mem, b_Pc[mi]], writes=[b_O0])
                        P.op("pe", MM(O1, Vmem[:, mi, h * 256 + 128:h * 256 + 256], Pc[mi], mi == 0, mi == 1), reads=[b_mem, b_Pc[mi]], writes=[b_O1])
                    P.op("act", ACT(rlc, L, AF.Ln), reads=[b_L], writes=[b_rlc])
                    P.op("act", ACT(rlc, rlc, AF.Exp, scale=-1.0), reads=[b_rlc], writes=[b_rlc])
                    P.op("dve", TT(o2T[:, 2 * h, :], O0, rlc, ALU.mult), reads=[b_O0, b_rlc], writes=[b_o2T])
                    P.op("dve", TT(o2T[:, 2 * h + 1, :], O1, rlc, ALU.mult), reads=[b_O1, b_rlc], writes=[b_o2T])
                wv, b_wv = wload([(lambda a: a, Wco_s[layer][:, :, :])])
                lhs = [(lambda s_, kc=kc: o2T[:, kc, s_ * 128:(s_ + 1) * 128]) for kc in range(8)]
                add_proj(lhs, b_o2T, lambda ci, n, wv=wv: wv[:, ci, n * 512:(n + 1) * 512], b_wv, then_norm=True)
                for kc in range(22):
                    P.op("sp", DMA(Wdn[:, kc, :], Wdn_s[layer][:, kc, :]), writes=[b_Wdn], dma=True)
                for j0 in range(0, 22, 4):
                    nj = min(4, 22 - j0)
                    wv, b_wv = wload([(lambda a, nj=nj: a[:, :, 0:nj * 128], Wgu_s[layer][:, :, j0 * 128:(j0 + nj) * 128]),
                                      (lambda a, nj=nj: a[:, :, 512:512 + nj * 128], Wgu_s[layer][:, :, DFF + j0 * 128:DFF + (j0 + nj) * 128])])
                    for jj in range(nj):
                        j = j0 + jj
                        gk, b_gk = nextbank(); uk, b_uk = nextbank()
                        for kc in range(8):
                            P.op("pe", MM(gk, wv[:, kc, jj * 128:(jj + 1) * 128], hT[:, kc, :], kc == 0, kc == 7), reads=[b_wv, b_hT], writes=[b_gk])
                        for kc in range(8):
                            P.op("pe", MM(uk, wv[:, kc, 512 + jj * 128:512 + (jj + 1) * 128], hT[:, kc, :], kc == 0, kc == 7), reads=[b_wv, b_hT], writes=[b_uk])
                        P.op("act", ACT(sgl, gk, AF.Silu), reads=[b_gk], writes=[b_sgl])
                        P.op("dve", TT(actT[:, j, :], sgl, uk, ALU.mult), reads=[b_sgl, b_uk], writes=[b_act])
                if tt + 1 < NT:
                    pre = preload_first()
                lhs = [(lambda s_, j=j: actT[:, j, s_ * 128:(s_ + 1) * 128]) for j in range(22)]
                add_proj(lhs, b_act, lambda ci, n: Wdn[:, ci, n * 512:(n + 1) * 512], b_Wdn)
                for s_ in range(4):
                    rows = slice(t0 + s_ * 128, t0 + (s_ + 1) * 128)
                    if last:
                        P.op("act", ACT(hb[s_ % 2], xt[:, s_, :], AF.Square, accum=ssr[:, 2:3]), reads=[b_xt[s_]], writes=[b_hb[s_ % 2], b_s[1]])
                        rms_rstd(ssr[:, 2:3], ssr[:, 3:4], float(D), b_s[1], b_s[1])
                        P.op("dve", STT(xt[:, s_, :], xt[:, s_, :], ssr[:, 3:4], fin_b, ALU.mult, ALU.mult), reads=[b_xt[s_], b_s[1], b_const], writes=[b_xt[s_]])
                    P.op("pool", DMA(xdst[rows, :], xt[:, s_, :]), reads=[b_xt[s_]], dma=True)
            P.barrier()

        for seq in range(NSEQ if stop_after != "P" else 0):
            for layer in range(2):
                xsrc = x_in[seq] if layer == 0 else X1
                xdst = X1 if layer == 0 else y_out[seq]
                pass_A(layer, xsrc)
                if stop_after == "A":
                    break
                pass_B(layer)
                if stop_after == "B":
                    break
                pass_C(layer, seq, xsrc, xdst, layer == 1)
                if stop_after == "C0":
                    break
        P.barrier()
        P.finalize()
        build.stats = {e: len(P.ops[e]) for e in ENGS}
        P.emit(nc, st)
    return nc


_CACHE = {}


def kernel(x_prompt, x_sample, mem_prompt, mem_sample, **w):
    S = x_prompt.shape[1]
    xs = np.concatenate([np.asarray(x_prompt, np.float32), np.asarray(x_sample, np.float32)], axis=0)
    ms = np.concatenate([np.asarray(mem_prompt, np.float32), np.asarray(mem_sample, np.float32)], axis=0)
    ntot = xs.shape[0]
    nseq = ntot // N_CORES
    key = (nseq, S)
    if key not in _CACHE:
        _CACHE[key] = build(nseq, S)
    nc = _CACHE[key]
    consts = host_consts(S)
    wd = {n: np.ascontiguousarray(np.asarray(w[n], np.float32)) for n in WNAMES}
    in_maps = []
    for c in range(N_CORES):
        m = {"x": np.ascontiguousarray(xs[c * nseq:(c + 1) * nseq]), "mem": np.ascontiguousarray(ms[c * nseq:(c + 1) * nseq])}
        m.update(wd)
        m.update(consts)
        in_maps.append(m)
    res = run_bass_kernel_spmd(nc, in_maps, core_ids=list(range(N_CORES)))
    ys = np.concatenate([np.asarray(r["y"], np.float32) for r in res.results], axis=0)
    nb = x_prompt.shape[0]
    return (ys[:nb], ys[nb:])
```

```python
import math
from contextlib import ExitStack

import ml_dtypes
import numpy as np

import concourse.bass as bass
import concourse.mybir as mybir
from concourse.bass_utils import run_bass_kernel_spmd

F32 = mybir.dt.float32
BF16 = mybir.dt.bfloat16
ALU = mybir.AluOpType
AF = mybir.ActivationFunctionType
AX = mybir.AxisListType

D = 1024
DFF = 2816
NMEM = 256
EPS = 1e-6
GRID_W = 64
N_CORES = 8

ENGS = ["pe", "act", "dve", "pool", "sp"]
N_DMA_SEMS = 24


class Buf:
    __slots__ = ("name", "w", "r", "excl")

    def __init__(self, name, excl=False):
        self.name = name
        self.w = None
        self.r = []
        self.excl = excl


class Op:
    __slots__ = ("eng", "fn", "deps", "signal", "sem", "val", "dma", "waits")

    def __init__(self, eng, fn, dma):
        self.eng = eng
        self.fn = fn
        self.dma = dma
        self.deps = set()
        self.signal = False
        self.sem = None
        self.val = 0
        self.waits = []


class Plan:
    def __init__(self):
        self.ops = {e: [] for e in ENGS}
        self.all = []
        self.dma_rr = {"sp": 0, "pool": 0, "act": 0}
        self.dma_last = [None] * N_DMA_SEMS
        self.nbuf = 0

    def buf(self, name=None, excl=False):
        self.nbuf += 1
        return Buf(name or f"b{self.nbuf}", excl)

    def op(self, eng, fn, reads=(), writes=(), dma=False, after=()):
        o = Op(eng, fn, dma)
        deps = o.deps
        for b in reads:
            if b.w is not None:
                deps.add(b.w)
            if b.excl:
                for q in b.r:
                    if q.eng != eng:
                        deps.add(q)
        for b in writes:
            if b.w is not None:
                deps.add(b.w)
            deps.update(b.r)
        for a in after:
            if a is not None:
                deps.add(a)
        if dma:
            half = N_DMA_SEMS // 2
            k = (self.dma_rr[eng] % half) + (half if eng == "pool" else 0)
            self.dma_rr[eng] += 1
            prev = self.dma_last[k]
            if prev is not None:
                deps.add(prev)
            self.dma_last[k] = o
            o.sem = ("dma", k)
        else:
            o.sem = ("eng", eng)
        for b in reads:
            if not dma:
                b.r = [q for q in b.r if q.dma or q.eng != eng]
            b.r.append(o)
        for b in writes:
            b.w = o
            b.r = []
        self.ops[eng].append(o)
        self.all.append(o)
        return o

    def barrier(self):
        lasts = []
        for e in ENGS:
            if self.ops[e]:
                lasts.append(self.ops[e][-1])
        lasts += [d for d in self.dma_last if d is not None]
        for e in ENGS:
            self.op(e, None, after=lasts)

    @staticmethod
    def _skip(d, o):
        return (not d.dma) and (not o.dma) and d.eng == "pe" and o.eng == "pe"

    def finalize(self):
        for e in ENGS:
            for o in self.ops[e]:
                for d in o.deps:
                    if d.dma or self._skip(d, o):
                        continue
                    d.signal = True
        cnt = {}
        for o in self.all:
            if o.dma:
                cnt[o.sem] = cnt.get(o.sem, 0) + 16
                o.val = cnt[o.sem]
                o.signal = True
            elif o.signal:
                if o.fn is None:
                    o.signal = False
                    continue
                cnt[o.sem] = cnt.get(o.sem, 0) + 1
                o.val = cnt[o.sem]
        for e in ENGS:
            waited = {}
            for o in self.ops[e]:
                need = {}
                for d in o.deps:
                    if self._skip(d, o) or d is o:
                        continue
                    if d.fn is None:
                        continue
                    v = need.get(d.sem, 0)
                    if d.val > v:
                        need[d.sem] = d.val
                for s, v in need.items():
                    if waited.get(s, 0) < v:
                        waited[s] = v
                        o.waits.append((s, v))
        self.counts = cnt

    def emit(self, nc, stack):
        sems = {}
        for e in ENGS:
            sems[("eng", e)] = stack.enter_context(nc.semaphore(f"s_{e}"))
        for k in range(N_DMA_SEMS):
            sems[("dma", k)] = stack.enter_context(nc.semaphore(f"s_dma{k}"))
        block = stack.enter_context(nc.Block())
        plan = self

        def replay(ename):
            def run(h):
                for o in plan.ops[ename]:
                    for s, v in o.waits:
                        h.wait_ge(sems[s], v)
                    if o.fn is None:
                        continue
                    inst = o.fn(h)
                    if o.signal:
                        inst.then_inc(sems[o.sem], 16 if o.dma else 1)
            return run

        block.tensor(replay("pe"))
        block.scalar(replay("act"))
        block.vector(replay("dve"))
        block.gpsimd(replay("pool"))
        block.sync(replay("sp"))


def MM(out, lhsT, rhs, start=True, stop=True):
    return lambda e: e.matmul(out, lhsT=lhsT, rhs=rhs, start=start, stop=stop)


def TR(out, in_, ident):
    return lambda e: e.transpose(out=out, in_=in_, identity=ident)


def ACT(out, in_, func, bias=None, scale=None, accum=None):
    kw = {}
    if bias is not None:
        kw["bias"] = bias
    if scale is not None:
        kw["scale"] = scale
    if accum is not None:
        kw["accum_out"] = accum
    return lambda e: e.activation(out=out, in_=in_, func=func, **kw)


def DMA(out, in_, slow=False):
    if slow:
        return lambda e: e.dma_start(out=out, in_=in_, allow_slow_non_contiguous=True)
    return lambda e: e.dma_start(out=out, in_=in_)


def TS(out, in0, s1, s2=None, op0=ALU.mult, op1=None):
    if op1 is None:
        return lambda e: e.tensor_scalar(out=out, in0=in0, scalar1=s1, scalar2=None, op0=op0)
    return lambda e: e.tensor_scalar(out=out, in0=in0, scalar1=s1, scalar2=s2, op0=op0, op1=op1)


def TT(out, in0, in1, op):
    return lambda e: e.tensor_tensor(out=out, in0=in0, in1=in1, op=op)


def STT(out, in0, scalar, in1, op0, op1):
    return lambda e: e.scalar_tensor_tensor(out=out, in0=in0, scalar=scalar, in1=in1, op0=op0, op1=op1)


def CP(out, in_):
    return lambda e: e.tensor_copy(out=out, in_=in_)


def ACP(out, in_):
    return lambda e: e.copy(out=out, in_=in_)


def RECIP(out, in_):
    return lambda e: e.reciprocal(out=out, in_=in_)


def RSUM(out, in_):
    return lambda e: e.tensor_reduce(out=out, in_=in_, axis=AX.X, op=ALU.add)


def MSET(ap, v):
    return lambda e: e.memset(ap, v)


class Arena:
    def __init__(self, t, n):
        self.t = t
        self.n = n
        self.off = 0

    def reset(self):
        self.off = 0

    def alloc(self, shape):
        n = 1
        for s in shape[1:]:
            n *= s
        n = (n + 15) // 16 * 16
        o = self.off
        self.off += n
        assert self.off <= self.n, (self.off, self.n, shape)
        m = 1
        for s in shape[1:]:
            m *= s
        v = self.t[0:shape[0], o:o + m]
        if len(shape) == 3:
            v = v.rearrange("p (a b) -> p a b", b=shape[2])
        elif len(shape) == 4:
            v = v.rearrange("p (a b c) -> p a b c", b=shape[2], c=shape[3])
        return v


AB_N = 82944
AF_N = 8768

WNAMES = ["norm_mix", "e_w_in", "e_q_norm", "e_k_norm", "e_lam_q1", "e_lam_k1", "e_lam_q2", "e_lam_k2",
          "e_subln", "e_w_out", "o_w_in", "o_q_norm", "o_kv_norm", "o_w_uq", "o_w_ukv", "o_w_out",
          "norm_cross", "norm_mem", "w_cq", "w_ckv", "w_co", "norm_ffn", "w_gu", "w_down", "final_norm"]
WSHAPES = {
    "norm_mix": [2, D], "e_w_in": [1, D, 2304], "e_q_norm": [1, 64], "e_k_norm": [1, 64],
    "e_lam_q1": [1, 64], "e_lam_k1": [1, 64], "e_lam_q2": [1, 64], "e_lam_k2": [1, 64],
    "e_subln": [1, 128], "e_w_out": [1, D, D], "o_w_in": [1, D, 672], "o_q_norm": [1, 384],
    "o_kv_norm": [1, 256], "o_w_uq": [1, 384, 1536], "o_w_ukv": [1, 256, 2048], "o_w_out": [1, D, D],
    "norm_cross": [2, D], "norm_mem": [2, D], "w_cq": [2, D, D], "w_ckv": [2, D, 2 * D], "w_co": [2, D, D],
    "norm_ffn": [2, D], "w_gu": [2, D, 2 * DFF], "w_down": [2, DFF, D], "final_norm": [D],
}


def host_consts(S):
    c = {}
    c["ident"] = np.eye(128, dtype=np.float32).astype(ml_dtypes.bfloat16)
    c["ones_b"] = np.ones((128, 128), dtype=np.float32).astype(ml_dtypes.bfloat16)
    c["ones_f"] = np.ones((128, 64), dtype=np.float32)
    es = np.zeros((32, 96), dtype=np.float32)
    es[np.arange(32), 64 + np.arange(32)] = 1.0
    c["esel"] = es.astype(ml_dtypes.bfloat16)
    s64 = np.zeros((128, 64), dtype=np.float32)
    s64[64, :] = 1.0
    c["sel64"] = s64.astype(ml_dtypes.bfloat16)
    t = np.arange(S)
    fa = (10000.0 ** (-np.arange(16, dtype=np.float32) / 16)).astype(np.float32)
    r = (t // GRID_W).astype(np.float32)
    cc = (t % GRID_W).astype(np.float32)
    angA = np.concatenate([r[:, None] * fa, cc[:, None] * fa], axis=-1).astype(np.float32)
    fl = (10000.0 ** (-np.arange(16, dtype=np.float32) / 16)).astype(np.float32)
    angL = (t.astype(np.float32)[:, None] * fl).astype(np.float32)

    def exp_tabs(ang):
        co = np.repeat(np.cos(ang), 2, axis=-1).astype(np.float32)
        si = np.repeat(np.sin(ang), 2, axis=-1).astype(np.float32)
        si[:, 0::2] *= -1.0
        return co, si

    c["cexpA"], c["sexpA"] = exp_tabs(angA)
    c["cexpL"], c["sexpL"] = exp_tabs(angL)
    slopes = (2.0 ** (-8.0 * np.arange(1, 5) / 4)).astype(np.float64)
    p = np.arange(128)[:, None].astype(np.float64)
    j = np.arange(512)[None, :].astype(np.float64)
    m = np.arange(896)[None, :].astype(np.float64)
    ab = np.zeros((4, 128, 512)); bl = np.zeros((4, 128, 512)); st = np.zeros((4, 128, 896))
    for h in range(4):
        ab[h] = np.exp(-slopes[h] * (j - p + 127))
        bl[h] = np.exp(-slopes[h] * (p - j + 511))
        st[h] = np.exp(-slopes[h] * np.abs(m - 384 - p))
    c["al_above"] = ab.astype(np.float32).astype(ml_dtypes.bfloat16)
    c["al_below"] = bl.astype(np.float32).astype(ml_dtypes.bfloat16)
    c["al_strip"] = st.astype(np.float32).astype(ml_dtypes.bfloat16)
    bc = np.zeros((128, 4, 64), dtype=np.float32)
    for h in range(4):
        for mm_ in range(1, 32):
            bc[:, h, mm_] = -slopes[h] * (128 * mm_ - 127)
        for mm_ in range(4, 32):
            bc[:, h, 32 + mm_] = -slopes[h] * (128 * mm_ - 511)
    c["al_bias"] = bc.reshape(128, 256)
    return c


CONST_SPECS = lambda S: {
    "ident": ([128, 128], BF16), "ones_b": ([128, 128], BF16), "ones_f": ([128, 64], F32), "esel": ([32, 96], BF16), "sel64": ([128, 64], BF16),
    "cexpA": ([S, 64], F32), "sexpA": ([S, 64], F32), "cexpL": ([S, 32], F32), "sexpL": ([S, 32], F32),
    "al_above": ([4, 128, 512], BF16), "al_below": ([4, 128, 512], BF16), "al_strip": ([4, 128, 896], BF16),
    "al_bias": ([128, 256], F32),
}


def build(NSEQ, S, stop_after=None):
    NK = S // 128
    NT = S // 512
    nc = bass.Bass("TRN2", target_bir_lowering=False)

    def din(name, shape, dt=F32):
        return nc.dram_tensor(name, list(shape), dt, kind="ExternalInput").ap()

    def dscr(name, shape, dt=BF16):
        return nc.dram_tensor(name, list(shape), dt).ap()

    x_in = din("x", [NSEQ, S, D])
    mem_in = din("mem", [NSEQ, NMEM, D])
    W = {n: din(n, WSHAPES[n]) for n in WNAMES}
    C = {n: din(n, sh, dt) for n, (sh, dt) in CONST_SPECS(S).items()}
    y_out = nc.dram_tensor("y", [NSEQ, S, D], F32, kind="ExternalOutput").ap()

    Win0_s = dscr("Win0_s", [128, 8, 2304])
    Wout0_s = dscr("Wout0_s", [128, 8, 1024])
    Win1_s = dscr("Win1_s", [128, 8, 672])
    Wuq_s = dscr("Wuq_s", [128, 3, 1536])
    Wkp_s = dscr("Wkp_s", [128, 2, 16, 96])
    Wv_s = dscr("Wv_s", [128, 2, 1024])
    Wout1_s = dscr("Wout1_s", [128, 8, 1024])
    Wcq_s = [dscr(f"Wcq_s{l}", [128, 8, 1024]) for l in range(2)]
    Wckv_s = [dscr(f"Wckv_s{l}", [128, 8, 2048]) for l in range(2)]
    Wco_s = [dscr(f"Wco_s{l}", [128, 8, 1024]) for l in range(2)]
    Wgu_s = [dscr(f"Wgu_s{l}", [128, 8, 2 * DFF]) for l in range(2)]
    Wdn_s = [dscr(f"Wdn_s{l}", [128, 22, 1024]) for l in range(2)]
    QTa = dscr("QTa", [8, 64, S]); QTb = dscr("QTb", [4, 128, S]); QT1 = dscr("QT1", [16, 96, S])
    KTa = dscr("KTa", [128, S]); KTb = dscr("KTb", [4, 128, S]); KT1 = dscr("KT1", [16, 96, S])
    Va = dscr("Va", [S, 2, 65]); Vb = dscr("Vb", [S, 4, 128]); V1 = dscr("V1", [S, 16, 65])
    OTa = dscr("OTa", [8, 64, S]); OTb = dscr("OTb", [4, 128, S]); OT1 = dscr("OT1", [16, 64, S])
    X1 = dscr("X1", [S, D], F32)

    st = ExitStack()
    with st:
        abt = st.enter_context(nc.sbuf_tensor("arena_b", [128, AB_N], BF16))
        aft = st.enter_context(nc.sbuf_tensor("arena_f", [128, AF_N], F32))
        cft = st.enter_context(nc.sbuf_tensor("const_f", [128, 1792], F32))
        cbt = st.enter_context(nc.sbuf_tensor("const_b", [128, 128 + 128 + 96 + 64], BF16))
        psf = [st.enter_context(nc.psum_tensor(f"psf{i}", [128, 512], F32)) for i in range(8)]
        psT = [psf[6][:].bitcast(BF16), psf[7][:].bitcast(BF16)]
        AB = Arena(abt, AB_N)
        AFa = Arena(aft, AF_N)
        P = Plan()

        ident = cbt[:, 0:128]
        ones_b = cbt[:, 128:256]
        esel = cbt[0:32, 256:352]
        sel64 = cbt[:, 352:416]
        ones_f = cft[:, 0:64]
        gcols = cft[:, 64:64 + 80]
        GC = {"mix0": 0, "mix1": 8, "cross0": 16, "cross1": 24, "mem0": 32, "mem1": 40, "ffn0": 48, "ffn1": 56,
              "qn": 64, "kvn": 67, "one": 69}
        gq_b = cft[:, 160:224]; gk_b = cft[:, 224:288]; gqs_b = cft[:, 288:352]; gks_b = cft[:, 352:416]
        lam_t = cft[:, 416:420]
        gsub_c = cft[:, 420:421]
        lamw = cft[:, 424:424 + 4 * 64]
        fin_b = cft[:, 768:1792]
        b_const = P.buf("const")

        def ld(dst, src, slow=False):
            P.op("sp", DMA(dst, src, slow), writes=[b_const], dma=True)

        ld(ident, C["ident"][:, :]); ld(ones_b, C["ones_b"][:, :]); ld(esel, C["esel"][:, :]); ld(ones_f, C["ones_f"][:, :]); ld(sel64, C["sel64"][:, :])
        for nm, src, kc in [("mix0", W["norm_mix"][0], 8), ("mix1", W["norm_mix"][1], 8),
                            ("cross0", W["norm_cross"][0], 8), ("cross1", W["norm_cross"][1], 8),
                            ("mem0", W["norm_mem"][0], 8), ("mem1", W["norm_mem"][1], 8),
                            ("ffn0", W["norm_ffn"][0], 8), ("ffn1", W["norm_ffn"][1], 8),
                            ("qn", W["o_q_norm"][0], 3), ("kvn", W["o_kv_norm"][0], 2)]:
            ld(gcols[:, GC[nm]:GC[nm] + kc], src.rearrange("(kc p) -> p kc", p=128), slow=True)
        P.op("dve", MSET(gcols[:, GC["one"]:GC["one"] + 1], 1.0), writes=[b_const])
        ld(gq_b, W["e_q_norm"][0].partition_broadcast(128)); ld(gk_b, W["e_k_norm"][0].partition_broadcast(128))
        ld(fin_b, W["final_norm"].partition_broadcast(128))
        for i, nm in enumerate(["e_lam_q1", "e_lam_k1", "e_lam_q2", "e_lam_k2"]):
            ld(lamw[:, i * 64:(i + 1) * 64], W[nm][0].partition_broadcast(128))
        ld(gsub_c, W["e_subln"][0].rearrange("(p o) -> p o", o=1), slow=True)
        for g, gs in [(gq_b, gqs_b), (gk_b, gks_b)]:
            gv = g.rearrange("p (i two) -> p i two", two=2); sv = gs.rearrange("p (i two) -> p i two", two=2)
            P.op("dve", CP(sv[:, :, 0:1], gv[:, :, 1:2]), reads=[b_const], writes=[b_const])
            P.op("dve", CP(sv[:, :, 1:2], gv[:, :, 0:1]), reads=[b_const], writes=[b_const])
        lam_init0 = 0.8 - 0.6 * math.exp(-0.3 * 0)
        tmpl = cft[:, 440 + 256:440 + 256 + 64]
        P.op("dve", TT(tmpl, lamw[:, 0:64], lamw[:, 64:128], ALU.mult), reads=[b_const], writes=[b_const])
        P.op("dve", RSUM(lam_t[:, 0:1], tmpl), reads=[b_const], writes=[b_const])
        P.op("dve", TT(tmpl, lamw[:, 128:192], lamw[:, 192:256], ALU.mult), reads=[b_const], writes=[b_const])
        P.op("dve", RSUM(lam_t[:, 1:2], tmpl), reads=[b_const], writes=[b_const])
        P.op("act", ACT(lam_t[:, 0:2], lam_t[:, 0:2], AF.Exp), reads=[b_const], writes=[b_const])
        P.op("dve", TT(lam_t[:, 2:3], lam_t[:, 0:1], lam_t[:, 1:2], ALU.subtract), reads=[b_const], writes=[b_const])
        P.op("dve", TS(lam_t[:, 3:4], lam_t[:, 2:3], lam_init0, -1.0, ALU.add, ALU.mult), reads=[b_const], writes=[b_const])
        nlam = lam_t[:, 3:4]
        P.op("dve", TS(gsub_c, gsub_c, 1.0 - lam_init0), reads=[b_const], writes=[b_const])

        AB.reset(); AFa.reset()
        NST = 3
        stf = [AFa.alloc([128, 2048]) for _ in range(NST)]
        stb = [AB.alloc([128, 2048]) for _ in range(NST)]
        zt = AB.alloc([128, 16, 32])
        b_stf = [P.buf() for _ in range(NST)]; b_stb = [P.buf() for _ in range(NST)]
        b_z = P.buf()
        P.op("dve", MSET(zt, 0.0), writes=[b_z])
        cstate = {"i": 0}

        def conv_piece(src2d, rows, cols, gain_col, outs):
            i = cstate["i"]; cstate["i"] += 1
            s = i % NST
            r0, r1 = rows; c0, c1 = cols
            np_, w = r1 - r0, c1 - c0
            P.op("sp", DMA(stf[s][0:np_, 0:w], src2d[r0:r1, c0:c1]), writes=[b_stf[s]], dma=True)
            if i % 2 == 0:
                P.op("dve", TS(stb[s][0:np_, 0:w], stf[s][0:np_, 0:w], gain_col[0:np_, :]), reads=[b_stf[s], b_const], writes=[b_stb[s]])
            else:
                P.op("act", ACT(stb[s][0:np_, 0:w], stf[s][0:np_, 0:w], AF.Copy, scale=gain_col[0:np_, :]), reads=[b_stf[s], b_const], writes=[b_stb[s]])
            for dst, vf in outs:
                P.op("pool", DMA(dst, vf(stb[s][0:np_, 0:w])), reads=[b_stb[s]], dma=True)

        def gc(nm, kc):
            return gcols[:, GC[nm] + kc:GC[nm] + kc + 1]

        one_c = gcols[:, GC["one"]:GC["one"] + 1]
        idv = lambda v: v

        def conv(src2d, KC, rows_p, col_ranges, dst, gname):
            for kc in range(KC):
                g = gc(gname, kc) if gname else one_c
                for (c0, c1, d0) in col_ranges:
                    for cc in range(c0, c1, 2048):
                        ce = min(cc + 2048, c1)
                        conv_piece(src2d, (kc * rows_p, (kc + 1) * rows_p), (cc, ce), g,
                                   [(dst[0:rows_p, kc, d0 + cc - c0:d0 + ce - c0], idv)])

        conv(W["e_w_in"][0], 8, 128, [(0, 512, 0), (768, 1280, 512), (512, 768, 1024), (1280, 2304, 1280)], Win0_s, "mix0")
        conv(W["e_w_out"][0], 8, 128, [(0, 1024, 0)], Wout0_s, None)
        conv(W["o_w_in"][0], 8, 128, [(0, 672, 0)], Win1_s, "mix1")
        conv(W["o_w_uq"][0], 3, 128, [(0, 1536, 0)], Wuq_s, "qn")
        for kc in range(2):
            conv_piece(W["o_w_ukv"][0], (kc * 128, (kc + 1) * 128), (0, 2048), gc("kvn", kc), [
                (Wkp_s[:, kc, :, 0:64], lambda v: v.rearrange("p (h e) -> p h e", e=128)[:, :, 0:64]),
                (Wv_s[:, kc, :].rearrange("p (h e) -> p h e", e=64), lambda v: v.rearrange("p (h e) -> p h e", e=128)[:, :, 64:128]),
            ])
            P.op("pool", DMA(Wkp_s[:, kc, :, 64:96], zt), reads=[b_z], dma=True)
        conv(W["o_w_out"][0], 8, 128, [(0, 1024, 0)], Wout1_s, None)
        for l in range(2):
            conv(W["w_cq"][l], 8, 128, [(0, 1024, 0)], Wcq_s[l], f"cross{l}")
            conv(W["w_ckv"][l], 8, 128, [(0, 2048, 0)], Wckv_s[l], f"mem{l}")
            conv(W["w_co"][l], 8, 128, [(0, 1024, 0)], Wco_s[l], None)
            conv(W["w_gu"][l], 8, 128, [(0, 2 * DFF, 0)], Wgu_s[l], f"ffn{l}")
            conv(W["w_down"][l], 22, 128, [(0, 1024, 0)], Wdn_s[l], None)
        P.barrier()

        bank_b = [P.buf(f"psf{i}", excl=True) for i in range(8)]
        ring = {"i": 0}

        def nextbank():
            i = ring["i"] % 6
            ring["i"] += 1
            return psf[i][:], bank_b[i]

        sring = {"i": 0}
        aring = {"i": 0}

        sbanks = {"l": [0, 1, 2]}

        def nextS():
            i = sbanks["l"][sring["i"] % len(sbanks["l"])]
            sring["i"] += 1
            return psf[i][:], bank_b[i]

        abanks = {"l": [3, 4, 5]}

        def nextA():
            i = abanks["l"][aring["i"] % len(abanks["l"])]
            aring["i"] += 1
            return psf[i][:], bank_b[i]

        tring = {"i": 0}

        def nextT():
            i = tring["i"] % 2
            tring["i"] += 1
            return psT[i][:, 0:512], bank_b[6 + i]

        def rms_rstd(ss, rs, n, b_ss, b_rs):
            P.op("act", ACT(rs, ss, AF.Ln, bias=EPS, scale=1.0 / n), reads=[b_ss], writes=[b_rs])
            P.op("act", ACT(rs, rs, AF.Exp, scale=-0.5), reads=[b_rs], writes=[b_rs])

        def rope(out, v, ta, tb, tmp, nh, nd, b_v, b_tab, b_tmp, b_out, eng="dve"):
            t1, t2 = tmp
            tab = ta.unsqueeze(1).to_broadcast([128, nh, nd])
            P.op(eng, TT(t1, v, tab, ALU.mult), reads=[b_v, b_tab], writes=[b_tmp])
            vv = v.rearrange("p h (i two) -> p h i two", two=2)
            t2v = t2.rearrange("p h (i two) -> p h i two", two=2)
            tbv = tb.rearrange("p (i two) -> p i two", two=2).unsqueeze(1).to_broadcast([128, nh, nd // 2, 2])
            P.op(eng, TT(t2v[:, :, :, 0:1], vv[:, :, :, 1:2], tbv[:, :, :, 0:1], ALU.mult), reads=[b_v, b_tab], writes=[b_tmp])
            P.op(eng, TT(t2v[:, :, :, 1:2], vv[:, :, :, 0:1], tbv[:, :, :, 1:2], ALU.mult), reads=[b_v, b_tab], writes=[b_tmp])
            P.op(eng, TT(out, t1, t2, ALU.add), reads=[b_tmp], writes=[b_out])

        def norm_to_hT(xt, b_xt, junk, b_junk, ss, rs, b_s, hb, b_hb, hT, b_hT, col0):
            P.op("act", ACT(hb, xt, AF.Square, accum=ss), reads=[b_xt], writes=[b_hb, b_s])
            rms_rstd(ss, rs, float(D), b_s, b_s)
            P.op("dve", TS(hb, xt, rs), reads=[b_xt, b_s], writes=[b_hb])
            for half in range(2):
                pt, b_pt = nextT()
                for k in range(4):
                    kc = half * 4 + k
                    P.op("pe", TR(pt[:, k * 128:(k + 1) * 128], hb[:, kc * 128:(kc + 1) * 128], ident), reads=[b_hb, b_const], writes=[b_pt])
                dst = hT[:, half * 4:half * 4 + 4, col0:col0 + 128]
                src = pt.rearrange("p (k t) -> p k t", t=128)
                if half == 0:
                    P.op("act", ACP(dst, src), reads=[b_pt], writes=[b_hT])
                else:
                    P.op("dve", CP(dst, src), reads=[b_pt], writes=[b_hT])

        def pass_A(layer, xsrc):
            AB.reset(); AFa.reset()
            ncols = 2304 if layer == 0 else 672
            Win = AB.alloc([128, 8, ncols]); b_Win = P.buf()
            Wsrc = Win0_s if layer == 0 else Win1_s
            for kc in range(8):
                P.op("sp", DMA(Win[:, kc, :], Wsrc[:, kc, :]), writes=[b_Win], dma=True)
            if layer == 1:
                Wuq = AB.alloc([128, 3, 1536]); Wkp = AB.alloc([128, 2, 16, 96]); Wv = AB.alloc([128, 2, 1024])
                for kc in range(3):
                    P.op("sp", DMA(Wuq[:, kc, :], Wuq_s[:, kc, :]), writes=[b_Win], dma=True)
                for kc in range(2):
                    P.op("sp", DMA(Wkp[:, kc, :, :], Wkp_s[:, kc, :, :]), writes=[b_Win], dma=True)
                    P.op("sp", DMA(Wv[:, kc, :], Wv_s[:, kc, :]), writes=[b_Win], dma=True)
            xt = [AFa.alloc([128, 1024]) for _ in range(2)]; b_xt = [P.buf() for _ in range(2)]
            junk = AFa.alloc([128, 1024]); b_junk = P.buf()
            ssr = [AFa.alloc([128, 4]) for _ in range(2)]; b_s = [P.buf() for _ in range(2)]
            hb = [AB.alloc([128, 1024]) for _ in range(2)]; b_hb = [P.buf() for _ in range(2)]
            hT = [AB.alloc([128, 8, 128]) for _ in range(2)]; b_hT = [P.buf() for _ in range(2)]
            tabs = [AFa.alloc([128, 128]) for _ in range(2)]; b_tabs = [P.buf() for _ in range(2)]
            tq = [AFa.alloc([128, 256]) for _ in range(2)]; b_tq = [P.buf() for _ in range(2)]
            t1 = AFa.alloc([128, 512]); t2 = AFa.alloc([128, 512]); b_t = P.buf()
            qn = AFa.alloc([128, 512]); b_qn = P.buf()
            sq8 = AFa.alloc([128, 16]); b_sq8 = P.buf()
            if layer == 0:
                nstT = 13
            else:
                nstT = 16
            if layer == 0:
                qf2 = [AB.alloc([128, 1024]) for _ in range(2)]; b_qf2 = [P.buf() for _ in range(2)]
                kf2 = [AB.alloc([128, 640]) for _ in range(2)]; b_kf2 = [P.buf() for _ in range(2)]
                stT = [AB.alloc([128, 13, 512]) for _ in range(2)]; b_stT = [P.buf() for _ in range(2)]
                vast = [AB.alloc([128, 2, 65]) for _ in range(2)]; b_va = [P.buf() for _ in range(2)]
                vbst = [AB.alloc([128, 512]) for _ in range(2)]; b_vb = [P.buf() for _ in range(2)]
                for v_ in vast:
                    P.op("dve", MSET(v_[:, :, 64:65], 1.0), writes=[b_va[0], b_va[1]])
            else:
                qf2 = [AB.alloc([128, 16, 96]) for _ in range(2)]; b_qf2 = [P.buf() for _ in range(2)]
                cn = AB.alloc([128, 640]); b_cn = P.buf()
                krf = AB.alloc([128, 32]); b_krf = P.buf()
                cT = AB.alloc([128, 5, 128]); b_cT = P.buf()
                krT = AB.alloc([32, 128]); b_krT = P.buf()
                stQ = [AB.alloc([96, 16, 512]) for _ in range(2)]; b_stQ = [P.buf() for _ in range(2)]
                stK = [AB.alloc([96, 16, 512]) for _ in range(2)]; b_stK = [P.buf() for _ in range(2)]
                v1st = [AB.alloc([128, 16, 65]) for _ in range(2)]; b_v1 = [P.buf() for _ in range(2)]
                for v_ in v1st:
                    P.op("dve", MSET(v_[:, :, 64:65], 1.0), writes=[b_v1[0], b_v1[1]])
            sc0 = 0.125
            sc1 = 96.0 ** -0.5
            tail = {"f": None}
            for i in range(NK):
                pa = i % 2
                qf = qf2[pa]; b_qf = b_qf2[pa]
                if layer == 0:
                    kf = kf2[pa]; b_kf = b_kf2[pa]
                t0 = i * 128
                blk = i // 4
                sb = blk % 2
                c4 = (i % 4) * 128
                P.op("sp", DMA(xt[pa], xsrc[t0:t0 + 128, :]), writes=[b_xt[pa]], dma=True)
                if layer == 0:
                    P.op("sp", DMA(tabs[pa][:, 0:64], C["cexpA"][t0:t0 + 128, :]), writes=[b_tabs[pa]], dma=True)
                    P.op("sp", DMA(tabs[pa][:, 64:128], C["sexpA"][t0:t0 + 128, :]), writes=[b_tabs[pa]], dma=True)
                else:
                    P.op("sp", DMA(tabs[pa][:, 0:32], C["cexpL"][t0:t0 + 128, :]), writes=[b_tabs[pa]], dma=True)
                    P.op("sp", DMA(tabs[pa][:, 64:96], C["sexpL"][t0:t0 + 128, :]), writes=[b_tabs[pa]], dma=True)
                norm_to_hT(xt[pa], b_xt[pa], junk, b_junk, ssr[pa][:, 0:1], ssr[pa][:, 1:2], b_s[pa], hb[pa], b_hb[pa], hT[pa], b_hT[pa], 0)
                if layer == 0:
                    P.op("dve", STT(tq[pa][:, 0:64], tabs[pa][:, 0:64], sc0, gq_b, ALU.mult, ALU.mult), reads=[b_tabs[pa], b_const], writes=[b_tq[pa]])
                    P.op("dve", STT(tq[pa][:, 64:128], tabs[pa][:, 64:128], sc0, gqs_b, ALU.mult, ALU.mult), reads=[b_tabs[pa], b_const], writes=[b_tq[pa]])
                    P.op("dve", TT(tq[pa][:, 128:192], tabs[pa][:, 0:64], gk_b, ALU.mult), reads=[b_tabs[pa], b_const], writes=[b_tq[pa]])
                    P.op("dve", TT(tq[pa][:, 192:256], tabs[pa][:, 64:128], gks_b, ALU.mult), reads=[b_tabs[pa], b_const], writes=[b_tq[pa]])
                    groups = [(0, 512), (512, 1024), (1024, 1280), (1280, 1792), (1792, 2304)]
                    pg = []
                    for (c0, c1) in groups:
                        bk, b_bk = nextbank()
                        for kc in range(8):
                            P.op("pe", MM(bk[:, 0:c1 - c0], hT[pa][:, kc, :], Win[:, kc, c0:c1], kc == 0, kc == 7), reads=[b_hT[pa], b_Win], writes=[b_bk])
                        pg.append((bk, b_bk))
                    (p0, b0), (p1, b1), (p2, b2), (p3, b3), (p4, b4) = pg
                    P.op("act", ACT(t1, p0, AF.Square), reads=[b0], writes=[b_t])
                    P.op("dve", RSUM(sq8[:, 0:8], t1.rearrange("p (h d) -> p h d", d=64)), reads=[b_t], writes=[b_sq8])
                    rms_rstd(sq8[:, 0:8], sq8[:, 0:8], 64.0, b_sq8, b_sq8)
                    qn3 = qn.rearrange("p (h d) -> p h d", d=64)
                    P.op("dve", TT(qn3, p0.rearrange("p (h d) -> p h d", d=64), sq8[:, 0:8].unsqueeze(2).to_broadcast([128, 8, 64]), ALU.mult), reads=[b0, b_sq8], writes=[b_qn])
                    rope(qf[:, 0:512].rearrange("p (h d) -> p h d", d=64), qn3, tq[pa][:, 0:64], tq[pa][:, 64:128],
                         (t1.rearrange("p (h d) -> p h d", d=64), t2.rearrange("p (h d) -> p h d", d=64)), 8, 64, b_qn, b_tq[pa], b_t, b_qf)
                    P.op("act", ACT(qf[:, 512:1024], p1, AF.Copy, scale=sc0), reads=[b1], writes=[b_qf])
                    P.op("act", ACT(t1[:, 0:128], p2[:, 0:128], AF.Square), reads=[b2], writes=[b_t])
                    P.op("dve", RSUM(sq8[:, 8:10], t1[:, 0:128].rearrange("p (h d) -> p h d", d=64)), reads=[b_t], writes=[b_sq8])
                    rms_rstd(sq8[:, 8:10], sq8[:, 8:10], 64.0, b_sq8, b_sq8)
                    kn3 = qn[:, 0:128].rearrange("p (h d) -> p h d", d=64)
                    P.op("dve", TT(kn3, p2[:, 0:128].rearrange("p (h d) -> p h d", d=64), sq8[:, 8:10].unsqueeze(2).to_broadcast([128, 2, 64]), ALU.mult), reads=[b2, b_sq8], writes=[b_qn])
                    rope(kf[:, 0:128].rearrange("p (h d) -> p h d", d=64), kn3, tq[pa][:, 128:192], tq[pa][:, 192:256],
                         (t1[:, 0:128].rearrange("p (h d) -> p h d", d=64), t2[:, 0:128].rearrange("p (h d) -> p h d", d=64)), 2, 64, b_qn, b_tq[pa], b_t, b_kf)
                    P.op("act", ACP(vast[pa][:, :, 0:64], p2[:, 128:256].rearrange("p (h d) -> p h d", d=64)), reads=[b2], writes=[b_va[pa]])
                    P.op("pool", DMA(Va[t0:t0 + 128, :, :], vast[pa]), reads=[b_va[pa]], dma=True)
                    P.op("act", ACP(kf[:, 128:640], p3), reads=[b3], writes=[b_kf])
                    P.op("dve", CP(vbst[pa], p4), reads=[b4], writes=[b_vb[pa]])
                    P.op("pool", DMA(Vb[t0:t0 + 128, :, :].rearrange("t h e -> t (h e)"), vbst[pa]), reads=[b_vb[pa]], dma=True)
                    def tail0(i=i, sb=sb, c4=c4, blk=blk, qf=qf, b_qf=b_qf, kf=kf, b_kf=b_kf):
                        srcs = [(qf, b_qf, c) for c in range(8)] + [(kf, b_kf, c) for c in range(5)]
                        for j0 in range(0, 13, 4):
                            pt, b_pt = nextT()
                            n = min(4, 13 - j0)
                            for k in range(n):
                                sa, sbuf_, c = srcs[j0 + k]
                                P.op("pe", TR(pt[:, k * 128:(k + 1) * 128], sa[:, c * 128:(c + 1) * 128], ident), reads=[sbuf_, b_const], writes=[b_pt])
                            dst = stT[sb][:, j0:j0 + n, c4:c4 + 128]
                            src = pt[:, 0:n * 128].rearrange("p (k t) -> p k t", t=128)
                            P.op("act" if (j0 // 4) % 2 == 0 else "dve", (ACP if (j0 // 4) % 2 == 0 else CP)(dst, src), reads=[b_pt], writes=[b_stT[sb]])
                        if i % 4 == 3:
                            tb0 = blk * 512
                            s_ = stT[sb]
                            for c in range(4):
                                P.op("pool", DMA(QTa[2 * c, :, tb0:tb0 + 512], s_[0:64, c, :]), reads=[b_stT[sb]], dma=True)
                                P.op("pool", DMA(QTa[2 * c + 1, :, tb0:tb0 + 512], s_[64:128, c, :]), reads=[b_stT[sb]], dma=True)
                                P.op("pool", DMA(QTb[c, :, tb0:tb0 + 512], s_[:, 4 + c, :]), reads=[b_stT[sb]], dma=True)
                                P.op("pool", DMA(KTb[c, :, tb0:tb0 + 512], s_[:, 9 + c, :]), reads=[b_stT[sb]], dma=True)
                            P.op("pool", DMA(KTa[:, tb0:tb0 + 512], s_[:, 8, :]), reads=[b_stT[sb]], dma=True)
                    if tail["f"] is not None:
                        tail["f"]()
                    tail["f"] = tail0
                else:
                    p0, b0 = nextbank(); p1, b1 = nextbank()
                    for kc in range(8):
                        P.op("pe", MM(p0[:, 0:384], hT[pa][:, kc, :], Win[:, kc, 0:384], kc == 0, kc == 7), reads=[b_hT[pa], b_Win], writes=[b0])
                    for kc in range(8):
                        P.op("pe", MM(p1[:, 0:288], hT[pa][:, kc, :], Win[:, kc, 384:672], kc == 0, kc == 7), reads=[b_hT[pa], b_Win], writes=[b1])
                    P.op("act", ACT(t1[:, 0:384], p0[:, 0:384], AF.Square, accum=sq8[:, 0:1]), reads=[b0], writes=[b_t, b_sq8])
                    P.op("act", ACT(t1[:, 0:256], p1[:, 0:256], AF.Square, accum=sq8[:, 1:2]), reads=[b1], writes=[b_t, b_sq8])
                    rms_rstd(sq8[:, 0:1], sq8[:, 2:3], 384.0, b_sq8, b_sq8)
                    rms_rstd(sq8[:, 1:2], sq8[:, 3:4], 256.0, b_sq8, b_sq8)
                    P.op("dve", TS(cn[:, 0:384], p0[:, 0:384], sq8[:, 2:3]), reads=[b0, b_sq8], writes=[b_cn])
                    P.op("dve", TS(cn[:, 384:640], p1[:, 0:256], sq8[:, 3:4]), reads=[b1, b_sq8], writes=[b_cn])
                    P.op("act", ACP(qn[:, 0:32], p1[:, 256:288]), reads=[b1], writes=[b_qn])
                    rope(krf.rearrange("p (h d) -> p h d", h=1), qn[:, 0:32].rearrange("p (h d) -> p h d", h=1), tabs[pa][:, 0:32], tabs[pa][:, 64:96],
                         (t1[:, 0:32].rearrange("p (h d) -> p h d", h=1), t2[:, 0:32].rearrange("p (h d) -> p h d", h=1)), 1, 32, b_qn, b_tabs[pa], b_t, b_krf)
                    for (j0, n) in [(0, 4), (4, 1)]:
                        pt, b_pt = nextT()
                        for k in range(n):
                            c = j0 + k
                            P.op("pe", TR(pt[:, k * 128:(k + 1) * 128], cn[:, c * 128:(c + 1) * 128], ident), reads=[b_cn, b_const], writes=[b_pt])
                        if j0 == 4:
                            P.op("pe", TR(pt[0:32, 128:256], krf, ident), reads=[b_krf, b_const], writes=[b_pt])
                            P.op("dve", CP(krT, pt[0:32, 128:256]), reads=[b_pt], writes=[b_krT])
                        P.op("act", ACP(cT[:, j0:j0 + n, :], pt[:, 0:n * 128].rearrange("p (k t) -> p k t", t=128)), reads=[b_pt], writes=[b_cT])
                    for (h0, h1) in [(0, 5), (5, 10), (10, 15), (15, 16)]:
                        bk, b_bk = nextbank()
                        nh = h1 - h0
                        for kc in range(3):
                            P.op("pe", MM(bk[:, 0:nh * 96], cT[:, kc, :], Wuq[:, kc, h0 * 96:h1 * 96], kc == 0, kc == 2), reads=[b_cT, b_Win], writes=[b_bk])
                        bv = bk[:, 0:nh * 96].rearrange("p (h d) -> p h d", d=96)
                        P.op("act", ACT(qf[:, h0:h1, 0:64], bv[:, :, 0:64], AF.Copy, scale=sc1), reads=[b_bk], writes=[b_qf])
                        P.op("act", ACT(qn[:, 0:nh * 32].rearrange("p (h d) -> p h d", d=32), bv[:, :, 64:96], AF.Copy, scale=sc1), reads=[b_bk], writes=[b_qn])
                        rope(qf[:, h0:h1, 64:96], qn[:, 0:nh * 32].rearrange("p (h d) -> p h d", d=32), tabs[pa][:, 0:32], tabs[pa][:, 64:96],
                             (t1[:, 0:nh * 32].rearrange("p (h d) -> p h d", d=32), t2[:, 0:nh * 32].rearrange("p (h d) -> p h d", d=32)), nh, 32, b_qn, b_tabs[pa], b_t, b_qf)
                    def tail1(i=i, sb=sb, c4=c4, blk=blk, qf=qf, b_qf=b_qf):
                        for j0 in range(0, 16, 4):
                            pt, b_pt = nextT()
                            for k in range(4):
                                P.op("pe", TR(pt[0:96, k * 128:(k + 1) * 128], qf[:, j0 + k, :], ident), reads=[b_qf, b_const], writes=[b_pt])
                            dst = stQ[sb][:, j0:j0 + 4, c4:c4 + 128]
                            src = pt[0:96, :].rearrange("p (k t) -> p k t", t=128)
                            P.op("act" if (j0 // 4) % 2 == 0 else "dve", (ACP if (j0 // 4) % 2 == 0 else CP)(dst, src), reads=[b_pt], writes=[b_stQ[sb]])
                        if i % 4 == 3:
                            tb0 = blk * 512
                            for h in range(16):
                                P.op("pool", DMA(QT1[h, :, tb0:tb0 + 512], stQ[sb][:, h, :]), reads=[b_stQ[sb]], dma=True)
                    for j0 in range(0, 16, 4):
                        bk, b_bk = nextbank()
                        for k in range(4):
                            h = j0 + k
                            o_ = bk[0:96, k * 128:(k + 1) * 128]
                            P.op("pe", MM(o_, esel, krT, True, False), reads=[b_krT, b_const], writes=[b_bk])
                            P.op("pe", MM(o_, Wkp[:, 0, h, :], cT[:, 3, :], False, False), reads=[b_cT, b_Win], writes=[b_bk])
                            P.op("pe", MM(o_, Wkp[:, 1, h, :], cT[:, 4, :], False, True), reads=[b_cT, b_Win], writes=[b_bk])
                        dst = stK[sb][:, j0:j0 + 4, c4:c4 + 128]
                        src = bk[0:96, :].rearrange("p (k t) -> p k t", t=128)
                        P.op("act" if (j0 // 4) % 2 == 1 else "dve", (ACP if (j0 // 4) % 2 == 1 else CP)(dst, src), reads=[b_bk], writes=[b_stK[sb]])
                    for hh in range(2):
                        bk, b_bk = nextbank()
                        for kc in range(2):
                            P.op("pe", MM(bk, cT[:, 3 + kc, :], Wv[:, kc, hh * 512:(hh + 1) * 512], kc == 0, kc == 1), reads=[b_cT, b_Win], writes=[b_bk])
                        P.op("dve" if hh == 0 else "act", (CP if hh == 0 else ACP)(v1st[pa][:, hh * 8:(hh + 1) * 8, 0:64], bk.rearrange("p (h d) -> p h d", d=64)), reads=[b_bk], writes=[b_v1[pa]])
                    P.op("pool", DMA(V1[t0:t0 + 128, :, :], v1st[pa]), reads=[b_v1[pa]], dma=True)
                    if i % 4 == 3:
                        tb0 = blk * 512
                        for h in range(16):
                            P.op("pool", DMA(KT1[h, :, tb0:tb0 + 512], stK[sb][:, h, :]), reads=[b_stK[sb]], dma=True)
                    if tail["f"] is not None:
                        tail["f"]()
                    tail["f"] = tail1
            if tail["f"] is not None:
                tail["f"]()
            P.barrier()

        LA = 3
        NP = 4

        def pass_B(layer):
            abanks["l"] = [4, 5, 6, 7]
            sbanks["l"] = [0, 1, 2, 3]
            ngroups = 1 if layer == 0 else 2
            for g in range(ngroups):
                AB.reset(); AFa.reset()
                b_kv = P.buf()
                VG = 8 if NK % 8 == 0 else 4
                NVG = NK // VG

                def vsrc(ap2, j):
                    return ap2[j * VG * 128:(j + 1) * VG * 128]

                if layer == 0:
                    KTa_sb = AB.alloc([128, S]); KTb_sb = AB.alloc([128, 4, S])
                    Va_sb = AB.alloc([128, NK, 2, 65]); Vb_sb = AB.alloc([128, NK, 4, 128])
                    al_ab = AB.alloc([128, 4, 512]); al_bl = AB.alloc([128, 4, 512]); al_st = AB.alloc([128, 4, 896])
                    al_bias = AFa.alloc([128, 256])
                    b_ka = P.buf(); b_va = [P.buf() for _ in range(NVG)]; b_al = P.buf()
                    b_kb = [P.buf() for _ in range(4)]; b_vb = [P.buf() for _ in range(NVG)]
                    P.op("sp", DMA(KTa_sb, KTa[:, :]), writes=[b_ka], dma=True)
                    for j in range(NVG):
                        P.op("sp", DMA(Va_sb[:, j * VG:(j + 1) * VG, :, :], vsrc(Va, j).rearrange("(kt p) h e -> p kt h e", p=128)), writes=[b_va[j]], dma=True)
                    for h in range(4):
                        P.op("sp", DMA(al_ab[:, h, :], C["al_above"][h]), writes=[b_al], dma=True)
                        P.op("sp", DMA(al_bl[:, h, :], C["al_below"][h]), writes=[b_al], dma=True)
                        P.op("sp", DMA(al_st[:, h, :], C["al_strip"][h]), writes=[b_al], dma=True)
                    P.op("sp", DMA(al_bias, C["al_bias"][:, :]), writes=[b_al], dma=True)
                    for h in range(4):
                        P.op("sp", DMA(KTb_sb[:, h, :], KTb[h, :, :]), writes=[b_kb[h]], dma=True)
                        if h == 0:
                            for j in range(NVG):
                                P.op("sp", DMA(Vb_sb[:, j * VG:(j + 1) * VG, :, :], vsrc(Vb, j).rearrange("(kt p) h e -> p kt h e", p=128)), writes=[b_vb[j]], dma=True)
                else:
                    KT_sb = AB.alloc([96, 8, S]); V_sb = AB.alloc([128, NK, 8, 65])
                    b_k1 = [P.buf() for _ in range(8)]; b_v1 = [P.buf() for _ in range(NVG)]
                    for h in range(8):
                        P.op("sp", DMA(KT_sb[:, h, :], KT1[g * 8 + h, :, :]), writes=[b_k1[h]], dma=True)
                        if h == 0:
                            for j in range(NVG):
                                P.op("sp", DMA(V_sb[:, j * VG:(j + 1) * VG, :, :], vsrc(V1, j)[:, g * 8:(g + 1) * 8, :].rearrange("(kt p) h e -> p kt h e", p=128)), writes=[b_v1[j]], dma=True)
                b_q = [P.buf() for _ in range(2)]; b_o = [P.buf() for _ in range(2)]
                b_qb1 = P.buf(); b_ob1 = P.buf()
                if layer == 0:
                    qa_t = [AB.alloc([128, 8, 512]) for _ in range(2)]
                    qb_t = AB.alloc([128, 4, 2, 512])
                    oa_st = AB.alloc([64, 8, 512]); ob_st = AB.alloc([128, 4, 512])
                    for pa_ in range(2):
                        P.op("dve", MSET(qa_t[pa_], 0.0), writes=[b_q[pa_]])
                    P.op("dve", MSET(qb_t, 0.0), writes=[b_qb1])
                else:
                    q1_t = [AB.alloc([96, 8, 512]) for _ in range(2)]
                    o1_st = [AB.alloc([64, 8, 512]) for _ in range(2)]
                Pt = [AB.alloc([128, 512]) for _ in range(NP)]; b_P = [P.buf() for _ in range(NP)]
                if layer == 0:
                    P2 = [AB.alloc([128, 512]) for _ in range(NP)]; b_P2 = [P.buf() for _ in range(NP)]
                    sqb = AB.alloc([128, 512]); b_sqb = P.buf()
                    tc_ = [AFa.alloc([128, 512]) for _ in range(2)]; b_tc = [P.buf() for _ in range(2)]
                    tt_ = AFa.alloc([128, 512]); b_tt = P.buf()
                rl = [AFa.alloc([128, 512]) for _ in range(2)]; b_rl = [P.buf() for _ in range(2)]
                rlb = [AB.alloc([128, 2, 512]) for _ in range(2)]; b_rlb = [P.buf() for _ in range(2)]
                for u_i in range(2):
                    P.op("dve", MSET(rlb[u_i], 0.0), writes=[b_rlb[u_i]])
                bcs = [AFa.alloc([128, 512]) for _ in range(2)]; b_bcs = [P.buf() for _ in range(2)]
                pctr = {"i": 0, "u": 0}
                pend = []

                def defer(n, fn, tag=None):
                    pend.append([n, fn, tag])

                def flush_for(bufs):
                    last = -1
                    for i_, it in enumerate(pend):
                        if it[2] is not None and any(it[2] is b_ for b_ in bufs):
                            last = i_
                    for _ in range(last + 1):
                        pend.pop(0)[1]()

                def tick():
                    for it in pend:
                        it[0] -= 1
                    while pend and pend[0][0] <= 0:
                        pend.pop(0)[1]()

                def load_qb(qt):
                    sl = slice(qt * 512, (qt + 1) * 512)
                    for h in range(4):
                        for c in range(2):
                            P.op("sp", DMA(qb_t[c * 64:c * 64 + 64, h, c, :], QTb[h, c * 64:c * 64 + 64, sl]), writes=[b_qb1], dma=True)

                def load_q(qt):
                    pa = qt % 2
                    sl = slice(qt * 512, (qt + 1) * 512)
                    if layer == 0:
                        for hq in range(8):
                            P.op("sp", DMA(qa_t[pa][(hq // 4) * 64:(hq // 4) * 64 + 64, hq, :], QTa[hq, :, sl]), writes=[b_q[pa]], dma=True)
                    else:
                        for h in range(8):
                            P.op("sp", DMA(q1_t[pa][:, h, :], QT1[g * 8 + h, :, sl]), writes=[b_q[pa]], dma=True)

                def store_oa(qt):
                    sl = slice(qt * 512, (qt + 1) * 512)
                    for hq in range(8):
                        P.op("pool", DMA(OTa[hq, :, sl], oa_st[:, hq, :]), reads=[b_o[0]], dma=True)

                def store_o(qt):
                    pa = qt % 2
                    sl = slice(qt * 512, (qt + 1) * 512)
                    if layer == 0:
                        for h in range(4):
                            P.op("pool", DMA(OTb[h, :, sl], ob_st[:, h, :]), reads=[b_ob1], dma=True)
                    else:
                        for h in range(8):
                            P.op("pool", DMA(OT1[g * 8 + h, :, sl], o1_st[pa][:, h, :]), reads=[b_o[pa]], dma=True)

                class AugUnit:
                    def __init__(self, Qap, Kfn, Vfn, o_dst, b_odst, b_qb):
                        self.Qap, self.Kfn, self.Vfn, self.o_dst, self.b_odst, self.b_qb = Qap, Kfn, Vfn, o_dst, b_odst, b_qb
                        self.kts = list(range(NK))

                    def start(self):
                        self.O, self.b_O = nextA()
                        flush_for([self.b_O])
                        self.u = pctr["u"] % 2; pctr["u"] += 1

                    def front(self, kt):
                        sbk, b_sbk = nextS()
                        kap, kbuf = self.Kfn(kt)
                        P.op("pe", MM(sbk, kap, self.Qap, True, True), reads=[kbuf, self.b_qb], writes=[b_sbk])
                        pi = pctr["i"] % NP; pctr["i"] += 1
                        P.op("act", ACT(Pt[pi], sbk, AF.Exp), reads=[b_sbk], writes=[b_P[pi]])
                        return Pt[pi], b_P[pi]

                    def back(self, kt, pinfo):
                        pt, b_pt = pinfo
                        vap, vbuf = self.Vfn(kt)
                        P.op("pe", MM(self.O[0:65, :], vap, pt, kt == self.kts[0], kt == self.kts[-1]), reads=[vbuf, b_pt], writes=[self.b_O])

                    def fin(self):
                        u = self.u; O = self.O; b_O = self.b_O
                        P.op("dve", RECIP(rl[u][64:65, :], O[64:65, :]), reads=[b_O], writes=[b_rl[u]])
                        P.op("dve", CP(rlb[u][64:65, 0, :], rl[u][64:65, :]), reads=[b_rl[u]], writes=[b_rlb[u]])
                        P.op("dve", TT(rlb[u][64:65, 1, :], rl[u][64:65, :], rlb[u][64:65, 0, :], ALU.subtract), reads=[b_rl[u], b_rlb[u]], writes=[b_rlb[u]])

                        def part2():
                            bcp, b_bcp = nextS()
                            P.op("pe", MM(bcp[0:64, :], sel64, rlb[u][:, 0, :], True, False), reads=[b_rlb[u], b_const], writes=[b_bcp])
                            P.op("pe", MM(bcp[0:64, :], sel64, rlb[u][:, 1, :], False, True), reads=[b_rlb[u], b_const], writes=[b_bcp])
                            P.op("act", ACP(bcs[u][0:64, :], bcp[0:64, :]), reads=[b_bcp], writes=[b_bcs[u]])
                            P.op("dve", TT(self.o_dst, O[0:64, :], bcs[u][0:64, :], ALU.mult), reads=[b_O, b_bcs[u]], writes=[self.b_odst])
                        defer(10, part2, b_O)

                def diff_kts(h, qt):
                    sl_ = 2.0 ** (-8.0 * (h + 1) / 4)
                    out = []
                    for kt in range(NK):
                        if kt < 4 * qt:
                            dmin = 512 * qt - 128 * kt - 127
                        elif kt >= 4 * qt + 4:
                            dmin = 128 * kt - 512 * qt - 511
                        else:
                            dmin = 0
                        if sl_ * dmin < 100.0:
                            out.append(kt)
                    return out

                class DiffUnit:
                    def __init__(self, h, c, qt, Qap, Kfn, Vfn, o_dst, b_odst, b_qb):
                        self.h, self.c, self.qt = h, c, qt
                        self.kts = diff_kts(h, qt)
                        self.Qap, self.Kfn, self.Vfn, self.o_dst, self.b_odst, self.b_qb = Qap, Kfn, Vfn, o_dst, b_odst, b_qb

                    def start(self):
                        self.O, self.b_O = nextA(); self.L, self.b_L = nextA()
                        flush_for([self.b_O, self.b_L])
                        self.u = pctr["u"] % 2; pctr["u"] += 1

                    def front(self, kt):
                        h, qt = self.h, self.qt
                        sbk, b_sbk = nextS()
                        kap, kbuf = self.Kfn(kt)
                        P.op("pe", MM(sbk, kap, self.Qap, True, True), reads=[kbuf, self.b_qb], writes=[b_sbk])
                        pi = pctr["i"] % NP; pctr["i"] += 1
                        if kt < 4 * qt:
                            m_ = (512 * qt - 128 * kt) // 128
                            bcol = al_bias[:, h * 64 + m_:h * 64 + m_ + 1]; tab = al_ab[:, h, :]
                        elif kt >= 4 * qt + 4:
                            m_ = (128 * kt - 512 * qt) // 128
                            bcol = al_bias[:, h * 64 + 32 + m_:h * 64 + 32 + m_ + 1]; tab = al_bl[:, h, :]
                        else:
                            dl = 512 * qt - 128 * kt
                            bcol = al_bias[:, h * 64:h * 64 + 1]; tab = al_st[:, h, 384 + dl:384 + dl + 512]
                        P.op("act", ACT(Pt[pi], sbk, AF.Exp, bias=bcol), reads=[b_sbk, b_al], writes=[b_P[pi]])
                        P.op("dve", TT(P2[pi], Pt[pi], tab, ALU.mult), reads=[b_P[pi], b_al], writes=[b_P2[pi]])
                        return P2[pi], b_P2[pi]

                    def back(self, kt, pinfo):
                        pt, b_pt = pinfo
                        vap, vbuf = self.Vfn(kt)
                        P.op("pe", MM(self.O, vap, pt, kt == self.kts[0], kt == self.kts[-1]), reads=[vbuf, b_pt], writes=[self.b_O])
                        P.op("pe", MM(self.L, ones_b, pt, kt == self.kts[0], kt == self.kts[-1]), reads=[b_const, b_pt], writes=[self.b_L])

                    def fin(self):
                        u = self.u; c = self.c
                        P.op("act", ACT(rl[u], self.L, AF.Ln), reads=[self.b_L], writes=[b_rl[u]])
                        P.op("act", ACT(rl[u], rl[u], AF.Exp, scale=-1.0), reads=[b_rl[u]], writes=[b_rl[u]])
                        P.op("dve", TT(tc_[c], self.O, rl[u], ALU.mult), reads=[self.b_O, b_rl[u]], writes=[b_tc[c]])
                        if c == 1:
                            P.op("dve", STT(tt_, tc_[1], nlam, tc_[0], ALU.mult, ALU.add), reads=[b_tc[0], b_tc[1], b_const], writes=[b_tt])
                            P.op("act", ACT(sqb, tt_, AF.Square), reads=[b_tt], writes=[b_sqb])

                            def part2():
                                ssb, b_ssb = nextS()
                                P.op("pe", MM(ssb, ones_b, sqb, True, True), reads=[b_const, b_sqb], writes=[b_ssb])
                                P.op("act", ACT(rl[u], ssb, AF.Ln, bias=EPS, scale=1.0 / 128), reads=[b_ssb], writes=[b_rl[u]])
                                P.op("act", ACT(rl[u], rl[u], AF.Exp, scale=-0.5), reads=[b_rl[u]], writes=[b_rl[u]])
                                P.op("dve", STT(self.o_dst, tt_, gsub_c, rl[u], ALU.mult, ALU.mult), reads=[b_tt, b_rl[u], b_const], writes=[self.b_odst])
                            defer(6, part2)

                stream = []
                for qt in range(NT):
                    pa = qt % 2
                    units = []
                    if layer == 0:
                        for hq in range(8):
                            kvh = hq // 4
                            units.append(AugUnit(qa_t[pa][:, hq, :],
                                                 lambda kt: (KTa_sb[:, kt * 128:(kt + 1) * 128], b_ka),
                                                 lambda kt, kvh=kvh: (Va_sb[:, kt, kvh, :], b_va[kt // VG]), oa_st[:, hq, :], b_o[0], b_q[pa]))
                        for h in range(4):
                            for c in range(2):
                                units.append(DiffUnit(h, c, qt, qb_t[:, h, c, :],
                                                      lambda kt, h=h: (KTb_sb[:, h, kt * 128:(kt + 1) * 128], b_kb[h]),
                                                      lambda kt, h=h: (Vb_sb[:, kt, h, :], b_vb[kt // VG]), ob_st[:, h, :], b_ob1, b_qb1))
                    else:
                        for h in range(8):
                            units.append(AugUnit(q1_t[pa][:, h, :], lambda kt, h=h: (KT_sb[:, h, kt * 128:(kt + 1) * 128], b_k1[h]),
                                                 lambda kt, h=h: (V_sb[:, kt, h, :], b_v1[kt // VG]), o1_st[pa][:, h, :], b_o[pa], b_q[pa]))
                    for ui, u_ in enumerate(units):
                        for kt in u_.kts:
                            stream.append((u_, kt, qt, ui == 0 and kt == u_.kts[0], ui == len(units) - 1 and kt == u_.kts[-1],
                                           layer == 0 and ui == 7 and kt == u_.kts[-1]))
                load_q(0)
                if layer == 0:
                    load_qb(0)
                inflight = []
                for idx in range(len(stream) + LA):
                    if idx < len(stream):
                        u_, kt, qt, first, lastq, _ = stream[idx]
                        if first and qt + 1 < NT:
                            load_q(qt + 1)
                        if kt == u_.kts[0]:
                            u_.start()
                        inflight.append((u_, kt, qt, u_.front(kt)))
                        if lastq and layer == 0 and qt + 1 < NT:
                            load_qb(qt + 1)
                    if idx >= LA:
                        u_, kt, qt, pinfo = inflight.pop(0)
                        u_.back(kt, pinfo)
                        if kt == u_.kts[-1]:
                            u_.fin()
                            if stream[idx - LA][4]:
                                defer(14, lambda qt=qt: store_o(qt))
                            if stream[idx - LA][5]:
                                defer(14, lambda qt=qt: store_oa(qt))
                    tick()
                while pend:
                    pend.pop(0)[1]()
                P.barrier()

        def pass_C(layer, seq, xsrc, xdst, last):
            abanks["l"] = [3, 4, 5]
            sbanks["l"] = [0, 1, 2]
            AB.reset(); AFa.reset()
            NW = 2
            b_w = [P.buf() for _ in range(NW)]
            wst = [AB.alloc([128, 8, 1024]) for _ in range(NW)]
            wctr = {"i": 0}
            Wdn = AB.alloc([128, 22, 1024]); b_Wdn = P.buf()
            Kmem = AB.alloc([128, 8, 256]); Vmem = AB.alloc([128, 2, 1024]); b_mem = P.buf()
            xts = [AFa.alloc([128, 4, 1024]) for _ in range(2)]; b_xts = [[P.buf() for _ in range(4)] for _ in range(2)]
            ssr = AFa.alloc([128, 8]); b_s = [P.buf() for _ in range(4)]
            ssn = AFa.alloc([128, 8]); b_sn = [P.buf() for _ in range(4)]
            rlc = AFa.alloc([128, 512]); b_rlc = P.buf()
            hb = [AB.alloc([128, 1024]) for _ in range(4)]; b_hb = [P.buf() for _ in range(4)]
            hT = AB.alloc([128, 8, 512]); b_hT = P.buf()
            qT = AB.alloc([128, 8, 512]); b_qT = P.buf()
            o2T = hT; b_o2T = b_hT
            ots = [AB.alloc([128, 8, 512]) for _ in range(2)]; b_ots = [P.buf() for _ in range(2)]
            actT = AB.alloc([128, 22, 512]); b_act = P.buf()
            Pc = [AB.alloc([128, 512]) for _ in range(2)]; b_Pc = [P.buf() for _ in range(2)]
            sgl = Pc[0]; b_sgl = b_Pc[0]

            def wload(view_fn_list):
                s_ = wctr["i"] % NW; wctr["i"] += 1
                for dfn, src in view_fn_list:
                    P.op("sp", DMA(dfn(wst[s_]), src), writes=[b_w[s_]], dma=True)
                return wst[s_], b_w[s_]

            def load_tile(tt):
                pa = tt % 2
                t0 = tt * 512
                sl = slice(t0, t0 + 512)
                for s_ in range(4):
                    P.op("sp", DMA(xts[pa][:, s_, :], xsrc[t0 + s_ * 128:t0 + (s_ + 1) * 128, :]), writes=[b_xts[pa][s_]], dma=True)
                if layer == 0:
                    for hq in range(8):
                        P.op("sp", DMA(ots[pa][(hq % 2) * 64:(hq % 2) * 64 + 64, hq // 2, :], OTa[hq, :, sl]), writes=[b_ots[pa]], dma=True)
                    for h in range(4):
                        P.op("sp", DMA(ots[pa][:, 4 + h, :], OTb[h, :, sl]), writes=[b_ots[pa]], dma=True)
                else:
                    for h in range(16):
                        P.op("sp", DMA(ots[pa][(h % 2) * 64:(h % 2) * 64 + 64, h // 2, :], OT1[h, :, sl]), writes=[b_ots[pa]], dma=True)

            load_tile(0)
            mt = xts[1][:, 0, :]; b_mt = b_xts[1][0]
            for mi in range(2):
                P.op("sp", DMA(mt, mem_in[seq, mi * 128:(mi + 1) * 128, :]), writes=[b_mt], dma=True)
                P.op("act", ACT(hb[0], mt, AF.Square, accum=ssr[:, 0:1]), reads=[b_mt], writes=[b_hb[0], b_s[0]])
                rms_rstd(ssr[:, 0:1], ssr[:, 1:2], float(D), b_s[0], b_s[0])
                P.op("dve", TS(hb[0], mt, ssr[:, 1:2]), reads=[b_mt, b_s[0]], writes=[b_hb[0]])
                for half in range(2):
                    pt, b_pt = nextT()
                    for k in range(4):
                        kc = half * 4 + k
                        P.op("pe", TR(pt[:, k * 128:(k + 1) * 128], hb[0][:, kc * 128:(kc + 1) * 128], ident), reads=[b_hb[0], b_const], writes=[b_pt])
                    P.op("act", ACP(hT[:, half * 4:half * 4 + 4, mi * 128:(mi + 1) * 128], pt.rearrange("p (k t) -> p k t", t=128)), reads=[b_pt], writes=[b_hT])
            for half in range(2):
                wv, b_wv = wload([(lambda a: a, Wckv_s[layer][:, :, half * 1024:(half + 1) * 1024])])
                if half == 0:
                    for m in range(8):
                        bk, b_bk = nextbank()
                        for kc in range(8):
                            P.op("pe", MM(bk[:, 0:256], wv[:, kc, m * 128:(m + 1) * 128], hT[:, kc, 0:256], kc == 0, kc == 7), reads=[b_wv, b_hT], writes=[b_bk])
                        P.op("act" if m % 2 else "dve", (ACP if m % 2 else CP)(Kmem[:, m, :], bk[:, 0:256]), reads=[b_bk], writes=[b_mem])
                else:
                    for mi in range(2):
                        for nh in range(2):
                            bk, b_bk = nextbank()
                            for kc in range(8):
                                P.op("pe", MM(bk, hT[:, kc, mi * 128:(mi + 1) * 128], wv[:, kc, nh * 512:(nh + 1) * 512], kc == 0, kc == 7), reads=[b_wv, b_hT], writes=[b_bk])
                            P.op("act" if nh else "dve", (ACP if nh else CP)(Vmem[:, mi, nh * 512:(nh + 1) * 512], bk), reads=[b_bk], writes=[b_mem])

            def preload_first():
                a_ = wload([(lambda a: a, (Wout0_s if layer == 0 else Wout1_s)[:, :, :])])
                b__ = wload([(lambda a: a, Wcq_s[layer][:, :, :])])
                return a_, b__

            pre = preload_first()
            for tt in range(NT):
                pa = tt % 2
                xt = xts[pa]; b_xt = b_xts[pa]
                ot = ots[pa]; b_ot = b_ots[pa]
                t0 = tt * 512
                (w_out, b_w_out), (w_cq, b_w_cq) = pre
                if tt + 1 < NT:
                    load_tile(tt + 1)

                def norm_stage1(s_):
                    ss_ = ssn[:, 2 * s_:2 * s_ + 1]; rs_ = ssn[:, 2 * s_ + 1:2 * s_ + 2]
                    P.op("act", ACT(hb[s_], xt[:, s_, :], AF.Square, accum=ss_), reads=[b_xt[s_]], writes=[b_hb[s_], b_sn[s_]])
                    rms_rstd(ss_, rs_, float(D), b_sn[s_], b_sn[s_])

                def norm_stage1b(s_):
                    rs_ = ssn[:, 2 * s_ + 1:2 * s_ + 2]
                    P.op("dve", TS(hb[s_], xt[:, s_, :], rs_), reads=[b_xt[s_], b_sn[s_]], writes=[b_hb[s_]])

                def norm_stage2():
                    for s_ in range(4):
                        for half in range(2):
                            pt, b_pt = nextT()
                            for k in range(4):
                                kc = half * 4 + k
                                P.op("pe", TR(pt[:, k * 128:(k + 1) * 128], hb[s_][:, kc * 128:(kc + 1) * 128], ident), reads=[b_hb[s_], b_const], writes=[b_pt])
                            dst = hT[:, half * 4:half * 4 + 4, s_ * 128:(s_ + 1) * 128]
                            src = pt.rearrange("p (k t) -> p k t", t=128)
                            if half == 0:
                                P.op("act", ACP(dst, src), reads=[b_pt], writes=[b_hT])
                            else:
                                P.op("dve", CP(dst, src), reads=[b_pt], writes=[b_hT])

                def add_proj(lhs_list, b_lhs, rhs_fn, b_rhs, then_norm=False):
                    for s_ in range(4):
                        for n in range(2):
                            bk, b_bk = nextbank()
                            nl = len(lhs_list)
                            for ci, lf in enumerate(lhs_list):
                                P.op("pe", MM(bk, lf(s_), rhs_fn(ci, n), ci == 0, ci == nl - 1), reads=[b_lhs, b_rhs], writes=[b_bk])
                            P.op("dve", TT(xt[:, s_, n * 512:(n + 1) * 512], xt[:, s_, n * 512:(n + 1) * 512], bk, ALU.add), reads=[b_bk, b_xt[s_]], writes=[b_xt[s_]])
                        if then_norm:
                            norm_stage1(s_)
                            if s_ > 0:
                                norm_stage1b(s_ - 1)
                    if then_norm:
                        norm_stage1b(3)
                        norm_stage2()

                wv, b_wv = w_out, b_w_out
                lhs = [(lambda s_, c=c: ot[:, c, s_ * 128:(s_ + 1) * 128]) for c in range(8)]
                add_proj(lhs, b_ot, lambda ci, n, wv=wv: wv[:, ci, n * 512:(n + 1) * 512], b_wv, then_norm=True)
                wv, b_wv = w_cq, b_w_cq
                for m in range(8):
                    bk, b_bk = nextbank()
                    for kc in range(8):
                        P.op("pe", MM(bk, wv[:, kc, m * 128:(m + 1) * 128], hT[:, kc, :], kc == 0, kc == 7), reads=[b_wv, b_hT], writes=[b_bk])
                    P.op("act", ACT(qT[:, m, :], bk, AF.Copy, scale=1.0 / 16), reads=[b_bk], writes=[b_qT])
                for h in range(4):
                    L, b_L = nextA(); O0, b_O0 = nextA(); O1, b_O1 = nextA()
                    for mi in range(2):
                        sbk, b_sbk = nextS()
                        for dc in range(2):
                            P.op("pe", MM(sbk, Kmem[:, 2 * h + dc, mi * 128:(mi + 1) * 128], qT[:, 2 * h + dc, :], dc == 0, dc == 1), reads=[b_mem, b_qT], writes=[b_sbk])
                        P.op("act", ACT(Pc[mi], sbk, AF.Exp), reads=[b_sbk], writes=[b_Pc[mi]])
                        P.op("pe", MM(L, ones_b, Pc[mi], mi == 0, mi == 1), reads=[b_const, b_Pc[mi]], writes=[b_L])
                        P.op("pe", MM(O0, Vmem[:, mi, h * 256:h * 256 + 128], Pc[mi], mi == 0, mi == 1), reads=[b_mem, b_Pc[mi]], writes=[b_O0])
                        P.op("pe", MM(O1, Vmem[:, mi, h * 256 + 128:h * 256 + 256], Pc[mi], mi == 0, mi == 1), reads=[b_mem, b_Pc[mi]], writes=[b_O1])
                    P.op("act", ACT(rlc, L, AF.Ln), reads=[b_L], writes=[b_rlc])
                    P.op("act", ACT(rlc, rlc, AF.Exp, scale=-1.0), reads=[b_rlc], writes=[b_rlc])
                    P.op("dve", TT(o2T[:, 2 * h, :], O0, rlc, ALU.mult), reads=[b_O0, b_rlc], writes=[b_o2T])
                    P.op("dve", TT(o2T[:, 2 * h + 1, :], O1, rlc, ALU.mult), reads=[b_O1, b_rlc], writes=[b_o2T])
                wv, b_wv = wload([(lambda a: a, Wco_s[layer][:, :, :])])
                lhs = [(lambda s_, kc=kc: o2T[:, kc, s_ * 128:(s_ + 1) * 128]) for kc in range(8)]
                add_proj(lhs, b_o2T, lambda ci, n, wv=wv: wv[:, ci, n * 512:(n + 1) * 512], b_wv, then_norm=True)
                for kc in range(22):
                    P.op("sp", DMA(Wdn[:, kc, :], Wdn_s[layer][:, kc, :]), writes=[b_Wdn], dma=True)
                for j0 in range(0, 22, 4):
                    nj = min(4, 22 - j0)
                    wv, b_wv = wload([(lambda a, nj=nj: a[:, :, 0:nj * 128], Wgu_s[layer][:, :, j0 * 128:(j0 + nj) * 128]),
                                      (lambda a, nj=nj: a[:, :, 512:512 + nj * 128], Wgu_s[layer][:, :, DFF + j0 * 128:DFF + (j0 + nj) * 128])])
                    for jj in range(nj):
                        j = j0 + jj
                        gk, b_gk = nextbank(); uk, b_uk = nextbank()
                        for kc in range(8):
                            P.op("pe", MM(gk, wv[:, kc, jj * 128:(jj + 1) * 128], hT[:, kc, :], kc == 0, kc == 7), reads=[b_wv, b_hT], writes=[b_gk])
                        for kc in range(8):
                            P.op("pe", MM(uk, wv[:, kc, 512 + jj * 128:512 + (jj + 1) * 128], hT[:, kc, :], kc == 0, kc == 7), reads=[b_wv, b_hT], writes=[b_uk])
                        P.op("act", ACT(sgl, gk, AF.Silu), reads=[b_gk], writes=[b_sgl])
                        P.op("dve", TT(actT[:, j, :], sgl, uk, ALU.mult), reads=[b_sgl, b_uk], writes=[b_act])
                if tt + 1 < NT:
                    pre = preload_first()
                lhs = [(lambda s_, j=j: actT[:, j, s_ * 128:(s_ + 1) * 128]) for j in range(22)]
                add_proj(lhs, b_act, lambda ci, n: Wdn[:, ci, n * 512:(n + 1) * 512], b_Wdn)
                for s_ in range(4):
                    rows = slice(t0 + s_ * 128, t0 + (s_ + 1) * 128)
                    if last:
                        P.op("act", ACT(hb[s_ % 2], xt[:, s_, :], AF.Square, accum=ssr[:, 2:3]), reads=[b_xt[s_]], writes=[b_hb[s_ % 2], b_s[1]])
                        rms_rstd(ssr[:, 2:3], ssr[:, 3:4], float(D), b_s[1], b_s[1])
                        P.op("dve", STT(xt[:, s_, :], xt[:, s_, :], ssr[:, 3:4], fin_b, ALU.mult, ALU.mult), reads=[b_xt[s_], b_s[1], b_const], writes=[b_xt[s_]])
                    P.op("pool", DMA(xdst[rows, :], xt[:, s_, :]), reads=[b_xt[s_]], dma=True)
            P.barrier()

        for seq in range(NSEQ if stop_after != "P" else 0):
            for layer in range(2):
                xsrc = x_in[seq] if layer == 0 else X1
                xdst = X1 if layer == 0 else y_out[seq]
                pass_A(layer, xsrc)
                if stop_after == "A":
                    break
                pass_B(layer)
                if stop_after == "B":
                    break
                pass_C(layer, seq, xsrc, xdst, layer == 1)
                if stop_after == "C0":
                    break
        P.barrier()
        P.finalize()
        build.stats = {e: len(P.ops[e]) for e in ENGS}
        P.emit(nc, st)
    return nc


_CACHE = {}


def kernel(x_prompt, x_sample, mem_prompt, mem_sample, **w):
    S = x_prompt.shape[1]
    xs = np.concatenate([np.asarray(x_prompt, np.float32), np.asarray(x_sample, np.float32)], axis=0)
    ms = np.concatenate([np.asarray(mem_prompt, np.float32), np.asarray(mem_sample, np.float32)], axis=0)
    ntot = xs.shape[0]
    nseq = ntot // N_CORES
    key = (nseq, S)
    if key not in _CACHE:
        _CACHE[key] = build(nseq, S)
    nc = _CACHE[key]
    consts = host_consts(S)
    wd = {n: np.ascontiguousarray(np.asarray(w[n], np.float32)) for n in WNAMES}
    in_maps = []
    for c in range(N_CORES):
        m = {"x": np.ascontiguousarray(xs[c * nseq:(c + 1) * nseq]), "mem": np.ascontiguousarray(ms[c * nseq:(c + 1) * nseq])}
        m.update(wd)
        m.update(consts)
        in_maps.append(m)
    res = run_bass_kernel_spmd(nc, in_maps, core_ids=list(range(N_CORES)))
    ys = np.concatenate([np.asarray(r["y"], np.float32) for r in res.results], axis=0)
    nb = x_prompt.shape[0]
    return (ys[:nb], ys[nb:])
```

```python
import math
from contextlib import ExitStack

import ml_dtypes
import numpy as np

import concourse.bass as bass
import concourse.mybir as mybir
from concourse.bass_utils import run_bass_kernel_spmd

F32 = mybir.dt.float32
BF16 = mybir.dt.bfloat16
ALU = mybir.AluOpType
AF = mybir.ActivationFunctionType
AX = mybir.AxisListType

D = 1024
DFF = 2816
NMEM = 256
EPS = 1e-6
GRID_W = 64
N_CORES = 8

ENGS = ["pe", "act", "dve", "pool", "sp"]
N_DMA_SEMS = 24


class Buf:
    __slots__ = ("name", "w", "r", "excl")

    def __init__(self, name, excl=False):
        self.name = name
        self.w = None
        self.r = []
        self.excl = excl


class Op:
    __slots__ = ("eng", "fn", "deps", "signal", "sem", "val", "dma", "waits")

    def __init__(self, eng, fn, dma):
        self.eng = eng
        self.fn = fn
        self.dma = dma
        self.deps = set()
        self.signal = False
        self.sem = None
        self.val = 0
        self.waits = []


class Plan:
    def __init__(self):
        self.ops = {e: [] for e in ENGS}
        self.all = []
        self.dma_rr = {"sp": 0, "pool": 0, "act": 0}
        self.dma_last = [None] * N_DMA_SEMS
        self.nbuf = 0

    def buf(self, name=None, excl=False):
        self.nbuf += 1
        return Buf(name or f"b{self.nbuf}", excl)

    def op(self, eng, fn, reads=(), writes=(), dma=False, after=()):
        o = Op(eng, fn, dma)
        deps = o.deps
        for b in reads:
            if b.w is not None:
                deps.add(b.w)
            if b.excl:
                for q in b.r:
                    if q.eng != eng:
                        deps.add(q)
        for b in writes:
            if b.w is not None:
                deps.add(b.w)
            deps.update(b.r)
        for a in after:
            if a is not None:
                deps.add(a)
        if dma:
            half = N_DMA_SEMS // 2
            k = (self.dma_rr[eng] % half) + (half if eng == "pool" else 0)
            self.dma_rr[eng] += 1
            prev = self.dma_last[k]
            if prev is not None:
                deps.add(prev)
            self.dma_last[k] = o
            o.sem = ("dma", k)
        else:
            o.sem = ("eng", eng)
        for b in reads:
            if not dma:
                b.r = [q for q in b.r if q.dma or q.eng != eng]
            b.r.append(o)
        for b in writes:
            b.w = o
            b.r = []
        self.ops[eng].append(o)
        self.all.append(o)
        return o

    def barrier(self):
        lasts = []
        for e in ENGS:
            if self.ops[e]:
                lasts.append(self.ops[e][-1])
        lasts += [d for d in self.dma_last if d is not None]
        for e in ENGS:
            self.op(e, None, after=lasts)

    @staticmethod
    def _skip(d, o):
        return (not d.dma) and (not o.dma) and d.eng == "pe" and o.eng == "pe"

    def finalize(self):
        for e in ENGS:
            for o in self.ops[e]:
                for d in o.deps:
                    if d.dma or self._skip(d, o):
                        continue
                    d.signal = True
        cnt = {}
        for o in self.all:
            if o.dma:
                cnt[o.sem] = cnt.get(o.sem, 0) + 16
                o.val = cnt[o.sem]
                o.signal = True
            elif o.signal:
                if o.fn is None:
                    o.signal = False
                    continue
                cnt[o.sem] = cnt.get(o.sem, 0) + 1
                o.val = cnt[o.sem]
        for e in ENGS:
            waited = {}
            for o in self.ops[e]:
                need = {}
                for d in o.deps:
                    if self._skip(d, o) or d is o:
                        continue
                    if d.fn is None:
                        continue
                    v = need.get(d.sem, 0)
                    if d.val > v:
                        need[d.sem] = d.val
                for s, v in need.items():
                    if waited.get(s, 0) < v:
                        waited[s] = v
                        o.waits.append((s, v))
        self.counts = cnt

    def emit(self, nc, stack):
        sems = {}
        for e in ENGS:
            sems[("eng", e)] = stack.enter_context(nc.semaphore(f"s_{e}"))
        for k in range(N_DMA_SEMS):
            sems[("dma", k)] = stack.enter_context(nc.semaphore(f"s_dma{k}"))
        block = stack.enter_context(nc.Block())
        plan = self

        def replay(ename):
            def run(h):
                for o in plan.ops[ename]:
                    for s, v in o.waits:
                        h.wait_ge(sems[s], v)
                    if o.fn is None:
                        continue
                    inst = o.fn(h)
                    if o.signal:
                        inst.then_inc(sems[o.sem], 16 if o.dma else 1)
            return run

        block.tensor(replay("pe"))
        block.scalar(replay("act"))
        block.vector(replay("dve"))
        block.gpsimd(replay("pool"))
        block.sync(replay("sp"))


def MM(out, lhsT, rhs, start=True, stop=True):
    return lambda e: e.matmul(out, lhsT=lhsT, rhs=rhs, start=start, stop=stop)


def TR(out, in_, ident):
    return lambda e: e.transpose(out=out, in_=in_, identity=ident)


def ACT(out, in_, func, bias=None, scale=None, accum=None):
    kw = {}
    if bias is not None:
        kw["bias"] = bias
    if scale is not None:
        kw["scale"] = scale
    if accum is not None:
        kw["accum_out"] = accum
    return lambda e: e.activation(out=out, in_=in_, func=func, **kw)


def DMA(out, in_, slow=False):
    if slow:
        return lambda e: e.dma_start(out=out, in_=in_, allow_slow_non_contiguous=True)
    return lambda e: e.dma_start(out=out, in_=in_)


def TS(out, in0, s1, s2=None, op0=ALU.mult, op1=None):
    if op1 is None:
        return lambda e: e.tensor_scalar(out=out, in0=in0, scalar1=s1, scalar2=None, op0=op0)
    return lambda e: e.tensor_scalar(out=out, in0=in0, scalar1=s1, scalar2=s2, op0=op0, op1=op1)


def TT(out, in0, in1, op):
    return lambda e: e.tensor_tensor(out=out, in0=in0, in1=in1, op=op)


def STT(out, in0, scalar, in1, op0, op1):
    return lambda e: e.scalar_tensor_tensor(out=out, in0=in0, scalar=scalar, in1=in1, op0=op0, op1=op1)


def CP(out, in_):
    return lambda e: e.tensor_copy(out=out, in_=in_)


def ACP(out, in_):
    return lambda e: e.copy(out=out, in_=in_)


def RECIP(out, in_):
    return lambda e: e.reciprocal(out=out, in_=in_)


def RSUM(out, in_):
    return lambda e: e.tensor_reduce(out=out, in_=in_, axis=AX.X, op=ALU.add)


def MSET(ap, v):
    return lambda e: e.memset(ap, v)


class Arena:
    def __init__(self, t, n):
        self.t = t
        self.n = n
        self.off = 0

    def reset(self):
        self.off = 0

    def alloc(self, shape):
        n = 1
        for s in shape[1:]:
            n *= s
        n = (n + 15) // 16 * 16
        o = self.off
        self.off += n
        assert self.off <= self.n, (self.off, self.n, shape)
        m = 1
        for s in shape[1:]:
            m *= s
        v = self.t[0:shape[0], o:o + m]
        if len(shape) == 3:
            v = v.rearrange("p (a b) -> p a b", b=shape[2])
        elif len(shape) == 4:
            v = v.rearrange("p (a b c) -> p a b c", b=shape[2], c=shape[3])
        return v


AB_N = 82944
AF_N = 8768

WNAMES = ["norm_mix", "e_w_in", "e_q_norm", "e_k_norm", "e_lam_q1", "e_lam_k1", "e_lam_q2", "e_lam_k2",
          "e_subln", "e_w_out", "o_w_in", "o_q_norm", "o_kv_norm", "o_w_uq", "o_w_ukv", "o_w_out",
          "norm_cross", "norm_mem", "w_cq", "w_ckv", "w_co", "norm_ffn", "w_gu", "w_down", "final_norm"]
WSHAPES = {
    "norm_mix": [2, D], "e_w_in": [1, D, 2304], "e_q_norm": [1, 64], "e_k_norm": [1, 64],
    "e_lam_q1": [1, 64], "e_lam_k1": [1, 64], "e_lam_q2": [1, 64], "e_lam_k2": [1, 64],
    "e_subln": [1, 128], "e_w_out": [1, D, D], "o_w_in": [1, D, 672], "o_q_norm": [1, 384],
    "o_kv_norm": [1, 256], "o_w_uq": [1, 384, 1536], "o_w_ukv": [1, 256, 2048], "o_w_out": [1, D, D],
    "norm_cross": [2, D], "norm_mem": [2, D], "w_cq": [2, D, D], "w_ckv": [2, D, 2 * D], "w_co": [2, D, D],
    "norm_ffn": [2, D], "w_gu": [2, D, 2 * DFF], "w_down": [2, DFF, D], "final_norm": [D],
}


def host_consts(S):
    c = {}
    c["ident"] = np.eye(128, dtype=np.float32).astype(ml_dtypes.bfloat16)
    c["ones_b"] = np.ones((128, 128), dtype=np.float32).astype(ml_dtypes.bfloat16)
    c["ones_f"] = np.ones((128, 64), dtype=np.float32)
    es = np.zeros((32, 96), dtype=np.float32)
    es[np.arange(32), 64 + np.arange(32)] = 1.0
    c["esel"] = es.astype(ml_dtypes.bfloat16)
    s64 = np.zeros((128, 64), dtype=np.float32)
    s64[64, :] = 1.0
    c["sel64"] = s64.astype(ml_dtypes.bfloat16)
    t = np.arange(S)
    fa = (10000.0 ** (-np.arange(16, dtype=np.float32) / 16)).astype(np.float32)
    r = (t // GRID_W).astype(np.float32)
    cc = (t % GRID_W).astype(np.float32)
    angA = np.concatenate([r[:, None] * fa, cc[:, None] * fa], axis=-1).astype(np.float32)
    fl = (10000.0 ** (-np.arange(16, dtype=np.float32) / 16)).astype(np.float32)
    angL = (t.astype(np.float32)[:, None] * fl).astype(np.float32)

    def exp_tabs(ang):
        co = np.repeat(np.cos(ang), 2, axis=-1).astype(np.float32)
        si = np.repeat(np.sin(ang), 2, axis=-1).astype(np.float32)
        si[:, 0::2] *= -1.0
        return co, si

    c["cexpA"], c["sexpA"] = exp_tabs(angA)
    c["cexpL"], c["sexpL"] = exp_tabs(angL)
    slopes = (2.0 ** (-8.0 * np.arange(1, 5) / 4)).astype(np.float64)
    p = np.arange(128)[:, None].astype(np.float64)
    j = np.arange(512)[None, :].astype(np.float64)
    m = np.arange(896)[None, :].astype(np.float64)
    ab = np.zeros((4, 128, 512)); bl = np.zeros((4, 128, 512)); st = np.zeros((4, 128, 896))
    for h in range(4):
        ab[h] = np.exp(-slopes[h] * (j - p + 127))
        bl[h] = np.exp(-slopes[h] * (p - j + 511))
        st[h] = np.exp(-slopes[h] * np.abs(m - 384 - p))
    c["al_above"] = ab.astype(np.float32).astype(ml_dtypes.bfloat16)
    c["al_below"] = bl.astype(np.float32).astype(ml_dtypes.bfloat16)
    c["al_strip"] = st.astype(np.float32).astype(ml_dtypes.bfloat16)
    bc = np.zeros((128, 4, 64), dtype=np.float32)
    for h in range(4):
        for mm_ in range(1, 32):
            bc[:, h, mm_] = -slopes[h] * (128 * mm_ - 127)
        for mm_ in range(4, 32):
            bc[:, h, 32 + mm_] = -slopes[h] * (128 * mm_ - 511)
    c["al_bias"] = bc.reshape(128, 256)
    return c


CONST_SPECS = lambda S: {
    "ident": ([128, 128], BF16), "ones_b": ([128, 128], BF16), "ones_f": ([128, 64], F32), "esel": ([32, 96], BF16), "sel64": ([128, 64], BF16),
    "cexpA": ([S, 64], F32), "sexpA": ([S, 64], F32), "cexpL": ([S, 32], F32), "sexpL": ([S, 32], F32),
    "al_above": ([4, 128, 512], BF16), "al_below": ([4, 128, 512], BF16), "al_strip": ([4, 128, 896], BF16),
    "al_bias": ([128, 256], F32),
}


def build(NSEQ, S, stop_after=None):
    NK = S // 128
    NT = S // 512
    nc = bass.Bass("TRN2", target_bir_lowering=False)

    def din(name, shape, dt=F32):
        return nc.dram_tensor(name, list(shape), dt, kind="ExternalInput").ap()

    def dscr(name, shape, dt=BF16):
        return nc.dram_tensor(name, list(shape), dt).ap()

    x_in = din("x", [NSEQ, S, D])
    mem_in = din("mem", [NSEQ, NMEM, D])
    W = {n: din(n, WSHAPES[n]) for n in WNAMES}
    C = {n: din(n, sh, dt) for n, (sh, dt) in CONST_SPECS(S).items()}
    y_out = nc.dram_tensor("y", [NSEQ, S, D], F32, kind="ExternalOutput").ap()

    Win0_s = dscr("Win0_s", [128, 8, 2304])
    Wout0_s = dscr("Wout0_s", [128, 8, 1024])
    Win1_s = dscr("Win1_s", [128, 8, 672])
    Wuq_s = dscr("Wuq_s", [128, 3, 1536])
    Wkp_s = dscr("Wkp_s", [128, 2, 16, 96])
    Wv_s = dscr("Wv_s", [128, 2, 1024])
    Wout1_s = dscr("Wout1_s", [128, 8, 1024])
    Wcq_s = [dscr(f"Wcq_s{l}", [128, 8, 1024]) for l in range(2)]
    Wckv_s = [dscr(f"Wckv_s{l}", [128, 8, 2048]) for l in range(2)]
    Wco_s = [dscr(f"Wco_s{l}", [128, 8, 1024]) for l in range(2)]
    Wgu_s = [dscr(f"Wgu_s{l}", [128, 8, 2 * DFF]) for l in range(2)]
    Wdn_s = [dscr(f"Wdn_s{l}", [128, 22, 1024]) for l in range(2)]
    QTa = dscr("QTa", [8, 64, S]); QTb = dscr("QTb", [4, 128, S]); QT1 = dscr("QT1", [16, 96, S])
    KTa = dscr("KTa", [128, S]); KTb = dscr("KTb", [4, 128, S]); KT1 = dscr("KT1", [16, 96, S])
    Va = dscr("Va", [S, 2, 65]); Vb = dscr("Vb", [S, 4, 128]); V1 = dscr("V1", [S, 16, 65])
    OTa = dscr("OTa", [8, 64, S]); OTb = dscr("OTb", [4, 128, S]); OT1 = dscr("OT1", [16, 64, S])
    X1 = dscr("X1", [S, D], F32)

    st = ExitStack()
    with st:
        abt = st.enter_context(nc.sbuf_tensor("arena_b", [128, AB_N], BF16))
        aft = st.enter_context(nc.sbuf_tensor("arena_f", [128, AF_N], F32))
        cft = st.enter_context(nc.sbuf_tensor("const_f", [128, 1792], F32))
        cbt = st.enter_context(nc.sbuf_tensor("const_b", [128, 128 + 128 + 96 + 64], BF16))
        psf = [st.enter_context(nc.psum_tensor(f"psf{i}", [128, 512], F32)) for i in range(8)]
        psT = [psf[6][:].bitcast(BF16), psf[7][:].bitcast(BF16)]
        AB = Arena(abt, AB_N)
        AFa = Arena(aft, AF_N)
        P = Plan()

        ident = cbt[:, 0:128]
        ones_b = cbt[:, 128:256]
        esel = cbt[0:32, 256:352]
        sel64 = cbt[:, 352:416]
        ones_f = cft[:, 0:64]
        gcols = cft[:, 64:64 + 80]
        GC = {"mix0": 0, "mix1": 8, "cross0": 16, "cross1": 24, "mem0": 32, "mem1": 40, "ffn0": 48, "ffn1": 56,
              "qn": 64, "kvn": 67, "one": 69}
        gq_b = cft[:, 160:224]; gk_b = cft[:, 224:288]; gqs_b = cft[:, 288:352]; gks_b = cft[:, 352:416]
        lam_t = cft[:, 416:420]
        gsub_c = cft[:, 420:421]
        lamw = cft[:, 424:424 + 4 * 64]
        fin_b = cft[:, 768:1792]
        b_const = P.buf("const")

        def ld(dst, src, slow=False):
            P.op("sp", DMA(dst, src, slow), writes=[b_const], dma=True)

        ld(ident, C["ident"][:, :]); ld(ones_b, C["ones_b"][:, :]); ld(esel, C["esel"][:, :]); ld(ones_f, C["ones_f"][:, :]); ld(sel64, C["sel64"][:, :])
        for nm, src, kc in [("mix0", W["norm_mix"][0], 8), ("mix1", W["norm_mix"][1], 8),
                            ("cross0", W["norm_cross"][0], 8), ("cross1", W["norm_cross"][1], 8),
                            ("mem0", W["norm_mem"][0], 8), ("mem1", W["norm_mem"][1], 8),
                            ("ffn0", W["norm_ffn"][0], 8), ("ffn1", W["norm_ffn"][1], 8),
                            ("qn", W["o_q_norm"][0], 3), ("kvn", W["o_kv_norm"][0], 2)]:
            ld(gcols[:, GC[nm]:GC[nm] + kc], src.rearrange("(kc p) -> p kc", p=128), slow=True)
        P.op("dve", MSET(gcols[:, GC["one"]:GC["one"] + 1], 1.0), writes=[b_const])
        ld(gq_b, W["e_q_norm"][0].partition_broadcast(128)); ld(gk_b, W["e_k_norm"][0].partition_broadcast(128))
        ld(fin_b, W["final_norm"].partition_broadcast(128))
        for i, nm in enumerate(["e_lam_q1", "e_lam_k1", "e_lam_q2", "e_lam_k2"]):
            ld(lamw[:, i * 64:(i + 1) * 64], W[nm][0].partition_broadcast(128))
        ld(gsub_c, W["e_subln"][0].rearrange("(p o) -> p o", o=1), slow=True)
        for g, gs in [(gq_b, gqs_b), (gk_b, gks_b)]:
            gv = g.rearrange("p (i two) -> p i two", two=2); sv = gs.rearrange("p (i two) -> p i two", two=2)
            P.op("dve", CP(sv[:, :, 0:1], gv[:, :, 1:2]), reads=[b_const], writes=[b_const])
            P.op("dve", CP(sv[:, :, 1:2], gv[:, :, 0:1]), reads=[b_const], writes=[b_const])
        lam_init0 = 0.8 - 0.6 * math.exp(-0.3 * 0)
        tmpl = cft[:, 440 + 256:440 + 256 + 64]
        P.op("dve", TT(tmpl, lamw[:, 0:64], lamw[:, 64:128], ALU.mult), reads=[b_const], writes=[b_const])
        P.op("dve", RSUM(lam_t[:, 0:1], tmpl), reads=[b_const], writes=[b_const])
        P.op("dve", TT(tmpl, lamw[:, 128:192], lamw[:, 192:256], ALU.mult), reads=[b_const], writes=[b_const])
        P.op("dve", RSUM(lam_t[:, 1:2], tmpl), reads=[b_const], writes=[b_const])
        P.op("act", ACT(lam_t[:, 0:2], lam_t[:, 0:2], AF.Exp), reads=[b_const], writes=[b_const])
        P.op("dve", TT(lam_t[:, 2:3], lam_t[:, 0:1], lam_t[:, 1:2], ALU.subtract), reads=[b_const], writes=[b_const])
        P.op("dve", TS(lam_t[:, 3:4], lam_t[:, 2:3], lam_init0, -1.0, ALU.add, ALU.mult), reads=[b_const], writes=[b_const])
        nlam = lam_t[:, 3:4]
        P.op("dve", TS(gsub_c, gsub_c, 1.0 - lam_init0), reads=[b_const], writes=[b_const])

        AB.reset(); AFa.reset()
        NST = 3
        stf = [AFa.alloc([128, 2048]) for _ in range(NST)]
        stb = [AB.alloc([128, 2048]) for _ in range(NST)]
        zt = AB.alloc([128, 16, 32])
        b_stf = [P.buf() for _ in range(NST)]; b_stb = [P.buf() for _ in range(NST)]
        b_z = P.buf()
        P.op("dve", MSET(zt, 0.0), writes=[b_z])
        cstate = {"i": 0}

        def conv_piece(src2d, rows, cols, gain_col, outs):
            i = cstate["i"]; cstate["i"] += 1
            s = i % NST
            r0, r1 = rows; c0, c1 = cols
            np_, w = r1 - r0, c1 - c0
            P.op("sp", DMA(stf[s][0:np_, 0:w], src2d[r0:r1, c0:c1]), writes=[b_stf[s]], dma=True)
            if i % 2 == 0:
                P.op("dve", TS(stb[s][0:np_, 0:w], stf[s][0:np_, 0:w], gain_col[0:np_, :]), reads=[b_stf[s], b_const], writes=[b_stb[s]])
            else:
                P.op("act", ACT(stb[s][0:np_, 0:w], stf[s][0:np_, 0:w], AF.Copy, scale=gain_col[0:np_, :]), reads=[b_stf[s], b_const], writes=[b_stb[s]])
            for dst, vf in outs:
                P.op("pool", DMA(dst, vf(stb[s][0:np_, 0:w])), reads=[b_stb[s]], dma=True)

        def gc(nm, kc):
            return gcols[:, GC[nm] + kc:GC[nm] + kc + 1]

        one_c = gcols[:, GC["one"]:GC["one"] + 1]
        idv = lambda v: v

        def conv(src2d, KC, rows_p, col_ranges, dst, gname):
            for kc in range(KC):
                g = gc(gname, kc) if gname else one_c
                for (c0, c1, d0) in col_ranges:
                    for cc in range(c0, c1, 2048):
                        ce = min(cc + 2048, c1)
                        conv_piece(src2d, (kc * rows_p, (kc + 1) * rows_p), (cc, ce), g,
                                   [(dst[0:rows_p, kc, d0 + cc - c0:d0 + ce - c0], idv)])

        conv(W["e_w_in"][0], 8, 128, [(0, 512, 0), (768, 1280, 512), (512, 768, 1024), (1280, 2304, 1280)], Win0_s, "mix0")
        conv(W["e_w_out"][0], 8, 128, [(0, 1024, 0)], Wout0_s, None)
        conv(W["o_w_in"][0], 8, 128, [(0, 672, 0)], Win1_s, "mix1")
        conv(W["o_w_uq"][0], 3, 128, [(0, 1536, 0)], Wuq_s, "qn")
        for kc in range(2):
            conv_piece(W["o_w_ukv"][0], (kc * 128, (kc + 1) * 128), (0, 2048), gc("kvn", kc), [
                (Wkp_s[:, kc, :, 0:64], lambda v: v.rearrange("p (h e) -> p h e", e=128)[:, :, 0:64]),
                (Wv_s[:, kc, :].rearrange("p (h e) -> p h e", e=64), lambda v: v.rearrange("p (h e) -> p h e", e=128)[:, :, 64:128]),
            ])
            P.op("pool", DMA(Wkp_s[:, kc, :, 64:96], zt), reads=[b_z], dma=True)
        conv(W["o_w_out"][0], 8, 128, [(0, 1024, 0)], Wout1_s, None)
        for l in range(2):
            conv(W["w_cq"][l], 8, 128, [(0, 1024, 0)], Wcq_s[l], f"cross{l}")
            conv(W["w_ckv"][l], 8, 128, [(0, 2048, 0)], Wckv_s[l], f"mem{l}")
            conv(W["w_co"][l], 8, 128, [(0, 1024, 0)], Wco_s[l], None)
            conv(W["w_gu"][l], 8, 128, [(0, 2 * DFF, 0)], Wgu_s[l], f"ffn{l}")
            conv(W["w_down"][l], 22, 128, [(0, 1024, 0)], Wdn_s[l], None)
        P.barrier()

        bank_b = [P.buf(f"psf{i}", excl=True) for i in range(8)]
        ring = {"i": 0}

        def nextbank():
            i = ring["i"] % 6
            ring["i"] += 1
            return psf[i][:], bank_b[i]

        sring = {"i": 0}
        aring = {"i": 0}

        sbanks = {"l": [0, 1, 2]}

        def nextS():
            i = sbanks["l"][sring["i"] % len(sbanks["l"])]
            sring["i"] += 1
            return psf[i][:], bank_b[i]

        abanks = {"l": [3, 4, 5]}

        def nextA():
            i = abanks["l"][aring["i"] % len(abanks["l"])]
            aring["i"] += 1
            return psf[i][:], bank_b[i]

        tring = {"i": 0}

        def nextT():
            i = tring["i"] % 2
            tring["i"] += 1
            return psT[i][:, 0:512], bank_b[6 + i]

        def rms_rstd(ss, rs, n, b_ss, b_rs):
            P.op("act", ACT(rs, ss, AF.Ln, bias=EPS, scale=1.0 / n), reads=[b_ss], writes=[b_rs])
            P.op("act", ACT(rs, rs, AF.Exp, scale=-0.5), reads=[b_rs], writes=[b_rs])

        def rope(out, v, ta, tb, tmp, nh, nd, b_v, b_tab, b_tmp, b_out, eng="dve"):
            t1, t2 = tmp
            tab = ta.unsqueeze(1).to_broadcast([128, nh, nd])
            P.op(eng, TT(t1, v, tab, ALU.mult), reads=[b_v, b_tab], writes=[b_tmp])
            vv = v.rearrange("p h (i two) -> p h i two", two=2)
            t2v = t2.rearrange("p h (i two) -> p h i two", two=2)
            tbv = tb.rearrange("p (i two) -> p i two", two=2).unsqueeze(1).to_broadcast([128, nh, nd // 2, 2])
            P.op(eng, TT(t2v[:, :, :, 0:1], vv[:, :, :, 1:2], tbv[:, :, :, 0:1], ALU.mult), reads=[b_v, b_tab], writes=[b_tmp])
            P.op(eng, TT(t2v[:, :, :, 1:2], vv[:, :, :, 0:1], tbv[:, :, :, 1:2], ALU.mult), reads=[b_v, b_tab], writes=[b_tmp])
            P.op(eng, TT(out, t1, t2, ALU.add), reads=[b_tmp], writes=[b_out])

        def norm_to_hT(xt, b_xt, junk, b_junk, ss, rs, b_s, hb, b_hb, hT, b_hT, col0):
            P.op("act", ACT(hb, xt, AF.Square, accum=ss), reads=[b_xt], writes=[b_hb, b_s])
            rms_rstd(ss, rs, float(D), b_s, b_s)
            P.op("dve", TS(hb, xt, rs), reads=[b_xt, b_s], writes=[b_hb])
            for half in range(2):
                pt, b_pt = nextT()
                for k in range(4):
                    kc = half * 4 + k
                    P.op("pe", TR(pt[:, k * 128:(k + 1) * 128], hb[:, kc * 128:(kc + 1) * 128], ident), reads=[b_hb, b_const], writes=[b_pt])
                dst = hT[:, half * 4:half * 4 + 4, col0:col0 + 128]
                src = pt.rearrange("p (k t) -> p k t", t=128)
                if half == 0:
                    P.op("act", ACP(dst, src), reads=[b_pt], writes=[b_hT])
                else:
                    P.op("dve", CP(dst, src), reads=[b_pt], writes=[b_hT])

        def pass_A(layer, xsrc):
            AB.reset(); AFa.reset()
            ncols = 2304 if layer == 0 else 672
            Win = AB.alloc([128, 8, ncols]); b_Win = P.buf()
            Wsrc = Win0_s if layer == 0 else Win1_s
            for kc in range(8):
                P.op("sp", DMA(Win[:, kc, :], Wsrc[:, kc, :]), writes=[b_Win], dma=True)
            if layer == 1:
                Wuq = AB.alloc([128, 3, 1536]); Wkp = AB.alloc([128, 2, 16, 96]); Wv = AB.alloc([128, 2, 1024])
                for kc in range(3):
                    P.op("sp", DMA(Wuq[:, kc, :], Wuq_s[:, kc, :]), writes=[b_Win], dma=True)
                for kc in range(2):
                    P.op("sp", DMA(Wkp[:, kc, :, :], Wkp_s[:, kc, :, :]), writes=[b_Win], dma=True)
                    P.op("sp", DMA(Wv[:, kc, :], Wv_s[:, kc, :]), writes=[b_Win], dma=True)
            xt = [AFa.alloc([128, 1024]) for _ in range(2)]; b_xt = [P.buf() for _ in range(2)]
            junk = AFa.alloc([128, 1024]); b_junk = P.buf()
            ssr = [AFa.alloc([128, 4]) for _ in range(2)]; b_s = [P.buf() for _ in range(2)]
            hb = [AB.alloc([128, 1024]) for _ in range(2)]; b_hb = [P.buf() for _ in range(2)]
            hT = [AB.alloc([128, 8, 128]) for _ in range(2)]; b_hT = [P.buf() for _ in range(2)]
            tabs = [AFa.alloc([128, 128]) for _ in range(2)]; b_tabs = [P.buf() for _ in range(2)]
            tq = [AFa.alloc([128, 256]) for _ in range(2)]; b_tq = [P.buf() for _ in range(2)]
            t1 = AFa.alloc([128, 512]); t2 = AFa.alloc([128, 512]); b_t = P.buf()
            qn = AFa.alloc([128, 512]); b_qn = P.buf()
            sq8 = AFa.alloc([128, 16]); b_sq8 = P.buf()
            if layer == 0:
                nstT = 13
            else:
                nstT = 16
            if layer == 0:
                qf2 = [AB.alloc([128, 1024]) for _ in range(2)]; b_qf2 = [P.buf() for _ in range(2)]
                kf2 = [AB.alloc([128, 640]) for _ in range(2)]; b_kf2 = [P.buf() for _ in range(2)]
                stT = [AB.alloc([128, 13, 512]) for _ in range(2)]; b_stT = [P.buf() for _ in range(2)]
                vast = [AB.alloc([128, 2, 65]) for _ in range(2)]; b_va = [P.buf() for _ in range(2)]
                vbst = [AB.alloc([128, 512]) for _ in range(2)]; b_vb = [P.buf() for _ in range(2)]
                for v_ in vast:
                    P.op("dve", MSET(v_[:, :, 64:65], 1.0), writes=[b_va[0], b_va[1]])
            else:
                qf2 = [AB.alloc([128, 16, 96]) for _ in range(2)]; b_qf2 = [P.buf() for _ in range(2)]
                cn = AB.alloc([128, 640]); b_cn = P.buf()
                krf = AB.alloc([128, 32]); b_krf = P.buf()
                cT = AB.alloc([128, 5, 128]); b_cT = P.buf()
                krT = AB.alloc([32, 128]); b_krT = P.buf()
                stQ = [AB.alloc([96, 16, 512]) for _ in range(2)]; b_stQ = [P.buf() for _ in range(2)]
                stK = [AB.alloc([96, 16, 512]) for _ in range(2)]; b_stK = [P.buf() for _ in range(2)]
                v1st = [AB.alloc([128, 16, 65]) for _ in range(2)]; b_v1 = [P.buf() for _ in range(2)]
                for v_ in v1st:
                    P.op("dve", MSET(v_[:, :, 64:65], 1.0), writes=[b_v1[0], b_v1[1]])
            sc0 = 0.125
            sc1 = 96.0 ** -0.5
            tail = {"f": None}
            for i in range(NK):
                pa = i % 2
                qf = qf2[pa]; b_qf = b_qf2[pa]
                if layer == 0:
                    kf = kf2[pa]; b_kf = b_kf2[pa]
                t0 = i * 128
                blk = i // 4
                sb = blk % 2
                c4 = (i % 4) * 128
                P.op("sp", DMA(xt[pa], xsrc[t0:t0 + 128, :]), writes=[b_xt[pa]], dma=True)
                if layer == 0:
                    P.op("sp", DMA(tabs[pa][:, 0:64], C["cexpA"][t0:t0 + 128, :]), writes=[b_tabs[pa]], dma=True)
                    P.op("sp", DMA(tabs[pa][:, 64:128], C["sexpA"][t0:t0 + 128, :]), writes=[b_tabs[pa]], dma=True)
                else:
                    P.op("sp", DMA(tabs[pa][:, 0:32], C["cexpL"][t0:t0 + 128, :]), writes=[b_tabs[pa]], dma=True)
                    P.op("sp", DMA(tabs[pa][:, 64:96], C["sexpL"][t0:t0 + 128, :]), writes=[b_tabs[pa]], dma=True)
                norm_to_hT(xt[pa], b_xt[pa], junk, b_junk, ssr[pa][:, 0:1], ssr[pa][:, 1:2], b_s[pa], hb[pa], b_hb[pa], hT[pa], b_hT[pa], 0)
                if layer == 0:
                    P.op("dve", STT(tq[pa][:, 0:64], tabs[pa][:, 0:64], sc0, gq_b, ALU.mult, ALU.mult), reads=[b_tabs[pa], b_const], writes=[b_tq[pa]])
                    P.op("dve", STT(tq[pa][:, 64:128], tabs[pa][:, 64:128], sc0, gqs_b, ALU.mult, ALU.mult), reads=[b_tabs[pa], b_const], writes=[b_tq[pa]])
                    P.op("dve", TT(tq[pa][:, 128:192], tabs[pa][:, 0:64], gk_b, ALU.mult), reads=[b_tabs[pa], b_const], writes=[b_tq[pa]])
                    P.op("dve", TT(tq[pa][:, 192:256], tabs[pa][:, 64:128], gks_b, ALU.mult), reads=[b_tabs[pa], b_const], writes=[b_tq[pa]])
                    groups = [(0, 512), (512, 1024), (1024, 1280), (1280, 1792), (1792, 2304)]
                    pg = []
                    for (c0, c1) in groups:
                        bk, b_bk = nextbank()
                        for kc in range(8):
                            P.op("pe", MM(bk[:, 0:c1 - c0], hT[pa][:, kc, :], Win[:, kc, c0:c1], kc == 0, kc == 7), reads=[b_hT[pa], b_Win], writes=[b_bk])
                        pg.append((bk, b_bk))
                    (p0, b0), (p1, b1), (p2, b2), (p3, b3), (p4, b4) = pg
                    P.op("act", ACT(t1, p0, AF.Square), reads=[b0], writes=[b_t])
                    P.op("dve", RSUM(sq8[:, 0:8], t1.rearrange("p (h d) -> p h d", d=64)), reads=[b_t], writes=[b_sq8])
                    rms_rstd(sq8[:, 0:8], sq8[:, 0:8], 64.0, b_sq8, b_sq8)
                    qn3 = qn.rearrange("p (h d) -> p h d", d=64)
                    P.op("dve", TT(qn3, p0.rearrange("p (h d) -> p h d", d=64), sq8[:, 0:8].unsqueeze(2).to_broadcast([128, 8, 64]), ALU.mult), reads=[b0, b_sq8], writes=[b_qn])
                    rope(qf[:, 0:512].rearrange("p (h d) -> p h d", d=64), qn3, tq[pa][:, 0:64], tq[pa][:, 64:128],
                         (t1.rearrange("p (h d) -> p h d", d=64), t2.rearrange("p (h d) -> p h d", d=64)), 8, 64, b_qn, b_tq[pa], b_t, b_qf)
                    P.op("act", ACT(qf[:, 512:1024], p1, AF.Copy, scale=sc0), reads=[b1], writes=[b_qf])
                    P.op("act", ACT(t1[:, 0:128], p2[:, 0:128], AF.Square), reads=[b2], writes=[b_t])
                    P.op("dve", RSUM(sq8[:, 8:10], t1[:, 0:128].rearrange("p (h d) -> p h d", d=64)), reads=[b_t], writes=[b_sq8])
                    rms_rstd(sq8[:, 8:10], sq8[:, 8:10], 64.0, b_sq8, b_sq8)
                    kn3 = qn[:, 0:128].rearrange("p (h d) -> p h d", d=64)
                    P.op("dve", TT(kn3, p2[:, 0:128].rearrange("p (h d) -> p h d", d=64), sq8[:, 8:10].unsqueeze(2).to_broadcast([128, 2, 64]), ALU.mult), reads=[b2, b_sq8], writes=[b_qn])
                    rope(kf[:, 0:128].rearrange("p (h d) -> p h d", d=64), kn3, tq[pa][:, 128:192], tq[pa][:, 192:256],
                         (t1[:, 0:128].rearrange("p (h d) -> p h d", d=64), t2[:, 0:128].rearrange("p (h d) -> p h d", d=64)), 2, 64, b_qn, b_tq[pa], b_t, b_kf)
                    P.op("act", ACP(vast[pa][:, :, 0:64], p2[:, 128:256].rearrange("p (h d) -> p h d", d=64)), reads=[b2], writes=[b_va[pa]])
                    P.op("pool", DMA(Va[t0:t0 + 128, :, :], vast[pa]), reads=[b_va[pa]], dma=True)
                    P.op("act", ACP(kf[:, 128:640], p3), reads=[b3], writes=[b_kf])
                    P.op("dve", CP(vbst[pa], p4), reads=[b4], writes=[b_vb[pa]])
                    P.op("pool", DMA(Vb[t0:t0 + 128, :, :].rearrange("t h e -> t (h e)"), vbst[pa]), reads=[b_vb[pa]], dma=True)
                    def tail0(i=i, sb=sb, c4=c4, blk=blk, qf=qf, b_qf=b_qf, kf=kf, b_kf=b_kf):
                        srcs = [(qf, b_qf, c) for c in range(8)] + [(kf, b_kf, c) for c in range(5)]
                        for j0 in range(0, 13, 4):
                            pt, b_pt = nextT()
                            n = min(4, 13 - j0)
                            for k in range(n):
                                sa, sbuf_, c = srcs[j0 + k]
                                P.op("pe", TR(pt[:, k * 128:(k + 1) * 128], sa[:, c * 128:(c + 1) * 128], ident), reads=[sbuf_, b_const], writes=[b_pt])
                            dst = stT[sb][:, j0:j0 + n, c4:c4 + 128]
                            src = pt[:, 0:n * 128].rearrange("p (k t) -> p k t", t=128)
                            P.op("act" if (j0 // 4) % 2 == 0 else "dve", (ACP if (j0 // 4) % 2 == 0 else CP)(dst, src), reads=[b_pt], writes=[b_stT[sb]])
                        if i % 4 == 3:
                            tb0 = blk * 512
                            s_ = stT[sb]
                            for c in range(4):
                                P.op("pool", DMA(QTa[2 * c, :, tb0:tb0 + 512], s_[0:64, c, :]), reads=[b_stT[sb]], dma=True)
                                P.op("pool", DMA(QTa[2 * c + 1, :, tb0:tb0 + 512], s_[64:128, c, :]), reads=[b_stT[sb]], dma=True)
                                P.op("pool", DMA(QTb[c, :, tb0:tb0 + 512], s_[:, 4 + c, :]), reads=[b_stT[sb]], dma=True)
                                P.op("pool", DMA(KTb[c, :, tb0:tb0 + 512], s_[:, 9 + c, :]), reads=[b_stT[sb]], dma=True)
                            P.op("pool", DMA(KTa[:, tb0:tb0 + 512], s_[:, 8, :]), reads=[b_stT[sb]], dma=True)
                    if tail["f"] is not None:
                        tail["f"]()
                    tail["f"] = tail0
                else:
                    p0, b0 = nextbank(); p1, b1 = nextbank()
                    for kc in range(8):
                        P.op("pe", MM(p0[:, 0:384], hT[pa][:, kc, :], Win[:, kc, 0:384], kc == 0, kc == 7), reads=[b_hT[pa], b_Win], writes=[b0])
                    for kc in range(8):
                        P.op("pe", MM(p1[:, 0:288], hT[pa][:, kc, :], Win[:, kc, 384:672], kc == 0, kc == 7), reads=[b_hT[pa], b_Win], writes=[b1])
                    P.op("act", ACT(t1[:, 0:384], p0[:, 0:384], AF.Square, accum=sq8[:, 0:1]), reads=[b0], writes=[b_t, b_sq8])
                    P.op("act", ACT(t1[:, 0:256], p1[:, 0:256], AF.Square, accum=sq8[:, 1:2]), reads=[b1], writes=[b_t, b_sq8])
                    rms_rstd(sq8[:, 0:1], sq8[:, 2:3], 384.0, b_sq8, b_sq8)
                    rms_rstd(sq8[:, 1:2], sq8[:, 3:4], 256.0, b_sq8, b_sq8)
                    P.op("dve", TS(cn[:, 0:384], p0[:, 0:384], sq8[:, 2:3]), reads=[b0, b_sq8], writes=[b_cn])
                    P.op("dve", TS(cn[:, 384:640], p1[:, 0:256], sq8[:, 3:4]), reads=[b1, b_sq8], writes=[b_cn])
                    P.op("act", ACP(qn[:, 0:32], p1[:, 256:288]), reads=[b1], writes=[b_qn])
                    rope(krf.rearrange("p (h d) -> p h d", h=1), qn[:, 0:32].rearrange("p (h d) -> p h d", h=1), tabs[pa][:, 0:32], tabs[pa][:, 64:96],
                         (t1[:, 0:32].rearrange("p (h d) -> p h d", h=1), t2[:, 0:32].rearrange("p (h d) -> p h d", h=1)), 1, 32, b_qn, b_tabs[pa], b_t, b_krf)
                    for (j0, n) in [(0, 4), (4, 1)]:
                        pt, b_pt = nextT()
                        for k in range(n):
                            c = j0 + k
                            P.op("pe", TR(pt[:, k * 128:(k + 1) * 128], cn[:, c * 128:(c + 1) * 128], ident), reads=[b_cn, b_const], writes=[b_pt])
                        if j0 == 4:
                            P.op("pe", TR(pt[0:32, 128:256], krf, ident), reads=[b_krf, b_const], writes=[b_pt])
                            P.op("dve", CP(krT, pt[0:32, 128:256]), reads=[b_pt], writes=[b_krT])
                        P.op("act", ACP(cT[:, j0:j0 + n, :], pt[:, 0:n * 128].rearrange("p (k t) -> p k t", t=128)), reads=[b_pt], writes=[b_cT])
                    for (h0, h1) in [(0, 5), (5, 10), (10, 15), (15, 16)]:
                        bk, b_bk = nextbank()
                        nh = h1 - h0
                        for kc in range(3):
                            P.op("pe", MM(bk[:, 0:nh * 96], cT[:, kc, :], Wuq[:, kc, h0 * 96:h1 * 96], kc == 0, kc == 2), reads=[b_cT, b_Win], writes=[b_bk])
                        bv = bk[:, 0:nh * 96].rearrange("p (h d) -> p h d", d=96)
                        P.op("act", ACT(qf[:, h0:h1, 0:64], bv[:, :, 0:64], AF.Copy, scale=sc1), reads=[b_bk], writes=[b_qf])
                        P.op("act", ACT(qn[:, 0:nh * 32].rearrange("p (h d) -> p h d", d=32), bv[:, :, 64:96], AF.Copy, scale=sc1), reads=[b_bk], writes=[b_qn])
                        rope(qf[:, h0:h1, 64:96], qn[:, 0:nh * 32].rearrange("p (h d) -> p h d", d=32), tabs[pa][:, 0:32], tabs[pa][:, 64:96],
                             (t1[:, 0:nh * 32].rearrange("p (h d) -> p h d", d=32), t2[:, 0:nh * 32].rearrange("p (h d) -> p h d", d=32)), nh, 32, b_qn, b_tabs[pa], b_t, b_qf)
                    def tail1(i=i, sb=sb, c4=c4, blk=blk, qf=qf, b_qf=b_qf):
                        for j0 in range(0, 16, 4):
                            pt, b_pt = nextT()
                            for k in range(4):
                                P.op("pe", TR(pt[0:96, k * 128:(k + 1) * 128], qf[:, j0 + k, :], ident), reads=[b_qf, b_const], writes=[b_pt])
                            dst = stQ[sb][:, j0:j0 + 4, c4:c4 + 128]
                            src = pt[0:96, :].rearrange("p (k t) -> p k t", t=128)
                            P.op("act" if (j0 // 4) % 2 == 0 else "dve", (ACP if (j0 // 4) % 2 == 0 else CP)(dst, src), reads=[b_pt], writes=[b_stQ[sb]])
                        if i % 4 == 3:
                            tb0 = blk * 512
                            for h in range(16):
                                P.op("pool", DMA(QT1[h, :, tb0:tb0 + 512], stQ[sb][:, h, :]), reads=[b_stQ[sb]], dma=True)
                    for j0 in range(0, 16, 4):
                        bk, b_bk = nextbank()
                        for k in range(4):
                            h = j0 + k
                            o_ = bk[0:96, k * 128:(k + 1) * 128]
                            P.op("pe", MM(o_, esel, krT, True, False), reads=[b_krT, b_const], writes=[b_bk])
                            P.op("pe", MM(o_, Wkp[:, 0, h, :], cT[:, 3, :], False, False), reads=[b_cT, b_Win], writes=[b_bk])
                            P.op("pe", MM(o_, Wkp[:, 1, h, :], cT[:, 4, :], False, True), reads=[b_cT, b_Win], writes=[b_bk])
                        dst = stK[sb][:, j0:j0 + 4, c4:c4 + 128]
                        src = bk[0:96, :].rearrange("p (k t) -> p k t", t=128)
                        P.op("act" if (j0 // 4) % 2 == 1 else "dve", (ACP if (j0 // 4) % 2 == 1 else CP)(dst, src), reads=[b_bk], writes=[b_stK[sb]])
                    for hh in range(2):
                        bk, b_bk = nextbank()
                        for kc in range(2):
                            P.op("pe", MM(bk, cT[:, 3 + kc, :], Wv[:, kc, hh * 512:(hh + 1) * 512], kc == 0, kc == 1), reads=[b_cT, b_Win], writes=[b_bk])
                        P.op("dve" if hh == 0 else "act", (CP if hh == 0 else ACP)(v1st[pa][:, hh * 8:(hh + 1) * 8, 0:64], bk.rearrange("p (h d) -> p h d", d=64)), reads=[b_bk], writes=[b_v1[pa]])
                    P.op("pool", DMA(V1[t0:t0 + 128, :, :], v1st[pa]), reads=[b_v1[pa]], dma=True)
                    if i % 4 == 3:
                        tb0 = blk * 512
                        for h in range(16):
                            P.op("pool", DMA(KT1[h, :, tb0:tb0 + 512], stK[sb][:, h, :]), reads=[b_stK[sb]], dma=True)
                    if tail["f"] is not None:
                        tail["f"]()
                    tail["f"] = tail1
            if tail["f"] is not None:
                tail["f"]()
            P.barrier()

        LA = 3
        NP = 4

        def pass_B(layer):
            abanks["l"] = [4, 5, 6, 7]
            sbanks["l"] = [0, 1, 2, 3]
            ngroups = 1 if layer == 0 else 2
            for g in range(ngroups):
                AB.reset(); AFa.reset()
                b_kv = P.buf()
                VG = 8 if NK % 8 == 0 else 4
                NVG = NK // VG

                def vsrc(ap2, j):
                    return ap2[j * VG * 128:(j + 1) * VG * 128]

                if layer == 0:
                    KTa_sb = AB.alloc([128, S]); KTb_sb = AB.alloc([128, 4, S])
                    Va_sb = AB.alloc([128, NK, 2, 65]); Vb_sb = AB.alloc([128, NK, 4, 128])
                    al_ab = AB.alloc([128, 4, 512]); al_bl = AB.alloc([128, 4, 512]); al_st = AB.alloc([128, 4, 896])
                    al_bias = AFa.alloc([128, 256])
                    b_ka = P.buf(); b_va = [P.buf() for _ in range(NVG)]; b_al = P.buf()
                    b_kb = [P.buf() for _ in range(4)]; b_vb = [P.buf() for _ in range(NVG)]
                    P.op("sp", DMA(KTa_sb, KTa[:, :]), writes=[b_ka], dma=True)
                    for j in range(NVG):
                        P.op("sp", DMA(Va_sb[:, j * VG:(j + 1) * VG, :, :], vsrc(Va, j).rearrange("(kt p) h e -> p kt h e", p=128)), writes=[b_va[j]], dma=True)
                    for h in range(4):
                        P.op("sp", DMA(al_ab[:, h, :], C["al_above"][h]), writes=[b_al], dma=True)
                        P.op("sp", DMA(al_bl[:, h, :], C["al_below"][h]), writes=[b_al], dma=True)
                        P.op("sp", DMA(al_st[:, h, :], C["al_strip"][h]), writes=[b_al], dma=True)
                    P.op("sp", DMA(al_bias, C["al_bias"][:, :]), writes=[b_al], dma=True)
                    for h in range(4):
                        P.op("sp", DMA(KTb_sb[:, h, :], KTb[h, :, :]), writes=[b_kb[h]], dma=True)
                        if h == 0:
                            for j in range(NVG):
                                P.op("sp", DMA(Vb_sb[:, j * VG:(j + 1) * VG, :, :], vsrc(Vb, j).rearrange("(kt p) h e -> p kt h e", p=128)), writes=[b_vb[j]], dma=True)
                else:
                    KT_sb = AB.alloc([96, 8, S]); V_sb = AB.alloc([128, NK, 8, 65])
                    b_k1 = [P.buf() for _ in range(8)]; b_v1 = [P.buf() for _ in range(NVG)]
                    for h in range(8):
                        P.op("sp", DMA(KT_sb[:, h, :], KT1[g * 8 + h, :, :]), writes=[b_k1[h]], dma=True)
                        if h == 0:
                            for j in range(NVG):
                                P.op("sp", DMA(V_sb[:, j * VG:(j + 1) * VG, :, :], vsrc(V1, j)[:, g * 8:(g + 1) * 8, :].rearrange("(kt p) h e -> p kt h e", p=128)), writes=[b_v1[j]], dma=True)
                b_q = [P.buf() for _ in range(2)]; b_o = [P.buf() for _ in range(2)]
                b_qb1 = P.buf(); b_ob1 = P.buf()
                if layer == 0:
                    qa_t = [AB.alloc([128, 8, 512]) for _ in range(2)]
                    qb_t = AB.alloc([128, 4, 2, 512])
                    oa_st = AB.alloc([64, 8, 512]); ob_st = AB.alloc([128, 4, 512])
                    for pa_ in range(2):
                        P.op("dve", MSET(qa_t[pa_], 0.0), writes=[b_q[pa_]])
                    P.op("dve", MSET(qb_t, 0.0), writes=[b_qb1])
                else:
                    q1_t = [AB.alloc([96, 8, 512]) for _ in range(2)]
                    o1_st = [AB.alloc([64, 8, 512]) for _ in range(2)]
                Pt = [AB.alloc([128, 512]) for _ in range(NP)]; b_P = [P.buf() for _ in range(NP)]
                if layer == 0:
                    P2 = [AB.alloc([128, 512]) for _ in range(NP)]; b_P2 = [P.buf() for _ in range(NP)]
                    sqb = AB.alloc([128, 512]); b_sqb = P.buf()
                    tc_ = [AFa.alloc([128, 512]) for _ in range(2)]; b_tc = [P.buf() for _ in range(2)]
                    tt_ = AFa.alloc([128, 512]); b_tt = P.buf()
                rl = [AFa.alloc([128, 512]) for _ in range(2)]; b_rl = [P.buf() for _ in range(2)]
                rlb = [AB.alloc([128, 2, 512]) for _ in range(2)]; b_rlb = [P.buf() for _ in range(2)]
                for u_i in range(2):
                    P.op("dve", MSET(rlb[u_i], 0.0), writes=[b_rlb[u_i]])
                bcs = [AFa.alloc([128, 512]) for _ in range(2)]; b_bcs = [P.buf() for _ in range(2)]
                pctr = {"i": 0, "u": 0}
                pend = []

                def defer(n, fn, tag=None):
                    pend.append([n, fn, tag])

                def flush_for(bufs):
                    last = -1
                    for i_, it in enumerate(pend):
                        if it[2] is not None and any(it[2] is b_ for b_ in bufs):
                            last = i_
                    for _ in range(last + 1):
                        pend.pop(0)[1]()

                def tick():
                    for it in pend:
                        it[0] -= 1
                    while pend and pend[0][0] <= 0:
                        pend.pop(0)[1]()

                def load_qb(qt):
                    sl = slice(qt * 512, (qt + 1) * 512)
                    for h in range(4):
                        for c in range(2):
                            P.op("sp", DMA(qb_t[c * 64:c * 64 + 64, h, c, :], QTb[h, c * 64:c * 64 + 64, sl]), writes=[b_qb1], dma=True)

                def load_q(qt):
                    pa = qt % 2
                    sl = slice(qt * 512, (qt + 1) * 512)
                    if layer == 0:
                        for hq in range(8):
                            P.op("sp", DMA(qa_t[pa][(hq // 4) * 64:(hq // 4) * 64 + 64, hq, :], QTa[hq, :, sl]), writes=[b_q[pa]], dma=True)
                    else:
                        for h in range(8):
                            P.op("sp", DMA(q1_t[pa][:, h, :], QT1[g * 8 + h, :, sl]), writes=[b_q[pa]], dma=True)

                def store_oa(qt):
                    sl = slice(qt * 512, (qt + 1) * 512)
                    for hq in range(8):
                        P.op("pool", DMA(OTa[hq, :, sl], oa_st[:, hq, :]), reads=[b_o[0]], dma=True)

                def store_o(qt):
                    pa = qt % 2
                    sl = slice(qt * 512, (qt + 1) * 512)
                    if layer == 0:
                        for h in range(4):
                            P.op("pool", DMA(OTb[h, :, sl], ob_st[:, h, :]), reads=[b_ob1], dma=True)
                    else:
                        for h in range(8):
                            P.op("pool", DMA(OT1[g * 8 + h, :, sl], o1_st[pa][:, h, :]), reads=[b_o[pa]], dma=True)

                class AugUnit:
                    def __init__(self, Qap, Kfn, Vfn, o_dst, b_odst, b_qb):
                        self.Qap, self.Kfn, self.Vfn, self.o_dst, self.b_odst, self.b_qb = Qap, Kfn, Vfn, o_dst, b_odst, b_qb
                        self.kts = list(range(NK))

                    def start(self):
                        self.O, self.b_O = nextA()
                        flush_for([self.b_O])
                        self.u = pctr["u"] % 2; pctr["u"] += 1

                    def front(self, kt):
                        sbk, b_sbk = nextS()
                        kap, kbuf = self.Kfn(kt)
                        P.op("pe", MM(sbk, kap, self.Qap, True, True), reads=[kbuf, self.b_qb], writes=[b_sbk])
                        pi = pctr["i"] % NP; pctr["i"] += 1
                        P.op("act", ACT(Pt[pi], sbk, AF.Exp), reads=[b_sbk], writes=[b_P[pi]])
                        return Pt[pi], b_P[pi]

                    def back(self, kt, pinfo):
                        pt, b_pt = pinfo
                        vap, vbuf = self.Vfn(kt)
                        P.op("pe", MM(self.O[0:65, :], vap, pt, kt == self.kts[0], kt == self.kts[-1]), reads=[vbuf, b_pt], writes=[self.b_O])

                    def fin(self):
                        u = self.u; O = self.O; b_O = self.b_O
                        P.op("dve", RECIP(rl[u][64:65, :], O[64:65, :]), reads=[b_O], writes=[b_rl[u]])
                        P.op("dve", CP(rlb[u][64:65, 0, :], rl[u][64:65, :]), reads=[b_rl[u]], writes=[b_rlb[u]])
                        P.op("dve", TT(rlb[u][64:65, 1, :], rl[u][64:65, :], rlb[u][64:65, 0, :], ALU.subtract), reads=[b_rl[u], b_rlb[u]], writes=[b_rlb[u]])

                        def part2():
                            bcp, b_bcp = nextS()
                            P.op("pe", MM(bcp[0:64, :], sel64, rlb[u][:, 0, :], True, False), reads=[b_rlb[u], b_const], writes=[b_bcp])
                            P.op("pe", MM(bcp[0:64, :], sel64, rlb[u][:, 1, :], False, True), reads=[b_rlb[u], b_const], writes=[b_bcp])
                            P.op("act", ACP(bcs[u][0:64, :], bcp[0:64, :]), reads=[b_bcp], writes=[b_bcs[u]])
                            P.op("dve", TT(self.o_dst, O[0:64, :], bcs[u][0:64, :], ALU.mult), reads=[b_O, b_bcs[u]], writes=[self.b_odst])
                        defer(10, part2, b_O)

                def diff_kts(h, qt):
                    sl_ = 2.0 ** (-8.0 * (h + 1) / 4)
                    out = []
                    for kt in range(NK):
                        if kt < 4 * qt:
                            dmin = 512 * qt - 128 * kt - 127
                        elif kt >= 4 * qt + 4:
                            dmin = 128 * kt - 512 * qt - 511
                        else:
                            dmin = 0
                        if sl_ * dmin < 100.0:
                            out.append(kt)
                    return out

                class DiffUnit:
                    def __init__(self, h, c, qt, Qap, Kfn, Vfn, o_dst, b_odst, b_qb):
                        self.h, self.c, self.qt = h, c, qt
                        self.kts = diff_kts(h, qt)
                        self.Qap, self.Kfn, self.Vfn, self.o_dst, self.b_odst, self.b_qb = Qap, Kfn, Vfn, o_dst, b_odst, b_qb

                    def start(self):
                        self.O, self.b_O = nextA(); self.L, self.b_L = nextA()
                        flush_for([self.b_O, self.b_L])
                        self.u = pctr["u"] % 2; pctr["u"] += 1

                    def front(self, kt):
                        h, qt = self.h, self.qt
                        sbk, b_sbk = nextS()
                        kap, kbuf = self.Kfn(kt)
                        P.op("pe", MM(sbk, kap, self.Qap, True, True), reads=[kbuf, self.b_qb], writes=[b_sbk])
                        pi = pctr["i"] % NP; pctr["i"] += 1
                        if kt < 4 * qt:
                            m_ = (512 * qt - 128 * kt) // 128
                            bcol = al_bias[:, h * 64 + m_:h * 64 + m_ + 1]; tab = al_ab[:, h, :]
                        elif kt >= 4 * qt + 4:
                            m_ = (128 * kt - 512 * qt) // 128
                            bcol = al_bias[:, h * 64 + 32 + m_:h * 64 + 32 + m_ + 1]; tab = al_bl[:, h, :]
                        else:
                            dl = 512 * qt - 128 * kt
                            bcol = al_bias[:, h * 64:h * 64 + 1]; tab = al_st[:, h, 384 + dl:384 + dl + 512]
                        P.op("act", ACT(Pt[pi], sbk, AF.Exp, bias=bcol), reads=[b_sbk, b_al], writes=[b_P[pi]])
                        P.op("dve", TT(P2[pi], Pt[pi], tab, ALU.mult), reads=[b_P[pi], b_al], writes=[b_P2[pi]])
                        return P2[pi], b_P2[pi]

                    def back(self, kt, pinfo):
                        pt, b_pt = pinfo
                        vap, vbuf = self.Vfn(kt)
                        P.op("pe", MM(self.O, vap, pt, kt == self.kts[0], kt == self.kts[-1]), reads=[vbuf, b_pt], writes=[self.b_O])
                        P.op("pe", MM(self.L, ones_b, pt, kt == self.kts[0], kt == self.kts[-1]), reads=[b_const, b_pt], writes=[self.b_L])

                    def fin(self):
                        u = self.u; c = self.c
                        P.op("act", ACT(rl[u], self.L, AF.Ln), reads=[self.b_L], writes=[b_rl[u]])
                        P.op("act", ACT(rl[u], rl[u], AF.Exp, scale=-1.0), reads=[b_rl[u]], writes=[b_rl[u]])
                        P.op("dve", TT(tc_[c], self.O, rl[u], ALU.mult), reads=[self.b_O, b_rl[u]], writes=[b_tc[c]])
                        if c == 1:
                            P.op("dve", STT(tt_, tc_[1], nlam, tc_[0], ALU.mult, ALU.add), reads=[b_tc[0], b_tc[1], b_const], writes=[b_tt])
                            P.op("act", ACT(sqb, tt_, AF.Square), reads=[b_tt], writes=[b_sqb])

                            def part2():
                                ssb, b_ssb = nextS()
                                P.op("pe", MM(ssb, ones_b, sqb, True, True), reads=[b_const, b_sqb], writes=[b_ssb])
                                P.op("act", ACT(rl[u], ssb, AF.Ln, bias=EPS, scale=1.0 / 128), reads=[b_ssb], writes=[b_rl[u]])
                                P.op("act", ACT(rl[u], rl[u], AF.Exp, scale=-0.5), reads=[b_rl[u]], writes=[b_rl[u]])
                                P.op("dve", STT(self.o_dst, tt_, gsub_c, rl[u], ALU.mult, ALU.mult), reads=[b_tt, b_rl[u], b_const], writes=[self.b_odst])
                            defer(6, part2)

                stream = []
                for qt in range(NT):
                    pa = qt % 2
                    units = []
                    if layer == 0:
                        for hq in range(8):
                            kvh = hq // 4
                            units.append(AugUnit(qa_t[pa][:, hq, :],
                                                 lambda kt: (KTa_sb[:, kt * 128:(kt + 1) * 128], b_ka),
                                                 lambda kt, kvh=kvh: (Va_sb[:, kt, kvh, :], b_va[kt // VG]), oa_st[:, hq, :], b_o[0], b_q[pa]))
                        for h in range(4):
                            for c in range(2):
                                units.append(DiffUnit(h, c, qt, qb_t[:, h, c, :],
                                                      lambda kt, h=h: (KTb_sb[:, h, kt * 128:(kt + 1) * 128], b_kb[h]),
                                                      lambda kt, h=h: (Vb_sb[:, kt, h, :], b_vb[kt // VG]), ob_st[:, h, :], b_ob1, b_qb1))
                    else:
                        for h in range(8):
                            units.append(AugUnit(q1_t[pa][:, h, :], lambda kt, h=h: (KT_sb[:, h, kt * 128:(kt + 1) * 128], b_k1[h]),
                                                 lambda kt, h=h: (V_sb[:, kt, h, :], b_v1[kt // VG]), o1_st[pa][:, h, :], b_o[pa], b_q[pa]))
                    for ui, u_ in enumerate(units):
                        for kt in u_.kts:
                            stream.append((u_, kt, qt, ui == 0 and kt == u_.kts[0], ui == len(units) - 1 and kt == u_.kts[-1],
                                           layer == 0 and ui == 7 and kt == u_.kts[-1]))
                load_q(0)
                if layer == 0:
                    load_qb(0)
                inflight = []
                for idx in range(len(stream) + LA):
                    if idx < len(stream):
                        u_, kt, qt, first, lastq, _ = stream[idx]
                        if first and qt + 1 < NT:
                            load_q(qt + 1)
                        if kt == u_.kts[0]:
                            u_.start()
                        inflight.append((u_, kt, qt, u_.front(kt)))
                        if lastq and layer == 0 and qt + 1 < NT:
                            load_qb(qt + 1)
                    if idx >= LA:
                        u_, kt, qt, pinfo = inflight.pop(0)
                        u_.back(kt, pinfo)
                        if kt == u_.kts[-1]:
                            u_.fin()
                            if stream[idx - LA][4]:
                                defer(14, lambda qt=qt: store_o(qt))
                            if stream[idx - LA][5]:
                                defer(14, lambda qt=qt: store_oa(qt))
                    tick()
                while pend:
                    pend.pop(0)[1]()
                P.barrier()

        def pass_C(layer, seq, xsrc, xdst, last):
            abanks["l"] = [3, 4, 5]
            sbanks["l"] = [0, 1, 2]
            AB.reset(); AFa.reset()
            NW = 3
            b_w = [P.buf() for _ in range(NW)]
            wst = [AB.alloc([128, 8, 1024]) for _ in range(NW)]
            wctr = {"i": 0}
            Wdn = AB.alloc([128, 22, 1024]); b_Wdn = P.buf()
            Kmem = AB.alloc([128, 8, 256]); Vmem = AB.alloc([128, 2, 1024]); b_mem = P.buf()
            xts = [AFa.alloc([128, 4, 1024]) for _ in range(2)]; b_xts = [[P.buf() for _ in range(4)] for _ in range(2)]
            ssr = AFa.alloc([128, 8]); b_s = [P.buf() for _ in range(4)]
            ssn = AFa.alloc([128, 8]); b_sn = [P.buf() for _ in range(4)]
            rlc = AFa.alloc([128, 512]); b_rlc = P.buf()
            hb = [AB.alloc([128, 1024]) for _ in range(4)]; b_hb = [P.buf() for _ in range(4)]
            hT = AB.alloc([128, 8, 512]); b_hT = P.buf()
            qT = AB.alloc([128, 8, 512]); b_qT = P.buf()
            o2T = hT; b_o2T = b_hT
            ots = [AB.alloc([128, 8, 512]) for _ in range(2)]; b_ots = [P.buf() for _ in range(2)]
            actT = AB.alloc([128, 22, 512]); b_act = P.buf()
            Pc = [hb[2][:, 0:512], hb[3][:, 0:512]]; b_Pc = [b_hb[2], b_hb[3]]
            sgl = Pc[0]; b_sgl = b_Pc[0]

            def wload(view_fn_list):
                s_ = wctr["i"] % NW; wctr["i"] += 1
                for dfn, src in view_fn_list:
                    P.op("sp", DMA(dfn(wst[s_]), src), writes=[b_w[s_]], dma=True)
                return wst[s_], b_w[s_]

            def load_tile(tt):
                pa = tt % 2
                t0 = tt * 512
                sl = slice(t0, t0 + 512)
                for s_ in range(4):
                    P.op("sp", DMA(xts[pa][:, s_, :], xsrc[t0 + s_ * 128:t0 + (s_ + 1) * 128, :]), writes=[b_xts[pa][s_]], dma=True)
                if layer == 0:
                    for hq in range(8):
                        P.op("sp", DMA(ots[pa][(hq % 2) * 64:(hq % 2) * 64 + 64, hq // 2, :], OTa[hq, :, sl]), writes=[b_ots[pa]], dma=True)
                    for h in range(4):
                        P.op("sp", DMA(ots[pa][:, 4 + h, :], OTb[h, :, sl]), writes=[b_ots[pa]], dma=True)
                else:
                    for h in range(16):
                        P.op("sp", DMA(ots[pa][(h % 2) * 64:(h % 2) * 64 + 64, h // 2, :], OT1[h, :, sl]), writes=[b_ots[pa]], dma=True)

            load_tile(0)
            mt = xts[1][:, 0, :]; b_mt = b_xts[1][0]
            for mi in range(2):
                P.op("sp", DMA(mt, mem_in[seq, mi * 128:(mi + 1) * 128, :]), writes=[b_mt], dma=True)
                P.op("act", ACT(hb[0], mt, AF.Square, accum=ssr[:, 0:1]), reads=[b_mt], writes=[b_hb[0], b_s[0]])
                rms_rstd(ssr[:, 0:1], ssr[:, 1:2], float(D), b_s[0], b_s[0])
                P.op("dve", TS(hb[0], mt, ssr[:, 1:2]), reads=[b_mt, b_s[0]], writes=[b_hb[0]])
                for half in range(2):
                    pt, b_pt = nextT()
                    for k in range(4):
                        kc = half * 4 + k
                        P.op("pe", TR(pt[:, k * 128:(k + 1) * 128], hb[0][:, kc * 128:(kc + 1) * 128], ident), reads=[b_hb[0], b_const], writes=[b_pt])
                    P.op("act", ACP(hT[:, half * 4:half * 4 + 4, mi * 128:(mi + 1) * 128], pt.rearrange("p (k t) -> p k t", t=128)), reads=[b_pt], writes=[b_hT])
            for half in range(2):
                wv, b_wv = wload([(lambda a: a, Wckv_s[layer][:, :, half * 1024:(half + 1) * 1024])])
                if half == 0:
                    for m in range(8):
                        bk, b_bk = nextbank()
                        for kc in range(8):
                            P.op("pe", MM(bk[:, 0:256], wv[:, kc, m * 128:(m + 1) * 128], hT[:, kc, 0:256], kc == 0, kc == 7), reads=[b_wv, b_hT], writes=[b_bk])
                        P.op("act" if m % 2 else "dve", (ACP if m % 2 else CP)(Kmem[:, m, :], bk[:, 0:256]), reads=[b_bk], writes=[b_mem])
                else:
                    for mi in range(2):
                        for nh in range(2):
                            bk, b_bk = nextbank()
                            for kc in range(8):
                                P.op("pe", MM(bk, hT[:, kc, mi * 128:(mi + 1) * 128], wv[:, kc, nh * 512:(nh + 1) * 512], kc == 0, kc == 7), reads=[b_wv, b_hT], writes=[b_bk])
                            P.op("act" if nh else "dve", (ACP if nh else CP)(Vmem[:, mi, nh * 512:(nh + 1) * 512], bk), reads=[b_bk], writes=[b_mem])

            def preload_first():
                a_ = wload([(lambda a: a, (Wout0_s if layer == 0 else Wout1_s)[:, :, :])])
                b__ = wload([(lambda a: a, Wcq_s[layer][:, :, :])])
                return a_, b__

            pre = preload_first()
            for tt in range(NT):
                pa = tt % 2
                xt = xts[pa]; b_xt = b_xts[pa]
                ot = ots[pa]; b_ot = b_ots[pa]
                t0 = tt * 512
                (w_out, b_w_out), (w_cq, b_w_cq) = pre
                if tt + 1 < NT:
                    load_tile(tt + 1)

                def norm_stage1(s_):
                    ss_ = ssn[:, 2 * s_:2 * s_ + 1]; rs_ = ssn[:, 2 * s_ + 1:2 * s_ + 2]
                    P.op("act", ACT(hb[s_], xt[:, s_, :], AF.Square, accum=ss_), reads=[b_xt[s_]], writes=[b_hb[s_], b_sn[s_]])
                    rms_rstd(ss_, rs_, float(D), b_sn[s_], b_sn[s_])

                def norm_stage1b(s_):
                    rs_ = ssn[:, 2 * s_ + 1:2 * s_ + 2]
                    P.op("dve", TS(hb[s_], xt[:, s_, :], rs_), reads=[b_xt[s_], b_sn[s_]], writes=[b_hb[s_]])

                def norm_stage2():
                    for s_ in range(4):
                        for half in range(2):
                            pt, b_pt = nextT()
                            for k in range(4):
                                kc = half * 4 + k
                                P.op("pe", TR(pt[:, k * 128:(k + 1) * 128], hb[s_][:, kc * 128:(kc + 1) * 128], ident), reads=[b_hb[s_], b_const], writes=[b_pt])
                            dst = hT[:, half * 4:half * 4 + 4, s_ * 128:(s_ + 1) * 128]
                            src = pt.rearrange("p (k t) -> p k t", t=128)
                            if half == 0:
                                P.op("act", ACP(dst, src), reads=[b_pt], writes=[b_hT])
                            else:
                                P.op("dve", CP(dst, src), reads=[b_pt], writes=[b_hT])

                def add_proj(lhs_list, b_lhs, rhs_fn, b_rhs, then_norm=False):
                    for s_ in range(4):
                        for n in range(2):
                            bk, b_bk = nextbank()
                            nl = len(lhs_list)
                            for ci, lf in enumerate(lhs_list):
                                P.op("pe", MM(bk, lf(s_), rhs_fn(ci, n), ci == 0, ci == nl - 1), reads=[b_lhs, b_rhs], writes=[b_bk])
                            P.op("dve", TT(xt[:, s_, n * 512:(n + 1) * 512], xt[:, s_, n * 512:(n + 1) * 512], bk, ALU.add), reads=[b_bk, b_xt[s_]], writes=[b_xt[s_]])
                        if then_norm:
                            norm_stage1(s_)
                            if s_ > 0:
                                norm_stage1b(s_ - 1)
                    if then_norm:
                        norm_stage1b(3)
                        norm_stage2()

                wv, b_wv = w_out, b_w_out
                lhs = [(lambda s_, c=c: ot[:, c, s_ * 128:(s_ + 1) * 128]) for c in range(8)]
                add_proj(lhs, b_ot, lambda ci, n, wv=wv: wv[:, ci, n * 512:(n + 1) * 512], b_wv, then_norm=True)
                wv, b_wv = w_cq, b_w_cq
                for m in range(8):
                    bk, b_bk = nextbank()
                    for kc in range(8):
                        P.op("pe", MM(bk, wv[:, kc, m * 128:(m + 1) * 128], hT[:, kc, :], kc == 0, kc == 7), reads=[b_wv, b_hT], writes=[b_bk])
                    P.op("act", ACT(qT[:, m, :], bk, AF.Copy, scale=1.0 / 16), reads=[b_bk], writes=[b_qT])
                for h in range(4):
                    L, b_L = nextA(); O0, b_O0 = nextA(); O1, b_O1 = nextA()
                    for mi in range(2):
                        sbk, b_sbk = nextS()
                        for dc in range(2):
                            P.op("pe", MM(sbk, Kmem[:, 2 * h + dc, mi * 128:(mi + 1) * 128], qT[:, 2 * h + dc, :], dc == 0, dc == 1), reads=[b_mem, b_qT], writes=[b_sbk])
                        P.op("act", ACT(Pc[mi], sbk, AF.Exp), reads=[b_sbk], writes=[b_Pc[mi]])
                        P.op("pe", MM(L, ones_b, Pc[mi], mi == 0, mi == 1), reads=[b_const, b_Pc[mi]], writes=[b_L])
                        P.op("pe", MM(O0, Vmem[:, mi, h * 256:h * 256 + 128], Pc[mi], mi == 0, mi == 1), reads=[b_mem, b_Pc[mi]], writes=[b_O0])
                        P.op("pe", MM(O1, Vmem[:, mi, h * 256 + 128:h * 256 + 256], Pc[mi], mi == 0, mi == 1), reads=[b_mem, b_Pc[mi]], writes=[b_O1])
                    P.op("act", ACT(rlc, L, AF.Ln), reads=[b_L], writes=[b_rlc])
                    P.op("act", ACT(rlc, rlc, AF.Exp, scale=-1.0), reads=[b_rlc], writes=[b_rlc])
                    P.op("dve", TT(o2T[:, 2 * h, :], O0, rlc, ALU.mult), reads=[b_O0, b_rlc], writes=[b_o2T])
                    P.op("dve", TT(o2T[:, 2 * h + 1, :], O1, rlc, ALU.mult), reads=[b_O1, b_rlc], writes=[b_o2T])
                wv, b_wv = wload([(lambda a: a, Wco_s[layer][:, :, :])])
                lhs = [(lambda s_, kc=kc: o2T[:, kc, s_ * 128:(s_ + 1) * 128]) for kc in range(8)]
                add_proj(lhs, b_o2T, lambda ci, n, wv=wv: wv[:, ci, n * 512:(n + 1) * 512], b_wv, then_norm=True)
                for kc in range(22):
                    P.op("sp", DMA(Wdn[:, kc, :], Wdn_s[layer][:, kc, :]), writes=[b_Wdn], dma=True)
                for j0 in range(0, 22, 4):
                    nj = min(4, 22 - j0)
                    wv, b_wv = wload([(lambda a, nj=nj: a[:, :, 0:nj * 128], Wgu_s[layer][:, :, j0 * 128:(j0 + nj) * 128]),
                                      (lambda a, nj=nj: a[:, :, 512:512 + nj * 128], Wgu_s[layer][:, :, DFF + j0 * 128:DFF + (j0 + nj) * 128])])
                    for jj in range(nj):
                        j = j0 + jj
                        gk, b_gk = nextbank(); uk, b_uk = nextbank()
                        for kc in range(8):
                            P.op("pe", MM(gk, wv[:, kc, jj * 128:(jj + 1) * 128], hT[:, kc, :], kc == 0, kc == 7), reads=[b_wv, b_hT], writes=[b_gk])
                        for kc in range(8):
                            P.op("pe", MM(uk, wv[:, kc, 512 + jj * 128:512 + (jj + 1) * 128], hT[:, kc, :], kc == 0, kc == 7), reads=[b_wv, b_hT], writes=[b_uk])
                        P.op("act", ACT(sgl, gk, AF.Silu), reads=[b_gk], writes=[b_sgl])
                        P.op("dve", TT(actT[:, j, :], sgl, uk, ALU.mult), reads=[b_sgl, b_uk], writes=[b_act])
                if tt + 1 < NT:
                    pre = preload_first()
                lhs = [(lambda s_, j=j: actT[:, j, s_ * 128:(s_ + 1) * 128]) for j in range(22)]
                add_proj(lhs, b_act, lambda ci, n: Wdn[:, ci, n * 512:(n + 1) * 512], b_Wdn)
                for s_ in range(4):
                    rows = slice(t0 + s_ * 128, t0 + (s_ + 1) * 128)
                    if last:
                        P.op("act", ACT(hb[s_ % 2], xt[:, s_, :], AF.Square, accum=ssr[:, 2:3]), reads=[b_xt[s_]], writes=[b_hb[s_ % 2], b_s[1]])
                        rms_rstd(ssr[:, 2:3], ssr[:, 3:4], float(D), b_s[1], b_s[1])
                        P.op("dve", STT(xt[:, s_, :], xt[:, s_, :], ssr[:, 3:4], fin_b, ALU.mult, ALU.mult), reads=[b_xt[s_], b_s[1], b_const], writes=[b_xt[s_]])
                    P.op("pool", DMA(xdst[rows, :], xt[:, s_, :]), reads=[b_xt[s_]], dma=True)
            P.barrier()

        for seq in range(NSEQ if stop_after != "P" else 0):
            for layer in range(2):
                xsrc = x_in[seq] if layer == 0 else X1
                xdst = X1 if layer == 0 else y_out[seq]
                pass_A(layer, xsrc)
                if stop_after == "A":
                    break
                pass_B(layer)
                if stop_after == "B":
                    break
                pass_C(layer, seq, xsrc, xdst, layer == 1)
                if stop_after == "C0":
                    break
        P.barrier()
        P.finalize()
        build.stats = {e: len(P.ops[e]) for e in ENGS}
        P.emit(nc, st)
    return nc


_CACHE = {}


def kernel(x_prompt, x_sample, mem_prompt, mem_sample, **w):
    S = x_prompt.shape[1]
    xs = np.concatenate([np.asarray(x_prompt, np.float32), np.asarray(x_sample, np.float32)], axis=0)
    ms = np.concatenate([np.asarray(mem_prompt, np.float32), np.asarray(mem_sample, np.float32)], axis=0)
    ntot = xs.shape[0]
    nseq = ntot // N_CORES
    key = (nseq, S)
    if key not in _CACHE:
        _CACHE[key] = build(nseq, S)
    nc = _CACHE[key]
    consts = host_consts(S)
    wd = {n: np.ascontiguousarray(np.asarray(w[n], np.float32)) for n in WNAMES}
    in_maps = []
    for c in range(N_CORES):
        m = {"x": np.ascontiguousarray(xs[c * nseq:(c + 1) * nseq]), "mem": np.ascontiguousarray(ms[c * nseq:(c + 1) * nseq])}
        m.update(wd)
        m.update(consts)
        in_maps.append(m)
    res = run_bass_kernel_spmd(nc, in_maps, core_ids=list(range(N_CORES)))
    ys = np.concatenate([np.asarray(r["y"], np.float32) for r in res.results], axis=0)
    nb = x_prompt.shape[0]
    return (ys[:nb], ys[nb:])
```
